# Optimizing a Trainium2 kernel written in Bass

```python
import jax
import jax.numpy as jnp
from jax import lax
import numpy as np

D_MODEL = 1024
BATCH = 8
SEQ = 4096
DEPTH = 2

GRID_W = 64
CTX_LEN = 256
MIX_W = D_MODEL
HEAD_DIM = 64
N_HEADS = MIX_W // HEAD_DIM
W_LORA = 64
A_LORA = 64
G_LORA = 128
LNX_EPS = 64e-5
LRU_BLOCKS = N_HEADS
LRU_BLOCK_W = MIX_W // LRU_BLOCKS
LRU_C = 8.0
LRU_CONV = 4
NA_KH = 8
NA_KW = 16
NA_QROWS = 2
NA_QCOLS = 16
NA_KCOLS = NA_QCOLS + NA_KW
ROPE_BASE = 10000.0
D_FF = 2816
FFN_CONV = 3
EPS = 1e-6
NEG_INF = -1e30
A_COLS = 3 * MIX_W + W_LORA + A_LORA + G_LORA
B_COLS = 2 * MIX_W
C_COLS = 3 * MIX_W
G_COLS = 3 * D_MODEL
N_IN = A_COLS + B_COLS + C_COLS + G_COLS

kernel_name = 'hybrid_rwkv7_rglru_natten_dit'


def rms_norm(x, g):
    xf = x.astype(jnp.float32)
    y = xf * lax.rsqrt(jnp.mean(xf * xf, axis=-1, keepdims=True) + EPS)
    return (y * g.astype(jnp.float32)).astype(x.dtype)


def split_cols(u, sizes):
    return jnp.split(u, [int(s) for s in np.cumsum(sizes)[:-1]], axis=-1)


def dwconv(u, w, left):
    width, T = w.shape[0], u.shape[1]
    up = jnp.pad(u, ((0, 0), (left, width - 1 - left), (0, 0)))
    out = w[0] * up[:, 0:T]
    for j in range(1, width):
        out = out + w[j] * up[:, j:j + T]
    return out


def token_shift(u, mu_prev, mu_next):
    zero = jnp.zeros_like(u[:, :1])
    u_prev = jnp.concatenate([zero, u[:, :-1]], axis=1)
    u_next = jnp.concatenate([u[:, 1:], zero], axis=1)
    return u + mu_prev * (u_prev - u) + mu_next * (u_next - u)


def wkv_scan(state, r, decay, k, v, kk, a, reverse):
    def step(S, inp):
        r_t, w_t, k_t, v_t, kk_t, a_t = inp
        sa = jnp.einsum('bhvk,bhk->bhv', S, -kk_t)
        S = (S * w_t[:, :, None, :] + sa[..., None] * (kk_t * a_t)[:, :, None, :]
             + v_t[..., None] * k_t[:, :, None, :])
        return S, jnp.einsum('bhvk,bhk->bhv', S, r_t)
    xs = tuple(jnp.swapaxes(t.astype(jnp.float32), 0, 1) for t in (r, decay, k, v, kk, a))
    state, ys = lax.scan(step, state, xs, reverse=reverse)
    return jnp.swapaxes(ys, 0, 1), state


def rwkv7_prepare(u, p):
    B, T, _ = u.shape
    heads = lambda t: t.reshape(B, T, N_HEADS, HEAD_DIM)
    u = token_shift(u, p['rwkv_mu'][0], p['rwkv_mu'][1])
    r, k, v, w_lo, a_lo, g_lo = split_cols(u, [MIX_W, MIX_W, MIX_W, W_LORA, A_LORA, G_LORA])
    kk = heads(k * p['rwkv_k_k']).astype(jnp.float32)
    kk = kk / jnp.maximum(jnp.sqrt(jnp.sum(kk * kk, axis=-1, keepdims=True)), 1e-12)
    w_lo = jnp.tanh(w_lo)
    per_dir = []
    for d in range(2):
        w_log = -jax.nn.softplus(-(p['rwkv_w0'][d] + w_lo @ p['rwkv_w_up'][d])) - 0.5
        decay = jnp.exp(-jnp.exp(w_log.astype(jnp.float32)))
        a = jax.nn.sigmoid(p['rwkv_a0'][d] + a_lo @ p['rwkv_a_up'][d])
        k_d = k * (1.0 + (a - 1.0) * p['rwkv_k_a'])
        per_dir.append((heads(decay), heads(k_d), heads(a)))
    g = jax.nn.sigmoid(g_lo) @ p['rwkv_g_up']
    return heads(r), heads(v), kk, per_dir, g


def rwkv7_output(y, r, v, k_f, k_b, g, p):
    B, T = y.shape[:2]
    mean = jnp.mean(y, axis=-1, keepdims=True)
    var = jnp.mean(jnp.square(y - mean), axis=-1, keepdims=True)
    yn = ((y - mean) * lax.rsqrt(var + LNX_EPS)).reshape(B, T, MIX_W)
    yn = (yn * p['rwkv_lnx_w'] + p['rwkv_lnx_b']).astype(v.dtype)
    bonus = jnp.sum(r * (k_f + k_b) * p['rwkv_r_k'], axis=-1, keepdims=True) * v
    return (yn + bonus.reshape(B, T, MIX_W)) * g


def rwkv7_branch(u_lat, u_ctx, p):
    r_l, v_l, kk_l, dirs_l, g_l = rwkv7_prepare(u_lat, p)
    r_c, v_c, kk_c, dirs_c, g_c = rwkv7_prepare(u_ctx, p)
    B = u_lat.shape[0]
    ys_l, ys_c = [], []
    for d in range(2):
        rev = d == 1
        s0 = jnp.zeros((B, N_HEADS, HEAD_DIM, HEAD_DIM), jnp.float32)
        dec_c, k_c, a_c = dirs_c[d]
        y_c, s_ctx = wkv_scan(s0, r_c, dec_c, k_c, v_c, kk_c, a_c, rev)
        dec_l, k_l, a_l = dirs_l[d]
        y_l, _ = wkv_scan(s_ctx, r_l, dec_l, k_l, v_l, kk_l, a_l, rev)
        ys_c.append(y_c)
        ys_l.append(y_l)
    out_l = rwkv7_output(ys_l[0] + ys_l[1], r_l, v_l, dirs_l[0][1], dirs_l[1][1], g_l, p)
    out_c = rwkv7_output(ys_c[0] + ys_c[1], r_c, v_c, dirs_c[0][1], dirs_c[1][1], g_c, p)
    return out_l, out_c


def lin_combine(e1, e2):
    a1, b1 = e1
    a2, b2 = e2
    return a1 * a2, a2 * b1 + b2


def linear_scan(a, b, h0, reverse):
    a_cum, b_cum = lax.associative_scan(lin_combine, (a, b), axis=1, reverse=reverse)
    h = a_cum * h0[:, None, :] + b_cum
    return h, (h[:, 0] if reverse else h[:, -1])


def rglru_prepare(u, p):
    xb, gb = split_cols(u, [MIX_W, MIX_W])
    xb = dwconv(xb, p['lru_conv_w'], LRU_CONV // 2) + p['lru_conv_b']
    return xb, jax.nn.gelu(gb)


def rglru_coeffs(xb, p, d):
    B, T, _ = xb.shape
    xg = xb.reshape(B, T, LRU_BLOCKS, LRU_BLOCK_W)
    r = jax.nn.sigmoid(jnp.einsum('btgi,gij->btgj', xg, p['lru_gate_a_w'][d]).reshape(B, T, MIX_W)
                       + p['lru_gate_a_b'][d])
    i = jax.nn.sigmoid(jnp.einsum('btgi,gij->btgj', xg, p['lru_gate_x_w'][d]).reshape(B, T, MIX_W)
                       + p['lru_gate_x_b'][d])
    log_a = -LRU_C * r.astype(jnp.float32) * jax.nn.softplus(-p['lru_lambda'][d].astype(jnp.float32))
    a = jnp.exp(log_a)
    b = jnp.sqrt(1.0 - a * a) * (i * xb).astype(jnp.float32)
    return a, b


def rglru_branch(u_lat, u_ctx, p):
    x_l, gate_l = rglru_prepare(u_lat, p)
    x_c, gate_c = rglru_prepare(u_ctx, p)
    B = u_lat.shape[0]
    h_l, h_c = [], []
    for d in range(2):
        rev = d == 1
        a_c, b_c = rglru_coeffs(x_c, p, d)
        hc, h_end = linear_scan(a_c, b_c, jnp.zeros((B, MIX_W), jnp.float32), rev)
        a_l, b_l = rglru_coeffs(x_l, p, d)
        hl, _ = linear_scan(a_l, b_l, h_end, rev)
        h_c.append(hc)
        h_l.append(hl)
    y_l = (h_l[0] + h_l[1]).astype(u_lat.dtype) * gate_l
    y_c = (h_c[0] + h_c[1]).astype(u_ctx.dtype) * gate_c
    return y_l, y_c


def rope_2d(x, rows, cols):
    half = HEAD_DIM // 2
    freqs = ROPE_BASE ** (-jnp.arange(0, half, 2, dtype=jnp.float32) / half)

    def rot(xp, pos):
        ang = pos.astype(jnp.float32)[:, None] * freqs[None, :]
        cos = jnp.cos(ang)[:, None, :].astype(x.dtype)
        sin = jnp.sin(ang)[:, None, :].astype(x.dtype)
        x1, x2 = xp[..., :half // 2], xp[..., half // 2:]
        return jnp.concatenate([x1 * cos - x2 * sin, x1 * sin + x2 * cos], axis=-1)
    return jnp.concatenate([rot(x[..., :half], rows), rot(x[..., half:], cols)], axis=-1)


def neighbourhood_attention(q, k, v, q_plain, k_ctx, v_ctx, rpb):
    B, H, rows, W, dh = q.shape
    kh = min(NA_KH, rows)
    kbh = min(kh + NA_QROWS - 1, rows)
    ncb = W // NA_QCOLS
    nq = NA_QROWS * NA_QCOLS
    nloc = kbh * NA_KCOLS
    scale = dh ** -0.5
    qcol = np.arange(W).reshape(ncb, NA_QCOLS)
    kcol = (np.clip(np.arange(ncb) * NA_QCOLS - NA_KW // 2, 0, W - NA_KCOLS)[:, None]
            + np.arange(NA_KCOLS)[None, :])
    cs = np.clip(qcol - NA_KW // 2, 0, W - NA_KW)
    col_ok = (kcol[:, None, :] >= cs[:, :, None]) & (kcol[:, None, :] < cs[:, :, None] + NA_KW)
    dcol = np.clip(kcol[:, None, :] - qcol[:, :, None] + NA_KW - 1, 0, 2 * NA_KW - 2)

    def to_blocks(t):
        t = t.reshape(B, H, NA_QROWS, ncb, NA_QCOLS, dh).transpose(0, 1, 3, 2, 4, 5)
        return t.reshape(B, H, ncb, nq, dh)

    def band(t, kr):
        t = lax.dynamic_slice_in_dim(t, kr, kbh, axis=2)[:, :, :, kcol]
        return t.transpose(0, 1, 3, 2, 4, 5).reshape(B, H, ncb, nloc, dh)

    def row_block(blk):
        r0 = blk * NA_QROWS
        kr = jnp.clip(r0 - kh // 2, 0, rows - kbh)
        qb = to_blocks(lax.dynamic_slice_in_dim(q, r0, NA_QROWS, axis=2))
        qpb = to_blocks(lax.dynamic_slice_in_dim(q_plain, r0, NA_QROWS, axis=2))
        kb = band(k, kr)
        vb = band(v, kr)
        qrow = r0 + jnp.arange(NA_QROWS)
        krow = kr + jnp.arange(kbh)
        rs = jnp.clip(qrow - kh // 2, 0, rows - kh)
        row_ok = (krow[None, :] >= rs[:, None]) & (krow[None, :] < rs[:, None] + kh)
        drow = jnp.clip(krow[None, :] - qrow[:, None] + NA_KH - 1, 0, 2 * NA_KH - 2)
        ok = (row_ok[None, :, None, :, None] & col_ok[:, None, :, None, :]).reshape(ncb, nq, nloc)
        bias = rpb[:, drow[None, :, None, :, None], dcol[:, None, :, None, :]].reshape(H, ncb, nq, nloc)
        s_loc = jnp.einsum('bhnqd,bhnkd->bhnqk', qb, kb).astype(jnp.float32) * scale + bias.astype(jnp.float32)
        s_loc = jnp.where(ok, s_loc, NEG_INF)
        s_ctx = jnp.einsum('bhnqd,bhld->bhnql', qpb, k_ctx).astype(jnp.float32) * scale
        prob = jax.nn.softmax(jnp.concatenate([s_loc, s_ctx], axis=-1), axis=-1).astype(v.dtype)
        out = (jnp.einsum('bhnqk,bhnkd->bhnqd', prob[..., :nloc], vb)
               + jnp.einsum('bhnql,bhld->bhnqd', prob[..., nloc:], v_ctx))
        out = out.reshape(B, H, ncb, NA_QROWS, NA_QCOLS, dh).transpose(0, 1, 3, 2, 4, 5)
        return out.reshape(B, H, NA_QROWS, W, dh)

    out = lax.map(row_block, jnp.arange(rows // NA_QROWS))
    return out.transpose(1, 2, 0, 3, 4, 5).reshape(B, H, rows, W, dh)


def na_branch(u_lat, u_ctx, p, with_ctx):
    B, T, _ = u_lat.shape
    rows = T // GRID_W
    L = u_ctx.shape[1]
    q, k, v = [t.reshape(B, T, N_HEADS, HEAD_DIM) for t in split_cols(u_lat, [MIX_W] * 3)]
    qc, kc, vc = [t.reshape(B, L, N_HEADS, HEAD_DIM).transpose(0, 2, 1, 3)
                  for t in split_cols(u_ctx, [MIX_W] * 3)]
    pos = jnp.arange(T)
    prow, pcol = pos // GRID_W, pos % GRID_W
    grid = lambda t: t.reshape(B, rows, GRID_W, N_HEADS, HEAD_DIM).transpose(0, 3, 1, 2, 4)
    y = neighbourhood_attention(grid(rope_2d(q, prow, pcol)), grid(rope_2d(k, prow, pcol)), grid(v),
                                grid(q), kc, vc, p['na_rpb'])
    y_lat = y.transpose(0, 2, 3, 1, 4).reshape(B, T, MIX_W)
    if not with_ctx:
        return y_lat, None
    s = jnp.einsum('bhqd,bhkd->bhqk', qc, kc).astype(jnp.float32) * HEAD_DIM ** -0.5
    prob = jax.nn.softmax(s, axis=-1).astype(vc.dtype)
    y_ctx = jnp.einsum('bhqk,bhkd->bhqd', prob, vc).transpose(0, 2, 1, 3).reshape(B, L, MIX_W)
    return y_lat, y_ctx


def merge_branches(u_gate, y_a, y_b, y_c, p):
    g_a, g_b, g_c = jnp.split(jax.nn.sigmoid(u_gate), 3, axis=-1)
    w_br = p['w_branch']
    m = g_a * (y_a @ w_br[0]) + g_b * (y_b @ w_br[1]) + g_c * (y_c @ w_br[2])
    return m @ p['w_out']


def conv_ffn(h, p):
    gate, up = jnp.split(h @ p['ffn_w_in'], 2, axis=-1)
    gate = dwconv(gate, p['ffn_conv_w'], FFN_CONV // 2) + p['ffn_conv_b']
    return (jax.nn.silu(gate) * up) @ p['ffn_w_out']


def hybrid_layer(x, xc, mod, mod_c, p, update_ctx):
    sh1, sc1, g1, sh2, sc2, g2 = jnp.split(mod, 6, axis=-1)
    sh1c, sc1c, g1c, sh2c, sc2c, g2c = jnp.split(mod_c, 6, axis=-1)
    ng = p['norm_g']
    h = rms_norm(x, ng[0]) * (1.0 + sc1) + sh1
    hc = rms_norm(xc, ng[0]) * (1.0 + sc1c) + sh1c
    u_a, u_b, u_c, u_g = split_cols(h @ p['w_in'], [A_COLS, B_COLS, C_COLS, G_COLS])
    u_ac, u_bc, u_cc, u_gc = split_cols(hc @ p['w_in'], [A_COLS, B_COLS, C_COLS, G_COLS])
    y_a, y_ac = rwkv7_branch(u_a, u_ac, p)
    y_b, y_bc = rglru_branch(u_b, u_bc, p)
    y_c, y_cc = na_branch(u_c, u_cc, p, update_ctx)
    x = x + g1 * rms_norm(merge_branches(u_g, y_a, y_b, y_c, p), ng[1])
    h = rms_norm(x, ng[2]) * (1.0 + sc2) + sh2
    x = x + g2 * rms_norm(conv_ffn(h, p), ng[3])
    if update_ctx:
        xc = xc + g1c * rms_norm(merge_branches(u_gc, y_ac, y_bc, y_cc, p), ng[1])
        hc = rms_norm(xc, ng[2]) * (1.0 + sc2c) + sh2c
        xc = xc + g2c * rms_norm(conv_ffn(hc, p), ng[3])
    return x, xc


def setup_inputs(seed: int = 0) -> dict:
    key = jax.random.key(seed)
    ks = iter(jax.random.split(key, 48))
    f32 = jnp.float32
    L = DEPTH

    def nrm(shape, scale):
        return jax.random.normal(next(ks), shape, f32) * scale

    def unif(shape, lo, hi):
        return jax.random.uniform(next(ks), shape, f32, lo, hi)

    lam_u = unif((L, 2, MIX_W), 0.9, 0.999)
    return {
        'x': nrm((BATCH, SEQ, D_MODEL), 1.0),
        'c': nrm((BATCH, D_MODEL), 1.0),
        'ctx': nrm((BATCH, CTX_LEN, D_MODEL), 1.0),
        'c_ctx': nrm((D_MODEL,), 1.0),
        'ada_w': nrm((L, D_MODEL, 6 * D_MODEL), 0.5 * D_MODEL ** -0.5),
        'ada_b': nrm((L, 6 * D_MODEL), 0.01),
        'norm_g': 1.0 + nrm((L, 4, D_MODEL), 0.05),
        'w_in': nrm((L, D_MODEL, N_IN), D_MODEL ** -0.5),
        'rwkv_mu': unif((L, 2, A_COLS), 0.0, 0.45),
        'rwkv_w0': unif((L, 2, MIX_W), -5.0, 1.0),
        'rwkv_w_up': nrm((L, 2, W_LORA, MIX_W), 0.5 * W_LORA ** -0.5),
        'rwkv_a0': nrm((L, 2, MIX_W), 0.1),
        'rwkv_a_up': nrm((L, 2, A_LORA, MIX_W), 0.5 * A_LORA ** -0.5),
        'rwkv_g_up': nrm((L, G_LORA, MIX_W), G_LORA ** -0.5),
        'rwkv_k_k': 0.85 + nrm((L, MIX_W), 0.05),
        'rwkv_k_a': 1.0 + nrm((L, MIX_W), 0.05),
        'rwkv_r_k': nrm((L, N_HEADS, HEAD_DIM), 0.1),
        'rwkv_lnx_w': 1.0 + nrm((L, MIX_W), 0.05),
        'rwkv_lnx_b': nrm((L, MIX_W), 0.01),
        'lru_conv_w': nrm((L, LRU_CONV, MIX_W), 0.5),
        'lru_conv_b': nrm((L, MIX_W), 0.01),
        'lru_gate_a_w': nrm((L, 2, LRU_BLOCKS, LRU_BLOCK_W, LRU_BLOCK_W), LRU_BLOCK_W ** -0.5),
        'lru_gate_a_b': nrm((L, 2, MIX_W), 0.01),
        'lru_gate_x_w': nrm((L, 2, LRU_BLOCKS, LRU_BLOCK_W, LRU_BLOCK_W), LRU_BLOCK_W ** -0.5),
        'lru_gate_x_b': nrm((L, 2, MIX_W), 0.01),
        'lru_lambda': jnp.log(lam_u) - jnp.log1p(-lam_u),
        'na_rpb': nrm((L, N_HEADS, 2 * NA_KH - 1, 2 * NA_KW - 1), 0.1),
        'w_branch': nrm((L, 3, MIX_W, D_MODEL), MIX_W ** -0.5),
        'w_out': nrm((L, D_MODEL, D_MODEL), D_MODEL ** -0.5),
        'ffn_w_in': nrm((L, D_MODEL, 2 * D_FF), D_MODEL ** -0.5),
        'ffn_conv_w': nrm((L, FFN_CONV, D_FF), FFN_CONV ** -0.5),
        'ffn_conv_b': nrm((L, D_FF), 0.01),
        'ffn_w_out': nrm((L, D_FF, D_MODEL), D_FF ** -0.5),
    }


def reference(x, c, ctx, c_ctx, ada_w, ada_b, norm_g, w_in, rwkv_mu, rwkv_w0, rwkv_w_up, rwkv_a0,
              rwkv_a_up, rwkv_g_up, rwkv_k_k, rwkv_k_a, rwkv_r_k, rwkv_lnx_w, rwkv_lnx_b, lru_conv_w,
              lru_conv_b, lru_gate_a_w, lru_gate_a_b, lru_gate_x_w, lru_gate_x_b, lru_lambda, na_rpb,
              w_branch, w_out, ffn_w_in, ffn_conv_w, ffn_conv_b, ffn_w_out):
    xc = ctx
    for l in range(DEPTH):
        p = {
            'norm_g': norm_g[l], 'w_in': w_in[l],
            'rwkv_mu': rwkv_mu[l], 'rwkv_w0': rwkv_w0[l], 'rwkv_w_up': rwkv_w_up[l],
            'rwkv_a0': rwkv_a0[l], 'rwkv_a_up': rwkv_a_up[l], 'rwkv_g_up': rwkv_g_up[l],
            'rwkv_k_k': rwkv_k_k[l], 'rwkv_k_a': rwkv_k_a[l], 'rwkv_r_k': rwkv_r_k[l],
            'rwkv_lnx_w': rwkv_lnx_w[l], 'rwkv_lnx_b': rwkv_lnx_b[l],
            'lru_conv_w': lru_conv_w[l], 'lru_conv_b': lru_conv_b[l],
            'lru_gate_a_w': lru_gate_a_w[l], 'lru_gate_a_b': lru_gate_a_b[l],
            'lru_gate_x_w': lru_gate_x_w[l], 'lru_gate_x_b': lru_gate_x_b[l],
            'lru_lambda': lru_lambda[l], 'na_rpb': na_rpb[l],
            'w_branch': w_branch[l], 'w_out': w_out[l],
            'ffn_w_in': ffn_w_in[l], 'ffn_conv_w': ffn_conv_w[l], 'ffn_conv_b': ffn_conv_b[l],
            'ffn_w_out': ffn_w_out[l],
        }
        mod = (jax.nn.silu(c) @ ada_w[l] + ada_b[l])[:, None, :]
        mod_c = jax.nn.silu(c_ctx) @ ada_w[l] + ada_b[l]
        x, xc = hybrid_layer(x, xc, mod, mod_c, p, l < DEPTH - 1)
    return x
```

```python
import contextlib
import numpy as np
import concourse.bass as bass
import concourse.mybir as mybir
from concourse.bass_utils import run_bass_kernel_spmd

F32 = mybir.dt.float32
BF16 = mybir.dt.bfloat16
AF = mybir.ActivationFunctionType
ALU = mybir.AluOpType
AX = mybir.AxisListType

D = 1024
KC = 8
LCTX = 256
DFF = 2816
NHEAD = 16
A_COLS = 3 * 1024 + 256
N_IN = 11520
NIN_X = N_IN + 2048
ENGS = ("pe", "act", "dve", "pool", "sp")
NDMA_SEMS = 24


class Buf:
    __slots__ = ("name", "w", "readers")

    def __init__(self, name):
        self.name = name
        self.w = None
        self.readers = []


def mkbufs(name, n):
    return [Buf(f"{name}{i}") for i in range(n)]


class Emit:
    def __init__(self, nc, stack):
        self.nc = nc
        self.prog = {e: [] for e in ENGS}
        self.count = {e: 0 for e in ENGS}
        self.seen = {e: {} for e in ENGS}
        self.dma_total = [0] * NDMA_SEMS
        self.dma_rr = 0
        self.n_instr = 0
        self.sems = {}
        for e in ENGS:
            self.sems[e] = stack.enter_context(nc.semaphore(f"c_{e}"))
        for k in range(NDMA_SEMS):
            self.sems[("dma", k)] = stack.enter_context(nc.semaphore(f"d_{k}"))

    def _deps(self, eng, reads, writes):
        deps = {}

        def add(d):
            if d is None:
                return
            k, v = d
            if deps.get(k, 0) < v:
                deps[k] = v
        for b in reads:
            add(b.w)
        for b in writes:
            add(b.w)
            for r in b.readers:
                add(r)
        waits = []
        seen = self.seen[eng]
        for k, v in deps.items():
            if seen.get(k, 0) < v:
                seen[k] = v
                waits.append((k, v))
        return waits

    def _commit(self, me, reads, writes):
        for b in reads:
            b.readers.append(me)
            if len(b.readers) > 32:
                mx = {}
                for k, v in b.readers:
                    if mx.get(k, 0) < v:
                        mx[k] = v
                b.readers = list(mx.items())
        for b in writes:
            b.w = me
            b.readers = []

    def op(self, eng, fn, reads=(), writes=()):
        waits = self._deps(eng, reads, writes)
        self.count[eng] += 1
        me = (eng, self.count[eng])
        self.prog[eng].append((waits, fn, (eng, 1)))
        self._commit(me, reads, writes)
        self.n_instr += 1 + len(waits)

    def dma(self, q, fn, reads=(), writes=()):
        k = self.dma_rr
        self.dma_rr = (self.dma_rr + 1) % NDMA_SEMS
        key = ("dma", k)
        waits = self._deps(q, reads, writes)
        prev = self.dma_total[k]
        if prev > 0 and self.seen[q].get(key, 0) < prev:
            self.seen[q][key] = prev
            waits.append((key, prev))
        self.dma_total[k] += 16
        me = (key, self.dma_total[k])
        self.prog[q].append((waits, fn, (key, 16)))
        self._commit(me, reads, writes)
        self.n_instr += 1 + len(waits)

    def flush(self):
        nc = self.nc
        prog = self.prog
        sems = self.sems
        dma_fin = [(("dma", k), v) for k, v in enumerate(self.dma_total) if v > 0]

        def run(name, eng):
            for waits, fn, inc in prog[name]:
                for k, v in waits:
                    eng.wait_ge(sems[k], v)
                fn(eng).then_inc(sems[inc[0]], inc[1])
            if name in ("sp", "pool", "act"):
                for k, v in dma_fin:
                    eng.wait_ge(sems[k], v)

        with nc.Block() as block:
            @block.tensor
            def _(t):
                run("pe", t)

            @block.scalar
            def _(a):
                run("act", a)

            @block.vector
            def _(v):
                run("dve", v)

            @block.gpsimd
            def _(g):
                run("pool", g)

            @block.sync
            def _(s):
                run("sp", s)
        for k, v in dma_fin:
            for e in ENGS:
                self.seen[e][k] = v
        self.prog = {e: [] for e in ENGS}


class Cfg:
    def __init__(self, T=4096, depth=2, dbg=False):
        self.T = T
        self.NT = LCTX + T
        self.depth = depth
        self.dbg = dbg
        self.phases = ("lru", "na", "rwkv", "merge", "ffn")
        self.tiles = [(0, LCTX, 1)] + [(LCTX + 512 * i, 512, 0) for i in range(T // 512)]


class Builder:
    def __init__(self, cfg):
        self.cfg = cfg
        self.nc = bass.Bass("TRN2", target_bir_lowering=False)
        self.dram = {}

    def din(self, name, shape, dt=F32):
        t = self.nc.dram_tensor(name, list(shape), dt, kind="ExternalInput").ap()
        self.dram[name] = t
        return t

    def dscratch(self, name, shape, dt=F32, out=False):
        kind = "ExternalOutput" if (out or self.cfg.dbg) else "Internal"
        t = self.nc.dram_tensor(name, list(shape), dt, kind=kind).ap()
        self.dram[name] = t
        return t

    def sb(self, st, name, shape, dt):
        self._uid = getattr(self, "_uid", 0) + 1
        return st.enter_context(self.nc.sbuf_tensor(f"{name}_{self._uid}", list(shape), dt))

    def ps(self, st, name, shape, dt=F32):
        return st.enter_context(self.nc.psum_tensor(name, list(shape), dt))

    def build(self, upto=99):
        cfg = self.cfg
        nc = self.nc
        NT, T = cfg.NT, cfg.T
        Ld = cfg.depth
        self.NCH = NIN_X // 128
        for name, shape in input_shapes(cfg).items():
            self.din(name, shape)
        self.dscratch("uT", [NIN_X, NT])
        self.dscratch("yA", [D, NT], BF16)
        self.dscratch("yB", [D, NT], BF16)
        self.dscratch("yC", [D, NT], BF16)
        self.dscratch("aT", [DFF, NT], BF16)
        self.dscratch("xcur", [D, NT])
        self.dscratch("xmid", [D, NT])
        self.dscratch("outT", [D, T], out=True)
        dr = self.dram
        with contextlib.ExitStack() as outer:
            em = Emit(nc, outer)
            self.em = em
            mod = self.sb(outer, "mod", [128, 48, 2], F32)
            ones_bf = self.sb(outer, "ones_bf", [128, 128], BF16)
            b_mod = Buf("mod")
            b_ones = Buf("ones")
            self.mod, self.b_mod, self.ones_bf, self.b_ones = mod, b_mod, ones_bf, b_ones
            em.op("pool", lambda e: e.memset(ones_bf[:], 1.0), writes=[b_ones])
            psb = [self.ps(outer, f"psb{i}", [128, 512]) for i in range(8)]
            b_ps = mkbufs("ps", 8)
            self.psb, self.b_ps = psb, b_ps
            for l in range(Ld):
                xsrc = dr["xc"] if l == 0 else dr["xcur"]
                self.phase_mod(l, dr["cvec"], dr["ada_w"], dr["ada_b"], mod, b_mod)
                self.phase_norm_inproj(l, xsrc, dr["norm_g"], dr["w_in"], dr["uT"], mod, b_mod, ones_bf, b_ones)
                if upto <= 2:
                    break
                if "lru" in cfg.phases:
                    self.phase_lru(l)
                if "na" in cfg.phases:
                    self.phase_na(l)
                if "rwkv" in cfg.phases:
                    self.phase_rwkv(l)
                if "merge" in cfg.phases:
                    self.phase_merge_ffn(l, xsrc)
        return nc

    def mm(self, out, lhsT, rhs, start=True, stop=True, r=(), w=()):
        self.em.op("pe", lambda e: e.matmul(out, lhsT=lhsT, rhs=rhs, start=start, stop=stop), r, w)

    def act(self, out, in_, func, r=(), w=(), scale=1.0, bias=0.0):
        self.em.op("act", lambda e: e.activation(out=out, in_=in_, func=func, scale=scale, bias=bias), r, w)

    def tt(self, eng, out, in0, in1, op, r=(), w=()):
        self.em.op(eng, lambda e: e.tensor_tensor(out=out, in0=in0, in1=in1, op=op), r, w)

    def ts(self, eng, out, in0, s1, s2, op0, op1=None, r=(), w=()):
        if op1 is None:
            self.em.op(eng, lambda e: e.tensor_scalar(out=out, in0=in0, scalar1=s1, scalar2=None, op0=op0), r, w)
        else:
            self.em.op(eng, lambda e: e.tensor_scalar(out=out, in0=in0, scalar1=s1, scalar2=s2, op0=op0, op1=op1), r, w)

    def stt(self, out, in0, sc, in1, op0, op1, r=(), w=()):
        self.em.op("dve", lambda e: e.scalar_tensor_tensor(out=out, in0=in0, scalar=sc, in1=in1, op0=op0, op1=op1), r, w)

    def cp(self, eng, out, in_, r=(), w=()):
        self.em.op(eng, lambda e: e.tensor_copy(out=out, in_=in_), r, w)

    def memset(self, eng, ap, val, w=()):
        self.em.op(eng, lambda e: e.memset(ap, val), (), w)

    def scan(self, out, d0, d1, init, r=(), w=()):
        self.em.op("dve", lambda e: e.tensor_tensor_scan(out=out, data0=d0, data1=d1, initial=init, op0=ALU.mult, op1=ALU.add), r, w)

    def dma(self, q, out, in_, r=(), w=()):
        self.em.dma(q, lambda e: e.dma_start(out=out, in_=in_), r, w)

    def segs(self):
        return [(0, 2, LCTX), (LCTX, LCTX + 4, self.cfg.T)]

    def phase_lru(self, l):
        cfg = self.cfg
        NT, T = cfg.NT, cfg.T
        psb, b_ps = self.psb, self.b_ps
        dr = self.dram
        uT, yB = dr["uT"], dr["yB"]
        B0 = A_COLS
        with contextlib.ExitStack() as st:
            cw = self.sb(st, "l_cw", [128, 8, 4], F32)
            cb = self.sb(st, "l_cb", [128, 8], F32)
            gab = self.sb(st, "l_gab", [128, 2, 2, 8], F32)
            lam = self.sb(st, "l_lam", [128, 2, 8], F32)
            cl = self.sb(st, "l_cl", [128, 2, 8], F32)
            b_par, b_cl = Buf("lpar"), Buf("lcl")
            self.dma("sp", cw[:], dr["lru_conv_w"][l], w=[b_par])
            self.dma("sp", cb[:], dr["lru_conv_b"][l], w=[b_par])
            self.dma("sp", gab[:], dr["lru_gate_b"][l], w=[b_par])
            self.dma("sp", lam[:], dr["lru_lambda"][l], w=[b_par])
            self.act(cl[:], lam[:], AF.Exp, [b_par], [b_cl], scale=-1.0)
            self.act(cl[:], cl[:], AF.Ln, [b_cl], [b_cl], bias=1.0)
            self.ts("dve", cl[:], cl[:], -8.0, None, ALU.mult, r=[b_cl], w=[b_cl])
            xp = self.sb(st, "l_xp", [128, NT + 6], F32)
            xb = self.sb(st, "l_xb", [128, NT], F32)
            xbb = self.sb(st, "l_xbb", [128, NT], BF16)
            gt = self.sb(st, "l_gt", [128, NT], F32)
            gtb = self.sb(st, "l_gtb", [128, NT], BF16)
            A = self.sb(st, "l_A", [128, NT], F32)
            Bt = self.sb(st, "l_B", [128, NT], F32)
            Ct = self.sb(st, "l_C", [128, NT], F32)
            hf = self.sb(st, "l_hf", [128, NT], F32)
            ys = self.sb(st, "l_ys", [128, NT], BF16)
            wgf = [self.sb(st, f"l_wgf{i}", [128, 128], F32) for i in range(2)]
            wgb = [self.sb(st, f"l_wgb{i}", [128, 128], BF16) for i in range(2)]
            b_xp, b_xb, b_xbb, b_gt, b_gtb, b_A, b_B, b_C, b_hf, b_ys = [Buf(n) for n in
                "xp xb xbb gt gtb A B C hf ys".split()]
            b_wgf, b_wgb = mkbufs("wgf", 2), mkbufs("wgb", 2)
            self.memset("pool", xp[:], 0.0, w=[b_xp])
            for i in range(2):
                self.memset("pool", wgf[i][:], 0.0, w=[b_wgf[i]])
            gw = [dr["lru_gate_a_w"], dr["lru_gate_x_w"]]

            def rev(ap):
                aps = [list(p) for p in ap.ap]
                n, stp = aps[-1][1], aps[-1][0]
                aps[-1] = [-stp, n]
                return bass.AP(ap.tensor, ap.offset + stp * (n - 1), aps)
            k = 0
            for j in range(8):
                for (d0, s0, n) in self.segs():
                    self.dma("sp", xp[:, s0:s0 + n], uT[B0 + j * 128:B0 + (j + 1) * 128, d0:d0 + n], w=[b_xp])
                self.dma("sp", gt[:], uT[B0 + 1024 + j * 128:B0 + 1024 + (j + 1) * 128, :], w=[b_gt])
                for (d0, s0, n) in self.segs():
                    self.ts("dve", xb[:, d0:d0 + n], xp[:, s0 - 2:s0 - 2 + n], cw[:, j, 0:1], cb[:, j:j + 1], ALU.mult, ALU.add,
                            r=[b_xp, b_par], w=[b_xb])
                    for tap in range(1, 4):
                        self.stt(xb[:, d0:d0 + n], xp[:, s0 - 2 + tap:s0 - 2 + tap + n], cw[:, j, tap:tap + 1], xb[:, d0:d0 + n],
                                 ALU.mult, ALU.add, r=[b_xp, b_par, b_xb], w=[b_xb])
                self.cp("pool", xbb[:], xb[:], r=[b_xb], w=[b_xbb])
                self.act(gtb[:], gt[:], AF.Gelu_apprx_tanh, [b_gt], [b_gtb])
                for d in range(2):
                    for g in range(2):
                        for hb in range(2):
                            self.dma("sp", wgf[g][hb * 64:(hb + 1) * 64, hb * 64:(hb + 1) * 64], gw[g][l, d, 2 * j + hb],
                                     w=[b_wgf[g]])
                        self.cp("pool", wgb[g][:], wgf[g][:], r=[b_wgf[g]], w=[b_wgb[g]])
                    for g, (dst, b_dst) in enumerate([(A, b_A), (Bt, b_B)]):
                        for (t0, n, seg) in cfg.tiles:
                            pb = k % 8
                            k += 1
                            self.mm(psb[pb][:, 0:n], wgb[g][:], xbb[:, t0:t0 + n], r=[b_wgb[g], b_xbb], w=[b_ps[pb]])
                            self.act(dst[:, t0:t0 + n], psb[pb][:, 0:n], AF.Sigmoid, [b_ps[pb], b_par], [b_dst],
                                     bias=gab[:, g, d, j:j + 1])
                    self.act(A[:], A[:], AF.Exp, [b_A, b_cl], [b_A], scale=cl[:, d, j:j + 1])
                    self.tt("dve", Ct[:], A[:], A[:], ALU.mult, r=[b_A], w=[b_C])
                    self.act(Ct[:], Ct[:], AF.Sqrt, [b_C], [b_C], scale=-1.0, bias=1.0)
                    self.tt("pool", Bt[:], Bt[:], xb[:], ALU.mult, r=[b_B, b_xb], w=[b_B])
                    self.tt("dve", Ct[:], Ct[:], Bt[:], ALU.mult, r=[b_C, b_B], w=[b_C])
                    if d == 0:
                        self.scan(hf[:], A[:], Ct[:], 0.0, r=[b_A, b_C], w=[b_hf])
                    else:
                        for (d0, s0, n) in self.segs():
                            self.cp("pool", Bt[:, d0:d0 + n], rev(A[:, d0:d0 + n]), r=[b_A], w=[b_B])
                            self.cp("pool", gt[:, d0:d0 + n], rev(Ct[:, d0:d0 + n]), r=[b_C], w=[b_gt])
                        self.scan(A[:], Bt[:], gt[:], 0.0, r=[b_B, b_gt, b_A], w=[b_A])
                        for (d0, s0, n) in self.segs():
                            self.cp("pool", Ct[:, d0:d0 + n], rev(A[:, d0:d0 + n]), r=[b_A], w=[b_C])
                        self.tt("dve", hf[:], hf[:], Ct[:], ALU.add, r=[b_hf, b_C], w=[b_hf])
                self.tt("dve", ys[:], hf[:], gtb[:], ALU.mult, r=[b_hf, b_gtb], w=[b_ys])
                self.dma("pool", yB[j * 128:(j + 1) * 128, :], ys[:], r=[b_ys])
            self.em.flush()

    def phase_na(self, l):
        cfg = self.cfg
        NT, T = cfg.NT, cfg.T
        psb, b_ps = self.psb, self.b_ps
        dr = self.dram
        uT, yC = dr["uT"], dr["yC"]
        update_ctx = (l < cfg.depth - 1) or getattr(cfg, 'force_ctx', False)
        rows = T // 64
        blocks, ntype = na_blocks(rows)
        NTB = NT // 128
        CQ, CK, CV, CQP, CKP = 42, 50, 58, 90, 98
        rp = dr["rpb_pad"]
        with contextlib.ExitStack() as st:
            ident = self.sb(st, "n_ident", [128, 128], F32)
            onesp = self.sb(st, "n_onesp", [128, 2, 128], BF16)
            maskb = self.sb(st, "n_maskb", [128, ntype, 512], BF16)
            mtmp = [self.sb(st, f"n_mtmp{i}", [128, 512], F32) for i in range(2)]
            b_id, b_op, b_mk = Buf("ident"), Buf("onesp"), Buf("maskb")
            b_mt = mkbufs("mtmp", 2)
            self.dma("sp", ident[:], dr["ident"][:, :], w=[b_id])
            self.memset("pool", onesp[:], 0.0, w=[b_op])
            self.memset("pool", onesp[:, 0, 0:64], 1.0, w=[b_op])
            self.memset("pool", onesp[:, 1, 64:128], 1.0, w=[b_op])
            for t in range(ntype):
                self.dma("sp", mtmp[t % 2][:], dr["na_mask"][t], w=[b_mt[t % 2]])
                self.cp("pool", maskb[:, t, :], mtmp[t % 2][:], r=[b_mt[t % 2]], w=[b_mk])
            NIN = 7
            tl = [[self.sb(st, f"n_tl{a}_{i}", [128, 512], F32) for i in range(2)] for a in range(NIN)]
            b_tl = [mkbufs(f"tl{a}_", 2) for a in range(NIN)]
            qpl = self.sb(st, "n_qpl", [128, NT], BF16)
            kpl = self.sb(st, "n_kpl", [128, LCTX], BF16)
            qrot = self.sb(st, "n_qrot", [128, T], BF16)
            krot = self.sb(st, "n_krot", [128, T], BF16)
            Vp = self.sb(st, "n_Vp", [128, NTB, 2, 128], BF16)
            Tc2 = self.sb(st, "n_Tc2", [128, 22 * 64], F32)
            biasd = self.sb(st, "n_biasd", [128, 8, 512], F32)
            bm = [self.sb(st, f"n_bm{i}", [128, ntype, 512], BF16) for i in range(2)]
            sT = [self.sb(st, f"n_sT{i}", [128, 512], F32) for i in range(2)]
            pT = [self.sb(st, f"n_pT{i}", [128, 512], BF16) for i in range(3)]
            rc = [self.sb(st, f"n_rc{i}", [128, 512], F32) for i in range(2)]
            yst = self.sb(st, "n_yst", [128, NT], BF16)
            b_qpl, b_kpl, b_qrot, b_krot, b_Vp, b_Tc2, b_biasd, b_yst = [Buf(n) for n in
                "qpl kpl qrot krot Vp Tc2 biasd yst".split()]
            b_bm, b_sT, b_pT, b_rc = mkbufs("bm", 2), mkbufs("sT", 2), mkbufs("pT", 3), mkbufs("rc", 2)
            self.memset("pool", Vp[:], 0.0, w=[b_Vp])
            self.memset("pool", yst[:], 0.0, w=[b_yst])

            def rev(ap):
                aps = [list(p) for p in ap.ap]
                n, stp = aps[-1][1], aps[-1][0]
                aps[-1] = [-stp, n]
                return bass.AP(ap.tensor, ap.offset + stp * (n - 1), aps)
            kq = 0
            ks = 0
            kp_ = 0
            kacc = 0
            for j in range(8):
                for ti, (t0, n, seg) in enumerate(cfg.tiles):
                    s = ti % 2
                    rowsrc = [CQ + j, CQP + j, CK + j, CKP + j, CV + j]
                    need = [0, 2, 4] if seg == 1 else [0, 1, 2, 3, 4]
                    for a in need:
                        c = rowsrc[a]
                        self.dma("sp", tl[a][s][:, 0:n], uT[c * 128:(c + 1) * 128, t0:t0 + n], w=[b_tl[a][s]])
                    if seg == 1:
                        self.act(qpl[:, t0:t0 + n], tl[0][s][:, 0:n], AF.Copy, [b_tl[0][s]], [b_qpl])
                        self.act(kpl[:, 0:n], tl[2][s][:, 0:n], AF.Copy, [b_tl[2][s]], [b_kpl])
                    else:
                        lt0 = t0 - LCTX
                        self.dma("sp", tl[5][s][:], dr["rope_cos"][:, lt0:lt0 + 512], w=[b_tl[5][s]])
                        self.dma("sp", tl[6][s][:], dr["rope_sin"][:, lt0:lt0 + 512], w=[b_tl[6][s]])
                        self.act(qpl[:, t0:t0 + n], tl[0][s][:], AF.Copy, [b_tl[0][s]], [b_qpl])
                        for (a, ap_, dst, b_dst) in [(0, 1, qrot, b_qrot), (2, 3, krot, b_krot)]:
                            self.tt("dve", tl[a][s][:], tl[a][s][:], tl[5][s][:], ALU.mult,
                                    r=[b_tl[a][s], b_tl[5][s]], w=[b_tl[a][s]])
                            self.tt("pool", tl[ap_][s][:], tl[ap_][s][:], tl[6][s][:], ALU.mult,
                                    r=[b_tl[ap_][s], b_tl[6][s]], w=[b_tl[ap_][s]])
                            self.tt("dve", dst[:, lt0:lt0 + 512], tl[a][s][:], tl[ap_][s][:], ALU.add,
                                    r=[b_tl[a][s], b_tl[ap_][s]], w=[b_dst])
                    nb = n // 128
                    pb = kq % 4
                    kq += 1
                    for q in range(nb):
                        self.em.op("pe", lambda e, pb=pb, q=q, s=s: e.transpose(
                            psb[pb][:, q * 128:(q + 1) * 128], tl[4][s][:, q * 128:(q + 1) * 128], ident[:]),
                            [b_tl[4][s], b_id], [b_ps[pb]])
                    tb0 = t0 // 128
                    pv = psb[pb][:, 0:nb * 128].rearrange("p (a b) -> p a b", b=128)
                    self.cp("dve", Vp[:, tb0:tb0 + nb, 0, 0:64], pv[:, :, 0:64], r=[b_ps[pb]], w=[b_Vp])
                    self.act(Vp[:, tb0:tb0 + nb, 1, 64:128], pv[:, :, 64:128], AF.Copy, [b_ps[pb]], [b_Vp])
                for hh in range(2):
                    h = 2 * j + hh
                    for krl in range(2):
                        base = ((l * 16 + h) * 24 + krl) * 128
                        src = bass.AP(rp.tensor, rp.offset + base, [[1, 64], [128, 22], [1, 64]])
                        self.dma("sp", Tc2[krl * 64:(krl + 1) * 64, :].rearrange("p (a b) -> p a b", b=64), src, w=[b_Tc2])
                    for di in range(8):
                        self.cp("pool", biasd[:, di, :], rev(Tc2[:, di * 128:di * 128 + 512]), r=[b_Tc2], w=[b_biasd])
                    for qb, items in enumerate(blocks):
                        for (kr0, ty, di) in items:
                            if ty is not None:
                                self.tt("pool", bm[hh][:, ty, :], biasd[:, di, :], maskb[:, ty, :], ALU.add,
                                        r=[b_biasd, b_mk], w=[b_bm[hh]])
                for qb, items in enumerate(blocks):
                    a1, a2 = 4 + 2 * (kacc % 2), 5 + 2 * (kacc % 2)
                    kacc += 1
                    q0t = qb * 512
                    work = []
                    for hh in range(2):
                        for (kr0, ty, di) in items:
                            work.append((hh, "loc", kr0, blocks_type(blocks, qb, kr0)))
                        for cc in range(2):
                            work.append((hh, "ctx", cc, None))
                    for wi, (hh, kind, a, ty) in enumerate(work):
                        hb = hh * 64
                        pb = kq % 4
                        kq += 1
                        s2 = kp_ % 3
                        kp_ += 1
                        if kind == "loc":
                            ktok = a * 64
                            self.mm(psb[pb][:, :], krot[hb:hb + 64, ktok:ktok + 128], qrot[hb:hb + 64, q0t:q0t + 512],
                                    r=[b_krot, b_qrot], w=[b_ps[pb]])
                            s1 = ks % 2
                            ks += 1
                            self.stt(sT[s1][:], psb[pb][:, :], 0.125, bm[hh][:, ty, :], ALU.mult, ALU.add,
                                     r=[b_ps[pb], b_bm[hh]], w=[b_sT[s1]])
                            self.act(pT[s2][:], sT[s1][:], AF.Exp, [b_sT[s1]], [b_pT[s2]])
                            vch = 2 + a // 2
                        else:
                            self.mm(psb[pb][:, :], kpl[hb:hb + 64, a * 128:(a + 1) * 128],
                                    qpl[hb:hb + 64, LCTX + q0t:LCTX + q0t + 512], r=[b_kpl, b_qpl], w=[b_ps[pb]])
                            self.act(pT[s2][:], psb[pb][:, :], AF.Exp, [b_ps[pb]], [b_pT[s2]], scale=0.125)
                            vch = a
                        first, last = (wi == 0), (wi == len(work) - 1)
                        self.mm(psb[a1][:, :], Vp[:, vch, hh, :], pT[s2][:], start=first, stop=last,
                                r=[b_Vp, b_pT[s2]], w=[b_ps[a1]])
                        self.mm(psb[a2][:, :], onesp[:, hh, :], pT[s2][:], start=first, stop=last,
                                r=[b_op, b_pT[s2]], w=[b_ps[a2]])
                    s3 = kacc % 2
                    self.em.op("dve", lambda e, s3=s3, a2=a2: e.reciprocal(out=rc[s3][:], in_=psb[a2][:, :]),
                               [b_ps[a2]], [b_rc[s3]])
                    self.tt("dve", yst[:, LCTX + q0t:LCTX + q0t + 512], psb[a1][:, :], rc[s3][:], ALU.mult,
                            r=[b_ps[a1], b_rc[s3]], w=[b_yst])
                if update_ctx:
                    a1, a2 = 4 + 2 * (kacc % 2), 5 + 2 * (kacc % 2)
                    kacc += 1
                    work = [(hh, cc) for hh in range(2) for cc in range(2)]
                    for wi, (hh, cc) in enumerate(work):
                        hb = hh * 64
                        pb = kq % 4
                        kq += 1
                        s2 = kp_ % 3
                        kp_ += 1
                        self.mm(psb[pb][:, 0:LCTX], kpl[hb:hb + 64, cc * 128:(cc + 1) * 128], qpl[hb:hb + 64, 0:LCTX],
                                r=[b_kpl, b_qpl], w=[b_ps[pb]])
                        self.act(pT[s2][:, 0:LCTX], psb[pb][:, 0:LCTX], AF.Exp, [b_ps[pb]], [b_pT[s2]], scale=0.125)
                        first, last = (wi == 0), (wi == len(work) - 1)
                        self.mm(psb[a1][:, 0:LCTX], Vp[:, cc, hh, :], pT[s2][:, 0:LCTX], start=first, stop=last,
                                r=[b_Vp, b_pT[s2]], w=[b_ps[a1]])
                        self.mm(psb[a2][:, 0:LCTX], onesp[:, hh, :], pT[s2][:, 0:LCTX], start=first, stop=last,
                                r=[b_op, b_pT[s2]], w=[b_ps[a2]])
                    s3 = kacc % 2
                    self.em.op("dve", lambda e, s3=s3, a2=a2: e.reciprocal(out=rc[s3][:, 0:LCTX], in_=psb[a2][:, 0:LCTX]),
                               [b_ps[a2]], [b_rc[s3]])
                    self.tt("dve", yst[:, 0:LCTX], psb[a1][:, 0:LCTX], rc[s3][:, 0:LCTX], ALU.mult,
                            r=[b_ps[a1], b_rc[s3]], w=[b_yst])
                self.dma("pool", yC[j * 128:(j + 1) * 128, :], yst[:], r=[b_yst])
            self.em.flush()

    def phase_rwkv(self, l):
        cfg = self.cfg
        NT, T = cfg.NT, cfg.T
        psb, b_ps = self.psb, self.b_ps
        dr = self.dram
        uT, yA = dr["uT"], dr["yA"]
        NCK = NT // 64
        CW = 0.6065306597126334
        SEGC = 16
        SEGN = SEGC * 64
        segs = [(0, 4)] + [(4 + 16 * i, 16) for i in range((NCK - 4) // 16)]
        with contextlib.ExitStack() as st:
            sb = lambda name, shape, dt=F32: self.sb(st, "r_" + name, shape, dt)
            cst_f = sb("cst_f", [128, 1024])
            ident_bf = sb("ident_bf", [128, 128], BF16)
            bdones = sb("bdones", [128, 128], BF16)
            istack = sb("istack", [128, 64], BF16)
            rkm = sb("rkm", [128, 2, 1024], BF16)
            cmask = sb("cmask", [128, SEGN])
            b_cst, b_k = Buf("cst"), Buf("rk_consts")
            for (dst, src, n) in [(ident_bf, dr["ident"], 128), (bdones, dr["bdones"], 128), (istack, dr["istack"], 64)]:
                self.dma("sp", cst_f[:, 0:n], src[:, :], w=[b_cst])
                self.cp("dve", dst[:], cst_f[:, 0:n], r=[b_cst], w=[b_k])
            for d in range(2):
                self.dma("sp", cst_f[:], dr["rk_mask"][d], w=[b_cst])
                self.cp("dve", rkm[:, d, :], cst_f[:], r=[b_cst], w=[b_k])
            self.memset("pool", cmask[:], 1.0, w=[b_k])
            self.memset("pool", cmask[:].rearrange("p (c s) -> p c s", s=64)[:, :, 0:1], 0.0, w=[b_k])
            mu = sb("mu", [128, 26, 2])
            c0 = sb("c0", [128, 26])
            w0a0 = sb("w0a0", [128, 2, 2, 8])
            vec = sb("vec", [128, 5, 8])
            omk = sb("omk", [128, 8])
            b_par = Buf("rpar")
            self.dma("sp", mu[:], dr["rwkv_mu"][l], w=[b_par])
            self.dma("sp", w0a0[:], dr["rwkv_w0a0"][l], w=[b_par])
            self.dma("sp", vec[:], dr["rwkv_vec"][l], w=[b_par])
            self.ts("dve", c0[:], mu[:, :, 0], -1.0, 1.0, ALU.mult, ALU.add, r=[b_par], w=[b_par])
            self.tt("dve", c0[:], c0[:], mu[:, :, 1], ALU.subtract, r=[b_par], w=[b_par])
            self.ts("dve", omk[:], vec[:, 1, :], -1.0, 1.0, ALU.mult, ALU.add, r=[b_par], w=[b_par])
            wst = sb("wst", [128, 1024])
            WA = sb("WA", [128, 2, 1024])
            GU = sb("GU", [128, 1024], BF16)
            b_wst, b_W = Buf("wst"), Buf("WA")
            for d in range(2):
                self.dma("sp", wst[0:64, :], dr["rwkv_w_up"][l, d], w=[b_wst])
                self.dma("sp", wst[64:128, :], dr["rwkv_a_up"][l, d], w=[b_wst])
                self.cp("pool", WA[:, d, :], wst[:], r=[b_wst], w=[b_W])
            self.dma("sp", wst[:], dr["rwkv_g_up"][l], w=[b_wst])
            self.cp("pool", GU[:], wst[:], r=[b_wst], w=[b_W])
            LW = sb("LW", [128, NT])
            GL = sb("GL", [128, NT], BF16)
            Yacc = sb("Yacc", [128, NCK, 64])
            ksum = sb("ksum", [128, NT])
            b_LW, b_GL, b_Yacc, b_ksum = Buf("LW"), Buf("GL"), Buf("Yacc"), Buf("ksum")
            xp = sb("xp", [128, SEGN + 2])
            rT, kT, vT, kap = sb("rT", [128, SEGN]), sb("kT", [128, SEGN]), sb("vT", [128, SEGN]), sb("kap", [128, SEGN])
            T1, T2, T3, T4 = [sb(f"T{i}", [128, SEGN]) for i in range(1, 5)]
            ynT = sb("ynT", [128, SEGN])
            Vb, gTb, sqb = sb("Vb", [128, SEGN], BF16), sb("gTb", [128, SEGN], BF16), sb("sqb", [128, SEGN], BF16)
            stk = sb("stk", [128, 4, SEGN], BF16)
            YBD = sb("YBD", [128, SEGC, 128], BF16)
            yst = sb("yst", [128, SEGN], BF16)
            gC = sb("gC", [128, SEGC])
            lnst = sb("lnst", [128, 6, SEGC])
            b_xp, b_rT, b_kT, b_vT, b_kap, b_T1, b_T2, b_T3, b_T4, b_ynT, b_Vb, b_gTb, b_sqb, b_stk, b_YBD, b_yst, b_gC, b_ln = [
                Buf(n) for n in "xp rT kT vT kap T1 T2 T3 T4 ynT Vb gTb sqb stk YBD yst gC lnst".split()]
            self.memset("pool", YBD[:], 0.0, w=[b_YBD])
            G = 4
            BDg = [sb(f"BDg{i}", [128, G, 5, 128], BF16) for i in range(2)]
            b_BDg = mkbufs("BDg", 2)
            for i in range(2):
                self.memset("pool", BDg[i][:], 0.0, w=[b_BDg[i]])
            SA = [sb(f"SA{i}", [128, 384], BF16) for i in range(2)]
            SB_ = [sb(f"SB{i}", [128, 512], BF16) for i in range(2)]
            SC = [sb(f"SC{i}", [128, 192], BF16) for i in range(2)]
            MN = [sb(f"MN{i}", [128, 256], BF16) for i in range(2)]
            Rb = [sb(f"Rb{i}", [128, 256], BF16) for i in range(2)]
            Pb = [sb(f"Pb{i}", [128, 128], BF16) for i in range(2)]
            MO = [sb(f"MO{i}", [128, 128], BF16) for i in range(2)]
            b_Pb, b_MO = mkbufs("Pb", 2), mkbufs("MO", 2)
            QP = [sb(f"QP{i}", [128, 256], BF16) for i in range(2)]
            AK = [sb(f"AK{i}", [128, 256], BF16) for i in range(2)]
            ST = sb("ST", [128, 64], BF16)
            S32 = sb("S32", [128, 64])
            S32g = sb("S32g", [128, 64])
            b_S32, b_S32g = Buf("S32"), Buf("S32g")
            b_SA, b_SB, b_SC, b_MN, b_Rb, b_QP, b_AK = [mkbufs(n, 2) for n in "SA SB SC MN Rb QP AK".split()]
            b_ST = Buf("ST")
            kps = [0]

            def shift(dst, b_dst, c, c0_, n):
                p0 = c0_ * 64
                p1 = p0 + n
                hasL = p0 not in (0, LCTX)
                hasR = p1 not in (LCTX, NT)
                if not hasL:
                    self.memset("pool", xp[:, 0:1], 0.0, w=[b_xp])
                if not hasR:
                    self.memset("pool", xp[:, n + 1:n + 2], 0.0, w=[b_xp])
                lo, hi = p0 - int(hasL), p1 + int(hasR)
                self.dma("sp", xp[:, 1 - int(hasL):1 + n + int(hasR)], uT[c * 128:(c + 1) * 128, lo:hi], w=[b_xp])
                self.ts("dve", dst[:, 0:n], xp[:, 1:1 + n], c0[:, c:c + 1], None, ALU.mult, r=[b_xp, b_par], w=[b_dst])
                self.stt(dst[:, 0:n], xp[:, 0:n], mu[:, c, 0:1], dst[:, 0:n], ALU.mult, ALU.add, r=[b_xp, b_par, b_dst], w=[b_dst])
                self.stt(dst[:, 0:n], xp[:, 2:2 + n], mu[:, c, 1:2], dst[:, 0:n], ALU.mult, ALU.add, r=[b_xp, b_par, b_dst], w=[b_dst])

            def tiles_of(n):
                return [(o, min(512, n - o)) for o in range(0, n, 512)]

            def nextps():
                kps[0] += 1
                return 7

            for (ck0, nck) in segs:
                n = nck * 64
                t0 = ck0 * 64
                shift(T1, b_T1, 24, ck0, n)
                self.act(LW[0:64, t0:t0 + n], T1[0:64, 0:n], AF.Tanh, [b_T1], [b_LW])
                self.act(LW[64:128, t0:t0 + n], T1[64:128, 0:n], AF.Copy, [b_T1], [b_LW])
                shift(T2, b_T2, 25, ck0, n)
                self.act(GL[:, t0:t0 + n], T2[:, 0:n], AF.Sigmoid, [b_T2], [b_GL])

            kbd = [0]
            for j in range(8):
                for d in range(2):
                    self.memset("pool", ST[:], 0.0, w=[b_ST])
                    self.memset("pool", S32[:], 0.0, w=[b_S32])
                    order = segs if d == 0 else [segs[0]] + segs[1:][::-1]
                    for (ck0, nck) in order:
                        n = nck * 64
                        t0 = ck0 * 64
                        shift(rT, b_rT, j, ck0, n)
                        shift(kT, b_kT, 8 + j, ck0, n)
                        shift(vT, b_vT, 16 + j, ck0, n)
                        self.act(Vb[:, 0:n], vT[:, 0:n], AF.Copy, [b_vT], [b_Vb])
                        self.ts("dve", kap[:, 0:n], kT[:, 0:n], vec[:, 0, j:j + 1], None, ALU.mult, r=[b_kT, b_par], w=[b_kap])
                        self.act(sqb[:, 0:n], kap[:, 0:n], AF.Square, [b_kap], [b_sqb])
                        for (o, m) in tiles_of(n):
                            pb = nextps()
                            self.mm(psb[pb][:, 0:m], bdones[:], sqb[:, o:o + m], r=[b_k, b_sqb], w=[b_ps[pb]])
                            self.act(T1[:, o:o + m], psb[pb][:, 0:m], AF.Sqrt, [b_ps[pb]], [b_T1], bias=1e-24)
                        self.em.op("dve", lambda e, n=n: e.reciprocal(out=T1[:, 0:n], in_=T1[:, 0:n]), [b_T1], [b_T1])
                        self.tt("dve", kap[:, 0:n], kap[:, 0:n], T1[:, 0:n], ALU.mult, r=[b_kap, b_T1], w=[b_kap])
                        if d == 1:
                            for (o, m) in tiles_of(n):
                                pb = nextps()
                                self.mm(psb[pb][:, 0:m], GU[:, j * 128:(j + 1) * 128], GL[:, t0 + o:t0 + o + m],
                                        r=[b_W, b_GL], w=[b_ps[pb]])
                                self.act(gTb[:, o:o + m], psb[pb][:, 0:m], AF.Copy, [b_ps[pb]], [b_gTb])
                        for (o, m) in tiles_of(n):
                            pb = nextps()
                            self.mm(psb[pb][:, 0:m], WA[0:64, d, j * 128:(j + 1) * 128], LW[0:64, t0 + o:t0 + o + m],
                                    r=[b_W, b_LW], w=[b_ps[pb]])
                            self.act(T1[:, o:o + m], psb[pb][:, 0:m], AF.Sigmoid, [b_ps[pb], b_par], [b_T1],
                                     bias=w0a0[:, 0, d, j:j + 1])
                            pb = nextps()
                            self.mm(psb[pb][:, 0:m], WA[64:128, d, j * 128:(j + 1) * 128], LW[64:128, t0 + o:t0 + o + m],
                                    r=[b_W, b_LW], w=[b_ps[pb]])
                            self.act(T2[:, o:o + m], psb[pb][:, 0:m], AF.Sigmoid, [b_ps[pb], b_par], [b_T2],
                                     bias=w0a0[:, 1, d, j:j + 1])
                        self.scan(T3[:, 0:n], cmask[:, 0:n], T1[:, 0:n], 0.0, r=[b_k, b_T1], w=[b_T3])
                        T3v = T3[:, 0:n].rearrange("p (c s) -> p c s", s=64)
                        T1v = T1[:, 0:n].rearrange("p (c s) -> p c s", s=64)
                        self.act(gC[:, 0:nck], T3v[:, :, 63], AF.Exp, [b_T3], [b_gC], scale=-CW)
                        if d == 1:
                            self.cp("pool", lnst[:, 0, 0:nck], T3v[:, :, 63], r=[b_T3], w=[b_ln])
                            self.tt("dve", T3[:, 0:n], T1[:, 0:n], T3[:, 0:n], ALU.subtract, r=[b_T1, b_T3], w=[b_T3])
                            self.tt("dve", T3v, T3v, lnst[:, 0, 0:nck].unsqueeze(2).to_broadcast([128, nck, 64]), ALU.add,
                                    r=[b_T3, b_ln], w=[b_T3])
                        self.tt("dve", T1[:, 0:n], T3[:, 0:n], T1[:, 0:n], ALU.subtract, r=[b_T1, b_T3], w=[b_T1])
                        self.act(T1[:, 0:n], T1[:, 0:n], AF.Exp, [b_T1], [b_T1], scale=-CW)
                        self.tt("dve", stk[:, 0, 0:n], kap[:, 0:n], T1[:, 0:n], ALU.mult, r=[b_kap, b_T1], w=[b_stk])
                        self.act(T4[:, 0:n], T3[:, 0:n], AF.Exp, [b_T3], [b_T4], scale=CW)
                        self.act(T3[:, 0:n], T3[:, 0:n], AF.Exp, [b_T3], [b_T3], scale=-CW)
                        self.tt("pool", stk[:, 1, 0:n], rT[:, 0:n], T3[:, 0:n], ALU.mult, r=[b_rT, b_T3], w=[b_stk])
                        self.ts("dve", T1[:, 0:n], T2[:, 0:n], vec[:, 1, j:j + 1], omk[:, j:j + 1], ALU.mult, ALU.add,
                                r=[b_T2, b_par], w=[b_T1])
                        self.tt("dve", T1[:, 0:n], T1[:, 0:n], kT[:, 0:n], ALU.mult, r=[b_T1, b_kT], w=[b_T1])
                        if d == 0:
                            self.cp("pool", ksum[:, t0:t0 + n], T1[:, 0:n], r=[b_T1], w=[b_ksum])
                        else:
                            self.tt("pool", ksum[:, t0:t0 + n], ksum[:, t0:t0 + n], T1[:, 0:n], ALU.add, r=[b_T1, b_ksum], w=[b_ksum])
                        self.tt("dve", stk[:, 3, 0:n], T1[:, 0:n], T4[:, 0:n], ALU.mult, r=[b_T1, b_T4], w=[b_stk])
                        self.tt("pool", T2[:, 0:n], T2[:, 0:n], kap[:, 0:n], ALU.mult, r=[b_T2, b_kap], w=[b_T2])
                        self.tt("dve", stk[:, 2, 0:n], T2[:, 0:n], T4[:, 0:n], ALU.mult, r=[b_T2, b_T4], w=[b_stk])
                        corder = list(range(nck)) if d == 0 else list(range(nck))[::-1]
                        for gi in range(0, nck, G):
                            grp = corder[gi:gi + G]
                            cl = min(grp)
                            bg = kbd[0] % 2
                            kbd[0] += 1
                            for qi in range(5):
                                for hh in range(2):
                                    hs = slice(hh * 64, (hh + 1) * 64)
                                    src = (Vb[hs, cl * 64:(cl + G) * 64] if qi == 4 else stk[hs, qi, cl * 64:(cl + G) * 64])
                                    src = src.rearrange("p (c s) -> p c s", s=64)
                                    eng = ("pool", "act", "dve")[(qi * 2 + hh) % 3]
                                    if eng == "act":
                                        self.act(BDg[bg][hs, :, qi, hs], src, AF.Copy, [b_stk, b_Vb], [b_BDg[bg]])
                                    else:
                                        self.cp(eng, BDg[bg][hs, :, qi, hs], src, r=[b_stk, b_Vb], w=[b_BDg[bg]])
                            for ci in grp:
                                g = ci - cl
                                p = kps[0] % 2
                                kps[0] += 1
                                bd = lambda q: BDg[bg][:, g, q, :]
                                rhs01 = BDg[bg][:, g, 0:2, :]
                                rhs23 = BDg[bg][:, g, 2:4, :]
                                rB = [b_BDg[bg], b_k]
                                GA, GB, GCc, PS_, PA, Fb, SQ = psb[0], psb[1], psb[2], psb[3], psb[4], psb[5], psb[6]
                                self.mm(GA[:, 0:256], bd(2), rhs01, r=rB, w=[b_ps[0]])
                                self.mm(GA[:, 256:384], bd(2), ident_bf[:], r=rB, w=[b_ps[0]])
                                self.mm(GB[:, 0:128], bd(3), bd(1), r=rB, w=[b_ps[1]])
                                self.mm(GB[:, 128:256], bd(3), ident_bf[:], r=rB, w=[b_ps[1]])
                                self.mm(GB[:, 256:512], bd(0), rhs23, r=rB, w=[b_ps[1]])
                                self.mm(GCc[:, 0:128], bd(0), ident_bf[:], r=rB, w=[b_ps[2]])
                                self.mm(GCc[:, 128:192], bd(4), istack[:], r=rB, w=[b_ps[2]])
                                self.tt("dve", SA[p][:], GA[:, 0:384], rkm[:, d, 0:384], ALU.mult, r=[b_ps[0], b_k], w=[b_SA[p]])
                                self.tt("dve", SB_[p][:], GB[:, :], rkm[:, d, 384:896], ALU.mult, r=[b_ps[1], b_k], w=[b_SB[p]])
                                self.act(SC[p][:], GCc[:, 0:192], AF.Copy, [b_ps[2]], [b_SC[p]])
                                self.tt("dve", MO[p][:], GB[:, 256:384], rkm[:, d, 896:1024], ALU.mult, r=[b_ps[1], b_k], w=[b_MO[p]])
                                self.tt("dve", Pb[0][:], ident_bf[:], SB_[p][:, 256:384], ALU.subtract, r=[b_k, b_SB[p]], w=[b_Pb[0]])
                                Mc, Nc, bM, bN = SB_[p][:, 256:384], SA[p][:, 0:128], b_SB[p], b_SA[p]
                                for lev in range(4):
                                    lastl = lev == 3
                                    mi = lev % 2
                                    self.mm(PS_[:, 128:256], Mc, Nc, r=[bM, bN], w=[b_ps[3]])
                                    if not lastl:
                                        self.mm(PS_[:, 0:128], Nc, Mc, r=[bM, bN], w=[b_ps[3]])
                                    lo = 128 if lastl else 0
                                    self.act(MN[mi][:, lo:256], PS_[:, lo:256], AF.Copy, [b_ps[3]], [b_MN[mi]])
                                    Mc, Nc, bM, bN = MN[mi][:, 0:128], MN[mi][:, 128:256], b_MN[mi], b_MN[mi]
                                    self.mm(PA[:, 0:128], Nc, Pb[lev % 2][:], r=[bN, b_Pb[lev % 2]], w=[b_ps[4]])
                                    self.tt("dve", Pb[(lev + 1) % 2][:], Pb[lev % 2][:], PA[:, 0:128], ALU.add,
                                            r=[b_Pb[lev % 2], b_ps[4]], w=[b_Pb[(lev + 1) % 2]])
                                Wt, bWt = Pb[0], b_Pb[0]
                                self.mm(PA[:, 0:256], Wt[:], SA[p][:, 128:384], r=[bWt, b_SA[p]], w=[b_ps[4]])
                                self.act(Rb[0][:], PA[:, 0:256], AF.Copy, [b_ps[4]], [b_Rb[0]])
                                self.mm(PS_[:, 0:256], MO[p][:], Rb[0][:], r=[b_MO[p], b_Rb[0]], w=[b_ps[3]])
                                self.tt("dve", Rb[1][:], SA[p][:, 128:384], PS_[:, 0:256], ALU.subtract, r=[b_SA[p], b_ps[3]], w=[b_Rb[1]])
                                self.mm(PA[:, 0:256], Wt[:], Rb[1][:], r=[bWt, b_Rb[1]], w=[b_ps[4]])
                                self.act(Rb[0][:], PA[:, 0:256], AF.Copy, [b_ps[4]], [b_Rb[0]])
                                XZ, bXZ = Rb[0], b_Rb[0]
                                self.mm(Fb[:, 0:256], SC[p][:, 0:128], XZ[:], r=[b_SC[p], bXZ], w=[b_ps[5]])
                                self.mm(Fb[:, 256:512], SB_[p][:, 384:512], XZ[:], r=[b_SB[p], bXZ], w=[b_ps[5]])
                                self.tt("dve", QP[p][:, 0:128], bd(1), Fb[:, 0:128], ALU.subtract, r=[b_BDg[bg], b_ps[5]], w=[b_QP[p]])
                                self.ts("dve", QP[p][:, 128:256], Fb[:, 128:256], -1.0, None, ALU.mult, r=[b_ps[5]], w=[b_QP[p]])
                                self.tt("dve", AK[p][:], SB_[p][:, 0:256], Fb[:, 256:512], ALU.subtract, r=[b_SB[p], b_ps[5]], w=[b_AK[p]])
                                vm = SC[p][:, 128:192]
                                self.ts("pool", S32g[:], S32[:], gC[:, ci:ci + 1], None, ALU.mult, r=[b_S32, b_gC], w=[b_S32g])
                                self.mm(SQ[:, 0:64], QP[p][:, 0:128], ST[:], start=True, stop=False, r=[b_QP[p], b_ST], w=[b_ps[6]])
                                self.mm(SQ[:, 0:64], AK[p][:, 0:128], vm, start=False, stop=True, r=[b_AK[p], b_SC[p]], w=[b_ps[6]])
                                self.mm(SQ[:, 64:128], QP[p][:, 128:256], ST[:], start=True, stop=False, r=[b_QP[p], b_ST], w=[b_ps[6]])
                                self.mm(SQ[:, 64:128], AK[p][:, 128:256], vm, start=False, stop=True, r=[b_AK[p], b_SC[p]], w=[b_ps[6]])
                                ck = ck0 + ci
                                if d == 0:
                                    self.cp("dve", Yacc[:, ck, :], SQ[:, 0:64], r=[b_ps[6]], w=[b_Yacc])
                                else:
                                    self.tt("dve", Yacc[:, ck, :], Yacc[:, ck, :], SQ[:, 0:64], ALU.add, r=[b_ps[6], b_Yacc], w=[b_Yacc])
                                self.stt(S32[:], SQ[:, 64:128], gC[:, ci:ci + 1], S32g[:], ALU.mult, ALU.add,
                                         r=[b_ps[6], b_gC, b_S32g], w=[b_S32])
                                self.act(ST[:], S32[:], AF.Copy, [b_S32], [b_ST])
                        if d == 1:
                            Ys = Yacc[:, ck0:ck0 + nck, :]
                            T3v = T3[:, 0:n].rearrange("p (c s) -> p c s", s=64)
                            mean, ssq, m2, var = lnst[:, 1, 0:nck], lnst[:, 2, 0:nck], lnst[:, 3, 0:nck], lnst[:, 4, 0:nck]
                            self.em.op("dve", lambda e, Ys=Ys, mean=mean: e.tensor_reduce(out=mean, in_=Ys, axis=AX.X, op=ALU.add),
                                       [b_Yacc], [b_ln])
                            self.act(T3v, Ys, AF.Square, [b_Yacc], [b_T3])
                            self.em.op("dve", lambda e, T3v=T3v, ssq=ssq: e.tensor_reduce(out=ssq, in_=T3v, axis=AX.X, op=ALU.add),
                                       [b_T3], [b_ln])
                            self.ts("dve", mean, mean, 1.0 / 64, None, ALU.mult, r=[b_ln], w=[b_ln])
                            self.tt("dve", m2, mean, mean, ALU.mult, r=[b_ln], w=[b_ln])
                            self.stt(var, ssq, 1.0 / 64, m2, ALU.mult, ALU.subtract, r=[b_ln], w=[b_ln])
                            self.act(var, var, AF.Sqrt, [b_ln], [b_ln], bias=64e-5)
                            self.em.op("dve", lambda e, var=var: e.reciprocal(out=var, in_=var), [b_ln], [b_ln])
                            self.tt("dve", T3v, Ys, mean.unsqueeze(2).to_broadcast([128, nck, 64]), ALU.subtract,
                                    r=[b_Yacc, b_ln], w=[b_T3])
                            for hh in range(2):
                                hs = slice(hh * 64, (hh + 1) * 64)
                                self.tt("dve", YBD[hs, 0:nck, hs], T3v[hs], var[hs].unsqueeze(2).to_broadcast([64, nck, 64]), ALU.mult,
                                        r=[b_T3, b_ln], w=[b_YBD])
                            for c8 in range(0, nck, 8):
                                m8 = min(8, nck - c8)
                                pb = 7
                                for ci in range(c8, c8 + m8):
                                    self.mm(psb[pb][:, (ci - c8) * 64:(ci - c8 + 1) * 64], YBD[:, ci, :], istack[:],
                                            r=[b_YBD, b_k], w=[b_ps[pb]])
                                self.ts("dve", ynT[:, c8 * 64:(c8 + m8) * 64], psb[pb][:, 0:m8 * 64], vec[:, 3, j:j + 1], vec[:, 4, j:j + 1],
                                        ALU.mult, ALU.add, r=[b_ps[pb], b_par], w=[b_ynT])
                            self.tt("pool", T1[:, 0:n], rT[:, 0:n], ksum[:, t0:t0 + n], ALU.mult, r=[b_rT, b_ksum], w=[b_T1])
                            self.ts("dve", sqb[:, 0:n], T1[:, 0:n], vec[:, 2, j:j + 1], None, ALU.mult, r=[b_T1, b_par], w=[b_sqb])
                            for (o, m) in tiles_of(n):
                                pb = 7
                                self.mm(psb[pb][:, 0:m], bdones[:], sqb[:, o:o + m], r=[b_k, b_sqb], w=[b_ps[pb]])
                                self.tt("dve", T2[:, o:o + m], psb[pb][:, 0:m], Vb[:, o:o + m], ALU.mult, r=[b_ps[pb], b_Vb], w=[b_T2])
                            self.tt("dve", ynT[:, 0:n], ynT[:, 0:n], T2[:, 0:n], ALU.add, r=[b_ynT, b_T2], w=[b_ynT])
                            self.tt("dve", yst[:, 0:n], ynT[:, 0:n], gTb[:, 0:n], ALU.mult, r=[b_ynT, b_gTb], w=[b_yst])
                            self.dma("pool", yA[j * 128:(j + 1) * 128, t0:t0 + n], yst[:, 0:n], r=[b_yst])
            self.em.flush()

    def rstd_tile(self, src, b_src, n, sq, b_sq, rs, b_rs, pb):
        psb, b_ps = self.psb, self.b_ps
        self.act(sq[:, :, 0:n], src[:, :, 0:n], AF.Square, [b_src], [b_sq])
        for kc in range(KC):
            self.mm(psb[pb][:, 0:n], self.ones_bf[:], sq[:, kc, 0:n], start=(kc == 0), stop=(kc == KC - 1),
                    r=[b_sq, self.b_ones], w=[b_ps[pb]])
        self.act(rs[:, 0:n], psb[pb][:, 0:n], AF.Sqrt, [b_ps[pb]], [b_rs], scale=1.0 / D, bias=1e-6)
        self.em.op("dve", lambda e: e.reciprocal(out=rs[:, 0:n], in_=rs[:, 0:n]), [b_rs], [b_rs])

    def phase_merge_ffn(self, l, xsrc):
        cfg = self.cfg
        NT, T = cfg.NT, cfg.T
        psb, b_ps = self.psb, self.b_ps
        dr = self.dram
        mod, b_mod = self.mod, self.b_mod
        uT, xmid, aT = dr["uT"], dr["xmid"], dr["aT"]
        last = l == cfg.depth - 1
        TW = 256
        tiles = [(t0, TW, 1 if t0 < LCTX else 0) for t0 in range(0, NT, TW)]
        with contextlib.ExitStack() as st0:
            h2T = self.sb(st0, "m_h2T", [128, KC, NT], BF16)
            b_h2 = mkbufs("h2T", len(tiles))
            ng = self.sb(st0, "m_ng", [128, 4, KC], F32)
            cols = self.sb(st0, "m_cols", [128, 4, KC, 2], F32)
            wst = self.sb(st0, "m_wst", [128, 1024], F32)
            b_ng, b_cols, b_wst = Buf("ng"), Buf("cols"), Buf("wst")
            self.dma("sp", ng[:], dr["norm_g"][l], w=[b_ng])
            for kc in range(KC):
                self.ts("dve", cols[:, 0, kc, :], mod[:, 16 + kc, :], ng[:, 1, kc:kc + 1], None, ALU.mult, r=[b_mod, b_ng], w=[b_cols])
                self.ts("dve", cols[:, 1, kc, :], mod[:, 32 + kc, :], 1.0, ng[:, 2, kc:kc + 1], ALU.add, ALU.mult, r=[b_mod, b_ng], w=[b_cols])
                self.ts("dve", cols[:, 2, kc, :], mod[:, 40 + kc, :], ng[:, 3, kc:kc + 1], None, ALU.mult, r=[b_mod, b_ng], w=[b_cols])
            with contextlib.ExitStack() as st:
                sb = lambda name, shape, dt=F32: self.sb(st, "m_" + name, shape, dt)
                Wbr = sb("Wbr", [128, 3, KC, 1024], BF16)
                Wo = sb("Wo", [128, KC, 1024], BF16)
                b_W = Buf("Wm")
                for br in range(3):
                    for kc in range(KC):
                        self.dma("sp", wst[:], dr["w_branch"][l, br, kc * 128:(kc + 1) * 128, :], w=[b_wst])
                        self.cp("pool", Wbr[:, br, kc, :], wst[:], r=[b_wst], w=[b_W])
                for kc in range(KC):
                    self.dma("sp", wst[:], dr["w_out"][l, kc * 128:(kc + 1) * 128, :], w=[b_wst])
                    self.cp("pool", Wo[:, kc, :], wst[:], r=[b_wst], w=[b_W])
                yt = [sb(f"yt{i}", [128, KC, TW], BF16) for i in range(3)]
                gt = sb("gt", [128, KC, TW])
                mt = sb("mt", [128, KC, TW])
                tmp = sb("tmp", [128, TW])
                mb = sb("mb", [128, KC, TW], BF16)
                xt = sb("xt", [128, KC, TW])
                mo = sb("mo", [128, KC, TW])
                sq = sb("sq", [128, KC, TW], BF16)
                rs = sb("rs", [128, TW])
                b_yt = mkbufs("yt", 3)
                b_gt, b_mt, b_tmp, b_mb, b_xt, b_mo, b_sq, b_rs = [Buf(n) for n in "gt mt tmp mb xt mo sq rs".split()]
                ysrc = [dr["yA"], dr["yB"], dr["yC"]]
                xview = xsrc.rearrange("(kc p) n -> p kc n", p=128)
                xmview = xmid.rearrange("(kc p) n -> p kc n", p=128)
                k = 0
                for ti, (t0, n, seg) in enumerate(tiles):
                    self.dma("sp", xt[:], xview[:, :, t0:t0 + n], w=[b_xt])
                    for br in range(3):
                        self.dma("sp", yt[br][:], ysrc[br].rearrange("(kc p) n -> p kc n", p=128)[:, :, t0:t0 + n], w=[b_yt[br]])
                        g0 = (66 + br * 8) * 128
                        self.dma("sp", gt[:], uT[g0:g0 + 1024, :].rearrange("(kc p) n -> p kc n", p=128)[:, :, t0:t0 + n], w=[b_gt])
                        self.act(gt[:], gt[:], AF.Sigmoid, [b_gt], [b_gt])
                        for oc in range(KC):
                            pb = k % 6
                            k += 1
                            for kc in range(KC):
                                self.mm(psb[pb][:, 0:n], Wbr[:, br, kc, oc * 128:(oc + 1) * 128], yt[br][:, kc, :],
                                        start=(kc == 0), stop=(kc == KC - 1), r=[b_W, b_yt[br]], w=[b_ps[pb]])
                            if br == 0:
                                self.tt("dve", mt[:, oc, :], psb[pb][:, 0:n], gt[:, oc, :], ALU.mult, r=[b_ps[pb], b_gt], w=[b_mt])
                            else:
                                self.tt("dve", tmp[:], psb[pb][:, 0:n], gt[:, oc, :], ALU.mult, r=[b_ps[pb], b_gt], w=[b_tmp])
                                self.tt("pool", mt[:, oc, :], mt[:, oc, :], tmp[:], ALU.add, r=[b_mt, b_tmp], w=[b_mt])
                    self.act(mb[:], mt[:], AF.Copy, [b_mt], [b_mb])
                    for oc in range(KC):
                        pb = k % 6
                        k += 1
                        for kc in range(KC):
                            self.mm(psb[pb][:, 0:n], Wo[:, kc, oc * 128:(oc + 1) * 128], mb[:, kc, :],
                                    start=(kc == 0), stop=(kc == KC - 1), r=[b_W, b_mb], w=[b_ps[pb]])
                        self.act(mo[:, oc, :], psb[pb][:, 0:n], AF.Copy, [b_ps[pb]], [b_mo])
                    self.rstd_tile(mo, b_mo, n, sq, b_sq, rs, b_rs, 6)
                    for oc in range(KC):
                        self.tt("dve", mo[:, oc, :], mo[:, oc, :], rs[:], ALU.mult, r=[b_mo, b_rs], w=[b_mo])
                        self.stt(xt[:, oc, :], mo[:, oc, :], cols[:, 0, oc, seg:seg + 1], xt[:, oc, :], ALU.mult, ALU.add,
                                 r=[b_mo, b_cols, b_xt], w=[b_xt])
                    self.dma("pool", xmview[:, :, t0:t0 + n], xt[:], r=[b_xt])
                    self.rstd_tile(xt, b_xt, n, sq, b_sq, rs, b_rs, 7)
                    for kc in range(KC):
                        self.tt("dve", mo[:, kc, :], xt[:, kc, :], rs[:], ALU.mult, r=[b_xt, b_rs, b_mo], w=[b_mo])
                        self.act(h2T[:, kc, t0:t0 + n], mo[:, kc, :], AF.Identity, [b_mo, b_cols, b_mod], [b_h2[ti]],
                                 scale=cols[:, 1, kc, seg:seg + 1], bias=mod[:, 24 + kc, seg:seg + 1])
                self.em.flush()
            with contextlib.ExitStack() as st:
                sb = lambda name, shape, dt=F32: self.sb(st, "f_" + name, shape, dt)
                cw = sb("cw", [128, 22, 3])
                cb = sb("cb", [128, 22])
                b_par = Buf("fpar")
                self.dma("sp", cw[:], dr["ffn_conv_w"][l], w=[b_par])
                self.dma("sp", cb[:], dr["ffn_conv_b"][l], w=[b_par])
                wf = [sb(f"wf{i}", [128, KC, 128]) for i in range(2)]
                wb = [sb(f"wb{i}", [128, KC, 128], BF16) for i in range(2)]
                b_wf, b_wb = mkbufs("fwf", 2), mkbufs("fwb", 2)
                gp = sb("gp", [128, NT + 6])
                gc = sb("gc", [128, NT])
                ast = sb("ast", [128, NT], BF16)
                b_gp, b_gc, b_ast = Buf("gp"), Buf("gc"), Buf("ast")
                self.memset("pool", gp[:], 0.0, w=[b_gp])
                wview = dr["ffn_w_in"][l].rearrange("(kc p) n -> p kc n", p=128)
                k = 0
                kw = 0
                alltiles = list(enumerate(tiles))
                h2tiles = self.cfg.tiles
                for jc in range(22):
                    for part in range(2):
                        s = kw % 2
                        kw += 1
                        c0 = part * DFF + jc * 128
                        self.dma("sp", wf[s][:], wview[:, :, c0:c0 + 128], w=[b_wf[s]])
                        self.cp("pool", wb[s][:], wf[s][:], r=[b_wf[s]], w=[b_wb[s]])
                        for (t0, n, seg) in h2tiles:
                            pb = k % 8
                            k += 1
                            hb = [b_h2[i] for i, (a, m, sg_) in alltiles if a < t0 + n and a + m > t0]
                            for kc in range(KC):
                                self.mm(psb[pb][:, 0:n], wb[s][:, kc, :], h2T[:, kc, t0:t0 + n], start=(kc == 0), stop=(kc == KC - 1),
                                        r=[b_wb[s]] + hb, w=[b_ps[pb]])
                            if part == 0:
                                p0 = t0 + 2 if t0 < LCTX else t0 + 4
                                self.act(gp[:, p0:p0 + n], psb[pb][:, 0:n], AF.Copy, [b_ps[pb]], [b_gp])
                            else:
                                self.tt("dve", ast[:, t0:t0 + n], psb[pb][:, 0:n], gc[:, t0:t0 + n], ALU.mult,
                                        r=[b_ps[pb], b_gc], w=[b_ast])
                        if part == 0:
                            for (d0, s0, n) in self.segs():
                                self.ts("dve", gc[:, d0:d0 + n], gp[:, s0 - 1:s0 - 1 + n], cw[:, jc, 0:1], cb[:, jc:jc + 1], ALU.mult, ALU.add,
                                        r=[b_gp, b_par], w=[b_gc])
                                for tap in (1, 2):
                                    self.stt(gc[:, d0:d0 + n], gp[:, s0 - 1 + tap:s0 - 1 + tap + n], cw[:, jc, tap:tap + 1], gc[:, d0:d0 + n],
                                             ALU.mult, ALU.add, r=[b_gp, b_par, b_gc], w=[b_gc])
                            self.act(gc[:], gc[:], AF.Silu, [b_gc], [b_gc])
                    self.dma("pool", aT[jc * 128:(jc + 1) * 128, :], ast[:], r=[b_ast])
                self.em.flush()
        with contextlib.ExitStack() as st:
            sb = lambda name, shape, dt=F32: self.sb(st, "g_" + name, shape, dt)
            ng = sb("ng", [128, 4, KC])
            cols = sb("cols", [128, KC, 2])
            wst = sb("wst", [128, 1024])
            Wf = sb("Wf", [128, 22, 1024], BF16)
            b_ng, b_cols, b_wst, b_W = Buf("ng"), Buf("cols"), Buf("wst"), Buf("Wf")
            self.dma("sp", ng[:], dr["norm_g"][l], w=[b_ng])
            for kc in range(KC):
                self.ts("dve", cols[:, kc, :], mod[:, 40 + kc, :], ng[:, 3, kc:kc + 1], None, ALU.mult, r=[b_mod, b_ng], w=[b_cols])
            for kc in range(22):
                self.dma("sp", wst[:], dr["ffn_w_out"][l, kc * 128:(kc + 1) * 128, :], w=[b_wst])
                self.cp("pool", Wf[:, kc, :], wst[:], r=[b_wst], w=[b_W])
            at = [sb(f"at{i}", [128, 22, 512], BF16) for i in range(2)]
            xt = [sb(f"xt{i}", [128, KC, 512]) for i in range(2)]
            fo = sb("fo", [128, KC, 512])
            sq = sb("sq", [128, KC, 512], BF16)
            rs = sb("rs", [128, 512])
            b_at, b_xt = mkbufs("at", 2), mkbufs("gxt", 2)
            b_fo, b_sq, b_rs = Buf("fo"), Buf("gsq"), Buf("grs")
            xmview = xmid.rearrange("(kc p) n -> p kc n", p=128)
            aview = aT.rearrange("(kc p) n -> p kc n", p=128)
            k = 0
            for ti, (t0, n, seg) in enumerate(cfg.tiles):
                if last and seg == 1:
                    continue
                s = ti % 2
                self.dma("sp", at[s][:, :, 0:n], aview[:, :, t0:t0 + n], w=[b_at[s]])
                self.dma("sp", xt[s][:, :, 0:n], xmview[:, :, t0:t0 + n], w=[b_xt[s]])
                for oc in range(KC):
                    pb = k % 6
                    k += 1
                    for kc in range(22):
                        self.mm(psb[pb][:, 0:n], Wf[:, kc, oc * 128:(oc + 1) * 128], at[s][:, kc, 0:n], start=(kc == 0), stop=(kc == 21),
                                r=[b_W, b_at[s]], w=[b_ps[pb]])
                    self.act(fo[:, oc, 0:n], psb[pb][:, 0:n], AF.Copy, [b_ps[pb]], [b_fo])
                self.rstd_tile(fo, b_fo, n, sq, b_sq, rs, b_rs, 6 + ti % 2)
                for oc in range(KC):
                    self.tt("dve", fo[:, oc, 0:n], fo[:, oc, 0:n], rs[:, 0:n], ALU.mult, r=[b_fo, b_rs], w=[b_fo])
                    self.stt(xt[s][:, oc, 0:n], fo[:, oc, 0:n], cols[:, oc, seg:seg + 1], xt[s][:, oc, 0:n], ALU.mult, ALU.add,
                             r=[b_fo, b_cols, b_xt[s]], w=[b_xt[s]])
                if last:
                    dst = dr["outT"].rearrange("(kc p) n -> p kc n", p=128)[:, :, t0 - LCTX:t0 - LCTX + n]
                else:
                    dst = dr["xcur"].rearrange("(kc p) n -> p kc n", p=128)[:, :, t0:t0 + n]
                self.dma("pool", dst, xt[s][:, :, 0:n], r=[b_xt[s]])
            self.em.flush()

    def phase_mod(self, l, cvec, ada_w, ada_b, mod, b_mod):
        em = self.em
        psb, b_ps = self.psb, self.b_ps
        with contextlib.ExitStack() as st:
            cv = self.sb(st, "cv", [128, KC, 2], F32)
            cs = self.sb(st, "cs", [128, KC, 2], BF16)
            sg = self.sb(st, "cv_sg", [128, KC, 2], F32)
            ab = self.sb(st, "ab", [128, 48], F32)
            wf = [self.sb(st, f"adaw_f{i}", [128, KC, 512], F32) for i in range(2)]
            wb = [self.sb(st, f"adaw_b{i}", [128, KC, 512], BF16) for i in range(2)]
            b_cv, b_cs, b_ab, b_sg = Buf("cv"), Buf("cs"), Buf("ab"), Buf("sg")
            b_wf, b_wb = mkbufs("wf", 2), mkbufs("wb", 2)
            em.dma("sp", lambda e: e.dma_start(out=cv[:], in_=cvec[:, :, :]), writes=[b_cv])
            em.dma("sp", lambda e: e.dma_start(out=ab[:], in_=ada_b[l]), writes=[b_ab])
            em.op("act", lambda e: e.activation(out=sg[:], in_=cv[:], func=AF.Sigmoid), reads=[b_cv], writes=[b_sg])
            em.op("dve", lambda e: e.tensor_tensor(out=cs[:], in0=cv[:], in1=sg[:], op=ALU.mult),
                  reads=[b_cv, b_sg], writes=[b_cs])
            wview = ada_w[l].rearrange("(kc p) n -> p kc n", p=128)
            for g in range(12):
                s = g % 2
                em.dma("sp", lambda e, s=s, g=g: e.dma_start(out=wf[s][:], in_=wview[:, :, g * 512:(g + 1) * 512]),
                       writes=[b_wf[s]])
                em.op("pool", lambda e, s=s: e.tensor_copy(out=wb[s][:], in_=wf[s][:]),
                      reads=[b_wf[s]], writes=[b_wb[s]])
                pb = g % 8
                for q in range(4):
                    i = g * 4 + q
                    for kc in range(KC):
                        em.op("pe", lambda e, s=s, q=q, kc=kc, pb=pb: e.matmul(
                            psb[pb][:, q * 2:q * 2 + 2], lhsT=wb[s][:, kc, q * 128:(q + 1) * 128],
                            rhs=cs[:, kc, :], start=(kc == 0), stop=(kc == KC - 1)),
                            reads=[b_wb[s], b_cs], writes=[b_ps[pb]])
                for q in range(4):
                    i = g * 4 + q
                    em.op("dve", lambda e, q=q, i=i, pb=pb: e.tensor_scalar(
                        out=mod[:, i, :], in0=psb[pb][:, q * 2:q * 2 + 2], scalar1=ab[:, i:i + 1], scalar2=None,
                        op0=ALU.add), reads=[b_ps[pb], b_ab], writes=[b_mod])
            em.flush()

    def phase_norm_inproj(self, l, xc, norm_g, w_in, uT, mod, b_mod, ones_bf, b_ones):
        cfg = self.cfg
        em = self.em
        NT = cfg.NT
        psb, b_ps = self.psb, self.b_ps
        with contextlib.ExitStack() as st:
            hT = self.sb(st, "hT", [128, KC, NT], BF16)
            b_h = mkbufs("hT", len(cfg.tiles))
            ng = self.sb(st, "ng", [128, 4, KC], F32)
            A1 = self.sb(st, "A1", [128, KC, 2], F32)
            b_ng, b_A1 = Buf("ng"), Buf("A1")
            em.dma("sp", lambda e: e.dma_start(out=ng[:], in_=norm_g[l]), writes=[b_ng])
            for kc in range(KC):
                em.op("dve", lambda e, kc=kc: e.tensor_scalar(
                    out=A1[:, kc, :], in0=mod[:, 8 + kc, :], scalar1=1.0, scalar2=ng[:, 0, kc:kc + 1],
                    op0=ALU.add, op1=ALU.mult), reads=[b_mod, b_ng], writes=[b_A1])
            with contextlib.ExitStack() as st2:
                xt = [self.sb(st2, f"xt{i}", [128, KC, 512], F32) for i in range(2)]
                sq = [self.sb(st2, f"sq{i}", [128, KC, 512], BF16) for i in range(2)]
                rs = [self.sb(st2, f"rs{i}", [128, 512], F32) for i in range(2)]
                tmp = [self.sb(st2, f"tmp{i}", [128, KC, 512], F32) for i in range(2)]
                b_xt, b_sq, b_rs, b_tmp = mkbufs("xt", 2), mkbufs("sq", 2), mkbufs("rs", 2), mkbufs("tmp", 2)
                xview = xc.rearrange("(kc p) n -> p kc n", p=128)
                for ti, (t0, n, seg) in enumerate(cfg.tiles):
                    s = ti % 2
                    pb = ti % 8
                    em.dma("sp", lambda e, s=s, t0=t0, n=n: e.dma_start(out=xt[s][:, :, 0:n], in_=xview[:, :, t0:t0 + n]),
                           writes=[b_xt[s]])
                    em.op("act", lambda e, s=s, n=n: e.activation(out=sq[s][:, :, 0:n], in_=xt[s][:, :, 0:n], func=AF.Square),
                          reads=[b_xt[s]], writes=[b_sq[s]])
                    for kc in range(KC):
                        em.op("pe", lambda e, s=s, n=n, kc=kc, pb=pb: e.matmul(
                            psb[pb][:, 0:n], lhsT=ones_bf[:], rhs=sq[s][:, kc, 0:n], start=(kc == 0), stop=(kc == KC - 1)),
                            reads=[b_sq[s], b_ones], writes=[b_ps[pb]])
                    em.op("act", lambda e, s=s, n=n, pb=pb: e.activation(
                        out=rs[s][:, 0:n], in_=psb[pb][:, 0:n], func=AF.Sqrt, scale=1.0 / D, bias=1e-6),
                        reads=[b_ps[pb]], writes=[b_rs[s]])
                    em.op("dve", lambda e, s=s, n=n: e.reciprocal(out=rs[s][:, 0:n], in_=rs[s][:, 0:n]),
                          reads=[b_rs[s]], writes=[b_rs[s]])
                    for kc in range(KC):
                        em.op("dve", lambda e, s=s, n=n, kc=kc: e.tensor_tensor(
                            out=tmp[s][:, kc, 0:n], in0=xt[s][:, kc, 0:n], in1=rs[s][:, 0:n], op=ALU.mult),
                            reads=[b_xt[s], b_rs[s]], writes=[b_tmp[s]])
                        em.op("act", lambda e, s=s, n=n, kc=kc, t0=t0, seg=seg: e.activation(
                            out=hT[:, kc, t0:t0 + n], in_=tmp[s][:, kc, 0:n], func=AF.Identity,
                            scale=A1[:, kc, seg:seg + 1], bias=mod[:, kc, seg:seg + 1]),
                            reads=[b_tmp[s], b_A1, b_mod], writes=[b_h[ti]])
                em.flush()
            with contextlib.ExitStack() as st2:
                wf = [self.sb(st2, f"wf{i}", [128, KC, 128], F32) for i in range(2)]
                wb = [self.sb(st2, f"wb{i}", [128, KC, 128], BF16) for i in range(2)]
                stg = [self.sb(st2, f"stg{i}", [128, NT], F32) for i in range(2)]
                b_wf, b_wb, b_stg = mkbufs("wf", 2), mkbufs("wb", 2), mkbufs("stg", 2)
                wview = w_in[l].rearrange("(kc p) n -> p kc n", p=128)
                k = 0
                for j in range(self.NCH):
                    s = j % 2
                    em.dma("sp", lambda e, s=s, j=j: e.dma_start(out=wf[s][:], in_=wview[:, :, j * 128:(j + 1) * 128]),
                           writes=[b_wf[s]])
                    em.op("pool", lambda e, s=s: e.tensor_copy(out=wb[s][:], in_=wf[s][:]),
                          reads=[b_wf[s]], writes=[b_wb[s]])
                    for ti, (t0, n, seg) in enumerate(cfg.tiles):
                        pb = k % 8
                        k += 1
                        for kc in range(KC):
                            em.op("pe", lambda e, s=s, n=n, kc=kc, pb=pb, t0=t0: e.matmul(
                                psb[pb][:, 0:n], lhsT=wb[s][:, kc, :], rhs=hT[:, kc, t0:t0 + n],
                                start=(kc == 0), stop=(kc == KC - 1)),
                                reads=[b_wb[s], b_h[ti]], writes=[b_ps[pb]])
                        if k % 2 == 0:
                            em.op("act", lambda e, s=s, n=n, pb=pb, t0=t0: e.activation(
                                out=stg[s][:, t0:t0 + n], in_=psb[pb][:, 0:n], func=AF.Copy),
                                reads=[b_ps[pb]], writes=[b_stg[s]])
                        else:
                            em.op("dve", lambda e, s=s, n=n, pb=pb, t0=t0: e.tensor_copy(
                                out=stg[s][:, t0:t0 + n], in_=psb[pb][:, 0:n]),
                                reads=[b_ps[pb]], writes=[b_stg[s]])
                    em.dma("pool", lambda e, s=s, j=j: e.dma_start(out=uT[j * 128:(j + 1) * 128, :], in_=stg[s][:]),
                           reads=[b_stg[s]], writes=[])
                em.flush()


def na_blocks(rows):
    nqb = rows // 8
    types = {}
    blocks = []
    for qb in range(nqb):
        q0 = qb * 8
        rs = lambda r: min(max(r - 4, 0), rows - 8)
        lo, hi = rs(q0), rs(q0 + 7) + 8
        cls = "f" if qb == 0 else ("l" if qb == nqb - 1 else "i")
        items = []
        for kr0 in range(lo, hi, 2):
            delta = kr0 - q0
            key = (cls, delta)
            if key not in types:
                types[key] = (len(types), q0, kr0)
            items.append((kr0, types[key][0], (delta + 4) // 2))
        blocks.append(items)
    return blocks, len(types)


def blocks_type(blocks, qb, kr0):
    for (k, ty, di) in blocks[qb]:
        if k == kr0:
            return ty
    raise KeyError


def na_mask_np(rows):
    nqb = rows // 8
    out = {}
    for qb in range(nqb):
        q0 = qb * 8
        rsf = lambda r: min(max(r - 4, 0), rows - 8)
        lo, hi = rsf(q0), rsf(q0 + 7) + 8
        cls = "f" if qb == 0 else ("l" if qb == nqb - 1 else "i")
        for kr0 in range(lo, hi, 2):
            key = (cls, kr0 - q0)
            if key in out:
                continue
            krow = kr0 + np.arange(2)[:, None, None, None]
            kc = np.arange(64)[None, :, None, None]
            qrow = q0 + np.arange(8)[None, None, :, None]
            qc = np.arange(64)[None, None, None, :]
            rs = np.clip(qrow - 4, 0, rows - 8)
            cs = np.clip(qc - 8, 0, 48)
            ok = (krow >= rs) & (krow < rs + 8) & (kc >= cs) & (kc < cs + 16)
            out[key] = np.where(ok, 0.0, -30000.0).reshape(128, 512).astype(np.float32)
    return np.stack(list(out.values()), axis=0)


def rope_tables(T):
    half = 32
    freqs = 10000.0 ** (-np.arange(0, half, 2, dtype=np.float32) / half)
    pos = np.arange(T)
    prow, pcol = pos // 64, pos % 64
    cos = np.zeros((64, T), np.float32)
    sin = np.zeros((64, T), np.float32)
    for d in range(64):
        p = prow if d < 32 else pcol
        dd = d % 32
        ang = p.astype(np.float32) * freqs[dd % 16]
        cos[d] = np.cos(ang)
        sin[d] = -np.sin(ang) if dd < 16 else np.sin(ang)
    return np.concatenate([cos, cos], 0), np.concatenate([sin, sin], 0)


def input_shapes(cfg):
    Ld, NT, T = cfg.depth, cfg.NT, cfg.T
    return {
        "xc": [D, NT], "cvec": [128, KC, 2],
        "ada_w": [Ld, D, 6 * D], "ada_b": [Ld, 128, 48], "norm_g": [Ld, 128, 4, KC],
        "w_in": [Ld, D, NIN_X],
        "lru_conv_w": [Ld, 128, 8, 4], "lru_conv_b": [Ld, 128, 8],
        "lru_gate_a_w": [Ld, 2, 16, 64, 64], "lru_gate_x_w": [Ld, 2, 16, 64, 64],
        "lru_gate_b": [Ld, 128, 2, 2, 8], "lru_lambda": [Ld, 128, 2, 8],
        "ident": [128, 128], "rope_cos": [128, T], "rope_sin": [128, T],
        "na_mask": [na_blocks(T // 64)[1], 128, 512], "rpb_pad": [Ld, 16, 24, 128],
        "bdones": [128, 128], "istack": [128, 64], "rk_mask": [2, 128, 1024],
        "rwkv_mu": [Ld, 128, 26, 2], "rwkv_w0a0": [Ld, 128, 2, 2, 8], "rwkv_vec": [Ld, 128, 5, 8],
        "w_branch": [Ld, 3, D, D], "w_out": [Ld, D, D], "ffn_w_in": [Ld, D, 2 * DFF], "ffn_w_out": [Ld, DFF, D],
        "ffn_conv_w": [Ld, 128, 22, 3], "ffn_conv_b": [Ld, 128, 22],
        "rwkv_w_up": [Ld, 2, 64, 1024], "rwkv_a_up": [Ld, 2, 64, 1024], "rwkv_g_up": [Ld, 128, 1024],
    }


def colfmt(v, n):
    v = np.asarray(v)
    return np.moveaxis(v.reshape(v.shape[:-1] + (n, 128)), -1, -2)


def rope_perm():
    idx = np.arange(1024)
    d = idx % 64
    dd = d % 32
    partner = np.where(dd < 16, idx + 16, idx - 16)
    return partner


def prep_shared(inp, cfg):
    f = lambda a: np.ascontiguousarray(a, dtype=np.float32)
    Ld = cfg.depth
    m = {}
    m["ada_w"] = f(inp["ada_w"][:Ld])
    m["ada_b"] = f(colfmt(inp["ada_b"][:Ld], 48))
    m["norm_g"] = f(colfmt(inp["norm_g"][:Ld], KC).transpose(0, 2, 1, 3))
    w_in = inp["w_in"][:Ld]
    C0 = A_COLS + 2048
    perm = rope_perm()
    wq = w_in[:, :, C0:C0 + 1024][:, :, perm]
    wk = w_in[:, :, C0 + 1024:C0 + 2048][:, :, perm]
    m["w_in"] = f(np.concatenate([w_in, wq, wk], axis=2))
    m["lru_conv_w"] = f(colfmt(inp["lru_conv_w"][:Ld], 8).transpose(0, 2, 3, 1))
    m["lru_conv_b"] = f(colfmt(inp["lru_conv_b"][:Ld], 8))
    m["lru_gate_a_w"] = f(inp["lru_gate_a_w"][:Ld])
    m["lru_gate_x_w"] = f(inp["lru_gate_x_w"][:Ld])
    gb = np.stack([inp["lru_gate_a_b"][:Ld], inp["lru_gate_x_b"][:Ld]], axis=1)
    m["lru_gate_b"] = f(colfmt(gb, 8).transpose(0, 3, 1, 2, 4))
    m["lru_lambda"] = f(colfmt(inp["lru_lambda"][:Ld], 8).transpose(0, 2, 1, 3))
    m["ident"] = np.eye(128, dtype=np.float32)
    cos, sin = rope_tables(cfg.T)
    m["rope_cos"], m["rope_sin"] = f(cos), f(sin)
    m["na_mask"] = f(na_mask_np(cfg.T // 64))
    rp = np.zeros((Ld, 16, 24, 128), np.float32)
    rp[:, :, 4:19, 48:79] = inp["na_rpb"][:Ld]
    m["rpb_pad"] = rp
    blk = np.kron(np.eye(2, dtype=np.float32), np.ones((64, 64), np.float32))
    m["bdones"] = blk
    m["istack"] = np.concatenate([np.eye(64, dtype=np.float32)] * 2, axis=0)
    i64 = np.arange(64)
    U = np.kron(np.eye(2), (i64[:, None] < i64[None, :])).astype(np.float32)
    UI = np.kron(np.eye(2), (i64[:, None] <= i64[None, :])).astype(np.float32)
    Lw, LI = U.T.copy(), UI.T.copy()
    ONE = np.ones((128, 128), np.float32)
    m32 = np.kron(np.eye(4), np.ones((32, 32))).astype(np.float32)
    fwd = np.concatenate([U * m32, UI, ONE, UI, ONE, Lw * m32, Lw, Lw * (1 - m32)], axis=1)
    bwd = np.concatenate([Lw * m32, LI, ONE, LI, ONE, U * m32, U, U * (1 - m32)], axis=1)
    m["rk_mask"] = np.stack([fwd, bwd], axis=0)
    m["rwkv_mu"] = f(colfmt(inp["rwkv_mu"][:Ld], 26).transpose(0, 2, 3, 1))
    w0a0 = np.stack([inp["rwkv_w0"][:Ld], inp["rwkv_a0"][:Ld]], axis=1)
    m["rwkv_w0a0"] = f(colfmt(w0a0, 8).transpose(0, 3, 1, 2, 4))
    vec = np.stack([inp["rwkv_k_k"][:Ld], inp["rwkv_k_a"][:Ld], inp["rwkv_r_k"][:Ld].reshape(Ld, 1024),
                    inp["rwkv_lnx_w"][:Ld], inp["rwkv_lnx_b"][:Ld]], axis=1)
    m["rwkv_vec"] = f(colfmt(vec, 8).transpose(0, 2, 1, 3))
    m["rwkv_w_up"] = f(inp["rwkv_w_up"][:Ld])
    m["rwkv_a_up"] = f(inp["rwkv_a_up"][:Ld])
    m["rwkv_g_up"] = f(inp["rwkv_g_up"][:Ld])
    for k in ("w_branch", "w_out", "ffn_w_in", "ffn_w_out"):
        m[k] = f(inp[k][:Ld])
    m["ffn_conv_w"] = f(colfmt(inp["ffn_conv_w"][:Ld], 22).transpose(0, 2, 3, 1))
    m["ffn_conv_b"] = f(colfmt(inp["ffn_conv_b"][:Ld], 22))
    return m


def prep_inputs(inp, b, cfg, shared=None):
    T = cfg.T
    f = lambda a: np.ascontiguousarray(a, dtype=np.float32)
    m = dict(shared if shared is not None else prep_shared(inp, cfg))
    m["xc"] = f(np.concatenate([inp["ctx"][b].T, inp["x"][b, :T].T], axis=1))
    cv = np.stack([inp["c"][b], inp["c_ctx"]], axis=1)
    m["cvec"] = f(cv.reshape(KC, 128, 2).transpose(1, 0, 2))
    return m


def kernel(**inputs):
    cfg = Cfg()
    bld = Builder(cfg)
    nc = bld.build()
    inp = {k: np.asarray(v) for k, v in inputs.items()}
    shared = prep_shared(inp, cfg)
    in_maps = [prep_inputs(inp, b, cfg, shared) for b in range(8)]
    res = run_bass_kernel_spmd(nc, in_maps, core_ids=list(range(8)))
    out = np.stack([r["outT"].T for r in res.results], axis=0)
    return out.astype(np.float32)
```

```python
import contextlib
import numpy as np
import concourse.bass as bass
import concourse.mybir as mybir
from concourse.bass_utils import run_bass_kernel_spmd

F32 = mybir.dt.float32
BF16 = mybir.dt.bfloat16
AF = mybir.ActivationFunctionType
ALU = mybir.AluOpType
AX = mybir.AxisListType

D = 1024
KC = 8
LCTX = 256
DFF = 2816
NHEAD = 16
A_COLS = 3 * 1024 + 256
N_IN = 11520
NIN_X = N_IN + 2048
ENGS = ("pe", "act", "dve", "pool", "sp")
NDMA_SEMS = 24
NODEFER = False


class Buf:
    __slots__ = ("name", "w", "readers")

    def __init__(self, name):
        self.name = name
        self.w = None
        self.readers = []


def mkbufs(name, n):
    return [Buf(f"{name}{i}") for i in range(n)]


class Emit:
    def __init__(self, nc, stack):
        self.nc = nc
        self.prog = {e: [] for e in ENGS}
        self.count = {e: 0 for e in ENGS}
        self.seen = {e: {} for e in ENGS}
        self.pend_inc = {}
        self.dma_total = [0] * NDMA_SEMS
        self.dma_rr = 0
        self.n_instr = 0
        self.sems = {}
        for e in ENGS:
            self.sems[e] = stack.enter_context(nc.semaphore(f"c_{e}"))
        for k in range(NDMA_SEMS):
            self.sems[("dma", k)] = stack.enter_context(nc.semaphore(f"d_{k}"))

    def _deps(self, eng, reads, writes):
        deps = {}

        def add(d):
            if d is None:
                return
            k, v = d
            if deps.get(k, 0) < v:
                deps[k] = v
        for b in reads:
            add(b.w)
        for b in writes:
            add(b.w)
            for r in b.readers:
                add(r)
        waits = []
        seen = self.seen[eng]
        for k, v in deps.items():
            if k == eng and v > self.count[eng]:
                continue
            if seen.get(k, 0) < v:
                seen[k] = v
                waits.append((k, v))
        return waits

    def _commit(self, me, reads, writes):
        for b in reads:
            b.readers.append(me)
            if len(b.readers) > 32:
                mx = {}
                for k, v in b.readers:
                    if mx.get(k, 0) < v:
                        mx[k] = v
                b.readers = list(mx.items())
        for b in writes:
            b.w = me
            b.readers = []

    def op(self, eng, fn, reads=(), writes=(), defer=False):
        waits = self._deps(eng, reads, writes)
        if defer and not NODEFER:
            me = (eng, self.count[eng] + 1)
            self.pend_inc[eng] = 1
            self.prog[eng].append((waits, fn, None))
        else:
            self.count[eng] += 1
            me = (eng, self.count[eng])
            self.prog[eng].append((waits, fn, (eng, 1)))
            self.pend_inc[eng] = 0
        self._commit(me, reads, writes)
        self.n_instr += 1 + len(waits)

    def dma(self, q, fn, reads=(), writes=()):
        k = self.dma_rr
        self.dma_rr = (self.dma_rr + 1) % NDMA_SEMS
        key = ("dma", k)
        waits = self._deps(q, reads, writes)
        prev = self.dma_total[k]
        if prev > 0 and self.seen[q].get(key, 0) < prev:
            self.seen[q][key] = prev
            waits.append((key, prev))
        self.dma_total[k] += 16
        me = (key, self.dma_total[k])
        self.prog[q].append((waits, fn, (key, 16)))
        self._commit(me, reads, writes)
        self.n_instr += 1 + len(waits)

    def flush(self):
        assert all(v == 0 for v in self.pend_inc.values()), "deferred semaphore increment left dangling"
        nc = self.nc
        prog = self.prog
        sems = self.sems
        dma_fin = [(("dma", k), v) for k, v in enumerate(self.dma_total) if v > 0]

        def run(name, eng):
            for waits, fn, inc in prog[name]:
                for k, v in waits:
                    eng.wait_ge(sems[k], v)
                ins = fn(eng)
                if inc is not None:
                    ins.then_inc(sems[inc[0]], inc[1])
            if name in ("sp", "pool", "act"):
                for k, v in dma_fin:
                    eng.wait_ge(sems[k], v)

        with nc.Block() as block:
            @block.tensor
            def _(t):
                run("pe", t)

            @block.scalar
            def _(a):
                run("act", a)

            @block.vector
            def _(v):
                run("dve", v)

            @block.gpsimd
            def _(g):
                run("pool", g)

            @block.sync
            def _(s):
                run("sp", s)
        for k, v in dma_fin:
            for e in ENGS:
                self.seen[e][k] = v
        self.prog = {e: [] for e in ENGS}


class Cfg:
    def __init__(self, T=4096, depth=2, dbg=False):
        self.T = T
        self.NT = LCTX + T
        self.depth = depth
        self.dbg = dbg
        self.phases = ("lru", "na", "rwkv", "merge", "ffn")
        self.tiles = [(0, LCTX, 1)] + [(LCTX + 512 * i, 512, 0) for i in range(T // 512)]


class Builder:
    def __init__(self, cfg):
        self.cfg = cfg
        self.nc = bass.Bass("TRN2", target_bir_lowering=False)
        self.dram = {}

    def din(self, name, shape, dt=F32):
        t = self.nc.dram_tensor(name, list(shape), dt, kind="ExternalInput").ap()
        self.dram[name] = t
        return t

    def dscratch(self, name, shape, dt=F32, out=False):
        kind = "ExternalOutput" if (out or self.cfg.dbg) else "Internal"
        t = self.nc.dram_tensor(name, list(shape), dt, kind=kind).ap()
        self.dram[name] = t
        return t

    def sb(self, st, name, shape, dt):
        self._uid = getattr(self, "_uid", 0) + 1
        return st.enter_context(self.nc.sbuf_tensor(f"{name}_{self._uid}", list(shape), dt))

    def ps(self, st, name, shape, dt=F32):
        return st.enter_context(self.nc.psum_tensor(name, list(shape), dt))

    def build(self, upto=99):
        cfg = self.cfg
        nc = self.nc
        NT, T = cfg.NT, cfg.T
        Ld = cfg.depth
        self.NCH = NIN_X // 128
        for name, shape in input_shapes(cfg).items():
            self.din(name, shape)
        self.dscratch("uT", [NIN_X, NT])
        self.dscratch("yA", [D, NT], BF16)
        self.dscratch("yB", [D, NT], BF16)
        self.dscratch("yC", [D, NT], BF16)
        self.dscratch("aT", [DFF, NT], BF16)
        self.dscratch("xcur", [D, NT])
        self.dscratch("xmid", [D, NT])
        self.dscratch("outT", [D, T], out=True)
        dr = self.dram
        with contextlib.ExitStack() as outer:
            em = Emit(nc, outer)
            self.em = em
            mod = self.sb(outer, "mod", [128, 48, 2], F32)
            ones_bf = self.sb(outer, "ones_bf", [128, 128], BF16)
            b_mod = Buf("mod")
            b_ones = Buf("ones")
            self.mod, self.b_mod, self.ones_bf, self.b_ones = mod, b_mod, ones_bf, b_ones
            em.op("pool", lambda e: e.memset(ones_bf[:], 1.0), writes=[b_ones])
            psb = [self.ps(outer, f"psb{i}", [128, 512]) for i in range(8)]
            b_ps = mkbufs("ps", 8)
            self.psb, self.b_ps = psb, b_ps
            for l in range(Ld):
                xsrc = dr["xc"] if l == 0 else dr["xcur"]
                self.phase_mod(l, dr["cvec"], dr["ada_w"], dr["ada_b"], mod, b_mod)
                self.phase_norm_inproj(l, xsrc, dr["norm_g"], dr["w_in"], dr["uT"], mod, b_mod, ones_bf, b_ones)
                if upto <= 2:
                    break
                if "lru" in cfg.phases:
                    self.phase_lru(l)
                if "na" in cfg.phases:
                    self.phase_na(l)
                if "rwkv" in cfg.phases:
                    self.phase_rwkv(l)
                if "merge" in cfg.phases:
                    self.phase_merge_ffn(l, xsrc)
        return nc

    def mm(self, out, lhsT, rhs, start=True, stop=True, r=(), w=(), defer=None):
        if defer is None:
            defer = not stop
        self.em.op("pe", lambda e: e.matmul(out, lhsT=lhsT, rhs=rhs, start=start, stop=stop), r, w, defer=defer)

    def act(self, out, in_, func, r=(), w=(), scale=1.0, bias=0.0):
        self.em.op("act", lambda e: e.activation(out=out, in_=in_, func=func, scale=scale, bias=bias), r, w)

    def tt(self, eng, out, in0, in1, op, r=(), w=()):
        self.em.op(eng, lambda e: e.tensor_tensor(out=out, in0=in0, in1=in1, op=op), r, w)

    def ts(self, eng, out, in0, s1, s2, op0, op1=None, r=(), w=()):
        if op1 is None:
            self.em.op(eng, lambda e: e.tensor_scalar(out=out, in0=in0, scalar1=s1, scalar2=None, op0=op0), r, w)
        else:
            self.em.op(eng, lambda e: e.tensor_scalar(out=out, in0=in0, scalar1=s1, scalar2=s2, op0=op0, op1=op1), r, w)

    def stt(self, out, in0, sc, in1, op0, op1, r=(), w=()):
        self.em.op("dve", lambda e: e.scalar_tensor_tensor(out=out, in0=in0, scalar=sc, in1=in1, op0=op0, op1=op1), r, w)

    def cp(self, eng, out, in_, r=(), w=()):
        self.em.op(eng, lambda e: e.tensor_copy(out=out, in_=in_), r, w)

    def memset(self, eng, ap, val, w=()):
        self.em.op(eng, lambda e: e.memset(ap, val), (), w)

    def scan(self, out, d0, d1, init, r=(), w=()):
        self.em.op("dve", lambda e: e.tensor_tensor_scan(out=out, data0=d0, data1=d1, initial=init, op0=ALU.mult, op1=ALU.add), r, w)

    def dma(self, q, out, in_, r=(), w=()):
        self.em.dma(q, lambda e: e.dma_start(out=out, in_=in_), r, w)

    def segs(self):
        return [(0, 2, LCTX), (LCTX, LCTX + 4, self.cfg.T)]

    def phase_lru(self, l):
        cfg = self.cfg
        NT, T = cfg.NT, cfg.T
        psb, b_ps = self.psb, self.b_ps
        dr = self.dram
        uT, yB = dr["uT"], dr["yB"]
        B0 = A_COLS
        with contextlib.ExitStack() as st:
            cw = self.sb(st, "l_cw", [128, 8, 4], F32)
            cb = self.sb(st, "l_cb", [128, 8], F32)
            gab = self.sb(st, "l_gab", [128, 2, 2, 8], F32)
            lam = self.sb(st, "l_lam", [128, 2, 8], F32)
            cl = self.sb(st, "l_cl", [128, 2, 8], F32)
            b_par, b_cl = Buf("lpar"), Buf("lcl")
            self.dma("sp", cw[:], dr["lru_conv_w"][l], w=[b_par])
            self.dma("sp", cb[:], dr["lru_conv_b"][l], w=[b_par])
            self.dma("sp", gab[:], dr["lru_gate_b"][l], w=[b_par])
            self.dma("sp", lam[:], dr["lru_lambda"][l], w=[b_par])
            self.act(cl[:], lam[:], AF.Exp, [b_par], [b_cl], scale=-1.0)
            self.act(cl[:], cl[:], AF.Ln, [b_cl], [b_cl], bias=1.0)
            self.ts("dve", cl[:], cl[:], -8.0, None, ALU.mult, r=[b_cl], w=[b_cl])
            xp = self.sb(st, "l_xp", [128, NT + 6], F32)
            xb = self.sb(st, "l_xb", [128, NT], F32)
            xbb = self.sb(st, "l_xbb", [128, NT], BF16)
            gt = self.sb(st, "l_gt", [128, NT], F32)
            gtb = self.sb(st, "l_gtb", [128, NT], BF16)
            A = self.sb(st, "l_A", [128, NT], F32)
            Bt = self.sb(st, "l_B", [128, NT], F32)
            Ct = self.sb(st, "l_C", [128, NT], F32)
            hf = self.sb(st, "l_hf", [128, NT], F32)
            ys = self.sb(st, "l_ys", [128, NT], BF16)
            wgf = [self.sb(st, f"l_wgf{i}", [128, 128], F32) for i in range(2)]
            wgb = [self.sb(st, f"l_wgb{i}", [128, 128], BF16) for i in range(2)]
            b_xp, b_xb, b_xbb, b_gt, b_gtb, b_A, b_B, b_C, b_hf, b_ys = [Buf(n) for n in
                "xp xb xbb gt gtb A B C hf ys".split()]
            b_wgf, b_wgb = mkbufs("wgf", 2), mkbufs("wgb", 2)
            self.memset("pool", xp[:], 0.0, w=[b_xp])
            for i in range(2):
                self.memset("pool", wgf[i][:], 0.0, w=[b_wgf[i]])
            gw = [dr["lru_gate_a_w"], dr["lru_gate_x_w"]]

            def rev(ap):
                aps = [list(p) for p in ap.ap]
                n, stp = aps[-1][1], aps[-1][0]
                aps[-1] = [-stp, n]
                return bass.AP(ap.tensor, ap.offset + stp * (n - 1), aps)
            k = 0
            for j in range(8):
                for (d0, s0, n) in self.segs():
                    self.dma("sp", xp[:, s0:s0 + n], uT[B0 + j * 128:B0 + (j + 1) * 128, d0:d0 + n], w=[b_xp])
                self.dma("sp", gt[:], uT[B0 + 1024 + j * 128:B0 + 1024 + (j + 1) * 128, :], w=[b_gt])
                for (d0, s0, n) in self.segs():
                    self.ts("dve", xb[:, d0:d0 + n], xp[:, s0 - 2:s0 - 2 + n], cw[:, j, 0:1], cb[:, j:j + 1], ALU.mult, ALU.add,
                            r=[b_xp, b_par], w=[b_xb])
                    for tap in range(1, 4):
                        self.stt(xb[:, d0:d0 + n], xp[:, s0 - 2 + tap:s0 - 2 + tap + n], cw[:, j, tap:tap + 1], xb[:, d0:d0 + n],
                                 ALU.mult, ALU.add, r=[b_xp, b_par, b_xb], w=[b_xb])
                self.cp("pool", xbb[:], xb[:], r=[b_xb], w=[b_xbb])
                self.act(gtb[:], gt[:], AF.Gelu_apprx_tanh, [b_gt], [b_gtb])
                for d in range(2):
                    for g in range(2):
                        for hb in range(2):
                            self.dma("sp", wgf[g][hb * 64:(hb + 1) * 64, hb * 64:(hb + 1) * 64], gw[g][l, d, 2 * j + hb],
                                     w=[b_wgf[g]])
                        self.cp("pool", wgb[g][:], wgf[g][:], r=[b_wgf[g]], w=[b_wgb[g]])
                    for g, (dst, b_dst) in enumerate([(A, b_A), (Bt, b_B)]):
                        for (t0, n, seg) in cfg.tiles:
                            pb = k % 8
                            k += 1
                            self.mm(psb[pb][:, 0:n], wgb[g][:], xbb[:, t0:t0 + n], r=[b_wgb[g], b_xbb], w=[b_ps[pb]])
                            self.act(dst[:, t0:t0 + n], psb[pb][:, 0:n], AF.Sigmoid, [b_ps[pb], b_par], [b_dst],
                                     bias=gab[:, g, d, j:j + 1])
                    self.act(A[:], A[:], AF.Exp, [b_A, b_cl], [b_A], scale=cl[:, d, j:j + 1])
                    self.tt("dve", Ct[:], A[:], A[:], ALU.mult, r=[b_A], w=[b_C])
                    self.act(Ct[:], Ct[:], AF.Sqrt, [b_C], [b_C], scale=-1.0, bias=1.0)
                    self.tt("pool", Bt[:], Bt[:], xb[:], ALU.mult, r=[b_B, b_xb], w=[b_B])
                    self.tt("dve", Ct[:], Ct[:], Bt[:], ALU.mult, r=[b_C, b_B], w=[b_C])
                    if d == 0:
                        self.scan(hf[:], A[:], Ct[:], 0.0, r=[b_A, b_C], w=[b_hf])
                    else:
                        for (d0, s0, n) in self.segs():
                            self.cp("pool", Bt[:, d0:d0 + n], rev(A[:, d0:d0 + n]), r=[b_A], w=[b_B])
                            self.cp("pool", gt[:, d0:d0 + n], rev(Ct[:, d0:d0 + n]), r=[b_C], w=[b_gt])
                        self.scan(A[:], Bt[:], gt[:], 0.0, r=[b_B, b_gt, b_A], w=[b_A])
                        for (d0, s0, n) in self.segs():
                            self.cp("pool", Ct[:, d0:d0 + n], rev(A[:, d0:d0 + n]), r=[b_A], w=[b_C])
                        self.tt("dve", hf[:], hf[:], Ct[:], ALU.add, r=[b_hf, b_C], w=[b_hf])
                self.tt("dve", ys[:], hf[:], gtb[:], ALU.mult, r=[b_hf, b_gtb], w=[b_ys])
                self.dma("pool", yB[j * 128:(j + 1) * 128, :], ys[:], r=[b_ys])
            self.em.flush()

    def phase_na(self, l):
        cfg = self.cfg
        NT, T = cfg.NT, cfg.T
        psb, b_ps = self.psb, self.b_ps
        dr = self.dram
        uT, yC = dr["uT"], dr["yC"]
        update_ctx = (l < cfg.depth - 1) or getattr(cfg, 'force_ctx', False)
        rows = T // 64
        blocks, ntype = na_blocks(rows)
        NTB = NT // 128
        CQ, CK, CV, CQP, CKP = 42, 50, 58, 90, 98
        rp = dr["rpb_pad"]
        with contextlib.ExitStack() as st:
            ident = self.sb(st, "n_ident", [128, 128], F32)
            onesp = self.sb(st, "n_onesp", [128, 2, 128], BF16)
            maskb = self.sb(st, "n_maskb", [128, ntype, 512], BF16)
            mtmp = [self.sb(st, f"n_mtmp{i}", [128, 512], F32) for i in range(2)]
            b_id, b_op, b_mk = Buf("ident"), Buf("onesp"), Buf("maskb")
            b_mt = mkbufs("mtmp", 2)
            self.dma("sp", ident[:], dr["ident"][:, :], w=[b_id])
            self.memset("pool", onesp[:], 0.0, w=[b_op])
            self.memset("pool", onesp[:, 0, 0:64], 1.0, w=[b_op])
            self.memset("pool", onesp[:, 1, 64:128], 1.0, w=[b_op])
            for t in range(ntype):
                self.dma("sp", mtmp[t % 2][:], dr["na_mask"][t], w=[b_mt[t % 2]])
                self.cp("pool", maskb[:, t, :], mtmp[t % 2][:], r=[b_mt[t % 2]], w=[b_mk])
            NIN = 7
            tl = [[self.sb(st, f"n_tl{a}_{i}", [128, 512], F32) for i in range(2)] for a in range(NIN)]
            b_tl = [mkbufs(f"tl{a}_", 2) for a in range(NIN)]
            qpl = self.sb(st, "n_qpl", [128, NT], BF16)
            kpl = self.sb(st, "n_kpl", [128, LCTX], BF16)
            qrot = self.sb(st, "n_qrot", [128, T], BF16)
            krot = self.sb(st, "n_krot", [128, T], BF16)
            Vp = self.sb(st, "n_Vp", [128, NTB, 2, 128], BF16)
            Tc2 = self.sb(st, "n_Tc2", [128, 22 * 64], F32)
            biasd = self.sb(st, "n_biasd", [128, 8, 512], F32)
            bm = [self.sb(st, f"n_bm{i}", [128, ntype, 512], BF16) for i in range(2)]
            sT = [self.sb(st, f"n_sT{i}", [128, 512], F32) for i in range(2)]
            pT = [self.sb(st, f"n_pT{i}", [128, 512], BF16) for i in range(3)]
            rc = [self.sb(st, f"n_rc{i}", [128, 512], F32) for i in range(2)]
            yst = self.sb(st, "n_yst", [128, NT], BF16)
            b_qpl, b_kpl, b_qrot, b_krot, b_Vp, b_Tc2, b_biasd, b_yst = [Buf(n) for n in
                "qpl kpl qrot krot Vp Tc2 biasd yst".split()]
            b_bm, b_sT, b_pT, b_rc = mkbufs("bm", 2), mkbufs("sT", 2), mkbufs("pT", 3), mkbufs("rc", 2)
            self.memset("pool", Vp[:], 0.0, w=[b_Vp])
            self.memset("pool", yst[:], 0.0, w=[b_yst])

            def rev(ap):
                aps = [list(p) for p in ap.ap]
                n, stp = aps[-1][1], aps[-1][0]
                aps[-1] = [-stp, n]
                return bass.AP(ap.tensor, ap.offset + stp * (n - 1), aps)
            kq = 0
            ks = 0
            kp_ = 0
            kacc = 0
            for j in range(8):
                for ti, (t0, n, seg) in enumerate(cfg.tiles):
                    s = ti % 2
                    rowsrc = [CQ + j, CQP + j, CK + j, CKP + j, CV + j]
                    need = [0, 2, 4] if seg == 1 else [0, 1, 2, 3, 4]
                    for a in need:
                        c = rowsrc[a]
                        self.dma("sp", tl[a][s][:, 0:n], uT[c * 128:(c + 1) * 128, t0:t0 + n], w=[b_tl[a][s]])
                    if seg == 1:
                        self.act(qpl[:, t0:t0 + n], tl[0][s][:, 0:n], AF.Copy, [b_tl[0][s]], [b_qpl])
                        self.act(kpl[:, 0:n], tl[2][s][:, 0:n], AF.Copy, [b_tl[2][s]], [b_kpl])
                    else:
                        lt0 = t0 - LCTX
                        self.dma("sp", tl[5][s][:], dr["rope_cos"][:, lt0:lt0 + 512], w=[b_tl[5][s]])
                        self.dma("sp", tl[6][s][:], dr["rope_sin"][:, lt0:lt0 + 512], w=[b_tl[6][s]])
                        self.act(qpl[:, t0:t0 + n], tl[0][s][:], AF.Copy, [b_tl[0][s]], [b_qpl])
                        for (a, ap_, dst, b_dst) in [(0, 1, qrot, b_qrot), (2, 3, krot, b_krot)]:
                            self.tt("dve", tl[a][s][:], tl[a][s][:], tl[5][s][:], ALU.mult,
                                    r=[b_tl[a][s], b_tl[5][s]], w=[b_tl[a][s]])
                            self.tt("pool", tl[ap_][s][:], tl[ap_][s][:], tl[6][s][:], ALU.mult,
                                    r=[b_tl[ap_][s], b_tl[6][s]], w=[b_tl[ap_][s]])
                            self.tt("dve", dst[:, lt0:lt0 + 512], tl[a][s][:], tl[ap_][s][:], ALU.add,
                                    r=[b_tl[a][s], b_tl[ap_][s]], w=[b_dst])
                    nb = n // 128
                    pb = kq % 4
                    kq += 1
                    for q in range(nb):
                        self.em.op("pe", lambda e, pb=pb, q=q, s=s: e.transpose(
                            psb[pb][:, q * 128:(q + 1) * 128], tl[4][s][:, q * 128:(q + 1) * 128], ident[:]),
                            [b_tl[4][s], b_id], [b_ps[pb]], defer=(q != nb - 1))
                    tb0 = t0 // 128
                    pv = psb[pb][:, 0:nb * 128].rearrange("p (a b) -> p a b", b=128)
                    self.cp("dve", Vp[:, tb0:tb0 + nb, 0, 0:64], pv[:, :, 0:64], r=[b_ps[pb]], w=[b_Vp])
                    self.act(Vp[:, tb0:tb0 + nb, 1, 64:128], pv[:, :, 64:128], AF.Copy, [b_ps[pb]], [b_Vp])
                for hh in range(2):
                    h = 2 * j + hh
                    for krl in range(2):
                        base = ((l * 16 + h) * 24 + krl) * 128
                        src = bass.AP(rp.tensor, rp.offset + base, [[1, 64], [128, 22], [1, 64]])
                        self.dma("sp", Tc2[krl * 64:(krl + 1) * 64, :].rearrange("p (a b) -> p a b", b=64), src, w=[b_Tc2])
                    for di in range(8):
                        self.cp("pool", biasd[:, di, :], rev(Tc2[:, di * 128:di * 128 + 512]), r=[b_Tc2], w=[b_biasd])
                    for qb, items in enumerate(blocks):
                        for (kr0, ty, di) in items:
                            if ty is not None:
                                self.tt("pool", bm[hh][:, ty, :], biasd[:, di, :], maskb[:, ty, :], ALU.add,
                                        r=[b_biasd, b_mk], w=[b_bm[hh]])
                for qb, items in enumerate(blocks):
                    a1, a2 = 4 + 2 * (kacc % 2), 5 + 2 * (kacc % 2)
                    kacc += 1
                    q0t = qb * 512
                    work = []
                    for hh in range(2):
                        for (kr0, ty, di) in items:
                            work.append((hh, "loc", kr0, blocks_type(blocks, qb, kr0)))
                        for cc in range(2):
                            work.append((hh, "ctx", cc, None))
                    for wi, (hh, kind, a, ty) in enumerate(work):
                        hb = hh * 64
                        pb = kq % 4
                        kq += 1
                        s2 = kp_ % 3
                        kp_ += 1
                        if kind == "loc":
                            ktok = a * 64
                            self.mm(psb[pb][:, :], krot[hb:hb + 64, ktok:ktok + 128], qrot[hb:hb + 64, q0t:q0t + 512],
                                    r=[b_krot, b_qrot], w=[b_ps[pb]])
                            s1 = ks % 2
                            ks += 1
                            self.stt(sT[s1][:], psb[pb][:, :], 0.125, bm[hh][:, ty, :], ALU.mult, ALU.add,
                                     r=[b_ps[pb], b_bm[hh]], w=[b_sT[s1]])
                            self.act(pT[s2][:], sT[s1][:], AF.Exp, [b_sT[s1]], [b_pT[s2]])
                            vch = 2 + a // 2
                        else:
                            self.mm(psb[pb][:, :], kpl[hb:hb + 64, a * 128:(a + 1) * 128],
                                    qpl[hb:hb + 64, LCTX + q0t:LCTX + q0t + 512], r=[b_kpl, b_qpl], w=[b_ps[pb]])
                            self.act(pT[s2][:], psb[pb][:, :], AF.Exp, [b_ps[pb]], [b_pT[s2]], scale=0.125)
                            vch = a
                        first, last = (wi == 0), (wi == len(work) - 1)
                        self.mm(psb[a1][:, :], Vp[:, vch, hh, :], pT[s2][:], start=first, stop=last,
                                r=[b_Vp, b_pT[s2]], w=[b_ps[a1]])
                        self.mm(psb[a2][:, :], onesp[:, hh, :], pT[s2][:], start=first, stop=last,
                                r=[b_op, b_pT[s2]], w=[b_ps[a2]])
                    s3 = kacc % 2
                    self.em.op("dve", lambda e, s3=s3, a2=a2: e.reciprocal(out=rc[s3][:], in_=psb[a2][:, :]),
                               [b_ps[a2]], [b_rc[s3]])
                    self.tt("dve", yst[:, LCTX + q0t:LCTX + q0t + 512], psb[a1][:, :], rc[s3][:], ALU.mult,
                            r=[b_ps[a1], b_rc[s3]], w=[b_yst])
                if update_ctx:
                    a1, a2 = 4 + 2 * (kacc % 2), 5 + 2 * (kacc % 2)
                    kacc += 1
                    work = [(hh, cc) for hh in range(2) for cc in range(2)]
                    for wi, (hh, cc) in enumerate(work):
                        hb = hh * 64
                        pb = kq % 4
                        kq += 1
                        s2 = kp_ % 3
                        kp_ += 1
                        self.mm(psb[pb][:, 0:LCTX], kpl[hb:hb + 64, cc * 128:(cc + 1) * 128], qpl[hb:hb + 64, 0:LCTX],
                                r=[b_kpl, b_qpl], w=[b_ps[pb]])
                        self.act(pT[s2][:, 0:LCTX], psb[pb][:, 0:LCTX], AF.Exp, [b_ps[pb]], [b_pT[s2]], scale=0.125)
                        first, last = (wi == 0), (wi == len(work) - 1)
                        self.mm(psb[a1][:, 0:LCTX], Vp[:, cc, hh, :], pT[s2][:, 0:LCTX], start=first, stop=last,
                                r=[b_Vp, b_pT[s2]], w=[b_ps[a1]])
                        self.mm(psb[a2][:, 0:LCTX], onesp[:, hh, :], pT[s2][:, 0:LCTX], start=first, stop=last,
                                r=[b_op, b_pT[s2]], w=[b_ps[a2]])
                    s3 = kacc % 2
                    self.em.op("dve", lambda e, s3=s3, a2=a2: e.reciprocal(out=rc[s3][:, 0:LCTX], in_=psb[a2][:, 0:LCTX]),
                               [b_ps[a2]], [b_rc[s3]])
                    self.tt("dve", yst[:, 0:LCTX], psb[a1][:, 0:LCTX], rc[s3][:, 0:LCTX], ALU.mult,
                            r=[b_ps[a1], b_rc[s3]], w=[b_yst])
                self.dma("pool", yC[j * 128:(j + 1) * 128, :], yst[:], r=[b_yst])
            self.em.flush()

    def phase_rwkv(self, l):
        cfg = self.cfg
        NT, T = cfg.NT, cfg.T
        psb, b_ps = self.psb, self.b_ps
        dr = self.dram
        uT, yA = dr["uT"], dr["yA"]
        NCK = NT // 64
        CW = 0.6065306597126334
        SEGC = 16
        SEGN = SEGC * 64
        segs = [(0, 4)] + [(4 + 16 * i, 16) for i in range((NCK - 4) // 16)]
        with contextlib.ExitStack() as st:
            sb = lambda name, shape, dt=F32: self.sb(st, "r_" + name, shape, dt)
            cst_f = sb("cst_f", [128, 1152])
            ident_bf = sb("ident_bf", [128, 128], BF16)
            bdones = sb("bdones", [128, 128], BF16)
            istack = sb("istack", [128, 64], BF16)
            rkm = sb("rkm", [128, 2, 1152], BF16)
            cmask = sb("cmask", [128, SEGN])
            b_cst, b_k = Buf("cst"), Buf("rk_consts")
            for (dst, src, n) in [(ident_bf, dr["ident"], 128), (bdones, dr["bdones"], 128), (istack, dr["istack"], 64)]:
                self.dma("sp", cst_f[:, 0:n], src[:, :], w=[b_cst])
                self.cp("dve", dst[:], cst_f[:, 0:n], r=[b_cst], w=[b_k])
            for d in range(2):
                self.dma("sp", cst_f[:], dr["rk_mask"][d], w=[b_cst])
                self.cp("dve", rkm[:, d, :], cst_f[:], r=[b_cst], w=[b_k])
            self.memset("pool", cmask[:], 1.0, w=[b_k])
            self.memset("pool", cmask[:].rearrange("p (c s) -> p c s", s=64)[:, :, 0:1], 0.0, w=[b_k])
            mu = sb("mu", [128, 26, 2])
            c0 = sb("c0", [128, 26])
            w0a0 = sb("w0a0", [128, 2, 2, 8])
            vec = sb("vec", [128, 5, 8])
            omk = sb("omk", [128, 8])
            b_par = Buf("rpar")
            self.dma("sp", mu[:], dr["rwkv_mu"][l], w=[b_par])
            self.dma("sp", w0a0[:], dr["rwkv_w0a0"][l], w=[b_par])
            self.dma("sp", vec[:], dr["rwkv_vec"][l], w=[b_par])
            self.ts("dve", c0[:], mu[:, :, 0], -1.0, 1.0, ALU.mult, ALU.add, r=[b_par], w=[b_par])
            self.tt("dve", c0[:], c0[:], mu[:, :, 1], ALU.subtract, r=[b_par], w=[b_par])
            self.ts("dve", omk[:], vec[:, 1, :], -1.0, 1.0, ALU.mult, ALU.add, r=[b_par], w=[b_par])
            wst = sb("wst", [128, 1024])
            WA = sb("WA", [128, 2, 1024])
            GU = sb("GU", [128, 1024], BF16)
            b_wst, b_W = Buf("wst"), Buf("WA")
            for d in range(2):
                self.dma("sp", wst[0:64, :], dr["rwkv_w_up"][l, d], w=[b_wst])
                self.dma("sp", wst[64:128, :], dr["rwkv_a_up"][l, d], w=[b_wst])
                self.cp("pool", WA[:, d, :], wst[:], r=[b_wst], w=[b_W])
            self.dma("sp", wst[:], dr["rwkv_g_up"][l], w=[b_wst])
            self.cp("pool", GU[:], wst[:], r=[b_wst], w=[b_W])
            LW = sb("LW", [128, NT])
            GL = sb("GL", [128, NT], BF16)
            Yacc = sb("Yacc", [128, NCK, 64])
            ksum = sb("ksum", [128, NT])
            b_LW, b_GL, b_Yacc, b_ksum = Buf("LW"), Buf("GL"), Buf("Yacc"), Buf("ksum")
            xp = sb("xp", [128, SEGN + 2])
            rT, kT, vT, kap = sb("rT", [128, SEGN]), sb("kT", [128, SEGN]), sb("vT", [128, SEGN]), sb("kap", [128, SEGN])
            T1, T2, T3, T4 = [sb(f"T{i}", [128, SEGN]) for i in range(1, 5)]
            ynT = sb("ynT", [128, SEGN])
            Vb, gTb, sqb = sb("Vb", [128, SEGN], BF16), sb("gTb", [128, SEGN], BF16), sb("sqb", [128, SEGN], BF16)
            stk = sb("stk", [128, 4, SEGN], BF16)
            YBD = sb("YBD", [128, SEGC, 128], BF16)
            yst = sb("yst", [128, SEGN], BF16)
            gC = sb("gC", [128, SEGC])
            lnst = sb("lnst", [128, 6, SEGC])
            b_xp, b_rT, b_kT, b_vT, b_kap, b_T1, b_T2, b_T3, b_T4, b_ynT, b_Vb, b_gTb, b_sqb, b_stk, b_YBD, b_yst, b_gC, b_ln = [
                Buf(n) for n in "xp rT kT vT kap T1 T2 T3 T4 ynT Vb gTb sqb stk YBD yst gC lnst".split()]
            self.memset("pool", YBD[:], 0.0, w=[b_YBD])
            G = 4
            BDg = [sb(f"BDg{i}", [128, G, 5, 128], BF16) for i in range(2)]
            b_BDg = mkbufs("BDg", 2)
            for i in range(2):
                self.memset("pool", BDg[i][:], 0.0, w=[b_BDg[i]])
            SA = [sb(f"SA{i}", [128, 512], BF16) for i in range(G)]
            SB_ = [sb(f"SB{i}", [128, 512], BF16) for i in range(G)]
            MN = [[sb(f"MN{i}_{k}", [128, 256], BF16) for k in range(2)] for i in range(G)]
            Rb = [[sb(f"Rb{i}_{k}", [128, 256], BF16) for k in range(2)] for i in range(G)]
            Pb = [[sb(f"Pb{i}_{k}", [128, 128], BF16) for k in range(2)] for i in range(G)]
            MO = [sb(f"MO{i}", [128, 128], BF16) for i in range(G)]
            QP = [sb(f"QP{i}", [128, 256], BF16) for i in range(G)]
            AK = [sb(f"AK{i}", [128, 256], BF16) for i in range(G)]
            VM = [sb(f"VM{i}", [128, G, 64], BF16) for i in range(2)]
            b_VM = mkbufs("VM", 2)
            b_MO = mkbufs("MO", G)
            pending = []
            ST = sb("ST", [128, 64], BF16)
            S32 = sb("S32", [128, 64])
            S32g = sb("S32g", [128, 64])
            b_S32, b_S32g = Buf("S32"), Buf("S32g")
            b_SA, b_SB, b_QP, b_AK = [mkbufs(n, G) for n in "SA SB QP AK".split()]
            b_MN, b_Rb, b_Pb = [[mkbufs(f"{n}{i}_", 2) for i in range(G)] for n in "MN Rb Pb".split()]
            b_ST = Buf("ST")
            kps = [0]

            def shift(dst, b_dst, c, c0_, n):
                p0 = c0_ * 64
                p1 = p0 + n
                hasL = p0 not in (0, LCTX)
                hasR = p1 not in (LCTX, NT)
                if not hasL:
                    self.memset("pool", xp[:, 0:1], 0.0, w=[b_xp])
                if not hasR:
                    self.memset("pool", xp[:, n + 1:n + 2], 0.0, w=[b_xp])
                lo, hi = p0 - int(hasL), p1 + int(hasR)
                self.dma("sp", xp[:, 1 - int(hasL):1 + n + int(hasR)], uT[c * 128:(c + 1) * 128, lo:hi], w=[b_xp])
                self.ts("dve", dst[:, 0:n], xp[:, 1:1 + n], c0[:, c:c + 1], None, ALU.mult, r=[b_xp, b_par], w=[b_dst])
                self.stt(dst[:, 0:n], xp[:, 0:n], mu[:, c, 0:1], dst[:, 0:n], ALU.mult, ALU.add, r=[b_xp, b_par, b_dst], w=[b_dst])
                self.stt(dst[:, 0:n], xp[:, 2:2 + n], mu[:, c, 1:2], dst[:, 0:n], ALU.mult, ALU.add, r=[b_xp, b_par, b_dst], w=[b_dst])

            def tiles_of(n):
                return [(o, min(512, n - o)) for o in range(0, n, 512)]

            def nextps():
                kps[0] += 1
                return 7

            for (ck0, nck) in segs:
                n = nck * 64
                t0 = ck0 * 64
                shift(T1, b_T1, 24, ck0, n)
                self.act(LW[0:64, t0:t0 + n], T1[0:64, 0:n], AF.Tanh, [b_T1], [b_LW])
                self.act(LW[64:128, t0:t0 + n], T1[64:128, 0:n], AF.Copy, [b_T1], [b_LW])
                shift(T2, b_T2, 25, ck0, n)
                self.act(GL[:, t0:t0 + n], T2[:, 0:n], AF.Sigmoid, [b_T2], [b_GL])

            kbd = [0]
            for j in range(8):
                for d in range(2):
                    self.memset("pool", ST[:], 0.0, w=[b_ST])
                    self.memset("pool", S32[:], 0.0, w=[b_S32])
                    order = segs if d == 0 else [segs[0]] + segs[1:][::-1]
                    for (ck0, nck) in order:
                        n = nck * 64
                        t0 = ck0 * 64
                        shift(rT, b_rT, j, ck0, n)
                        shift(kT, b_kT, 8 + j, ck0, n)
                        shift(vT, b_vT, 16 + j, ck0, n)
                        self.act(Vb[:, 0:n], vT[:, 0:n], AF.Copy, [b_vT], [b_Vb])
                        self.ts("dve", kap[:, 0:n], kT[:, 0:n], vec[:, 0, j:j + 1], None, ALU.mult, r=[b_kT, b_par], w=[b_kap])
                        self.act(sqb[:, 0:n], kap[:, 0:n], AF.Square, [b_kap], [b_sqb])
                        for (o, m) in tiles_of(n):
                            pb = nextps()
                            self.mm(psb[pb][:, 0:m], bdones[:], sqb[:, o:o + m], r=[b_k, b_sqb], w=[b_ps[pb]])
                            self.act(T1[:, o:o + m], psb[pb][:, 0:m], AF.Sqrt, [b_ps[pb]], [b_T1], bias=1e-24)
                        self.em.op("dve", lambda e, n=n: e.reciprocal(out=T1[:, 0:n], in_=T1[:, 0:n]), [b_T1], [b_T1])
                        self.tt("dve", kap[:, 0:n], kap[:, 0:n], T1[:, 0:n], ALU.mult, r=[b_kap, b_T1], w=[b_kap])
                        if d == 1:
                            for (o, m) in tiles_of(n):
                                pb = nextps()
                                self.mm(psb[pb][:, 0:m], GU[:, j * 128:(j + 1) * 128], GL[:, t0 + o:t0 + o + m],
                                        r=[b_W, b_GL], w=[b_ps[pb]])
                                self.act(gTb[:, o:o + m], psb[pb][:, 0:m], AF.Copy, [b_ps[pb]], [b_gTb])
                        for (o, m) in tiles_of(n):
                            pb = nextps()
                            self.mm(psb[pb][:, 0:m], WA[0:64, d, j * 128:(j + 1) * 128], LW[0:64, t0 + o:t0 + o + m],
                                    r=[b_W, b_LW], w=[b_ps[pb]])
                            self.act(T1[:, o:o + m], psb[pb][:, 0:m], AF.Sigmoid, [b_ps[pb], b_par], [b_T1],
                                     bias=w0a0[:, 0, d, j:j + 1])
                            pb = nextps()
                            self.mm(psb[pb][:, 0:m], WA[64:128, d, j * 128:(j + 1) * 128], LW[64:128, t0 + o:t0 + o + m],
                                    r=[b_W, b_LW], w=[b_ps[pb]])
                            self.act(T2[:, o:o + m], psb[pb][:, 0:m], AF.Sigmoid, [b_ps[pb], b_par], [b_T2],
                                     bias=w0a0[:, 1, d, j:j + 1])
                        self.scan(T3[:, 0:n], cmask[:, 0:n], T1[:, 0:n], 0.0, r=[b_k, b_T1], w=[b_T3])
                        T3v = T3[:, 0:n].rearrange("p (c s) -> p c s", s=64)
                        T1v = T1[:, 0:n].rearrange("p (c s) -> p c s", s=64)
                        self.act(gC[:, 0:nck], T3v[:, :, 63], AF.Exp, [b_T3], [b_gC], scale=-CW)
                        if d == 1:
                            self.cp("pool", lnst[:, 0, 0:nck], T3v[:, :, 63], r=[b_T3], w=[b_ln])
                            self.tt("dve", T3[:, 0:n], T1[:, 0:n], T3[:, 0:n], ALU.subtract, r=[b_T1, b_T3], w=[b_T3])
                            self.tt("dve", T3v, T3v, lnst[:, 0, 0:nck].unsqueeze(2).to_broadcast([128, nck, 64]), ALU.add,
                                    r=[b_T3, b_ln], w=[b_T3])
                        self.tt("dve", T1[:, 0:n], T3[:, 0:n], T1[:, 0:n], ALU.subtract, r=[b_T1, b_T3], w=[b_T1])
                        self.act(T1[:, 0:n], T1[:, 0:n], AF.Exp, [b_T1], [b_T1], scale=-CW)
                        self.tt("dve", stk[:, 0, 0:n], kap[:, 0:n], T1[:, 0:n], ALU.mult, r=[b_kap, b_T1], w=[b_stk])
                        self.act(T4[:, 0:n], T3[:, 0:n], AF.Exp, [b_T3], [b_T4], scale=CW)
                        self.act(T3[:, 0:n], T3[:, 0:n], AF.Exp, [b_T3], [b_T3], scale=-CW)
                        self.tt("pool", stk[:, 1, 0:n], rT[:, 0:n], T3[:, 0:n], ALU.mult, r=[b_rT, b_T3], w=[b_stk])
                        self.ts("dve", T1[:, 0:n], T2[:, 0:n], vec[:, 1, j:j + 1], omk[:, j:j + 1], ALU.mult, ALU.add,
                                r=[b_T2, b_par], w=[b_T1])
                        self.tt("dve", T1[:, 0:n], T1[:, 0:n], kT[:, 0:n], ALU.mult, r=[b_T1, b_kT], w=[b_T1])
                        if d == 0:
                            self.cp("pool", ksum[:, t0:t0 + n], T1[:, 0:n], r=[b_T1], w=[b_ksum])
                        else:
                            self.tt("pool", ksum[:, t0:t0 + n], ksum[:, t0:t0 + n], T1[:, 0:n], ALU.add, r=[b_T1, b_ksum], w=[b_ksum])
                        self.tt("dve", stk[:, 3, 0:n], T1[:, 0:n], T4[:, 0:n], ALU.mult, r=[b_T1, b_T4], w=[b_stk])
                        self.tt("pool", T2[:, 0:n], T2[:, 0:n], kap[:, 0:n], ALU.mult, r=[b_T2, b_kap], w=[b_T2])
                        self.tt("dve", stk[:, 2, 0:n], T2[:, 0:n], T4[:, 0:n], ALU.mult, r=[b_T2, b_T4], w=[b_stk])
                        corder = list(range(nck)) if d == 0 else list(range(nck))[::-1]
                        for gi in range(0, nck, G):
                            grp = corder[gi:gi + G]
                            cl = min(grp)
                            bg = kbd[0] % 2
                            kbd[0] += 1
                            for qi in range(5):
                                for hh in range(2):
                                    hs = slice(hh * 64, (hh + 1) * 64)
                                    src = (Vb[hs, cl * 64:(cl + G) * 64] if qi == 4 else stk[hs, qi, cl * 64:(cl + G) * 64])
                                    src = src.rearrange("p (c s) -> p c s", s=64)
                                    eng = ("pool", "act", "pool")[(qi * 2 + hh) % 3]
                                    if eng == "act":
                                        self.act(BDg[bg][hs, :, qi, hs], src, AF.Copy, [b_stk, b_Vb], [b_BDg[bg]])
                                    else:
                                        self.cp(eng, BDg[bg][hs, :, qi, hs], src, r=[b_stk, b_Vb], w=[b_BDg[bg]])
                            rB = [b_BDg[bg], b_k]
                            R4 = range(len(grp))
                            gof = [ci - cl for ci in grp]
                            bd = lambda i, q: BDg[bg][:, gof[i], q, :]
                            slot = lambda i: psb[i]
                            bsl = lambda i: b_ps[i]
                            for i in R4:
                                self.mm(psb[4][:, i * 64:(i + 1) * 64], bd(i, 4), istack[:], r=rB, w=[b_ps[4]], defer=(i != len(grp) - 1))
                            self.act(VM[bg][:, 0:len(grp), :], psb[4][:, 0:64 * len(grp)].rearrange("p (c s) -> p c s", s=64), AF.Copy,
                                     [b_ps[4]], [b_VM[bg]])
                            pend = pending[:]
                            del pending[:]

                            def drain(k=1):
                                for _ in range(k):
                                    if pend:
                                        pend.pop(0)()
                            for i in R4:
                                self.mm(slot(i)[:, 0:256], bd(i, 2), BDg[bg][:, gof[i], 0:2, :], r=rB, w=[bsl(i)], defer=True)
                                self.mm(slot(i)[:, 256:384], bd(i, 2), ident_bf[:], r=rB, w=[bsl(i)], defer=True)
                                self.mm(slot(i)[:, 384:512], bd(i, 0), ident_bf[:], r=rB, w=[bsl(i)])
                            drain()
                            for i in R4:
                                self.tt("dve", SA[i][:], slot(i)[:, :], rkm[:, d, 0:512], ALU.mult, r=[bsl(i), b_k], w=[b_SA[i]])
                            for i in R4:
                                self.mm(slot(i)[:, 0:128], bd(i, 3), bd(i, 1), r=rB, w=[bsl(i)], defer=True)
                                self.mm(slot(i)[:, 128:256], bd(i, 3), ident_bf[:], r=rB, w=[bsl(i)], defer=True)
                                self.mm(slot(i)[:, 256:512], bd(i, 0), BDg[bg][:, gof[i], 2:4, :], r=rB, w=[bsl(i)])
                            drain()
                            for i in R4:
                                self.tt("dve", SB_[i][:], slot(i)[:, :], rkm[:, d, 512:1024], ALU.mult, r=[bsl(i), b_k], w=[b_SB[i]])
                                self.tt("dve", MO[i][:], slot(i)[:, 256:384], rkm[:, d, 1024:1152], ALU.mult, r=[bsl(i), b_k], w=[b_MO[i]])
                            for i in R4:
                                self.tt("pool", Pb[i][0][:], ident_bf[:], SB_[i][:, 256:384], ALU.subtract, r=[b_k, b_SB[i]], w=[b_Pb[i][0]])
                            cur = [(SB_[i][:, 256:384], SA[i][:, 0:128], b_SB[i], b_SA[i]) for i in R4]
                            for lev in range(4):
                                lastl = lev == 3
                                mi = lev % 2
                                for i in R4:
                                    Mc, Nc, bM, bN = cur[i]
                                    self.mm(slot(i)[:, 128:256], Mc, Nc, r=[bM, bN], w=[bsl(i)], defer=(not lastl))
                                    if not lastl:
                                        self.mm(slot(i)[:, 0:128], Nc, Mc, r=[bM, bN], w=[bsl(i)])
                                drain()
                                lo = 128 if lastl else 0
                                for i in R4:
                                    self.act(MN[i][mi][:, lo:256], slot(i)[:, lo:256], AF.Copy, [bsl(i)], [b_MN[i][mi]])
                                    cur[i] = (MN[i][mi][:, 0:128], MN[i][mi][:, 128:256], b_MN[i][mi], b_MN[i][mi])
                                for i in R4:
                                    self.mm(slot(i)[:, 256:384], cur[i][1], Pb[i][lev % 2][:], r=[cur[i][3], b_Pb[i][lev % 2]], w=[bsl(i)])
                                for i in R4:
                                    self.tt("dve", Pb[i][(lev + 1) % 2][:], Pb[i][lev % 2][:], slot(i)[:, 256:384], ALU.add,
                                            r=[b_Pb[i][lev % 2], bsl(i)], w=[b_Pb[i][(lev + 1) % 2]])
                            drain(4)
                            for i in R4:
                                self.mm(slot(i)[:, 0:256], Pb[i][0][:], SA[i][:, 128:384], r=[b_Pb[i][0], b_SA[i]], w=[bsl(i)])
                            for i in R4:
                                self.act(Rb[i][0][:], slot(i)[:, 0:256], AF.Copy, [bsl(i)], [b_Rb[i][0]])
                            for i in R4:
                                self.mm(slot(i)[:, 256:512], MO[i][:], Rb[i][0][:], r=[b_MO[i], b_Rb[i][0]], w=[bsl(i)])
                            for i in R4:
                                self.tt("dve", Rb[i][1][:], SA[i][:, 128:384], slot(i)[:, 256:512], ALU.subtract, r=[b_SA[i], bsl(i)], w=[b_Rb[i][1]])
                            for i in R4:
                                self.mm(slot(i)[:, 0:256], Pb[i][0][:], Rb[i][1][:], r=[b_Pb[i][0], b_Rb[i][1]], w=[bsl(i)])
                            for i in R4:
                                self.act(Rb[i][0][:], slot(i)[:, 0:256], AF.Copy, [bsl(i)], [b_Rb[i][0]])
                            for i in R4:
                                self.mm(slot(i)[:, 256:512], SA[i][:, 384:512], Rb[i][0][:], r=[b_SA[i], b_Rb[i][0]], w=[bsl(i)], defer=True)
                                self.mm(slot(i)[:, 0:256], SB_[i][:, 384:512], Rb[i][0][:], r=[b_SB[i], b_Rb[i][0]], w=[bsl(i)])
                            for i in R4:
                                self.tt("dve", QP[i][:, 0:128], bd(i, 1), slot(i)[:, 256:384], ALU.subtract, r=[b_BDg[bg], bsl(i)], w=[b_QP[i]])
                                self.ts("dve", QP[i][:, 128:256], slot(i)[:, 384:512], -1.0, None, ALU.mult, r=[bsl(i)], w=[b_QP[i]])
                                self.tt("dve", AK[i][:], SB_[i][:, 0:256], slot(i)[:, 0:256], ALU.subtract, r=[b_SB[i], bsl(i)], w=[b_AK[i]])
                            for i in R4:
                                def step(i=i, ci=grp[i], bg=bg, d=d, ck0=ck0):
                                    SQ = psb[6][:, i * 128:(i + 1) * 128]
                                    vm = VM[bg][:, i, :]
                                    self.ts("pool", S32g[:], S32[:], gC[:, ci:ci + 1], None, ALU.mult, r=[b_S32, b_gC], w=[b_S32g])
                                    self.mm(SQ[:, 0:64], QP[i][:, 0:128], ST[:], start=True, stop=False, r=[b_QP[i], b_ST], w=[b_ps[6]])
                                    self.mm(SQ[:, 0:64], AK[i][:, 0:128], vm, start=False, stop=True, r=[b_AK[i], b_VM[bg]], w=[b_ps[6]])
                                    self.mm(SQ[:, 64:128], QP[i][:, 128:256], ST[:], start=True, stop=False, r=[b_QP[i], b_ST], w=[b_ps[6]])
                                    self.mm(SQ[:, 64:128], AK[i][:, 128:256], vm, start=False, stop=True, r=[b_AK[i], b_VM[bg]], w=[b_ps[6]])
                                    self.stt(S32[:], SQ[:, 64:128], gC[:, ci:ci + 1], S32g[:], ALU.mult, ALU.add,
                                             r=[b_ps[6], b_gC, b_S32g], w=[b_S32])
                                    self.act(ST[:], S32[:], AF.Copy, [b_S32], [b_ST])
                                    ck = ck0 + ci
                                    if d == 0:
                                        self.cp("dve", Yacc[:, ck, :], SQ[:, 0:64], r=[b_ps[6]], w=[b_Yacc])
                                    else:
                                        self.tt("dve", Yacc[:, ck, :], Yacc[:, ck, :], SQ[:, 0:64], ALU.add, r=[b_ps[6], b_Yacc], w=[b_Yacc])
                                pending.append(step)
                            while pend:
                                pend.pop(0)()
                        while pending:
                            pending.pop(0)()
                        if d == 1:
                            Ys = Yacc[:, ck0:ck0 + nck, :]
                            T3v = T3[:, 0:n].rearrange("p (c s) -> p c s", s=64)
                            mean, ssq, m2, var = lnst[:, 1, 0:nck], lnst[:, 2, 0:nck], lnst[:, 3, 0:nck], lnst[:, 4, 0:nck]
                            self.em.op("dve", lambda e, Ys=Ys, mean=mean: e.tensor_reduce(out=mean, in_=Ys, axis=AX.X, op=ALU.add),
                                       [b_Yacc], [b_ln])
                            self.act(T3v, Ys, AF.Square, [b_Yacc], [b_T3])
                            self.em.op("dve", lambda e, T3v=T3v, ssq=ssq: e.tensor_reduce(out=ssq, in_=T3v, axis=AX.X, op=ALU.add),
                                       [b_T3], [b_ln])
                            self.ts("dve", mean, mean, 1.0 / 64, None, ALU.mult, r=[b_ln], w=[b_ln])
                            self.tt("dve", m2, mean, mean, ALU.mult, r=[b_ln], w=[b_ln])
                            self.stt(var, ssq, 1.0 / 64, m2, ALU.mult, ALU.subtract, r=[b_ln], w=[b_ln])
                            self.act(var, var, AF.Sqrt, [b_ln], [b_ln], bias=64e-5)
                            self.em.op("dve", lambda e, var=var: e.reciprocal(out=var, in_=var), [b_ln], [b_ln])
                            self.tt("dve", T3v, Ys, mean.unsqueeze(2).to_broadcast([128, nck, 64]), ALU.subtract,
                                    r=[b_Yacc, b_ln], w=[b_T3])
                            for hh in range(2):
                                hs = slice(hh * 64, (hh + 1) * 64)
                                self.tt("dve", YBD[hs, 0:nck, hs], T3v[hs], var[hs].unsqueeze(2).to_broadcast([64, nck, 64]), ALU.mult,
                                        r=[b_T3, b_ln], w=[b_YBD])
                            for c8 in range(0, nck, 8):
                                m8 = min(8, nck - c8)
                                pb = 7
                                for ci in range(c8, c8 + m8):
                                    self.mm(psb[pb][:, (ci - c8) * 64:(ci - c8 + 1) * 64], YBD[:, ci, :], istack[:],
                                            r=[b_YBD, b_k], w=[b_ps[pb]], defer=(ci != c8 + m8 - 1))
                                self.ts("dve", ynT[:, c8 * 64:(c8 + m8) * 64], psb[pb][:, 0:m8 * 64], vec[:, 3, j:j + 1], vec[:, 4, j:j + 1],
                                        ALU.mult, ALU.add, r=[b_ps[pb], b_par], w=[b_ynT])
                            self.tt("pool", T1[:, 0:n], rT[:, 0:n], ksum[:, t0:t0 + n], ALU.mult, r=[b_rT, b_ksum], w=[b_T1])
                            self.ts("dve", sqb[:, 0:n], T1[:, 0:n], vec[:, 2, j:j + 1], None, ALU.mult, r=[b_T1, b_par], w=[b_sqb])
                            for (o, m) in tiles_of(n):
                                pb = 7
                                self.mm(psb[pb][:, 0:m], bdones[:], sqb[:, o:o + m], r=[b_k, b_sqb], w=[b_ps[pb]])
                                self.tt("dve", T2[:, o:o + m], psb[pb][:, 0:m], Vb[:, o:o + m], ALU.mult, r=[b_ps[pb], b_Vb], w=[b_T2])
                            self.tt("dve", ynT[:, 0:n], ynT[:, 0:n], T2[:, 0:n], ALU.add, r=[b_ynT, b_T2], w=[b_ynT])
                            self.tt("dve", yst[:, 0:n], ynT[:, 0:n], gTb[:, 0:n], ALU.mult, r=[b_ynT, b_gTb], w=[b_yst])
                            self.dma("pool", yA[j * 128:(j + 1) * 128, t0:t0 + n], yst[:, 0:n], r=[b_yst])
            self.em.flush()

    def rstd_tile(self, src, b_src, n, sq, b_sq, rs, b_rs, pb):
        psb, b_ps = self.psb, self.b_ps
        self.act(sq[:, :, 0:n], src[:, :, 0:n], AF.Square, [b_src], [b_sq])
        for kc in range(KC):
            self.mm(psb[pb][:, 0:n], self.ones_bf[:], sq[:, kc, 0:n], start=(kc == 0), stop=(kc == KC - 1),
                    r=[b_sq, self.b_ones], w=[b_ps[pb]])
        self.act(rs[:, 0:n], psb[pb][:, 0:n], AF.Sqrt, [b_ps[pb]], [b_rs], scale=1.0 / D, bias=1e-6)
        self.em.op("dve", lambda e: e.reciprocal(out=rs[:, 0:n], in_=rs[:, 0:n]), [b_rs], [b_rs])

    def phase_merge_ffn(self, l, xsrc):
        cfg = self.cfg
        NT, T = cfg.NT, cfg.T
        psb, b_ps = self.psb, self.b_ps
        dr = self.dram
        mod, b_mod = self.mod, self.b_mod
        uT, xmid, aT = dr["uT"], dr["xmid"], dr["aT"]
        last = l == cfg.depth - 1
        TW = 256
        tiles = [(t0, TW, 1 if t0 < LCTX else 0) for t0 in range(0, NT, TW)]
        with contextlib.ExitStack() as st0:
            h2T = self.sb(st0, "m_h2T", [128, KC, NT], BF16)
            b_h2 = mkbufs("h2T", len(tiles))
            ng = self.sb(st0, "m_ng", [128, 4, KC], F32)
            cols = self.sb(st0, "m_cols", [128, 4, KC, 2], F32)
            wst = self.sb(st0, "m_wst", [128, 1024], F32)
            b_ng, b_cols, b_wst = Buf("ng"), Buf("cols"), Buf("wst")
            self.dma("sp", ng[:], dr["norm_g"][l], w=[b_ng])
            for kc in range(KC):
                self.ts("dve", cols[:, 0, kc, :], mod[:, 16 + kc, :], ng[:, 1, kc:kc + 1], None, ALU.mult, r=[b_mod, b_ng], w=[b_cols])
                self.ts("dve", cols[:, 1, kc, :], mod[:, 32 + kc, :], 1.0, ng[:, 2, kc:kc + 1], ALU.add, ALU.mult, r=[b_mod, b_ng], w=[b_cols])
                self.ts("dve", cols[:, 2, kc, :], mod[:, 40 + kc, :], ng[:, 3, kc:kc + 1], None, ALU.mult, r=[b_mod, b_ng], w=[b_cols])
            with contextlib.ExitStack() as st:
                sb = lambda name, shape, dt=F32: self.sb(st, "m_" + name, shape, dt)
                Wbr = sb("Wbr", [128, 3, KC, 1024], BF16)
                Wo = sb("Wo", [128, KC, 1024], BF16)
                b_W = Buf("Wm")
                for br in range(3):
                    for kc in range(KC):
                        self.dma("sp", wst[:], dr["w_branch"][l, br, kc * 128:(kc + 1) * 128, :], w=[b_wst])
                        self.cp("pool", Wbr[:, br, kc, :], wst[:], r=[b_wst], w=[b_W])
                for kc in range(KC):
                    self.dma("sp", wst[:], dr["w_out"][l, kc * 128:(kc + 1) * 128, :], w=[b_wst])
                    self.cp("pool", Wo[:, kc, :], wst[:], r=[b_wst], w=[b_W])
                yt = [sb(f"yt{i}", [128, KC, TW], BF16) for i in range(3)]
                gt = sb("gt", [128, KC, TW])
                mt = sb("mt", [128, KC, TW])
                tmp = sb("tmp", [128, TW])
                mb = sb("mb", [128, KC, TW], BF16)
                xt = sb("xt", [128, KC, TW])
                mo = sb("mo", [128, KC, TW])
                sq = sb("sq", [128, KC, TW], BF16)
                rs = sb("rs", [128, TW])
                b_yt = mkbufs("yt", 3)
                b_gt, b_mt, b_tmp, b_mb, b_xt, b_mo, b_sq, b_rs = [Buf(n) for n in "gt mt tmp mb xt mo sq rs".split()]
                ysrc = [dr["yA"], dr["yB"], dr["yC"]]
                xview = xsrc.rearrange("(kc p) n -> p kc n", p=128)
                xmview = xmid.rearrange("(kc p) n -> p kc n", p=128)
                k = 0
                for ti, (t0, n, seg) in enumerate(tiles):
                    self.dma("sp", xt[:], xview[:, :, t0:t0 + n], w=[b_xt])
                    for br in range(3):
                        self.dma("sp", yt[br][:], ysrc[br].rearrange("(kc p) n -> p kc n", p=128)[:, :, t0:t0 + n], w=[b_yt[br]])
                        g0 = (66 + br * 8) * 128
                        self.dma("sp", gt[:], uT[g0:g0 + 1024, :].rearrange("(kc p) n -> p kc n", p=128)[:, :, t0:t0 + n], w=[b_gt])
                        self.act(gt[:], gt[:], AF.Sigmoid, [b_gt], [b_gt])
                        for oc in range(KC):
                            pb = k % 6
                            k += 1
                            for kc in range(KC):
                                self.mm(psb[pb][:, 0:n], Wbr[:, br, kc, oc * 128:(oc + 1) * 128], yt[br][:, kc, :],
                                        start=(kc == 0), stop=(kc == KC - 1), r=[b_W, b_yt[br]], w=[b_ps[pb]])
                            if br == 0:
                                self.tt("dve", mt[:, oc, :], psb[pb][:, 0:n], gt[:, oc, :], ALU.mult, r=[b_ps[pb], b_gt], w=[b_mt])
                            else:
                                self.tt("dve", tmp[:], psb[pb][:, 0:n], gt[:, oc, :], ALU.mult, r=[b_ps[pb], b_gt], w=[b_tmp])
                                self.tt("pool", mt[:, oc, :], mt[:, oc, :], tmp[:], ALU.add, r=[b_mt, b_tmp], w=[b_mt])
                    self.act(mb[:], mt[:], AF.Copy, [b_mt], [b_mb])
                    for oc in range(KC):
                        pb = k % 6
                        k += 1
                        for kc in range(KC):
                            self.mm(psb[pb][:, 0:n], Wo[:, kc, oc * 128:(oc + 1) * 128], mb[:, kc, :],
                                    start=(kc == 0), stop=(kc == KC - 1), r=[b_W, b_mb], w=[b_ps[pb]])
                        self.act(mo[:, oc, :], psb[pb][:, 0:n], AF.Copy, [b_ps[pb]], [b_mo])
                    self.rstd_tile(mo, b_mo, n, sq, b_sq, rs, b_rs, 6)
                    for oc in range(KC):
                        self.tt("dve", mo[:, oc, :], mo[:, oc, :], rs[:], ALU.mult, r=[b_mo, b_rs], w=[b_mo])
                        self.stt(xt[:, oc, :], mo[:, oc, :], cols[:, 0, oc, seg:seg + 1], xt[:, oc, :], ALU.mult, ALU.add,
                                 r=[b_mo, b_cols, b_xt], w=[b_xt])
                    self.dma("pool", xmview[:, :, t0:t0 + n], xt[:], r=[b_xt])
                    self.rstd_tile(xt, b_xt, n, sq, b_sq, rs, b_rs, 7)
                    for kc in range(KC):
                        self.tt("dve", mo[:, kc, :], xt[:, kc, :], rs[:], ALU.mult, r=[b_xt, b_rs, b_mo], w=[b_mo])
                        self.act(h2T[:, kc, t0:t0 + n], mo[:, kc, :], AF.Identity, [b_mo, b_cols, b_mod], [b_h2[ti]],
                                 scale=cols[:, 1, kc, seg:seg + 1], bias=mod[:, 24 + kc, seg:seg + 1])
                self.em.flush()
            with contextlib.ExitStack() as st:
                sb = lambda name, shape, dt=F32: self.sb(st, "f_" + name, shape, dt)
                cw = sb("cw", [128, 22, 3])
                cb = sb("cb", [128, 22])
                b_par = Buf("fpar")
                self.dma("sp", cw[:], dr["ffn_conv_w"][l], w=[b_par])
                self.dma("sp", cb[:], dr["ffn_conv_b"][l], w=[b_par])
                wf = [sb(f"wf{i}", [128, KC, 128]) for i in range(2)]
                wb = [sb(f"wb{i}", [128, KC, 128], BF16) for i in range(2)]
                b_wf, b_wb = mkbufs("fwf", 2), mkbufs("fwb", 2)
                gp = sb("gp", [128, NT + 6])
                gc = sb("gc", [128, NT])
                ast = sb("ast", [128, NT], BF16)
                b_gp, b_gc, b_ast = Buf("gp"), Buf("gc"), Buf("ast")
                self.memset("pool", gp[:], 0.0, w=[b_gp])
                wview = dr["ffn_w_in"][l].rearrange("(kc p) n -> p kc n", p=128)
                k = 0
                kw = 0
                alltiles = list(enumerate(tiles))
                h2tiles = self.cfg.tiles
                for jc in range(22):
                    for part in range(2):
                        s = kw % 2
                        kw += 1
                        c0 = part * DFF + jc * 128
                        self.dma("sp", wf[s][:], wview[:, :, c0:c0 + 128], w=[b_wf[s]])
                        self.cp("pool", wb[s][:], wf[s][:], r=[b_wf[s]], w=[b_wb[s]])
                        for (t0, n, seg) in h2tiles:
                            pb = k % 8
                            k += 1
                            hb = [b_h2[i] for i, (a, m, sg_) in alltiles if a < t0 + n and a + m > t0]
                            for kc in range(KC):
                                self.mm(psb[pb][:, 0:n], wb[s][:, kc, :], h2T[:, kc, t0:t0 + n], start=(kc == 0), stop=(kc == KC - 1),
                                        r=[b_wb[s]] + hb, w=[b_ps[pb]])
                            if part == 0:
                                p0 = t0 + 2 if t0 < LCTX else t0 + 4
                                self.act(gp[:, p0:p0 + n], psb[pb][:, 0:n], AF.Copy, [b_ps[pb]], [b_gp])
                            else:
                                self.tt("dve", ast[:, t0:t0 + n], psb[pb][:, 0:n], gc[:, t0:t0 + n], ALU.mult,
                                        r=[b_ps[pb], b_gc], w=[b_ast])
                        if part == 0:
                            for (d0, s0, n) in self.segs():
                                self.ts("dve", gc[:, d0:d0 + n], gp[:, s0 - 1:s0 - 1 + n], cw[:, jc, 0:1], cb[:, jc:jc + 1], ALU.mult, ALU.add,
                                        r=[b_gp, b_par], w=[b_gc])
                                for tap in (1, 2):
                                    self.stt(gc[:, d0:d0 + n], gp[:, s0 - 1 + tap:s0 - 1 + tap + n], cw[:, jc, tap:tap + 1], gc[:, d0:d0 + n],
                                             ALU.mult, ALU.add, r=[b_gp, b_par, b_gc], w=[b_gc])
                            self.act(gc[:], gc[:], AF.Silu, [b_gc], [b_gc])
                    self.dma("pool", aT[jc * 128:(jc + 1) * 128, :], ast[:], r=[b_ast])
                self.em.flush()
        with contextlib.ExitStack() as st:
            sb = lambda name, shape, dt=F32: self.sb(st, "g_" + name, shape, dt)
            ng = sb("ng", [128, 4, KC])
            cols = sb("cols", [128, KC, 2])
            wst = sb("wst", [128, 1024])
            Wf = sb("Wf", [128, 22, 1024], BF16)
            b_ng, b_cols, b_wst, b_W = Buf("ng"), Buf("cols"), Buf("wst"), Buf("Wf")
            self.dma("sp", ng[:], dr["norm_g"][l], w=[b_ng])
            for kc in range(KC):
                self.ts("dve", cols[:, kc, :], mod[:, 40 + kc, :], ng[:, 3, kc:kc + 1], None, ALU.mult, r=[b_mod, b_ng], w=[b_cols])
            for kc in range(22):
                self.dma("sp", wst[:], dr["ffn_w_out"][l, kc * 128:(kc + 1) * 128, :], w=[b_wst])
                self.cp("pool", Wf[:, kc, :], wst[:], r=[b_wst], w=[b_W])
            at = [sb(f"at{i}", [128, 22, 512], BF16) for i in range(2)]
            xt = [sb(f"xt{i}", [128, KC, 512]) for i in range(2)]
            fo = sb("fo", [128, KC, 512])
            sq = sb("sq", [128, KC, 512], BF16)
            rs = sb("rs", [128, 512])
            b_at, b_xt = mkbufs("at", 2), mkbufs("gxt", 2)
            b_fo, b_sq, b_rs = Buf("fo"), Buf("gsq"), Buf("grs")
            xmview = xmid.rearrange("(kc p) n -> p kc n", p=128)
            aview = aT.rearrange("(kc p) n -> p kc n", p=128)
            k = 0
            for ti, (t0, n, seg) in enumerate(cfg.tiles):
                if last and seg == 1:
                    continue
                s = ti % 2
                self.dma("sp", at[s][:, :, 0:n], aview[:, :, t0:t0 + n], w=[b_at[s]])
                self.dma("sp", xt[s][:, :, 0:n], xmview[:, :, t0:t0 + n], w=[b_xt[s]])
                for oc in range(KC):
                    pb = k % 6
                    k += 1
                    for kc in range(22):
                        self.mm(psb[pb][:, 0:n], Wf[:, kc, oc * 128:(oc + 1) * 128], at[s][:, kc, 0:n], start=(kc == 0), stop=(kc == 21),
                                r=[b_W, b_at[s]], w=[b_ps[pb]])
                    self.act(fo[:, oc, 0:n], psb[pb][:, 0:n], AF.Copy, [b_ps[pb]], [b_fo])
                self.rstd_tile(fo, b_fo, n, sq, b_sq, rs, b_rs, 6 + ti % 2)
                for oc in range(KC):
                    self.tt("dve", fo[:, oc, 0:n], fo[:, oc, 0:n], rs[:, 0:n], ALU.mult, r=[b_fo, b_rs], w=[b_fo])
                    self.stt(xt[s][:, oc, 0:n], fo[:, oc, 0:n], cols[:, oc, seg:seg + 1], xt[s][:, oc, 0:n], ALU.mult, ALU.add,
                             r=[b_fo, b_cols, b_xt[s]], w=[b_xt[s]])
                if last:
                    dst = dr["outT"].rearrange("(kc p) n -> p kc n", p=128)[:, :, t0 - LCTX:t0 - LCTX + n]
                else:
                    dst = dr["xcur"].rearrange("(kc p) n -> p kc n", p=128)[:, :, t0:t0 + n]
                self.dma("pool", dst, xt[s][:, :, 0:n], r=[b_xt[s]])
            self.em.flush()

    def phase_mod(self, l, cvec, ada_w, ada_b, mod, b_mod):
        em = self.em
        psb, b_ps = self.psb, self.b_ps
        with contextlib.ExitStack() as st:
            cv = self.sb(st, "cv", [128, KC, 2], F32)
            cs = self.sb(st, "cs", [128, KC, 2], BF16)
            sg = self.sb(st, "cv_sg", [128, KC, 2], F32)
            ab = self.sb(st, "ab", [128, 48], F32)
            wf = [self.sb(st, f"adaw_f{i}", [128, KC, 512], F32) for i in range(2)]
            wb = [self.sb(st, f"adaw_b{i}", [128, KC, 512], BF16) for i in range(2)]
            b_cv, b_cs, b_ab, b_sg = Buf("cv"), Buf("cs"), Buf("ab"), Buf("sg")
            b_wf, b_wb = mkbufs("wf", 2), mkbufs("wb", 2)
            em.dma("sp", lambda e: e.dma_start(out=cv[:], in_=cvec[:, :, :]), writes=[b_cv])
            em.dma("sp", lambda e: e.dma_start(out=ab[:], in_=ada_b[l]), writes=[b_ab])
            em.op("act", lambda e: e.activation(out=sg[:], in_=cv[:], func=AF.Sigmoid), reads=[b_cv], writes=[b_sg])
            em.op("dve", lambda e: e.tensor_tensor(out=cs[:], in0=cv[:], in1=sg[:], op=ALU.mult),
                  reads=[b_cv, b_sg], writes=[b_cs])
            wview = ada_w[l].rearrange("(kc p) n -> p kc n", p=128)
            for g in range(12):
                s = g % 2
                em.dma("sp", lambda e, s=s, g=g: e.dma_start(out=wf[s][:], in_=wview[:, :, g * 512:(g + 1) * 512]),
                       writes=[b_wf[s]])
                em.op("pool", lambda e, s=s: e.tensor_copy(out=wb[s][:], in_=wf[s][:]),
                      reads=[b_wf[s]], writes=[b_wb[s]])
                pb = g % 8
                for q in range(4):
                    i = g * 4 + q
                    for kc in range(KC):
                        em.op("pe", lambda e, s=s, q=q, kc=kc, pb=pb: e.matmul(
                            psb[pb][:, q * 2:q * 2 + 2], lhsT=wb[s][:, kc, q * 128:(q + 1) * 128],
                            rhs=cs[:, kc, :], start=(kc == 0), stop=(kc == KC - 1)),
                            reads=[b_wb[s], b_cs], writes=[b_ps[pb]], defer=(kc != KC - 1))
                for q in range(4):
                    i = g * 4 + q
                    em.op("dve", lambda e, q=q, i=i, pb=pb: e.tensor_scalar(
                        out=mod[:, i, :], in0=psb[pb][:, q * 2:q * 2 + 2], scalar1=ab[:, i:i + 1], scalar2=None,
                        op0=ALU.add), reads=[b_ps[pb], b_ab], writes=[b_mod])
            em.flush()

    def phase_norm_inproj(self, l, xc, norm_g, w_in, uT, mod, b_mod, ones_bf, b_ones):
        cfg = self.cfg
        em = self.em
        NT = cfg.NT
        psb, b_ps = self.psb, self.b_ps
        with contextlib.ExitStack() as st:
            hT = self.sb(st, "hT", [128, KC, NT], BF16)
            b_h = mkbufs("hT", len(cfg.tiles))
            ng = self.sb(st, "ng", [128, 4, KC], F32)
            A1 = self.sb(st, "A1", [128, KC, 2], F32)
            b_ng, b_A1 = Buf("ng"), Buf("A1")
            em.dma("sp", lambda e: e.dma_start(out=ng[:], in_=norm_g[l]), writes=[b_ng])
            for kc in range(KC):
                em.op("dve", lambda e, kc=kc: e.tensor_scalar(
                    out=A1[:, kc, :], in0=mod[:, 8 + kc, :], scalar1=1.0, scalar2=ng[:, 0, kc:kc + 1],
                    op0=ALU.add, op1=ALU.mult), reads=[b_mod, b_ng], writes=[b_A1])
            with contextlib.ExitStack() as st2:
                xt = [self.sb(st2, f"xt{i}", [128, KC, 512], F32) for i in range(2)]
                sq = [self.sb(st2, f"sq{i}", [128, KC, 512], BF16) for i in range(2)]
                rs = [self.sb(st2, f"rs{i}", [128, 512], F32) for i in range(2)]
                tmp = [self.sb(st2, f"tmp{i}", [128, KC, 512], F32) for i in range(2)]
                b_xt, b_sq, b_rs, b_tmp = mkbufs("xt", 2), mkbufs("sq", 2), mkbufs("rs", 2), mkbufs("tmp", 2)
                xview = xc.rearrange("(kc p) n -> p kc n", p=128)
                for ti, (t0, n, seg) in enumerate(cfg.tiles):
                    s = ti % 2
                    pb = ti % 8
                    em.dma("sp", lambda e, s=s, t0=t0, n=n: e.dma_start(out=xt[s][:, :, 0:n], in_=xview[:, :, t0:t0 + n]),
                           writes=[b_xt[s]])
                    em.op("act", lambda e, s=s, n=n: e.activation(out=sq[s][:, :, 0:n], in_=xt[s][:, :, 0:n], func=AF.Square),
                          reads=[b_xt[s]], writes=[b_sq[s]])
                    for kc in range(KC):
                        em.op("pe", lambda e, s=s, n=n, kc=kc, pb=pb: e.matmul(
                            psb[pb][:, 0:n], lhsT=ones_bf[:], rhs=sq[s][:, kc, 0:n], start=(kc == 0), stop=(kc == KC - 1)),
                            reads=[b_sq[s], b_ones], writes=[b_ps[pb]], defer=(kc != KC - 1))
                    em.op("act", lambda e, s=s, n=n, pb=pb: e.activation(
                        out=rs[s][:, 0:n], in_=psb[pb][:, 0:n], func=AF.Sqrt, scale=1.0 / D, bias=1e-6),
                        reads=[b_ps[pb]], writes=[b_rs[s]])
                    em.op("dve", lambda e, s=s, n=n: e.reciprocal(out=rs[s][:, 0:n], in_=rs[s][:, 0:n]),
                          reads=[b_rs[s]], writes=[b_rs[s]])
                    for kc in range(KC):
                        em.op("dve", lambda e, s=s, n=n, kc=kc: e.tensor_tensor(
                            out=tmp[s][:, kc, 0:n], in0=xt[s][:, kc, 0:n], in1=rs[s][:, 0:n], op=ALU.mult),
                            reads=[b_xt[s], b_rs[s]], writes=[b_tmp[s]])
                        em.op("act", lambda e, s=s, n=n, kc=kc, t0=t0, seg=seg: e.activation(
                            out=hT[:, kc, t0:t0 + n], in_=tmp[s][:, kc, 0:n], func=AF.Identity,
                            scale=A1[:, kc, seg:seg + 1], bias=mod[:, kc, seg:seg + 1]),
                            reads=[b_tmp[s], b_A1, b_mod], writes=[b_h[ti]])
                em.flush()
            with contextlib.ExitStack() as st2:
                wf = [self.sb(st2, f"wf{i}", [128, KC, 128], F32) for i in range(2)]
                wb = [self.sb(st2, f"wb{i}", [128, KC, 128], BF16) for i in range(2)]
                stg = [self.sb(st2, f"stg{i}", [128, NT], F32) for i in range(2)]
                b_wf, b_wb, b_stg = mkbufs("wf", 2), mkbufs("wb", 2), mkbufs("stg", 2)
                wview = w_in[l].rearrange("(kc p) n -> p kc n", p=128)
                k = 0
                for j in range(self.NCH):
                    s = j % 2
                    em.dma("sp", lambda e, s=s, j=j: e.dma_start(out=wf[s][:], in_=wview[:, :, j * 128:(j + 1) * 128]),
                           writes=[b_wf[s]])
                    em.op("pool", lambda e, s=s: e.tensor_copy(out=wb[s][:], in_=wf[s][:]),
                          reads=[b_wf[s]], writes=[b_wb[s]])
                    for ti, (t0, n, seg) in enumerate(cfg.tiles):
                        pb = k % 8
                        k += 1
                        for kc in range(KC):
                            em.op("pe", lambda e, s=s, n=n, kc=kc, pb=pb, t0=t0: e.matmul(
                                psb[pb][:, 0:n], lhsT=wb[s][:, kc, :], rhs=hT[:, kc, t0:t0 + n],
                                start=(kc == 0), stop=(kc == KC - 1)),
                                reads=[b_wb[s], b_h[ti]], writes=[b_ps[pb]], defer=(kc != KC - 1))
                        if k % 2 == 0:
                            em.op("act", lambda e, s=s, n=n, pb=pb, t0=t0: e.activation(
                                out=stg[s][:, t0:t0 + n], in_=psb[pb][:, 0:n], func=AF.Copy),
                                reads=[b_ps[pb]], writes=[b_stg[s]])
                        else:
                            em.op("dve", lambda e, s=s, n=n, pb=pb, t0=t0: e.tensor_copy(
                                out=stg[s][:, t0:t0 + n], in_=psb[pb][:, 0:n]),
                                reads=[b_ps[pb]], writes=[b_stg[s]])
                    em.dma("pool", lambda e, s=s, j=j: e.dma_start(out=uT[j * 128:(j + 1) * 128, :], in_=stg[s][:]),
                           reads=[b_stg[s]], writes=[])
                em.flush()


def na_blocks(rows):
    nqb = rows // 8
    types = {}
    blocks = []
    for qb in range(nqb):
        q0 = qb * 8
        rs = lambda r: min(max(r - 4, 0), rows - 8)
        lo, hi = rs(q0), rs(q0 + 7) + 8
        cls = "f" if qb == 0 else ("l" if qb == nqb - 1 else "i")
        items = []
        for kr0 in range(lo, hi, 2):
            delta = kr0 - q0
            key = (cls, delta)
            if key not in types:
                types[key] = (len(types), q0, kr0)
            items.append((kr0, types[key][0], (delta + 4) // 2))
        blocks.append(items)
    return blocks, len(types)


def blocks_type(blocks, qb, kr0):
    for (k, ty, di) in blocks[qb]:
        if k == kr0:
            return ty
    raise KeyError


def na_mask_np(rows):
    nqb = rows // 8
    out = {}
    for qb in range(nqb):
        q0 = qb * 8
        rsf = lambda r: min(max(r - 4, 0), rows - 8)
        lo, hi = rsf(q0), rsf(q0 + 7) + 8
        cls = "f" if qb == 0 else ("l" if qb == nqb - 1 else "i")
        for kr0 in range(lo, hi, 2):
            key = (cls, kr0 - q0)
            if key in out:
                continue
            krow = kr0 + np.arange(2)[:, None, None, None]
            kc = np.arange(64)[None, :, None, None]
            qrow = q0 + np.arange(8)[None, None, :, None]
            qc = np.arange(64)[None, None, None, :]
            rs = np.clip(qrow - 4, 0, rows - 8)
            cs = np.clip(qc - 8, 0, 48)
            ok = (krow >= rs) & (krow < rs + 8) & (kc >= cs) & (kc < cs + 16)
            out[key] = np.where(ok, 0.0, -30000.0).reshape(128, 512).astype(np.float32)
    return np.stack(list(out.values()), axis=0)


def rope_tables(T):
    half = 32
    freqs = 10000.0 ** (-np.arange(0, half, 2, dtype=np.float32) / half)
    pos = np.arange(T)
    prow, pcol = pos // 64, pos % 64
    cos = np.zeros((64, T), np.float32)
    sin = np.zeros((64, T), np.float32)
    for d in range(64):
        p = prow if d < 32 else pcol
        dd = d % 32
        ang = p.astype(np.float32) * freqs[dd % 16]
        cos[d] = np.cos(ang)
        sin[d] = -np.sin(ang) if dd < 16 else np.sin(ang)
    return np.concatenate([cos, cos], 0), np.concatenate([sin, sin], 0)


def input_shapes(cfg):
    Ld, NT, T = cfg.depth, cfg.NT, cfg.T
    return {
        "xc": [D, NT], "cvec": [128, KC, 2],
        "ada_w": [Ld, D, 6 * D], "ada_b": [Ld, 128, 48], "norm_g": [Ld, 128, 4, KC],
        "w_in": [Ld, D, NIN_X],
        "lru_conv_w": [Ld, 128, 8, 4], "lru_conv_b": [Ld, 128, 8],
        "lru_gate_a_w": [Ld, 2, 16, 64, 64], "lru_gate_x_w": [Ld, 2, 16, 64, 64],
        "lru_gate_b": [Ld, 128, 2, 2, 8], "lru_lambda": [Ld, 128, 2, 8],
        "ident": [128, 128], "rope_cos": [128, T], "rope_sin": [128, T],
        "na_mask": [na_blocks(T // 64)[1], 128, 512], "rpb_pad": [Ld, 16, 24, 128],
        "bdones": [128, 128], "istack": [128, 64], "rk_mask": [2, 128, 1152],
        "rwkv_mu": [Ld, 128, 26, 2], "rwkv_w0a0": [Ld, 128, 2, 2, 8], "rwkv_vec": [Ld, 128, 5, 8],
        "w_branch": [Ld, 3, D, D], "w_out": [Ld, D, D], "ffn_w_in": [Ld, D, 2 * DFF], "ffn_w_out": [Ld, DFF, D],
        "ffn_conv_w": [Ld, 128, 22, 3], "ffn_conv_b": [Ld, 128, 22],
        "rwkv_w_up": [Ld, 2, 64, 1024], "rwkv_a_up": [Ld, 2, 64, 1024], "rwkv_g_up": [Ld, 128, 1024],
    }


def colfmt(v, n):
    v = np.asarray(v)
    return np.moveaxis(v.reshape(v.shape[:-1] + (n, 128)), -1, -2)


def rope_perm():
    idx = np.arange(1024)
    d = idx % 64
    dd = d % 32
    partner = np.where(dd < 16, idx + 16, idx - 16)
    return partner


def prep_shared(inp, cfg):
    f = lambda a: np.ascontiguousarray(a, dtype=np.float32)
    Ld = cfg.depth
    m = {}
    m["ada_w"] = f(inp["ada_w"][:Ld])
    m["ada_b"] = f(colfmt(inp["ada_b"][:Ld], 48))
    m["norm_g"] = f(colfmt(inp["norm_g"][:Ld], KC).transpose(0, 2, 1, 3))
    w_in = inp["w_in"][:Ld]
    C0 = A_COLS + 2048
    perm = rope_perm()
    wq = w_in[:, :, C0:C0 + 1024][:, :, perm]
    wk = w_in[:, :, C0 + 1024:C0 + 2048][:, :, perm]
    m["w_in"] = f(np.concatenate([w_in, wq, wk], axis=2))
    m["lru_conv_w"] = f(colfmt(inp["lru_conv_w"][:Ld], 8).transpose(0, 2, 3, 1))
    m["lru_conv_b"] = f(colfmt(inp["lru_conv_b"][:Ld], 8))
    m["lru_gate_a_w"] = f(inp["lru_gate_a_w"][:Ld])
    m["lru_gate_x_w"] = f(inp["lru_gate_x_w"][:Ld])
    gb = np.stack([inp["lru_gate_a_b"][:Ld], inp["lru_gate_x_b"][:Ld]], axis=1)
    m["lru_gate_b"] = f(colfmt(gb, 8).transpose(0, 3, 1, 2, 4))
    m["lru_lambda"] = f(colfmt(inp["lru_lambda"][:Ld], 8).transpose(0, 2, 1, 3))
    m["ident"] = np.eye(128, dtype=np.float32)
    cos, sin = rope_tables(cfg.T)
    m["rope_cos"], m["rope_sin"] = f(cos), f(sin)
    m["na_mask"] = f(na_mask_np(cfg.T // 64))
    rp = np.zeros((Ld, 16, 24, 128), np.float32)
    rp[:, :, 4:19, 48:79] = inp["na_rpb"][:Ld]
    m["rpb_pad"] = rp
    blk = np.kron(np.eye(2, dtype=np.float32), np.ones((64, 64), np.float32))
    m["bdones"] = blk
    m["istack"] = np.concatenate([np.eye(64, dtype=np.float32)] * 2, axis=0)
    i64 = np.arange(64)
    U = np.kron(np.eye(2), (i64[:, None] < i64[None, :])).astype(np.float32)
    UI = np.kron(np.eye(2), (i64[:, None] <= i64[None, :])).astype(np.float32)
    Lw, LI = U.T.copy(), UI.T.copy()
    ONE = np.ones((128, 128), np.float32)
    m32 = np.kron(np.eye(4), np.ones((32, 32))).astype(np.float32)
    fwd = np.concatenate([U * m32, UI, ONE, ONE, UI, ONE, Lw * m32, Lw, Lw * (1 - m32)], axis=1)
    bwd = np.concatenate([Lw * m32, LI, ONE, ONE, LI, ONE, U * m32, U, U * (1 - m32)], axis=1)
    m["rk_mask"] = np.stack([fwd, bwd], axis=0)
    m["rwkv_mu"] = f(colfmt(inp["rwkv_mu"][:Ld], 26).transpose(0, 2, 3, 1))
    w0a0 = np.stack([inp["rwkv_w0"][:Ld], inp["rwkv_a0"][:Ld]], axis=1)
    m["rwkv_w0a0"] = f(colfmt(w0a0, 8).transpose(0, 3, 1, 2, 4))
    vec = np.stack([inp["rwkv_k_k"][:Ld], inp["rwkv_k_a"][:Ld], inp["rwkv_r_k"][:Ld].reshape(Ld, 1024),
                    inp["rwkv_lnx_w"][:Ld], inp["rwkv_lnx_b"][:Ld]], axis=1)
    m["rwkv_vec"] = f(colfmt(vec, 8).transpose(0, 2, 1, 3))
    m["rwkv_w_up"] = f(inp["rwkv_w_up"][:Ld])
    m["rwkv_a_up"] = f(inp["rwkv_a_up"][:Ld])
    m["rwkv_g_up"] = f(inp["rwkv_g_up"][:Ld])
    for k in ("w_branch", "w_out", "ffn_w_in", "ffn_w_out"):
        m[k] = f(inp[k][:Ld])
    m["ffn_conv_w"] = f(colfmt(inp["ffn_conv_w"][:Ld], 22).transpose(0, 2, 3, 1))
    m["ffn_conv_b"] = f(colfmt(inp["ffn_conv_b"][:Ld], 22))
    return m


def prep_inputs(inp, b, cfg, shared=None):
    T = cfg.T
    f = lambda a: np.ascontiguousarray(a, dtype=np.float32)
    m = dict(shared if shared is not None else prep_shared(inp, cfg))
    m["xc"] = f(np.concatenate([inp["ctx"][b].T, inp["x"][b, :T].T], axis=1))
    cv = np.stack([inp["c"][b], inp["c_ctx"]], axis=1)
    m["cvec"] = f(cv.reshape(KC, 128, 2).transpose(1, 0, 2))
    return m


def kernel(**inputs):
    cfg = Cfg()
    bld = Builder(cfg)
    nc = bld.build()
    inp = {k: np.asarray(v) for k, v in inputs.items()}
    shared = prep_shared(inp, cfg)
    in_maps = [prep_inputs(inp, b, cfg, shared) for b in range(8)]
    res = run_bass_kernel_spmd(nc, in_maps, core_ids=list(range(8)))
    out = np.stack([r["outT"].T for r in res.results], axis=0)
    return out.astype(np.float32)
```

```python
import contextlib
import numpy as np
import concourse.bass as bass
import concourse.mybir as mybir
from concourse.bass_utils import run_bass_kernel_spmd

F32 = mybir.dt.float32
BF16 = mybir.dt.bfloat16
AF = mybir.ActivationFunctionType
ALU = mybir.AluOpType
AX = mybir.AxisListType

D = 1024
KC = 8
LCTX = 256
DFF = 2816
NHEAD = 16
A_COLS = 3 * 1024 + 256
N_IN = 11520
NIN_X = N_IN + 2048
ENGS = ("pe", "act", "dve", "pool", "sp")
NDMA_SEMS = 24
NODEFER = False


class Buf:
    __slots__ = ("name", "w", "readers")

    def __init__(self, name):
        self.name = name
        self.w = None
        self.readers = []


def mkbufs(name, n):
    return [Buf(f"{name}{i}") for i in range(n)]


class Emit:
    def __init__(self, nc, stack):
        self.nc = nc
        self.prog = {e: [] for e in ENGS}
        self.count = {e: 0 for e in ENGS}
        self.seen = {e: {} for e in ENGS}
        self.pend_inc = {}
        self.capture_list = None
        self.dma_total = [0] * NDMA_SEMS
        self.dma_rr = 0
        self.n_instr = 0
        self.sems = {}
        for e in ENGS:
            self.sems[e] = stack.enter_context(nc.semaphore(f"c_{e}"))
        for k in range(NDMA_SEMS):
            self.sems[("dma", k)] = stack.enter_context(nc.semaphore(f"d_{k}"))

    def _deps(self, eng, reads, writes):
        deps = {}

        def add(d):
            if d is None:
                return
            k, v = d
            if deps.get(k, 0) < v:
                deps[k] = v
        for b in reads:
            add(b.w)
        for b in writes:
            add(b.w)
            for r in b.readers:
                add(r)
        waits = []
        seen = self.seen[eng]
        for k, v in deps.items():
            if k == eng and v > self.count[eng]:
                continue
            if seen.get(k, 0) < v:
                seen[k] = v
                waits.append((k, v))
        return waits

    def _commit(self, me, reads, writes):
        for b in reads:
            b.readers.append(me)
            if len(b.readers) > 32:
                mx = {}
                for k, v in b.readers:
                    if mx.get(k, 0) < v:
                        mx[k] = v
                b.readers = list(mx.items())
        for b in writes:
            b.w = me
            b.readers = []

    def op(self, eng, fn, reads=(), writes=(), defer=False):
        if self.capture_list is not None:
            self.capture_list.append(("op", eng, fn, reads, writes, defer))
            return
        waits = self._deps(eng, reads, writes)
        if defer and not NODEFER:
            me = (eng, self.count[eng] + 1)
            self.pend_inc[eng] = 1
            self.prog[eng].append((waits, fn, None))
        else:
            self.count[eng] += 1
            me = (eng, self.count[eng])
            self.prog[eng].append((waits, fn, (eng, 1)))
            self.pend_inc[eng] = 0
        self._commit(me, reads, writes)
        self.n_instr += 1 + len(waits)

    def dma(self, q, fn, reads=(), writes=()):
        if self.capture_list is not None:
            self.capture_list.append(("dma", q, fn, reads, writes))
            return
        k = self.dma_rr
        self.dma_rr = (self.dma_rr + 1) % NDMA_SEMS
        key = ("dma", k)
        waits = self._deps(q, reads, writes)
        prev = self.dma_total[k]
        if prev > 0 and self.seen[q].get(key, 0) < prev:
            self.seen[q][key] = prev
            waits.append((key, prev))
        self.dma_total[k] += 16
        me = (key, self.dma_total[k])
        self.prog[q].append((waits, fn, (key, 16)))
        self._commit(me, reads, writes)
        self.n_instr += 1 + len(waits)

    def captured(self, f):
        lst = []
        self.capture_list = lst
        f()
        self.capture_list = None
        return lst

    def replay(self, lst, k):
        while k > 0 and lst:
            e = lst.pop(0)
            if e[0] == "op":
                self.op(*e[1:])
            else:
                self.dma(*e[1:])
            k -= 1

    def flush(self):
        assert all(v == 0 for v in self.pend_inc.values()), "deferred semaphore increment left dangling"
        nc = self.nc
        prog = self.prog
        sems = self.sems
        dma_fin = [(("dma", k), v) for k, v in enumerate(self.dma_total) if v > 0]

        def run(name, eng):
            for waits, fn, inc in prog[name]:
                for k, v in waits:
                    eng.wait_ge(sems[k], v)
                ins = fn(eng)
                if inc is not None:
                    ins.then_inc(sems[inc[0]], inc[1])
            if name in ("sp", "pool", "act"):
                for k, v in dma_fin:
                    eng.wait_ge(sems[k], v)

        with nc.Block() as block:
            @block.tensor
            def _(t):
                run("pe", t)

            @block.scalar
            def _(a):
                run("act", a)

            @block.vector
            def _(v):
                run("dve", v)

            @block.gpsimd
            def _(g):
                run("pool", g)

            @block.sync
            def _(s):
                run("sp", s)
        for k, v in dma_fin:
            for e in ENGS:
                self.seen[e][k] = v
        self.prog = {e: [] for e in ENGS}


class Cfg:
    def __init__(self, T=4096, depth=2, dbg=False):
        self.T = T
        self.NT = LCTX + T
        self.depth = depth
        self.dbg = dbg
        self.phases = ("lru", "na", "rwkv", "merge", "ffn")
        self.tiles = [(0, LCTX, 1)] + [(LCTX + 512 * i, 512, 0) for i in range(T // 512)]


class Builder:
    def __init__(self, cfg):
        self.cfg = cfg
        self.nc = bass.Bass("TRN2", target_bir_lowering=False)
        self.dram = {}

    def din(self, name, shape, dt=F32):
        t = self.nc.dram_tensor(name, list(shape), dt, kind="ExternalInput").ap()
        self.dram[name] = t
        return t

    def dscratch(self, name, shape, dt=F32, out=False):
        kind = "ExternalOutput" if (out or self.cfg.dbg) else "Internal"
        t = self.nc.dram_tensor(name, list(shape), dt, kind=kind).ap()
        self.dram[name] = t
        return t

    def sb(self, st, name, shape, dt):
        self._uid = getattr(self, "_uid", 0) + 1
        return st.enter_context(self.nc.sbuf_tensor(f"{name}_{self._uid}", list(shape), dt))

    def ps(self, st, name, shape, dt=F32):
        return st.enter_context(self.nc.psum_tensor(name, list(shape), dt))

    def build(self, upto=99):
        cfg = self.cfg
        nc = self.nc
        NT, T = cfg.NT, cfg.T
        Ld = cfg.depth
        self.NCH = NIN_X // 128
        for name, shape in input_shapes(cfg).items():
            self.din(name, shape)
        self.dscratch("uT", [NIN_X, NT])
        self.dscratch("yA", [D, NT], BF16)
        self.dscratch("yB", [D, NT], BF16)
        self.dscratch("yC", [D, NT], BF16)
        self.dscratch("aT", [DFF, NT], BF16)
        self.dscratch("xcur", [D, NT])
        self.dscratch("xmid", [D, NT])
        self.dscratch("outT", [D, T], out=True)
        dr = self.dram
        with contextlib.ExitStack() as outer:
            em = Emit(nc, outer)
            self.em = em
            mod = self.sb(outer, "mod", [128, 48, 2], F32)
            ones_bf = self.sb(outer, "ones_bf", [128, 128], BF16)
            b_mod = Buf("mod")
            b_ones = Buf("ones")
            self.mod, self.b_mod, self.ones_bf, self.b_ones = mod, b_mod, ones_bf, b_ones
            em.op("pool", lambda e: e.memset(ones_bf[:], 1.0), writes=[b_ones])
            psb = [self.ps(outer, f"psb{i}", [128, 512]) for i in range(8)]
            b_ps = mkbufs("ps", 8)
            self.psb, self.b_ps = psb, b_ps
            for l in range(Ld):
                xsrc = dr["xc"] if l == 0 else dr["xcur"]
                self.phase_mod(l, dr["cvec"], dr["ada_w"], dr["ada_b"], mod, b_mod)
                self.phase_norm_inproj(l, xsrc, dr["norm_g"], dr["w_in"], dr["uT"], mod, b_mod, ones_bf, b_ones)
                if upto <= 2:
                    break
                if "lru" in cfg.phases:
                    self.phase_lru(l)
                if "na" in cfg.phases:
                    self.phase_na(l)
                if "rwkv" in cfg.phases:
                    self.phase_rwkv(l)
                if "merge" in cfg.phases:
                    self.phase_merge_ffn(l, xsrc)
        return nc

    def mm(self, out, lhsT, rhs, start=True, stop=True, r=(), w=(), defer=None):
        if defer is None:
            defer = not stop
        self.em.op("pe", lambda e: e.matmul(out, lhsT=lhsT, rhs=rhs, start=start, stop=stop), r, w, defer=defer)

    def act(self, out, in_, func, r=(), w=(), scale=1.0, bias=0.0):
        self.em.op("act", lambda e: e.activation(out=out, in_=in_, func=func, scale=scale, bias=bias), r, w)

    def tt(self, eng, out, in0, in1, op, r=(), w=()):
        self.em.op(eng, lambda e: e.tensor_tensor(out=out, in0=in0, in1=in1, op=op), r, w)

    def ts(self, eng, out, in0, s1, s2, op0, op1=None, r=(), w=()):
        if op1 is None:
            self.em.op(eng, lambda e: e.tensor_scalar(out=out, in0=in0, scalar1=s1, scalar2=None, op0=op0), r, w)
        else:
            self.em.op(eng, lambda e: e.tensor_scalar(out=out, in0=in0, scalar1=s1, scalar2=s2, op0=op0, op1=op1), r, w)

    def stt(self, out, in0, sc, in1, op0, op1, r=(), w=()):
        self.em.op("dve", lambda e: e.scalar_tensor_tensor(out=out, in0=in0, scalar=sc, in1=in1, op0=op0, op1=op1), r, w)

    def cp(self, eng, out, in_, r=(), w=()):
        self.em.op(eng, lambda e: e.tensor_copy(out=out, in_=in_), r, w)

    def memset(self, eng, ap, val, w=()):
        self.em.op(eng, lambda e: e.memset(ap, val), (), w)

    def scan(self, out, d0, d1, init, r=(), w=()):
        self.em.op("dve", lambda e: e.tensor_tensor_scan(out=out, data0=d0, data1=d1, initial=init, op0=ALU.mult, op1=ALU.add), r, w)

    def dma(self, q, out, in_, r=(), w=()):
        self.em.dma(q, lambda e: e.dma_start(out=out, in_=in_), r, w)

    def segs(self):
        return [(0, 2, LCTX), (LCTX, LCTX + 4, self.cfg.T)]

    def phase_lru(self, l):
        cfg = self.cfg
        NT, T = cfg.NT, cfg.T
        psb, b_ps = self.psb, self.b_ps
        dr = self.dram
        uT, yB = dr["uT"], dr["yB"]
        B0 = A_COLS
        with contextlib.ExitStack() as st:
            cw = self.sb(st, "l_cw", [128, 8, 4], F32)
            cb = self.sb(st, "l_cb", [128, 8], F32)
            gab = self.sb(st, "l_gab", [128, 2, 2, 8], F32)
            lam = self.sb(st, "l_lam", [128, 2, 8], F32)
            cl = self.sb(st, "l_cl", [128, 2, 8], F32)
            b_par, b_cl = Buf("lpar"), Buf("lcl")
            self.dma("sp", cw[:], dr["lru_conv_w"][l], w=[b_par])
            self.dma("sp", cb[:], dr["lru_conv_b"][l], w=[b_par])
            self.dma("sp", gab[:], dr["lru_gate_b"][l], w=[b_par])
            self.dma("sp", lam[:], dr["lru_lambda"][l], w=[b_par])
            self.act(cl[:], lam[:], AF.Exp, [b_par], [b_cl], scale=-1.0)
            self.act(cl[:], cl[:], AF.Ln, [b_cl], [b_cl], bias=1.0)
            self.ts("dve", cl[:], cl[:], -8.0, None, ALU.mult, r=[b_cl], w=[b_cl])
            xp = self.sb(st, "l_xp", [128, NT + 6], F32)
            xb = self.sb(st, "l_xb", [128, NT], F32)
            xbb = self.sb(st, "l_xbb", [128, NT], BF16)
            gt = self.sb(st, "l_gt", [128, NT], F32)
            gtb = self.sb(st, "l_gtb", [128, NT], BF16)
            A = self.sb(st, "l_A", [128, NT], F32)
            Bt = self.sb(st, "l_B", [128, NT], F32)
            Ct = self.sb(st, "l_C", [128, NT], F32)
            hf = self.sb(st, "l_hf", [128, NT], F32)
            ys = self.sb(st, "l_ys", [128, NT], BF16)
            wgf = [self.sb(st, f"l_wgf{i}", [128, 128], F32) for i in range(2)]
            wgb = [self.sb(st, f"l_wgb{i}", [128, 128], BF16) for i in range(2)]
            b_xp, b_xb, b_xbb, b_gt, b_gtb, b_A, b_B, b_C, b_hf, b_ys = [Buf(n) for n in
                "xp xb xbb gt gtb A B C hf ys".split()]
            b_wgf, b_wgb = mkbufs("wgf", 2), mkbufs("wgb", 2)
            self.memset("pool", xp[:], 0.0, w=[b_xp])
            for i in range(2):
                self.memset("pool", wgf[i][:], 0.0, w=[b_wgf[i]])
            gw = [dr["lru_gate_a_w"], dr["lru_gate_x_w"]]

            def rev(ap):
                aps = [list(p) for p in ap.ap]
                n, stp = aps[-1][1], aps[-1][0]
                aps[-1] = [-stp, n]
                return bass.AP(ap.tensor, ap.offset + stp * (n - 1), aps)
            k = 0
            for j in range(8):
                for (d0, s0, n) in self.segs():
                    self.dma("sp", xp[:, s0:s0 + n], uT[B0 + j * 128:B0 + (j + 1) * 128, d0:d0 + n], w=[b_xp])
                self.dma("sp", gt[:], uT[B0 + 1024 + j * 128:B0 + 1024 + (j + 1) * 128, :], w=[b_gt])
                for (d0, s0, n) in self.segs():
                    self.ts("dve", xb[:, d0:d0 + n], xp[:, s0 - 2:s0 - 2 + n], cw[:, j, 0:1], cb[:, j:j + 1], ALU.mult, ALU.add,
                            r=[b_xp, b_par], w=[b_xb])
                    for tap in range(1, 4):
                        self.stt(xb[:, d0:d0 + n], xp[:, s0 - 2 + tap:s0 - 2 + tap + n], cw[:, j, tap:tap + 1], xb[:, d0:d0 + n],
                                 ALU.mult, ALU.add, r=[b_xp, b_par, b_xb], w=[b_xb])
                self.cp("pool", xbb[:], xb[:], r=[b_xb], w=[b_xbb])
                self.act(gtb[:], gt[:], AF.Gelu_apprx_tanh, [b_gt], [b_gtb])
                for d in range(2):
                    for g in range(2):
                        for hb in range(2):
                            self.dma("sp", wgf[g][hb * 64:(hb + 1) * 64, hb * 64:(hb + 1) * 64], gw[g][l, d, 2 * j + hb],
                                     w=[b_wgf[g]])
                        self.cp("pool", wgb[g][:], wgf[g][:], r=[b_wgf[g]], w=[b_wgb[g]])
                    for g, (dst, b_dst) in enumerate([(A, b_A), (Bt, b_B)]):
                        for (t0, n, seg) in cfg.tiles:
                            pb = k % 8
                            k += 1
                            self.mm(psb[pb][:, 0:n], wgb[g][:], xbb[:, t0:t0 + n], r=[b_wgb[g], b_xbb], w=[b_ps[pb]])
                            self.act(dst[:, t0:t0 + n], psb[pb][:, 0:n], AF.Sigmoid, [b_ps[pb], b_par], [b_dst],
                                     bias=gab[:, g, d, j:j + 1])
                    self.act(A[:], A[:], AF.Exp, [b_A, b_cl], [b_A], scale=cl[:, d, j:j + 1])
                    self.tt("dve", Ct[:], A[:], A[:], ALU.mult, r=[b_A], w=[b_C])
                    self.act(Ct[:], Ct[:], AF.Sqrt, [b_C], [b_C], scale=-1.0, bias=1.0)
                    self.tt("pool", Bt[:], Bt[:], xb[:], ALU.mult, r=[b_B, b_xb], w=[b_B])
                    self.tt("dve", Ct[:], Ct[:], Bt[:], ALU.mult, r=[b_C, b_B], w=[b_C])
                    if d == 0:
                        self.scan(hf[:], A[:], Ct[:], 0.0, r=[b_A, b_C], w=[b_hf])
                    else:
                        for (d0, s0, n) in self.segs():
                            self.cp("pool", Bt[:, d0:d0 + n], rev(A[:, d0:d0 + n]), r=[b_A], w=[b_B])
                            self.cp("pool", gt[:, d0:d0 + n], rev(Ct[:, d0:d0 + n]), r=[b_C], w=[b_gt])
                        self.scan(A[:], Bt[:], gt[:], 0.0, r=[b_B, b_gt, b_A], w=[b_A])
                        for (d0, s0, n) in self.segs():
                            self.cp("pool", Ct[:, d0:d0 + n], rev(A[:, d0:d0 + n]), r=[b_A], w=[b_C])
                        self.tt("dve", hf[:], hf[:], Ct[:], ALU.add, r=[b_hf, b_C], w=[b_hf])
                self.tt("dve", ys[:], hf[:], gtb[:], ALU.mult, r=[b_hf, b_gtb], w=[b_ys])
                self.dma("pool", yB[j * 128:(j + 1) * 128, :], ys[:], r=[b_ys])
            self.em.flush()

    def phase_na(self, l):
        cfg = self.cfg
        NT, T = cfg.NT, cfg.T
        psb, b_ps = self.psb, self.b_ps
        dr = self.dram
        uT, yC = dr["uT"], dr["yC"]
        update_ctx = (l < cfg.depth - 1) or getattr(cfg, 'force_ctx', False)
        rows = T // 64
        blocks, ntype = na_blocks(rows)
        NTB = NT // 128
        CQ, CK, CV, CQP, CKP = 42, 50, 58, 90, 98
        rp = dr["rpb_pad"]
        with contextlib.ExitStack() as st:
            ident = self.sb(st, "n_ident", [128, 128], F32)
            onesp = self.sb(st, "n_onesp", [128, 2, 128], BF16)
            maskb = self.sb(st, "n_maskb", [128, ntype, 512], BF16)
            mtmp = [self.sb(st, f"n_mtmp{i}", [128, 512], F32) for i in range(2)]
            b_id, b_op, b_mk = Buf("ident"), Buf("onesp"), Buf("maskb")
            b_mt = mkbufs("mtmp", 2)
            self.dma("sp", ident[:], dr["ident"][:, :], w=[b_id])
            self.memset("pool", onesp[:], 0.0, w=[b_op])
            self.memset("pool", onesp[:, 0, 0:64], 1.0, w=[b_op])
            self.memset("pool", onesp[:, 1, 64:128], 1.0, w=[b_op])
            for t in range(ntype):
                self.dma("sp", mtmp[t % 2][:], dr["na_mask"][t], w=[b_mt[t % 2]])
                self.cp("pool", maskb[:, t, :], mtmp[t % 2][:], r=[b_mt[t % 2]], w=[b_mk])
            NIN = 7
            tl = [[self.sb(st, f"n_tl{a}_{i}", [128, 512], F32) for i in range(2)] for a in range(NIN)]
            b_tl = [mkbufs(f"tl{a}_", 2) for a in range(NIN)]
            qpl = self.sb(st, "n_qpl", [128, NT], BF16)
            kpl = self.sb(st, "n_kpl", [128, LCTX], BF16)
            qrot = self.sb(st, "n_qrot", [128, T], BF16)
            krot = self.sb(st, "n_krot", [128, T], BF16)
            Vp = self.sb(st, "n_Vp", [128, NTB, 2, 128], BF16)
            Tc2 = self.sb(st, "n_Tc2", [128, 22 * 64], F32)
            biasd = self.sb(st, "n_biasd", [128, 8, 512], F32)
            bm = [self.sb(st, f"n_bm{i}", [128, ntype, 512], BF16) for i in range(2)]
            sT = [self.sb(st, f"n_sT{i}", [128, 512], F32) for i in range(2)]
            pT = [self.sb(st, f"n_pT{i}", [128, 512], BF16) for i in range(3)]
            rc = [self.sb(st, f"n_rc{i}", [128, 512], F32) for i in range(2)]
            yst = self.sb(st, "n_yst", [128, NT], BF16)
            b_qpl, b_kpl, b_qrot, b_krot, b_Vp, b_Tc2, b_biasd, b_yst = [Buf(n) for n in
                "qpl kpl qrot krot Vp Tc2 biasd yst".split()]
            b_bm, b_sT, b_pT, b_rc = mkbufs("bm", 2), mkbufs("sT", 2), mkbufs("pT", 3), mkbufs("rc", 2)
            self.memset("pool", Vp[:], 0.0, w=[b_Vp])
            self.memset("pool", yst[:], 0.0, w=[b_yst])

            def rev(ap):
                aps = [list(p) for p in ap.ap]
                n, stp = aps[-1][1], aps[-1][0]
                aps[-1] = [-stp, n]
                return bass.AP(ap.tensor, ap.offset + stp * (n - 1), aps)
            kq = 0
            ks = 0
            kp_ = 0
            kacc = 0
            for j in range(8):
                for ti, (t0, n, seg) in enumerate(cfg.tiles):
                    s = ti % 2
                    rowsrc = [CQ + j, CQP + j, CK + j, CKP + j, CV + j]
                    need = [0, 2, 4] if seg == 1 else [0, 1, 2, 3, 4]
                    for a in need:
                        c = rowsrc[a]
                        self.dma("sp", tl[a][s][:, 0:n], uT[c * 128:(c + 1) * 128, t0:t0 + n], w=[b_tl[a][s]])
                    if seg == 1:
                        self.act(qpl[:, t0:t0 + n], tl[0][s][:, 0:n], AF.Copy, [b_tl[0][s]], [b_qpl])
                        self.act(kpl[:, 0:n], tl[2][s][:, 0:n], AF.Copy, [b_tl[2][s]], [b_kpl])
                    else:
                        lt0 = t0 - LCTX
                        self.dma("sp", tl[5][s][:], dr["rope_cos"][:, lt0:lt0 + 512], w=[b_tl[5][s]])
                        self.dma("sp", tl[6][s][:], dr["rope_sin"][:, lt0:lt0 + 512], w=[b_tl[6][s]])
                        self.act(qpl[:, t0:t0 + n], tl[0][s][:], AF.Copy, [b_tl[0][s]], [b_qpl])
                        for (a, ap_, dst, b_dst, eng) in [(0, 1, qrot, b_qrot, "dve"), (2, 3, krot, b_krot, "pool")]:
                            self.tt(eng, tl[a][s][:], tl[a][s][:], tl[5][s][:], ALU.mult,
                                    r=[b_tl[a][s], b_tl[5][s]], w=[b_tl[a][s]])
                            self.tt(eng, tl[ap_][s][:], tl[ap_][s][:], tl[6][s][:], ALU.mult,
                                    r=[b_tl[ap_][s], b_tl[6][s]], w=[b_tl[ap_][s]])
                            self.tt(eng, dst[:, lt0:lt0 + 512], tl[a][s][:], tl[ap_][s][:], ALU.add,
                                    r=[b_tl[a][s], b_tl[ap_][s]], w=[b_dst])
                    nb = n // 128
                    pb = kq % 4
                    kq += 1
                    for q in range(nb):
                        self.em.op("pe", lambda e, pb=pb, q=q, s=s: e.transpose(
                            psb[pb][:, q * 128:(q + 1) * 128], tl[4][s][:, q * 128:(q + 1) * 128], ident[:]),
                            [b_tl[4][s], b_id], [b_ps[pb]], defer=(q != nb - 1))
                    tb0 = t0 // 128
                    pv = psb[pb][:, 0:nb * 128].rearrange("p (a b) -> p a b", b=128)
                    self.cp("dve", Vp[:, tb0:tb0 + nb, 0, 0:64], pv[:, :, 0:64], r=[b_ps[pb]], w=[b_Vp])
                    self.act(Vp[:, tb0:tb0 + nb, 1, 64:128], pv[:, :, 64:128], AF.Copy, [b_ps[pb]], [b_Vp])
                for hh in range(2):
                    h = 2 * j + hh
                    for krl in range(2):
                        base = ((l * 16 + h) * 24 + krl) * 128
                        src = bass.AP(rp.tensor, rp.offset + base, [[1, 64], [128, 22], [1, 64]])
                        self.dma("sp", Tc2[krl * 64:(krl + 1) * 64, :].rearrange("p (a b) -> p a b", b=64), src, w=[b_Tc2])
                    for di in range(8):
                        self.cp("pool", biasd[:, di, :], rev(Tc2[:, di * 128:di * 128 + 512]), r=[b_Tc2], w=[b_biasd])
                    for qb, items in enumerate(blocks):
                        for (kr0, ty, di) in items:
                            if ty is not None:
                                self.tt(("dve", "pool")[ty % 2], bm[hh][:, ty, :], biasd[:, di, :], maskb[:, ty, :], ALU.add,
                                        r=[b_biasd, b_mk], w=[b_bm[hh]])
                qk_list, pv_list = [], []
                for qb, items in enumerate(blocks):
                    a1, a2 = 4 + 2 * (kacc % 2), 5 + 2 * (kacc % 2)
                    kacc += 1
                    s3 = kacc % 2
                    q0t = qb * 512
                    work = []
                    for hh in range(2):
                        for (kr0, ty, di) in items:
                            work.append((hh, "loc", kr0, ty))
                        for cc in range(2):
                            work.append((hh, "ctx", cc, None))
                    for wi, (hh, kind, a, ty) in enumerate(work):
                        hb = hh * 64
                        pb = kq % 4
                        kq += 1
                        s2 = kp_ % 3
                        kp_ += 1
                        first, last = (wi == 0), (wi == len(work) - 1)
                        if kind == "loc":
                            s1 = ks % 2
                            ks += 1

                            def qk(hb=hb, pb=pb, s2=s2, s1=s1, a=a, ty=ty, hh=hh, q0t=q0t):
                                ktok = a * 64
                                self.mm(psb[pb][:, :], krot[hb:hb + 64, ktok:ktok + 128], qrot[hb:hb + 64, q0t:q0t + 512],
                                        r=[b_krot, b_qrot], w=[b_ps[pb]])
                                self.stt(sT[s1][:], psb[pb][:, :], 0.125, bm[hh][:, ty, :], ALU.mult, ALU.add,
                                         r=[b_ps[pb], b_bm[hh]], w=[b_sT[s1]])
                                self.act(pT[s2][:], sT[s1][:], AF.Exp, [b_sT[s1]], [b_pT[s2]])
                            vch = 2 + a // 2
                        else:
                            def qk(hb=hb, pb=pb, s2=s2, a=a, q0t=q0t):
                                self.mm(psb[pb][:, :], kpl[hb:hb + 64, a * 128:(a + 1) * 128],
                                        qpl[hb:hb + 64, LCTX + q0t:LCTX + q0t + 512], r=[b_kpl, b_qpl], w=[b_ps[pb]])
                                self.act(pT[s2][:], psb[pb][:, :], AF.Exp, [b_ps[pb]], [b_pT[s2]], scale=0.125)
                            vch = a

                        def pv(a1=a1, a2=a2, vch=vch, hh=hh, s2=s2, first=first, last=last, s3=s3, q0t=q0t):
                            self.mm(psb[a1][:, :], Vp[:, vch, hh, :], pT[s2][:], start=first, stop=last,
                                    r=[b_Vp, b_pT[s2]], w=[b_ps[a1]], defer=True)
                            self.mm(psb[a2][:, :], onesp[:, hh, :], pT[s2][:], start=first, stop=last,
                                    r=[b_op, b_pT[s2]], w=[b_ps[a2]], defer=(not last))
                            if last:
                                self.em.op("dve", lambda e: e.reciprocal(out=rc[s3][:], in_=psb[a2][:, :]),
                                           [b_ps[a2]], [b_rc[s3]])
                                self.tt("dve", yst[:, LCTX + q0t:LCTX + q0t + 512], psb[a1][:, :], rc[s3][:], ALU.mult,
                                        r=[b_ps[a1], b_rc[s3]], w=[b_yst])
                        qk_list.append(qk)
                        pv_list.append(pv)
                LA = 2
                for idx in range(len(qk_list) + LA):
                    if idx < len(qk_list):
                        qk_list[idx]()
                    if idx - LA >= 0:
                        pv_list[idx - LA]()
                if update_ctx:
                    a1, a2 = 4 + 2 * (kacc % 2), 5 + 2 * (kacc % 2)
                    kacc += 1
                    work = [(hh, cc) for hh in range(2) for cc in range(2)]
                    for wi, (hh, cc) in enumerate(work):
                        hb = hh * 64
                        pb = kq % 4
                        kq += 1
                        s2 = kp_ % 3
                        kp_ += 1
                        self.mm(psb[pb][:, 0:LCTX], kpl[hb:hb + 64, cc * 128:(cc + 1) * 128], qpl[hb:hb + 64, 0:LCTX],
                                r=[b_kpl, b_qpl], w=[b_ps[pb]])
                        self.act(pT[s2][:, 0:LCTX], psb[pb][:, 0:LCTX], AF.Exp, [b_ps[pb]], [b_pT[s2]], scale=0.125)
                        first, last = (wi == 0), (wi == len(work) - 1)
                        self.mm(psb[a1][:, 0:LCTX], Vp[:, cc, hh, :], pT[s2][:, 0:LCTX], start=first, stop=last,
                                r=[b_Vp, b_pT[s2]], w=[b_ps[a1]])
                        self.mm(psb[a2][:, 0:LCTX], onesp[:, hh, :], pT[s2][:, 0:LCTX], start=first, stop=last,
                                r=[b_op, b_pT[s2]], w=[b_ps[a2]])
                    s3 = kacc % 2
                    self.em.op("dve", lambda e, s3=s3, a2=a2: e.reciprocal(out=rc[s3][:, 0:LCTX], in_=psb[a2][:, 0:LCTX]),
                               [b_ps[a2]], [b_rc[s3]])
                    self.tt("dve", yst[:, 0:LCTX], psb[a1][:, 0:LCTX], rc[s3][:, 0:LCTX], ALU.mult,
                            r=[b_ps[a1], b_rc[s3]], w=[b_yst])
                self.dma("pool", yC[j * 128:(j + 1) * 128, :], yst[:], r=[b_yst])
            self.em.flush()

    def phase_rwkv(self, l):
        cfg = self.cfg
        NT, T = cfg.NT, cfg.T
        psb, b_ps = self.psb, self.b_ps
        dr = self.dram
        uT, yA = dr["uT"], dr["yA"]
        NCK = NT // 64
        CW = 0.6065306597126334
        SEGC = 16
        SEGN = SEGC * 64
        segs = [(0, 4)] + [(4 + 16 * i, 16) for i in range((NCK - 4) // 16)]
        with contextlib.ExitStack() as st:
            sb = lambda name, shape, dt=F32: self.sb(st, "r_" + name, shape, dt)
            ident_bf = sb("ident_bf", [128, 128], BF16)
            bdones = sb("bdones", [128, 128], BF16)
            istack = sb("istack", [128, 64], BF16)
            rkm = sb("rkm", [128, 2, 1152], BF16)
            cmask = sb("cmask", [128, SEGN])
            mu = sb("mu", [128, 26, 2])
            c0 = sb("c0", [128, 26])
            w0a0 = sb("w0a0", [128, 2, 2, 8])
            vec = sb("vec", [128, 5, 8])
            omk = sb("omk", [128, 8])
            WA = sb("WA", [128, 2, 1024], BF16)
            GU = sb("GU", [128, 1024], BF16)
            b_cst, b_k, b_par, b_wst, b_W = Buf("cst"), Buf("rk_consts"), Buf("rpar"), Buf("wst"), Buf("WA")
            with contextlib.ExitStack() as st_tmp:
                cst_f = self.sb(st_tmp, "r_cst_f", [128, 1152], F32)
                wst = self.sb(st_tmp, "r_wst", [128, 1024], F32)
                for (dst, src, n) in [(ident_bf, dr["ident"], 128), (bdones, dr["bdones"], 128), (istack, dr["istack"], 64)]:
                    self.dma("sp", cst_f[:, 0:n], src[:, :], w=[b_cst])
                    self.cp("dve", dst[:], cst_f[:, 0:n], r=[b_cst], w=[b_k])
                for d in range(2):
                    self.dma("sp", cst_f[:], dr["rk_mask"][d], w=[b_cst])
                    self.cp("dve", rkm[:, d, :], cst_f[:], r=[b_cst], w=[b_k])
                self.memset("pool", cmask[:], 1.0, w=[b_k])
                self.memset("pool", cmask[:].rearrange("p (c s) -> p c s", s=64)[:, :, 0:1], 0.0, w=[b_k])
                self.dma("sp", mu[:], dr["rwkv_mu"][l], w=[b_par])
                self.dma("sp", w0a0[:], dr["rwkv_w0a0"][l], w=[b_par])
                self.dma("sp", vec[:], dr["rwkv_vec"][l], w=[b_par])
                self.ts("dve", c0[:], mu[:, :, 0], -1.0, 1.0, ALU.mult, ALU.add, r=[b_par], w=[b_par])
                self.tt("dve", c0[:], c0[:], mu[:, :, 1], ALU.subtract, r=[b_par], w=[b_par])
                self.ts("dve", omk[:], vec[:, 1, :], -1.0, 1.0, ALU.mult, ALU.add, r=[b_par], w=[b_par])
                for d in range(2):
                    self.dma("sp", wst[0:64, :], dr["rwkv_w_up"][l, d], w=[b_wst])
                    self.dma("sp", wst[64:128, :], dr["rwkv_a_up"][l, d], w=[b_wst])
                    self.cp("pool", WA[:, d, :], wst[:], r=[b_wst], w=[b_W])
                self.dma("sp", wst[:], dr["rwkv_g_up"][l], w=[b_wst])
                self.cp("pool", GU[:], wst[:], r=[b_wst], w=[b_W])
                self.em.flush()
            LW = sb("LW", [128, NT], BF16)
            GL = sb("GL", [128, NT], BF16)
            Yacc = sb("Yacc", [128, NCK, 64])
            ksum = sb("ksum", [128, NT])
            b_LW, b_GL, b_Yacc, b_ksum = Buf("LW"), Buf("GL"), Buf("Yacc"), Buf("ksum")
            xps = [sb(f"xp{i}", [128, SEGN + 2]) for i in range(3)]
            b_xps = mkbufs("xp", 3)
            kxp = [0]
            rTs = [sb(f"rT{i}", [128, SEGN]) for i in range(2)]
            kT, vT, kap = sb("kT", [128, SEGN]), sb("vT", [128, SEGN]), sb("kap", [128, SEGN])
            T1, T2, T3, T4 = [sb(f"T{i}", [128, SEGN]) for i in range(1, 5)]
            F1, F2, F3 = [sb(f"F{i}", [128, SEGN]) for i in range(1, 4)]
            ynT = sb("ynT", [128, SEGN])
            Vbs = [sb(f"Vb{i}", [128, SEGN], BF16) for i in range(2)]
            gTbs = [sb(f"gTb{i}", [128, SEGN], BF16) for i in range(2)]
            sqb, fsqb = sb("sqb", [128, SEGN], BF16), sb("fsqb", [128, SEGN], BF16)
            stks = [sb(f"stk{i}", [128, 4, SEGN], BF16) for i in range(2)]
            YBD = sb("YBD", [128, SEGC, 128], BF16)
            yst = sb("yst", [128, SEGN], BF16)
            gCs = [sb(f"gC{i}", [128, SEGC]) for i in range(2)]
            lnst = sb("lnst", [128, 6, SEGC])
            ptot = sb("ptot", [128, SEGC])
            b_kT, b_vT, b_kap, b_T1, b_T2, b_T3, b_T4, b_ynT, b_sqb, b_YBD, b_yst, b_ln, b_F1, b_F2, b_F3, b_fsqb, b_ptot = [
                Buf(n) for n in "kT vT kap T1 T2 T3 T4 ynT sqb YBD yst lnst F1 F2 F3 fsqb ptot".split()]
            b_rTs, b_Vbs, b_gTbs, b_stks, b_gCs = [mkbufs(n, 2) for n in "rT Vb gTb stk gC".split()]
            self.memset("pool", YBD[:], 0.0, w=[b_YBD])
            G = 4
            BDg = [sb(f"BDg{i}", [128, G, 5, 128], BF16) for i in range(2)]
            b_BDg = mkbufs("BDg", 2)
            for i in range(2):
                self.memset("pool", BDg[i][:], 0.0, w=[b_BDg[i]])
            SA = [sb(f"SA{i}", [128, 512], BF16) for i in range(G)]
            SB_ = [sb(f"SB{i}", [128, 512], BF16) for i in range(G)]
            MN = [[sb(f"MN{i}_{k}", [128, 256], BF16) for k in range(2)] for i in range(G)]
            Rb = [[sb(f"Rb{i}_{k}", [128, 256], BF16) for k in range(2)] for i in range(G)]
            Pb = [[sb(f"Pb{i}_{k}", [128, 128], BF16) for k in range(2)] for i in range(G)]
            MO = [sb(f"MO{i}", [128, 128], BF16) for i in range(G)]
            QP = [sb(f"QP{i}", [128, 256], BF16) for i in range(G)]
            AK = [sb(f"AK{i}", [128, 256], BF16) for i in range(G)]
            VM = [sb(f"VM{i}", [128, G, 64], BF16) for i in range(2)]
            b_VM = mkbufs("VM", 2)
            b_MO = mkbufs("MO", G)
            pending = []
            ST = sb("ST", [128, 64], BF16)
            S32 = sb("S32", [128, 64])
            S32g = sb("S32g", [128, 64])
            b_S32, b_S32g = Buf("S32"), Buf("S32g")
            b_SA, b_SB, b_QP, b_AK = [mkbufs(n, G) for n in "SA SB QP AK".split()]
            b_MN, b_Rb, b_Pb = [[mkbufs(f"{n}{i}_", 2) for i in range(G)] for n in "MN Rb Pb".split()]
            b_ST = Buf("ST")
            kps = [0]

            def shift(dst, b_dst, c, c0_, n):
                xi = kxp[0] % 3
                kxp[0] += 1
                xp, b_xp = xps[xi], b_xps[xi]
                p0 = c0_ * 64
                p1 = p0 + n
                hasL = p0 not in (0, LCTX)
                hasR = p1 not in (LCTX, NT)
                if not hasL:
                    self.memset("pool", xp[:, 0:1], 0.0, w=[b_xp])
                if not hasR:
                    self.memset("pool", xp[:, n + 1:n + 2], 0.0, w=[b_xp])
                lo, hi = p0 - int(hasL), p1 + int(hasR)
                self.dma("sp", xp[:, 1 - int(hasL):1 + n + int(hasR)], uT[c * 128:(c + 1) * 128, lo:hi], w=[b_xp])
                self.act(dst[:, 0:n], xp[:, 1:1 + n], AF.Copy, [b_xp, b_par], [b_dst], scale=c0[:, c:c + 1])
                self.stt(dst[:, 0:n], xp[:, 0:n], mu[:, c, 0:1], dst[:, 0:n], ALU.mult, ALU.add, r=[b_xp, b_par, b_dst], w=[b_dst])
                self.stt(dst[:, 0:n], xp[:, 2:2 + n], mu[:, c, 1:2], dst[:, 0:n], ALU.mult, ALU.add, r=[b_xp, b_par, b_dst], w=[b_dst])

            def tiles_of(n):
                return [(o, min(512, n - o)) for o in range(0, n, 512)]

            def nextps():
                kps[0] += 1
                return 7

            for (ck0, nck) in segs:
                n = nck * 64
                t0 = ck0 * 64
                shift(T1, b_T1, 24, ck0, n)
                self.act(LW[0:64, t0:t0 + n], T1[0:64, 0:n], AF.Tanh, [b_T1], [b_LW])
                self.act(LW[64:128, t0:t0 + n], T1[64:128, 0:n], AF.Copy, [b_T1], [b_LW])
                shift(T2, b_T2, 25, ck0, n)
                self.act(GL[:, t0:t0 + n], T2[:, 0:n], AF.Sigmoid, [b_T2], [b_GL])

            kbd = [0]

            def prep(j, d, ck0, nck, sp):
                n = nck * 64
                t0 = ck0 * 64
                rT, b_rT, Vb, b_Vb, gTb, b_gTb, stk, b_stk, gC, b_gC = (rTs[sp], b_rTs[sp], Vbs[sp], b_Vbs[sp], gTbs[sp], b_gTbs[sp],
                                                                       stks[sp], b_stks[sp], gCs[sp], b_gCs[sp])
                for (o, m) in tiles_of(n):
                    pb = nextps()
                    self.mm(psb[pb][:, 0:m], WA[0:64, d, j * 128:(j + 1) * 128], LW[0:64, t0 + o:t0 + o + m],
                            r=[b_W, b_LW], w=[b_ps[pb]])
                    self.act(T1[:, o:o + m], psb[pb][:, 0:m], AF.Sigmoid, [b_ps[pb], b_par], [b_T1],
                             bias=w0a0[:, 0, d, j:j + 1])
                    pb = nextps()
                    self.mm(psb[pb][:, 0:m], WA[64:128, d, j * 128:(j + 1) * 128], LW[64:128, t0 + o:t0 + o + m],
                            r=[b_W, b_LW], w=[b_ps[pb]])
                    self.act(T2[:, o:o + m], psb[pb][:, 0:m], AF.Sigmoid, [b_ps[pb], b_par], [b_T2],
                             bias=w0a0[:, 1, d, j:j + 1])
                shift(rT, b_rT, j, ck0, n)
                shift(kT, b_kT, 8 + j, ck0, n)
                shift(vT, b_vT, 16 + j, ck0, n)
                self.scan(T3[:, 0:n], cmask[:, 0:n], T1[:, 0:n], 0.0, r=[b_k, b_T1], w=[b_T3])
                T3v = T3[:, 0:n].rearrange("p (c s) -> p c s", s=64)
                self.act(gC[:, 0:nck], T3v[:, :, 63], AF.Exp, [b_T3], [b_gC], scale=-CW)
                if d == 1:
                    self.cp("pool", ptot[:, 0:nck], T3v[:, :, 63], r=[b_T3], w=[b_ptot])
                    self.tt("dve", T3[:, 0:n], T1[:, 0:n], T3[:, 0:n], ALU.subtract, r=[b_T1, b_T3], w=[b_T3])
                    self.tt("dve", T3v, T3v, ptot[:, 0:nck].unsqueeze(2).to_broadcast([128, nck, 64]), ALU.add,
                            r=[b_T3, b_ptot], w=[b_T3])
                self.tt("dve", T1[:, 0:n], T3[:, 0:n], T1[:, 0:n], ALU.subtract, r=[b_T1, b_T3], w=[b_T1])
                self.act(T1[:, 0:n], T1[:, 0:n], AF.Exp, [b_T1], [b_T1], scale=-CW)
                self.act(T4[:, 0:n], T3[:, 0:n], AF.Exp, [b_T3], [b_T4], scale=CW)
                self.act(T3[:, 0:n], T3[:, 0:n], AF.Exp, [b_T3], [b_T3], scale=-CW)
                self.act(Vb[:, 0:n], vT[:, 0:n], AF.Copy, [b_vT], [b_Vb])
                self.ts("pool", kap[:, 0:n], kT[:, 0:n], vec[:, 0, j:j + 1], None, ALU.mult, r=[b_kT, b_par], w=[b_kap])
                self.act(sqb[:, 0:n], kap[:, 0:n], AF.Square, [b_kap], [b_sqb])
                for (o, m) in tiles_of(n):
                    pb = nextps()
                    self.mm(psb[pb][:, 0:m], bdones[:], sqb[:, o:o + m], r=[b_k, b_sqb], w=[b_ps[pb]])
                    self.act(F3[:, o:o + m], psb[pb][:, 0:m], AF.Sqrt, [b_ps[pb]], [b_F3], bias=1e-24)
                self.em.op("dve", lambda e, n=n: e.reciprocal(out=F3[:, 0:n], in_=F3[:, 0:n]), [b_F3], [b_F3])
                self.tt("dve", kap[:, 0:n], kap[:, 0:n], F3[:, 0:n], ALU.mult, r=[b_kap, b_F3], w=[b_kap])
                if d == 1:
                    for (o, m) in tiles_of(n):
                        pb = nextps()
                        self.mm(psb[pb][:, 0:m], GU[:, j * 128:(j + 1) * 128], GL[:, t0 + o:t0 + o + m],
                                r=[b_W, b_GL], w=[b_ps[pb]])
                        self.act(gTb[:, o:o + m], psb[pb][:, 0:m], AF.Copy, [b_ps[pb]], [b_gTb])
                self.tt("dve", stk[:, 0, 0:n], kap[:, 0:n], T1[:, 0:n], ALU.mult, r=[b_kap, b_T1], w=[b_stk])
                self.tt("pool", stk[:, 1, 0:n], rT[:, 0:n], T3[:, 0:n], ALU.mult, r=[b_rT, b_T3], w=[b_stk])
                self.ts("dve", T1[:, 0:n], T2[:, 0:n], vec[:, 1, j:j + 1], omk[:, j:j + 1], ALU.mult, ALU.add,
                        r=[b_T2, b_par], w=[b_T1])
                self.tt("dve", T1[:, 0:n], T1[:, 0:n], kT[:, 0:n], ALU.mult, r=[b_T1, b_kT], w=[b_T1])
                if d == 0:
                    self.cp("pool", ksum[:, t0:t0 + n], T1[:, 0:n], r=[b_T1], w=[b_ksum])
                else:
                    self.tt("pool", ksum[:, t0:t0 + n], ksum[:, t0:t0 + n], T1[:, 0:n], ALU.add, r=[b_T1, b_ksum], w=[b_ksum])
                self.tt("dve", stk[:, 3, 0:n], T1[:, 0:n], T4[:, 0:n], ALU.mult, r=[b_T1, b_T4], w=[b_stk])
                self.tt("pool", T2[:, 0:n], T2[:, 0:n], kap[:, 0:n], ALU.mult, r=[b_T2, b_kap], w=[b_T2])
                self.tt("dve", stk[:, 2, 0:n], T2[:, 0:n], T4[:, 0:n], ALU.mult, r=[b_T2, b_T4], w=[b_stk])

            def finalize(j, ck0, nck, sp):
                n = nck * 64
                t0 = ck0 * 64
                rT, b_rT, Vb, b_Vb, gTb, b_gTb = rTs[sp], b_rTs[sp], Vbs[sp], b_Vbs[sp], gTbs[sp], b_gTbs[sp]
                Ys = Yacc[:, ck0:ck0 + nck, :]
                T3v = F3[:, 0:n].rearrange("p (c s) -> p c s", s=64)
                mean, ssq, m2, var = lnst[:, 1, 0:nck], lnst[:, 2, 0:nck], lnst[:, 3, 0:nck], lnst[:, 4, 0:nck]
                self.em.op("dve", lambda e, Ys=Ys, mean=mean: e.tensor_reduce(out=mean, in_=Ys, axis=AX.X, op=ALU.add),
                           [b_Yacc], [b_ln])
                self.act(T3v, Ys, AF.Square, [b_Yacc], [b_F3])
                self.em.op("dve", lambda e, T3v=T3v, ssq=ssq: e.tensor_reduce(out=ssq, in_=T3v, axis=AX.X, op=ALU.add),
                           [b_F3], [b_ln])
                self.ts("dve", mean, mean, 1.0 / 64, None, ALU.mult, r=[b_ln], w=[b_ln])
                self.tt("dve", m2, mean, mean, ALU.mult, r=[b_ln], w=[b_ln])
                self.stt(var, ssq, 1.0 / 64, m2, ALU.mult, ALU.subtract, r=[b_ln], w=[b_ln])
                self.act(var, var, AF.Sqrt, [b_ln], [b_ln], bias=64e-5)
                self.em.op("dve", lambda e, var=var: e.reciprocal(out=var, in_=var), [b_ln], [b_ln])
                self.tt("dve", T3v, Ys, mean.unsqueeze(2).to_broadcast([128, nck, 64]), ALU.subtract,
                        r=[b_Yacc, b_ln], w=[b_F3])
                for hh in range(2):
                    hs = slice(hh * 64, (hh + 1) * 64)
                    self.tt("dve", YBD[hs, 0:nck, hs], T3v[hs], var[hs].unsqueeze(2).to_broadcast([64, nck, 64]), ALU.mult,
                            r=[b_F3, b_ln], w=[b_YBD])
                for c8 in range(0, nck, 8):
                    m8 = min(8, nck - c8)
                    pb = 7
                    for ci in range(c8, c8 + m8):
                        self.mm(psb[pb][:, (ci - c8) * 64:(ci - c8 + 1) * 64], YBD[:, ci, :], istack[:],
                                r=[b_YBD, b_k], w=[b_ps[pb]], defer=(ci != c8 + m8 - 1))
                    self.ts("dve", ynT[:, c8 * 64:(c8 + m8) * 64], psb[pb][:, 0:m8 * 64], vec[:, 3, j:j + 1], vec[:, 4, j:j + 1],
                            ALU.mult, ALU.add, r=[b_ps[pb], b_par], w=[b_ynT])
                self.tt("pool", F1[:, 0:n], rT[:, 0:n], ksum[:, t0:t0 + n], ALU.mult, r=[b_rT, b_ksum], w=[b_F1])
                self.ts("dve", fsqb[:, 0:n], F1[:, 0:n], vec[:, 2, j:j + 1], None, ALU.mult, r=[b_F1, b_par], w=[b_fsqb])
                for (o, m) in tiles_of(n):
                    pb = 7
                    self.mm(psb[pb][:, 0:m], bdones[:], fsqb[:, o:o + m], r=[b_k, b_fsqb], w=[b_ps[pb]])
                    self.tt("dve", F2[:, o:o + m], psb[pb][:, 0:m], Vb[:, o:o + m], ALU.mult, r=[b_ps[pb], b_Vb], w=[b_F2])
                self.tt("dve", ynT[:, 0:n], ynT[:, 0:n], F2[:, 0:n], ALU.add, r=[b_ynT, b_F2], w=[b_ynT])
                self.tt("dve", yst[:, 0:n], ynT[:, 0:n], gTb[:, 0:n], ALU.mult, r=[b_ynT, b_gTb], w=[b_yst])
                self.dma("pool", yA[j * 128:(j + 1) * 128, t0:t0 + n], yst[:, 0:n], r=[b_yst])

            for j in range(8):
                for d in range(2):
                    self.memset("pool", ST[:], 0.0, w=[b_ST])
                    self.memset("pool", S32[:], 0.0, w=[b_S32])
                    order = segs if d == 0 else [segs[0]] + segs[1:][::-1]
                    bgq = []
                    prep(j, d, order[0][0], order[0][1], 0)
                    for si, (ck0, nck) in enumerate(order):
                        sp = si % 2
                        n = nck * 64
                        t0 = ck0 * 64
                        rT, b_rT, Vb, b_Vb, gTb, b_gTb, stk, b_stk, gC, b_gC = (rTs[sp], b_rTs[sp], Vbs[sp], b_Vbs[sp], gTbs[sp],
                                                                               b_gTbs[sp], stks[sp], b_stks[sp], gCs[sp], b_gCs[sp])
                        if si + 1 < len(order):
                            nxt = order[si + 1]
                            bgq += self.em.captured(lambda: prep(j, d, nxt[0], nxt[1], 1 - sp))
                        nstage = 30 * ((nck + G - 1) // G)
                        bgk = max(1, (len(bgq) + nstage - 1) // nstage)

                        def bgrun(k=None):
                            self.em.replay(bgq, bgk if k is None else k)
                        corder = list(range(nck)) if d == 0 else list(range(nck))[::-1]
                        for gi in range(0, nck, G):
                            grp = corder[gi:gi + G]
                            cl = min(grp)
                            bg = kbd[0] % 2
                            kbd[0] += 1
                            for qi in range(5):
                                for hh in range(2):
                                    hs = slice(hh * 64, (hh + 1) * 64)
                                    src = (Vb[hs, cl * 64:(cl + G) * 64] if qi == 4 else stk[hs, qi, cl * 64:(cl + G) * 64])
                                    src = src.rearrange("p (c s) -> p c s", s=64)
                                    eng = ("pool", "act", "pool")[(qi * 2 + hh) % 3]
                                    if eng == "act":
                                        self.act(BDg[bg][hs, :, qi, hs], src, AF.Copy, [b_stk, b_Vb], [b_BDg[bg]])
                                    else:
                                        self.cp(eng, BDg[bg][hs, :, qi, hs], src, r=[b_stk, b_Vb], w=[b_BDg[bg]])
                            rB = [b_BDg[bg], b_k]
                            R4 = range(len(grp))
                            gof = [ci - cl for ci in grp]
                            bd = lambda i, q: BDg[bg][:, gof[i], q, :]
                            slot = lambda i: psb[i]
                            bsl = lambda i: b_ps[i]
                            for i in R4:
                                self.mm(psb[4][:, i * 64:(i + 1) * 64], bd(i, 4), istack[:], r=rB, w=[b_ps[4]], defer=(i != len(grp) - 1))
                            self.act(VM[bg][:, 0:len(grp), :], psb[4][:, 0:64 * len(grp)].rearrange("p (c s) -> p c s", s=64), AF.Copy,
                                     [b_ps[4]], [b_VM[bg]])
                            pend = pending[:]
                            del pending[:]

                            def drain(k=1):
                                for _ in range(k):
                                    if pend:
                                        pend.pop(0)()
                            for i in R4:
                                self.mm(slot(i)[:, 0:256], bd(i, 2), BDg[bg][:, gof[i], 0:2, :], r=rB, w=[bsl(i)], defer=True)
                                self.mm(slot(i)[:, 256:384], bd(i, 2), ident_bf[:], r=rB, w=[bsl(i)], defer=True)
                                self.mm(slot(i)[:, 384:512], bd(i, 0), ident_bf[:], r=rB, w=[bsl(i)])
                            drain()
                            bgrun()
                            for i in R4:
                                self.tt("dve", SA[i][:], slot(i)[:, :], rkm[:, d, 0:512], ALU.mult, r=[bsl(i), b_k], w=[b_SA[i]])
                            bgrun()
                            for i in R4:
                                self.mm(slot(i)[:, 0:128], bd(i, 3), bd(i, 1), r=rB, w=[bsl(i)], defer=True)
                                self.mm(slot(i)[:, 128:256], bd(i, 3), ident_bf[:], r=rB, w=[bsl(i)], defer=True)
                                self.mm(slot(i)[:, 256:512], bd(i, 0), BDg[bg][:, gof[i], 2:4, :], r=rB, w=[bsl(i)])
                            drain()
                            bgrun()
                            for i in R4:
                                self.tt("dve", SB_[i][:], slot(i)[:, :], rkm[:, d, 512:1024], ALU.mult, r=[bsl(i), b_k], w=[b_SB[i]])
                                self.tt("dve", MO[i][:], slot(i)[:, 256:384], rkm[:, d, 1024:1152], ALU.mult, r=[bsl(i), b_k], w=[b_MO[i]])
                            bgrun()
                            for i in R4:
                                self.tt("pool", Pb[i][0][:], ident_bf[:], SB_[i][:, 256:384], ALU.subtract, r=[b_k, b_SB[i]], w=[b_Pb[i][0]])
                            cur = [(SB_[i][:, 256:384], SA[i][:, 0:128], b_SB[i], b_SA[i]) for i in R4]
                            for lev in range(4):
                                lastl = lev == 3
                                mi = lev % 2
                                for i in R4:
                                    Mc, Nc, bM, bN = cur[i]
                                    self.mm(slot(i)[:, 128:256], Mc, Nc, r=[bM, bN], w=[bsl(i)], defer=(not lastl))
                                    if not lastl:
                                        self.mm(slot(i)[:, 0:128], Nc, Mc, r=[bM, bN], w=[bsl(i)])
                                drain()
                                bgrun()
                                lo = 128 if lastl else 0
                                for i in R4:
                                    self.act(MN[i][mi][:, lo:256], slot(i)[:, lo:256], AF.Copy, [bsl(i)], [b_MN[i][mi]])
                                    cur[i] = (MN[i][mi][:, 0:128], MN[i][mi][:, 128:256], b_MN[i][mi], b_MN[i][mi])
                                bgrun()
                                for i in R4:
                                    self.mm(slot(i)[:, 256:384], cur[i][1], Pb[i][lev % 2][:], r=[cur[i][3], b_Pb[i][lev % 2]], w=[bsl(i)])
                                bgrun()
                                for i in R4:
                                    self.tt("dve", Pb[i][(lev + 1) % 2][:], Pb[i][lev % 2][:], slot(i)[:, 256:384], ALU.add,
                                            r=[b_Pb[i][lev % 2], bsl(i)], w=[b_Pb[i][(lev + 1) % 2]])
                            drain(4)
                            for i in R4:
                                self.mm(slot(i)[:, 0:256], Pb[i][0][:], SA[i][:, 128:384], r=[b_Pb[i][0], b_SA[i]], w=[bsl(i)])
                            bgrun()
                            for i in R4:
                                self.act(Rb[i][0][:], slot(i)[:, 0:256], AF.Copy, [bsl(i)], [b_Rb[i][0]])
                            for i in R4:
                                self.mm(slot(i)[:, 256:512], MO[i][:], Rb[i][0][:], r=[b_MO[i], b_Rb[i][0]], w=[bsl(i)])
                            bgrun()
                            for i in R4:
                                self.tt("dve", Rb[i][1][:], SA[i][:, 128:384], slot(i)[:, 256:512], ALU.subtract, r=[b_SA[i], bsl(i)], w=[b_Rb[i][1]])
                            bgrun()
                            for i in R4:
                                self.mm(slot(i)[:, 0:256], Pb[i][0][:], Rb[i][1][:], r=[b_Pb[i][0], b_Rb[i][1]], w=[bsl(i)])
                            for i in R4:
                                self.act(Rb[i][0][:], slot(i)[:, 0:256], AF.Copy, [bsl(i)], [b_Rb[i][0]])
                            bgrun()
                            for i in R4:
                                self.mm(slot(i)[:, 256:512], SA[i][:, 384:512], Rb[i][0][:], r=[b_SA[i], b_Rb[i][0]], w=[bsl(i)], defer=True)
                                self.mm(slot(i)[:, 0:256], SB_[i][:, 384:512], Rb[i][0][:], r=[b_SB[i], b_Rb[i][0]], w=[bsl(i)])
                            bgrun()
                            for i in R4:
                                self.tt("dve", QP[i][:, 0:128], bd(i, 1), slot(i)[:, 256:384], ALU.subtract, r=[b_BDg[bg], bsl(i)], w=[b_QP[i]])
                                self.ts("dve", QP[i][:, 128:256], slot(i)[:, 384:512], -1.0, None, ALU.mult, r=[bsl(i)], w=[b_QP[i]])
                                self.tt("dve", AK[i][:], SB_[i][:, 0:256], slot(i)[:, 0:256], ALU.subtract, r=[b_SB[i], bsl(i)], w=[b_AK[i]])
                            for i in R4:
                                def step(i=i, ci=grp[i], bg=bg, d=d, ck0=ck0):
                                    SQ = psb[6][:, i * 128:(i + 1) * 128]
                                    vm = VM[bg][:, i, :]
                                    self.ts("pool", S32g[:], S32[:], gC[:, ci:ci + 1], None, ALU.mult, r=[b_S32, b_gC], w=[b_S32g])
                                    self.mm(SQ[:, 0:64], QP[i][:, 0:128], ST[:], start=True, stop=False, r=[b_QP[i], b_ST], w=[b_ps[6]])
                                    self.mm(SQ[:, 0:64], AK[i][:, 0:128], vm, start=False, stop=True, r=[b_AK[i], b_VM[bg]], w=[b_ps[6]])
                                    self.mm(SQ[:, 64:128], QP[i][:, 128:256], ST[:], start=True, stop=False, r=[b_QP[i], b_ST], w=[b_ps[6]])
                                    self.mm(SQ[:, 64:128], AK[i][:, 128:256], vm, start=False, stop=True, r=[b_AK[i], b_VM[bg]], w=[b_ps[6]])
                                    self.stt(S32[:], SQ[:, 64:128], gC[:, ci:ci + 1], S32g[:], ALU.mult, ALU.add,
                                             r=[b_ps[6], b_gC, b_S32g], w=[b_S32])
                                    self.act(ST[:], S32[:], AF.Copy, [b_S32], [b_ST])
                                    ck = ck0 + ci
                                    if d == 0:
                                        self.cp("dve", Yacc[:, ck, :], SQ[:, 0:64], r=[b_ps[6]], w=[b_Yacc])
                                    else:
                                        self.tt("dve", Yacc[:, ck, :], Yacc[:, ck, :], SQ[:, 0:64], ALU.add, r=[b_ps[6], b_Yacc], w=[b_Yacc])
                                pending.append(step)
                            while pend:
                                pend.pop(0)()
                        while pending:
                            pending.pop(0)()
                        bgrun(10 ** 9)
                        if d == 1:
                            bgq += self.em.captured(lambda ck0=ck0, nck=nck, sp=sp: finalize(j, ck0, nck, sp))
                    bgrun(10 ** 9) if False else self.em.replay(bgq, 10 ** 9)
            self.em.flush()

    def rstd_tile(self, src, b_src, n, sq, b_sq, rs, b_rs, pb):
        psb, b_ps = self.psb, self.b_ps
        self.act(sq[:, :, 0:n], src[:, :, 0:n], AF.Square, [b_src], [b_sq])
        for kc in range(KC):
            self.mm(psb[pb][:, 0:n], self.ones_bf[:], sq[:, kc, 0:n], start=(kc == 0), stop=(kc == KC - 1),
                    r=[b_sq, self.b_ones], w=[b_ps[pb]])
        self.act(rs[:, 0:n], psb[pb][:, 0:n], AF.Sqrt, [b_ps[pb]], [b_rs], scale=1.0 / D, bias=1e-6)
        self.em.op("dve", lambda e: e.reciprocal(out=rs[:, 0:n], in_=rs[:, 0:n]), [b_rs], [b_rs])

    def phase_merge_ffn(self, l, xsrc):
        cfg = self.cfg
        NT, T = cfg.NT, cfg.T
        psb, b_ps = self.psb, self.b_ps
        dr = self.dram
        mod, b_mod = self.mod, self.b_mod
        uT, xmid, aT = dr["uT"], dr["xmid"], dr["aT"]
        last = l == cfg.depth - 1
        TW = 256
        tiles = [(t0, TW, 1 if t0 < LCTX else 0) for t0 in range(0, NT, TW)]
        with contextlib.ExitStack() as st0:
            h2T = self.sb(st0, "m_h2T", [128, KC, NT], BF16)
            b_h2 = mkbufs("h2T", len(tiles))
            ng = self.sb(st0, "m_ng", [128, 4, KC], F32)
            cols = self.sb(st0, "m_cols", [128, 4, KC, 2], F32)
            wst = self.sb(st0, "m_wst", [128, 1024], F32)
            b_ng, b_cols, b_wst = Buf("ng"), Buf("cols"), Buf("wst")
            self.dma("sp", ng[:], dr["norm_g"][l], w=[b_ng])
            for kc in range(KC):
                self.ts("dve", cols[:, 0, kc, :], mod[:, 16 + kc, :], ng[:, 1, kc:kc + 1], None, ALU.mult, r=[b_mod, b_ng], w=[b_cols])
                self.ts("dve", cols[:, 1, kc, :], mod[:, 32 + kc, :], 1.0, ng[:, 2, kc:kc + 1], ALU.add, ALU.mult, r=[b_mod, b_ng], w=[b_cols])
                self.ts("dve", cols[:, 2, kc, :], mod[:, 40 + kc, :], ng[:, 3, kc:kc + 1], None, ALU.mult, r=[b_mod, b_ng], w=[b_cols])
            with contextlib.ExitStack() as st:
                sb = lambda name, shape, dt=F32: self.sb(st, "m_" + name, shape, dt)
                Wbr = sb("Wbr", [128, 3, KC, 1024], BF16)
                Wo = sb("Wo", [128, KC, 1024], BF16)
                b_W = Buf("Wm")
                for br in range(3):
                    for kc in range(KC):
                        self.dma("sp", wst[:], dr["w_branch"][l, br, kc * 128:(kc + 1) * 128, :], w=[b_wst])
                        self.cp("pool", Wbr[:, br, kc, :], wst[:], r=[b_wst], w=[b_W])
                for kc in range(KC):
                    self.dma("sp", wst[:], dr["w_out"][l, kc * 128:(kc + 1) * 128, :], w=[b_wst])
                    self.cp("pool", Wo[:, kc, :], wst[:], r=[b_wst], w=[b_W])
                yt = [sb(f"yt{i}", [128, KC, TW], BF16) for i in range(3)]
                gt = sb("gt", [128, KC, TW])
                mt = sb("mt", [128, KC, TW])
                tmp = sb("tmp", [128, TW])
                mb = sb("mb", [128, KC, TW], BF16)
                xt = sb("xt", [128, KC, TW])
                mo = sb("mo", [128, KC, TW])
                sq = sb("sq", [128, KC, TW], BF16)
                rs = sb("rs", [128, TW])
                b_yt = mkbufs("yt", 3)
                b_gt, b_mt, b_tmp, b_mb, b_xt, b_mo, b_sq, b_rs = [Buf(n) for n in "gt mt tmp mb xt mo sq rs".split()]
                ysrc = [dr["yA"], dr["yB"], dr["yC"]]
                xview = xsrc.rearrange("(kc p) n -> p kc n", p=128)
                xmview = xmid.rearrange("(kc p) n -> p kc n", p=128)
                k = 0
                for ti, (t0, n, seg) in enumerate(tiles):
                    self.dma("sp", xt[:], xview[:, :, t0:t0 + n], w=[b_xt])
                    for br in range(3):
                        self.dma("sp", yt[br][:], ysrc[br].rearrange("(kc p) n -> p kc n", p=128)[:, :, t0:t0 + n], w=[b_yt[br]])
                        g0 = (66 + br * 8) * 128
                        self.dma("sp", gt[:], uT[g0:g0 + 1024, :].rearrange("(kc p) n -> p kc n", p=128)[:, :, t0:t0 + n], w=[b_gt])
                        self.act(gt[:], gt[:], AF.Sigmoid, [b_gt], [b_gt])
                        for oc in range(KC):
                            pb = k % 6
                            k += 1
                            for kc in range(KC):
                                self.mm(psb[pb][:, 0:n], Wbr[:, br, kc, oc * 128:(oc + 1) * 128], yt[br][:, kc, :],
                                        start=(kc == 0), stop=(kc == KC - 1), r=[b_W, b_yt[br]], w=[b_ps[pb]])
                            if br == 0:
                                self.tt("dve", mt[:, oc, :], psb[pb][:, 0:n], gt[:, oc, :], ALU.mult, r=[b_ps[pb], b_gt], w=[b_mt])
                            else:
                                self.tt("dve", tmp[:], psb[pb][:, 0:n], gt[:, oc, :], ALU.mult, r=[b_ps[pb], b_gt], w=[b_tmp])
                                self.tt("pool", mt[:, oc, :], mt[:, oc, :], tmp[:], ALU.add, r=[b_mt, b_tmp], w=[b_mt])
                    self.act(mb[:], mt[:], AF.Copy, [b_mt], [b_mb])
                    for oc in range(KC):
                        pb = k % 6
                        k += 1
                        for kc in range(KC):
                            self.mm(psb[pb][:, 0:n], Wo[:, kc, oc * 128:(oc + 1) * 128], mb[:, kc, :],
                                    start=(kc == 0), stop=(kc == KC - 1), r=[b_W, b_mb], w=[b_ps[pb]])
                        self.act(mo[:, oc, :], psb[pb][:, 0:n], AF.Copy, [b_ps[pb]], [b_mo])
                    self.rstd_tile(mo, b_mo, n, sq, b_sq, rs, b_rs, 6)
                    for oc in range(KC):
                        self.tt("dve", mo[:, oc, :], mo[:, oc, :], rs[:], ALU.mult, r=[b_mo, b_rs], w=[b_mo])
                        self.stt(xt[:, oc, :], mo[:, oc, :], cols[:, 0, oc, seg:seg + 1], xt[:, oc, :], ALU.mult, ALU.add,
                                 r=[b_mo, b_cols, b_xt], w=[b_xt])
                    self.dma("pool", xmview[:, :, t0:t0 + n], xt[:], r=[b_xt])
                    self.rstd_tile(xt, b_xt, n, sq, b_sq, rs, b_rs, 7)
                    for kc in range(KC):
                        self.tt("dve", mo[:, kc, :], xt[:, kc, :], rs[:], ALU.mult, r=[b_xt, b_rs, b_mo], w=[b_mo])
                        self.act(h2T[:, kc, t0:t0 + n], mo[:, kc, :], AF.Identity, [b_mo, b_cols, b_mod], [b_h2[ti]],
                                 scale=cols[:, 1, kc, seg:seg + 1], bias=mod[:, 24 + kc, seg:seg + 1])
                self.em.flush()
            with contextlib.ExitStack() as st:
                sb = lambda name, shape, dt=F32: self.sb(st, "f_" + name, shape, dt)
                cw = sb("cw", [128, 22, 3])
                cb = sb("cb", [128, 22])
                b_par = Buf("fpar")
                self.dma("sp", cw[:], dr["ffn_conv_w"][l], w=[b_par])
                self.dma("sp", cb[:], dr["ffn_conv_b"][l], w=[b_par])
                wf = [sb(f"wf{i}", [128, KC, 128]) for i in range(2)]
                wb = [sb(f"wb{i}", [128, KC, 128], BF16) for i in range(2)]
                b_wf, b_wb = mkbufs("fwf", 2), mkbufs("fwb", 2)
                gp = sb("gp", [128, NT + 6])
                gc = sb("gc", [128, NT])
                ast = sb("ast", [128, NT], BF16)
                b_gp, b_gc, b_ast = Buf("gp"), Buf("gc"), Buf("ast")
                self.memset("pool", gp[:], 0.0, w=[b_gp])
                wview = dr["ffn_w_in"][l].rearrange("(kc p) n -> p kc n", p=128)
                k = 0
                kw = 0
                alltiles = list(enumerate(tiles))
                h2tiles = self.cfg.tiles
                for jc in range(22):
                    for part in range(2):
                        s = kw % 2
                        kw += 1
                        c0 = part * DFF + jc * 128
                        self.dma("sp", wf[s][:], wview[:, :, c0:c0 + 128], w=[b_wf[s]])
                        self.cp("pool", wb[s][:], wf[s][:], r=[b_wf[s]], w=[b_wb[s]])
                        for (t0, n, seg) in h2tiles:
                            pb = k % 8
                            k += 1
                            hb = [b_h2[i] for i, (a, m, sg_) in alltiles if a < t0 + n and a + m > t0]
                            for kc in range(KC):
                                self.mm(psb[pb][:, 0:n], wb[s][:, kc, :], h2T[:, kc, t0:t0 + n], start=(kc == 0), stop=(kc == KC - 1),
                                        r=[b_wb[s]] + hb, w=[b_ps[pb]])
                            if part == 0:
                                p0 = t0 + 2 if t0 < LCTX else t0 + 4
                                self.act(gp[:, p0:p0 + n], psb[pb][:, 0:n], AF.Copy, [b_ps[pb]], [b_gp])
                            else:
                                self.tt("dve", ast[:, t0:t0 + n], psb[pb][:, 0:n], gc[:, t0:t0 + n], ALU.mult,
                                        r=[b_ps[pb], b_gc], w=[b_ast])
                        if part == 0:
                            for (d0, s0, n) in self.segs():
                                self.ts("dve", gc[:, d0:d0 + n], gp[:, s0 - 1:s0 - 1 + n], cw[:, jc, 0:1], cb[:, jc:jc + 1], ALU.mult, ALU.add,
                                        r=[b_gp, b_par], w=[b_gc])
                                for tap in (1, 2):
                                    self.stt(gc[:, d0:d0 + n], gp[:, s0 - 1 + tap:s0 - 1 + tap + n], cw[:, jc, tap:tap + 1], gc[:, d0:d0 + n],
                                             ALU.mult, ALU.add, r=[b_gp, b_par, b_gc], w=[b_gc])
                            self.act(gc[:], gc[:], AF.Silu, [b_gc], [b_gc])
                    self.dma("pool", aT[jc * 128:(jc + 1) * 128, :], ast[:], r=[b_ast])
                self.em.flush()
        with contextlib.ExitStack() as st:
            sb = lambda name, shape, dt=F32: self.sb(st, "g_" + name, shape, dt)
            ng = sb("ng", [128, 4, KC])
            cols = sb("cols", [128, KC, 2])
            wst = sb("wst", [128, 1024])
            Wf = sb("Wf", [128, 22, 1024], BF16)
            b_ng, b_cols, b_wst, b_W = Buf("ng"), Buf("cols"), Buf("wst"), Buf("Wf")
            self.dma("sp", ng[:], dr["norm_g"][l], w=[b_ng])
            for kc in range(KC):
                self.ts("dve", cols[:, kc, :], mod[:, 40 + kc, :], ng[:, 3, kc:kc + 1], None, ALU.mult, r=[b_mod, b_ng], w=[b_cols])
            for kc in range(22):
                self.dma("sp", wst[:], dr["ffn_w_out"][l, kc * 128:(kc + 1) * 128, :], w=[b_wst])
                self.cp("pool", Wf[:, kc, :], wst[:], r=[b_wst], w=[b_W])
            at = [sb(f"at{i}", [128, 22, 512], BF16) for i in range(2)]
            xt = [sb(f"xt{i}", [128, KC, 512]) for i in range(2)]
            fo = sb("fo", [128, KC, 512])
            sq = sb("sq", [128, KC, 512], BF16)
            rs = sb("rs", [128, 512])
            b_at, b_xt = mkbufs("at", 2), mkbufs("gxt", 2)
            b_fo, b_sq, b_rs = Buf("fo"), Buf("gsq"), Buf("grs")
            xmview = xmid.rearrange("(kc p) n -> p kc n", p=128)
            aview = aT.rearrange("(kc p) n -> p kc n", p=128)
            k = 0
            for ti, (t0, n, seg) in enumerate(cfg.tiles):
                if last and seg == 1:
                    continue
                s = ti % 2
                self.dma("sp", at[s][:, :, 0:n], aview[:, :, t0:t0 + n], w=[b_at[s]])
                self.dma("sp", xt[s][:, :, 0:n], xmview[:, :, t0:t0 + n], w=[b_xt[s]])
                for oc in range(KC):
                    pb = k % 6
                    k += 1
                    for kc in range(22):
                        self.mm(psb[pb][:, 0:n], Wf[:, kc, oc * 128:(oc + 1) * 128], at[s][:, kc, 0:n], start=(kc == 0), stop=(kc == 21),
                                r=[b_W, b_at[s]], w=[b_ps[pb]])
                    self.act(fo[:, oc, 0:n], psb[pb][:, 0:n], AF.Copy, [b_ps[pb]], [b_fo])
                self.rstd_tile(fo, b_fo, n, sq, b_sq, rs, b_rs, 6 + ti % 2)
                for oc in range(KC):
                    self.tt("dve", fo[:, oc, 0:n], fo[:, oc, 0:n], rs[:, 0:n], ALU.mult, r=[b_fo, b_rs], w=[b_fo])
                    self.stt(xt[s][:, oc, 0:n], fo[:, oc, 0:n], cols[:, oc, seg:seg + 1], xt[s][:, oc, 0:n], ALU.mult, ALU.add,
                             r=[b_fo, b_cols, b_xt[s]], w=[b_xt[s]])
                if last:
                    dst = dr["outT"].rearrange("(kc p) n -> p kc n", p=128)[:, :, t0 - LCTX:t0 - LCTX + n]
                else:
                    dst = dr["xcur"].rearrange("(kc p) n -> p kc n", p=128)[:, :, t0:t0 + n]
                self.dma("pool", dst, xt[s][:, :, 0:n], r=[b_xt[s]])
            self.em.flush()

    def phase_mod(self, l, cvec, ada_w, ada_b, mod, b_mod):
        em = self.em
        psb, b_ps = self.psb, self.b_ps
        with contextlib.ExitStack() as st:
            cv = self.sb(st, "cv", [128, KC, 2], F32)
            cs = self.sb(st, "cs", [128, KC, 2], BF16)
            sg = self.sb(st, "cv_sg", [128, KC, 2], F32)
            ab = self.sb(st, "ab", [128, 48], F32)
            wf = [self.sb(st, f"adaw_f{i}", [128, KC, 512], F32) for i in range(2)]
            wb = [self.sb(st, f"adaw_b{i}", [128, KC, 512], BF16) for i in range(2)]
            b_cv, b_cs, b_ab, b_sg = Buf("cv"), Buf("cs"), Buf("ab"), Buf("sg")
            b_wf, b_wb = mkbufs("wf", 2), mkbufs("wb", 2)
            em.dma("sp", lambda e: e.dma_start(out=cv[:], in_=cvec[:, :, :]), writes=[b_cv])
            em.dma("sp", lambda e: e.dma_start(out=ab[:], in_=ada_b[l]), writes=[b_ab])
            em.op("act", lambda e: e.activation(out=sg[:], in_=cv[:], func=AF.Sigmoid), reads=[b_cv], writes=[b_sg])
            em.op("dve", lambda e: e.tensor_tensor(out=cs[:], in0=cv[:], in1=sg[:], op=ALU.mult),
                  reads=[b_cv, b_sg], writes=[b_cs])
            wview = ada_w[l].rearrange("(kc p) n -> p kc n", p=128)
            for g in range(12):
                s = g % 2
                em.dma("sp", lambda e, s=s, g=g: e.dma_start(out=wf[s][:], in_=wview[:, :, g * 512:(g + 1) * 512]),
                       writes=[b_wf[s]])
                em.op("pool", lambda e, s=s: e.tensor_copy(out=wb[s][:], in_=wf[s][:]),
                      reads=[b_wf[s]], writes=[b_wb[s]])
                pb = g % 8
                for q in range(4):
                    i = g * 4 + q
                    for kc in range(KC):
                        em.op("pe", lambda e, s=s, q=q, kc=kc, pb=pb: e.matmul(
                            psb[pb][:, q * 2:q * 2 + 2], lhsT=wb[s][:, kc, q * 128:(q + 1) * 128],
                            rhs=cs[:, kc, :], start=(kc == 0), stop=(kc == KC - 1)),
                            reads=[b_wb[s], b_cs], writes=[b_ps[pb]], defer=(kc != KC - 1))
                for q in range(4):
                    i = g * 4 + q
                    em.op("dve", lambda e, q=q, i=i, pb=pb: e.tensor_scalar(
                        out=mod[:, i, :], in0=psb[pb][:, q * 2:q * 2 + 2], scalar1=ab[:, i:i + 1], scalar2=None,
                        op0=ALU.add), reads=[b_ps[pb], b_ab], writes=[b_mod])
            em.flush()

    def phase_norm_inproj(self, l, xc, norm_g, w_in, uT, mod, b_mod, ones_bf, b_ones):
        cfg = self.cfg
        em = self.em
        NT = cfg.NT
        psb, b_ps = self.psb, self.b_ps
        with contextlib.ExitStack() as st:
            hT = self.sb(st, "hT", [128, KC, NT], BF16)
            b_h = mkbufs("hT", len(cfg.tiles))
            ng = self.sb(st, "ng", [128, 4, KC], F32)
            A1 = self.sb(st, "A1", [128, KC, 2], F32)
            b_ng, b_A1 = Buf("ng"), Buf("A1")
            em.dma("sp", lambda e: e.dma_start(out=ng[:], in_=norm_g[l]), writes=[b_ng])
            for kc in range(KC):
                em.op("dve", lambda e, kc=kc: e.tensor_scalar(
                    out=A1[:, kc, :], in0=mod[:, 8 + kc, :], scalar1=1.0, scalar2=ng[:, 0, kc:kc + 1],
                    op0=ALU.add, op1=ALU.mult), reads=[b_mod, b_ng], writes=[b_A1])
            with contextlib.ExitStack() as st2:
                xt = [self.sb(st2, f"xt{i}", [128, KC, 512], F32) for i in range(2)]
                sq = [self.sb(st2, f"sq{i}", [128, KC, 512], BF16) for i in range(2)]
                rs = [self.sb(st2, f"rs{i}", [128, 512], F32) for i in range(2)]
                tmp = [self.sb(st2, f"tmp{i}", [128, KC, 512], F32) for i in range(2)]
                b_xt, b_sq, b_rs, b_tmp = mkbufs("xt", 2), mkbufs("sq", 2), mkbufs("rs", 2), mkbufs("tmp", 2)
                xview = xc.rearrange("(kc p) n -> p kc n", p=128)
                for ti, (t0, n, seg) in enumerate(cfg.tiles):
                    s = ti % 2
                    pb = ti % 8
                    em.dma("sp", lambda e, s=s, t0=t0, n=n: e.dma_start(out=xt[s][:, :, 0:n], in_=xview[:, :, t0:t0 + n]),
                           writes=[b_xt[s]])
                    em.op("act", lambda e, s=s, n=n: e.activation(out=sq[s][:, :, 0:n], in_=xt[s][:, :, 0:n], func=AF.Square),
                          reads=[b_xt[s]], writes=[b_sq[s]])
                    for kc in range(KC):
                        em.op("pe", lambda e, s=s, n=n, kc=kc, pb=pb: e.matmul(
                            psb[pb][:, 0:n], lhsT=ones_bf[:], rhs=sq[s][:, kc, 0:n], start=(kc == 0), stop=(kc == KC - 1)),
                            reads=[b_sq[s], b_ones], writes=[b_ps[pb]], defer=(kc != KC - 1))
                    em.op("act", lambda e, s=s, n=n, pb=pb: e.activation(
                        out=rs[s][:, 0:n], in_=psb[pb][:, 0:n], func=AF.Sqrt, scale=1.0 / D, bias=1e-6),
                        reads=[b_ps[pb]], writes=[b_rs[s]])
                    em.op("dve", lambda e, s=s, n=n: e.reciprocal(out=rs[s][:, 0:n], in_=rs[s][:, 0:n]),
                          reads=[b_rs[s]], writes=[b_rs[s]])
                    for kc in range(KC):
                        em.op("dve", lambda e, s=s, n=n, kc=kc: e.tensor_tensor(
                            out=tmp[s][:, kc, 0:n], in0=xt[s][:, kc, 0:n], in1=rs[s][:, 0:n], op=ALU.mult),
                            reads=[b_xt[s], b_rs[s]], writes=[b_tmp[s]])
                        em.op("act", lambda e, s=s, n=n, kc=kc, t0=t0, seg=seg: e.activation(
                            out=hT[:, kc, t0:t0 + n], in_=tmp[s][:, kc, 0:n], func=AF.Identity,
                            scale=A1[:, kc, seg:seg + 1], bias=mod[:, kc, seg:seg + 1]),
                            reads=[b_tmp[s], b_A1, b_mod], writes=[b_h[ti]])
                em.flush()
            with contextlib.ExitStack() as st2:
                wf = [self.sb(st2, f"wf{i}", [128, KC, 128], F32) for i in range(2)]
                wb = [self.sb(st2, f"wb{i}", [128, KC, 128], BF16) for i in range(2)]
                stg = [self.sb(st2, f"stg{i}", [128, NT], F32) for i in range(2)]
                b_wf, b_wb, b_stg = mkbufs("wf", 2), mkbufs("wb", 2), mkbufs("stg", 2)
                wview = w_in[l].rearrange("(kc p) n -> p kc n", p=128)
                k = 0
                for j in range(self.NCH):
                    s = j % 2
                    em.dma("sp", lambda e, s=s, j=j: e.dma_start(out=wf[s][:], in_=wview[:, :, j * 128:(j + 1) * 128]),
                           writes=[b_wf[s]])
                    em.op("pool", lambda e, s=s: e.tensor_copy(out=wb[s][:], in_=wf[s][:]),
                          reads=[b_wf[s]], writes=[b_wb[s]])
                    for ti, (t0, n, seg) in enumerate(cfg.tiles):
                        pb = k % 8
                        k += 1
                        for kc in range(KC):
                            em.op("pe", lambda e, s=s, n=n, kc=kc, pb=pb, t0=t0: e.matmul(
                                psb[pb][:, 0:n], lhsT=wb[s][:, kc, :], rhs=hT[:, kc, t0:t0 + n],
                                start=(kc == 0), stop=(kc == KC - 1)),
                                reads=[b_wb[s], b_h[ti]], writes=[b_ps[pb]], defer=(kc != KC - 1))
                        if k % 2 == 0:
                            em.op("act", lambda e, s=s, n=n, pb=pb, t0=t0: e.activation(
                                out=stg[s][:, t0:t0 + n], in_=psb[pb][:, 0:n], func=AF.Copy),
                                reads=[b_ps[pb]], writes=[b_stg[s]])
                        else:
                            em.op("dve", lambda e, s=s, n=n, pb=pb, t0=t0: e.tensor_copy(
                                out=stg[s][:, t0:t0 + n], in_=psb[pb][:, 0:n]),
                                reads=[b_ps[pb]], writes=[b_stg[s]])
                    em.dma("pool", lambda e, s=s, j=j: e.dma_start(out=uT[j * 128:(j + 1) * 128, :], in_=stg[s][:]),
                           reads=[b_stg[s]], writes=[])
                em.flush()


def na_blocks(rows):
    nqb = rows // 8
    types = {}
    blocks = []
    for qb in range(nqb):
        q0 = qb * 8
        rs = lambda r: min(max(r - 4, 0), rows - 8)
        lo, hi = rs(q0), rs(q0 + 7) + 8
        cls = "f" if qb == 0 else ("l" if qb == nqb - 1 else "i")
        items = []
        for kr0 in range(lo, hi, 2):
            delta = kr0 - q0
            key = (cls, delta)
            if key not in types:
                types[key] = (len(types), q0, kr0)
            items.append((kr0, types[key][0], (delta + 4) // 2))
        blocks.append(items)
    return blocks, len(types)


def blocks_type(blocks, qb, kr0):
    for (k, ty, di) in blocks[qb]:
        if k == kr0:
            return ty
    raise KeyError


def na_mask_np(rows):
    nqb = rows // 8
    out = {}
    for qb in range(nqb):
        q0 = qb * 8
        rsf = lambda r: min(max(r - 4, 0), rows - 8)
        lo, hi = rsf(q0), rsf(q0 + 7) + 8
        cls = "f" if qb == 0 else ("l" if qb == nqb - 1 else "i")
        for kr0 in range(lo, hi, 2):
            key = (cls, kr0 - q0)
            if key in out:
                continue
            krow = kr0 + np.arange(2)[:, None, None, None]
            kc = np.arange(64)[None, :, None, None]
            qrow = q0 + np.arange(8)[None, None, :, None]
            qc = np.arange(64)[None, None, None, :]
            rs = np.clip(qrow - 4, 0, rows - 8)
            cs = np.clip(qc - 8, 0, 48)
            ok = (krow >= rs) & (krow < rs + 8) & (kc >= cs) & (kc < cs + 16)
            out[key] = np.where(ok, 0.0, -30000.0).reshape(128, 512).astype(np.float32)
    return np.stack(list(out.values()), axis=0)


def rope_tables(T):
    half = 32
    freqs = 10000.0 ** (-np.arange(0, half, 2, dtype=np.float32) / half)
    pos = np.arange(T)
    prow, pcol = pos // 64, pos % 64
    cos = np.zeros((64, T), np.float32)
    sin = np.zeros((64, T), np.float32)
    for d in range(64):
        p = prow if d < 32 else pcol
        dd = d % 32
        ang = p.astype(np.float32) * freqs[dd % 16]
        cos[d] = np.cos(ang)
        sin[d] = -np.sin(ang) if dd < 16 else np.sin(ang)
    return np.concatenate([cos, cos], 0), np.concatenate([sin, sin], 0)


def input_shapes(cfg):
    Ld, NT, T = cfg.depth, cfg.NT, cfg.T
    return {
        "xc": [D, NT], "cvec": [128, KC, 2],
        "ada_w": [Ld, D, 6 * D], "ada_b": [Ld, 128, 48], "norm_g": [Ld, 128, 4, KC],
        "w_in": [Ld, D, NIN_X],
        "lru_conv_w": [Ld, 128, 8, 4], "lru_conv_b": [Ld, 128, 8],
        "lru_gate_a_w": [Ld, 2, 16, 64, 64], "lru_gate_x_w": [Ld, 2, 16, 64, 64],
        "lru_gate_b": [Ld, 128, 2, 2, 8], "lru_lambda": [Ld, 128, 2, 8],
        "ident": [128, 128], "rope_cos": [128, T], "rope_sin": [128, T],
        "na_mask": [na_blocks(T // 64)[1], 128, 512], "rpb_pad": [Ld, 16, 24, 128],
        "bdones": [128, 128], "istack": [128, 64], "rk_mask": [2, 128, 1152],
        "rwkv_mu": [Ld, 128, 26, 2], "rwkv_w0a0": [Ld, 128, 2, 2, 8], "rwkv_vec": [Ld, 128, 5, 8],
        "w_branch": [Ld, 3, D, D], "w_out": [Ld, D, D], "ffn_w_in": [Ld, D, 2 * DFF], "ffn_w_out": [Ld, DFF, D],
        "ffn_conv_w": [Ld, 128, 22, 3], "ffn_conv_b": [Ld, 128, 22],
        "rwkv_w_up": [Ld, 2, 64, 1024], "rwkv_a_up": [Ld, 2, 64, 1024], "rwkv_g_up": [Ld, 128, 1024],
    }


def colfmt(v, n):
    v = np.asarray(v)
    return np.moveaxis(v.reshape(v.shape[:-1] + (n, 128)), -1, -2)


def rope_perm():
    idx = np.arange(1024)
    d = idx % 64
    dd = d % 32
    partner = np.where(dd < 16, idx + 16, idx - 16)
    return partner


def prep_shared(inp, cfg):
    f = lambda a: np.ascontiguousarray(a, dtype=np.float32)
    Ld = cfg.depth
    m = {}
    m["ada_w"] = f(inp["ada_w"][:Ld])
    m["ada_b"] = f(colfmt(inp["ada_b"][:Ld], 48))
    m["norm_g"] = f(colfmt(inp["norm_g"][:Ld], KC).transpose(0, 2, 1, 3))
    w_in = inp["w_in"][:Ld]
    C0 = A_COLS + 2048
    perm = rope_perm()
    wq = w_in[:, :, C0:C0 + 1024][:, :, perm]
    wk = w_in[:, :, C0 + 1024:C0 + 2048][:, :, perm]
    m["w_in"] = f(np.concatenate([w_in, wq, wk], axis=2))
    m["lru_conv_w"] = f(colfmt(inp["lru_conv_w"][:Ld], 8).transpose(0, 2, 3, 1))
    m["lru_conv_b"] = f(colfmt(inp["lru_conv_b"][:Ld], 8))
    m["lru_gate_a_w"] = f(inp["lru_gate_a_w"][:Ld])
    m["lru_gate_x_w"] = f(inp["lru_gate_x_w"][:Ld])
    gb = np.stack([inp["lru_gate_a_b"][:Ld], inp["lru_gate_x_b"][:Ld]], axis=1)
    m["lru_gate_b"] = f(colfmt(gb, 8).transpose(0, 3, 1, 2, 4))
    m["lru_lambda"] = f(colfmt(inp["lru_lambda"][:Ld], 8).transpose(0, 2, 1, 3))
    m["ident"] = np.eye(128, dtype=np.float32)
    cos, sin = rope_tables(cfg.T)
    m["rope_cos"], m["rope_sin"] = f(cos), f(sin)
    m["na_mask"] = f(na_mask_np(cfg.T // 64))
    rp = np.zeros((Ld, 16, 24, 128), np.float32)
    rp[:, :, 4:19, 48:79] = inp["na_rpb"][:Ld]
    m["rpb_pad"] = rp
    blk = np.kron(np.eye(2, dtype=np.float32), np.ones((64, 64), np.float32))
    m["bdones"] = blk
    m["istack"] = np.concatenate([np.eye(64, dtype=np.float32)] * 2, axis=0)
    i64 = np.arange(64)
    U = np.kron(np.eye(2), (i64[:, None] < i64[None, :])).astype(np.float32)
    UI = np.kron(np.eye(2), (i64[:, None] <= i64[None, :])).astype(np.float32)
    Lw, LI = U.T.copy(), UI.T.copy()
    ONE = np.ones((128, 128), np.float32)
    m32 = np.kron(np.eye(4), np.ones((32, 32))).astype(np.float32)
    fwd = np.concatenate([U * m32, UI, ONE, ONE, UI, ONE, Lw * m32, Lw, Lw * (1 - m32)], axis=1)
    bwd = np.concatenate([Lw * m32, LI, ONE, ONE, LI, ONE, U * m32, U, U * (1 - m32)], axis=1)
    m["rk_mask"] = np.stack([fwd, bwd], axis=0)
    m["rwkv_mu"] = f(colfmt(inp["rwkv_mu"][:Ld], 26).transpose(0, 2, 3, 1))
    w0a0 = np.stack([inp["rwkv_w0"][:Ld], inp["rwkv_a0"][:Ld]], axis=1)
    m["rwkv_w0a0"] = f(colfmt(w0a0, 8).transpose(0, 3, 1, 2, 4))
    vec = np.stack([inp["rwkv_k_k"][:Ld], inp["rwkv_k_a"][:Ld], inp["rwkv_r_k"][:Ld].reshape(Ld, 1024),
                    inp["rwkv_lnx_w"][:Ld], inp["rwkv_lnx_b"][:Ld]], axis=1)
    m["rwkv_vec"] = f(colfmt(vec, 8).transpose(0, 2, 1, 3))
    m["rwkv_w_up"] = f(inp["rwkv_w_up"][:Ld])
    m["rwkv_a_up"] = f(inp["rwkv_a_up"][:Ld])
    m["rwkv_g_up"] = f(inp["rwkv_g_up"][:Ld])
    for k in ("w_branch", "w_out", "ffn_w_in", "ffn_w_out"):
        m[k] = f(inp[k][:Ld])
    m["ffn_conv_w"] = f(colfmt(inp["ffn_conv_w"][:Ld], 22).transpose(0, 2, 3, 1))
    m["ffn_conv_b"] = f(colfmt(inp["ffn_conv_b"][:Ld], 22))
    return m


def prep_inputs(inp, b, cfg, shared=None):
    T = cfg.T
    f = lambda a: np.ascontiguousarray(a, dtype=np.float32)
    m = dict(shared if shared is not None else prep_shared(inp, cfg))
    m["xc"] = f(np.concatenate([inp["ctx"][b].T, inp["x"][b, :T].T], axis=1))
    cv = np.stack([inp["c"][b], inp["c_ctx"]], axis=1)
    m["cvec"] = f(cv.reshape(KC, 128, 2).transpose(1, 0, 2))
    return m


def kernel(**inputs):
    cfg = Cfg()
    bld = Builder(cfg)
    nc = bld.build()
    inp = {k: np.asarray(v) for k, v in inputs.items()}
    shared = prep_shared(inp, cfg)
    in_maps = [prep_inputs(inp, b, cfg, shared) for b in range(8)]
    res = run_bass_kernel_spmd(nc, in_maps, core_ids=list(range(8)))
    out = np.stack([r["outT"].T for r in res.results], axis=0)
    return out.astype(np.float32)
```

```python
import contextlib
import numpy as np
import concourse.bass as bass
import concourse.mybir as mybir
from concourse.bass_utils import run_bass_kernel_spmd

F32 = mybir.dt.float32
BF16 = mybir.dt.bfloat16
AF = mybir.ActivationFunctionType
ALU = mybir.AluOpType
AX = mybir.AxisListType

D = 1024
KC = 8
LCTX = 256
DFF = 2816
NHEAD = 16
A_COLS = 3 * 1024 + 256
N_IN = 11520
NIN_X = N_IN + 2048
ENGS = ("pe", "act", "dve", "pool", "sp")
NDMA_SEMS = 24
NODEFER = False


class Buf:
    __slots__ = ("name", "w", "readers")

    def __init__(self, name):
        self.name = name
        self.w = None
        self.readers = []


def mkbufs(name, n):
    return [Buf(f"{name}{i}") for i in range(n)]


class Emit:
    def __init__(self, nc, stack):
        self.nc = nc
        self.prog = {e: [] for e in ENGS}
        self.count = {e: 0 for e in ENGS}
        self.seen = {e: {} for e in ENGS}
        self.pend_inc = {}
        self.capture_list = None
        self.dma_total = [0] * NDMA_SEMS
        self.dma_rr = 0
        self.n_instr = 0
        self.sems = {}
        for e in ENGS:
            self.sems[e] = stack.enter_context(nc.semaphore(f"c_{e}"))
        for k in range(NDMA_SEMS):
            self.sems[("dma", k)] = stack.enter_context(nc.semaphore(f"d_{k}"))

    def _deps(self, eng, reads, writes):
        deps = {}

        def add(d):
            if d is None:
                return
            k, v = d
            if deps.get(k, 0) < v:
                deps[k] = v
        for b in reads:
            add(b.w)
        for b in writes:
            add(b.w)
            for r in b.readers:
                add(r)
        waits = []
        seen = self.seen[eng]
        for k, v in deps.items():
            if k == eng and v > self.count[eng]:
                continue
            if seen.get(k, 0) < v:
                seen[k] = v
                waits.append((k, v))
        return waits

    def _commit(self, me, reads, writes):
        for b in reads:
            b.readers.append(me)
            if len(b.readers) > 32:
                mx = {}
                for k, v in b.readers:
                    if mx.get(k, 0) < v:
                        mx[k] = v
                b.readers = list(mx.items())
        for b in writes:
            b.w = me
            b.readers = []

    def op(self, eng, fn, reads=(), writes=(), defer=False):
        if self.capture_list is not None:
            self.capture_list.append(("op", eng, fn, reads, writes, defer))
            return
        waits = self._deps(eng, reads, writes)
        if defer and not NODEFER:
            me = (eng, self.count[eng] + 1)
            self.pend_inc[eng] = 1
            self.prog[eng].append((waits, fn, None))
        else:
            self.count[eng] += 1
            me = (eng, self.count[eng])
            self.prog[eng].append((waits, fn, (eng, 1)))
            self.pend_inc[eng] = 0
        self._commit(me, reads, writes)
        self.n_instr += 1 + len(waits)

    def dma(self, q, fn, reads=(), writes=()):
        if self.capture_list is not None:
            self.capture_list.append(("dma", q, fn, reads, writes))
            return
        k = self.dma_rr
        self.dma_rr = (self.dma_rr + 1) % NDMA_SEMS
        key = ("dma", k)
        waits = self._deps(q, reads, writes)
        prev = self.dma_total[k]
        if prev > 0 and self.seen[q].get(key, 0) < prev:
            self.seen[q][key] = prev
            waits.append((key, prev))
        self.dma_total[k] += 16
        me = (key, self.dma_total[k])
        self.prog[q].append((waits, fn, (key, 16)))
        self._commit(me, reads, writes)
        self.n_instr += 1 + len(waits)

    def captured(self, f):
        lst = []
        self.capture_list = lst
        f()
        self.capture_list = None
        return lst

    def replay(self, lst, k):
        while k > 0 and lst:
            e = lst.pop(0)
            if e[0] == "op":
                self.op(*e[1:])
            else:
                self.dma(*e[1:])
            k -= 1

    def flush(self):
        assert all(v == 0 for v in self.pend_inc.values()), "deferred semaphore increment left dangling"
        nc = self.nc
        prog = self.prog
        sems = self.sems
        dma_fin = [(("dma", k), v) for k, v in enumerate(self.dma_total) if v > 0]

        def run(name, eng):
            for waits, fn, inc in prog[name]:
                for k, v in waits:
                    eng.wait_ge(sems[k], v)
                ins = fn(eng)
                if inc is not None:
                    ins.then_inc(sems[inc[0]], inc[1])
            if name in ("sp", "pool", "act"):
                for k, v in dma_fin:
                    eng.wait_ge(sems[k], v)

        with nc.Block() as block:
            @block.tensor
            def _(t):
                run("pe", t)

            @block.scalar
            def _(a):
                run("act", a)

            @block.vector
            def _(v):
                run("dve", v)

            @block.gpsimd
            def _(g):
                run("pool", g)

            @block.sync
            def _(s):
                run("sp", s)
        for k, v in dma_fin:
            for e in ENGS:
                self.seen[e][k] = v
        self.prog = {e: [] for e in ENGS}


class Cfg:
    def __init__(self, T=4096, depth=2, dbg=False):
        self.T = T
        self.NT = LCTX + T
        self.depth = depth
        self.dbg = dbg
        self.phases = ("lru", "na", "rwkv", "merge", "ffn")
        self.tiles = [(0, LCTX, 1)] + [(LCTX + 512 * i, 512, 0) for i in range(T // 512)]


class Builder:
    def __init__(self, cfg):
        self.cfg = cfg
        self.nc = bass.Bass("TRN2", target_bir_lowering=False)
        self.dram = {}

    def din(self, name, shape, dt=F32):
        t = self.nc.dram_tensor(name, list(shape), dt, kind="ExternalInput").ap()
        self.dram[name] = t
        return t

    def dscratch(self, name, shape, dt=F32, out=False):
        kind = "ExternalOutput" if (out or self.cfg.dbg) else "Internal"
        t = self.nc.dram_tensor(name, list(shape), dt, kind=kind).ap()
        self.dram[name] = t
        return t

    def sb(self, st, name, shape, dt):
        self._uid = getattr(self, "_uid", 0) + 1
        return st.enter_context(self.nc.sbuf_tensor(f"{name}_{self._uid}", list(shape), dt))

    def ps(self, st, name, shape, dt=F32):
        return st.enter_context(self.nc.psum_tensor(name, list(shape), dt))

    def build(self, upto=99):
        cfg = self.cfg
        nc = self.nc
        NT, T = cfg.NT, cfg.T
        Ld = cfg.depth
        self.NCH = NIN_X // 128
        for name, shape in input_shapes(cfg).items():
            self.din(name, shape)
        self.dscratch("uT", [NIN_X, NT])
        self.dscratch("yA", [D, NT], BF16)
        self.dscratch("yB", [D, NT], BF16)
        self.dscratch("yC", [D, NT], BF16)
        self.dscratch("aT", [DFF, NT], BF16)
        self.dscratch("xcur", [D, NT])
        self.dscratch("xmid", [D, NT])
        self.dscratch("outT", [D, T], out=True)
        dr = self.dram
        with contextlib.ExitStack() as outer:
            em = Emit(nc, outer)
            self.em = em
            mod = self.sb(outer, "mod", [128, 48, 2], F32)
            ones_bf = self.sb(outer, "ones_bf", [128, 128], BF16)
            b_mod = Buf("mod")
            b_ones = Buf("ones")
            self.mod, self.b_mod, self.ones_bf, self.b_ones = mod, b_mod, ones_bf, b_ones
            em.op("pool", lambda e: e.memset(ones_bf[:], 1.0), writes=[b_ones])
            psb = [self.ps(outer, f"psb{i}", [128, 512]) for i in range(8)]
            b_ps = mkbufs("ps", 8)
            self.psb, self.b_ps = psb, b_ps
            for l in range(Ld):
                xsrc = dr["xc"] if l == 0 else dr["xcur"]
                self.phase_mod(l, dr["cvec"], dr["ada_w"], dr["ada_b"], mod, b_mod)
                self.phase_norm_inproj(l, xsrc, dr["norm_g"], dr["w_in"], dr["uT"], mod, b_mod, ones_bf, b_ones)
                if upto <= 2:
                    break
                if "lru" in cfg.phases:
                    self.phase_lru(l)
                if "na" in cfg.phases:
                    self.phase_na(l)
                if "rwkv" in cfg.phases:
                    self.phase_rwkv(l)
                if "merge" in cfg.phases:
                    self.phase_merge_ffn(l, xsrc)
        return nc

    def mm(self, out, lhsT, rhs, start=True, stop=True, r=(), w=(), defer=None):
        if defer is None:
            defer = not stop
        self.em.op("pe", lambda e: e.matmul(out, lhsT=lhsT, rhs=rhs, start=start, stop=stop), r, w, defer=defer)

    def act(self, out, in_, func, r=(), w=(), scale=1.0, bias=0.0):
        self.em.op("act", lambda e: e.activation(out=out, in_=in_, func=func, scale=scale, bias=bias), r, w)

    def tt(self, eng, out, in0, in1, op, r=(), w=()):
        self.em.op(eng, lambda e: e.tensor_tensor(out=out, in0=in0, in1=in1, op=op), r, w)

    def ts(self, eng, out, in0, s1, s2, op0, op1=None, r=(), w=()):
        if op1 is None:
            self.em.op(eng, lambda e: e.tensor_scalar(out=out, in0=in0, scalar1=s1, scalar2=None, op0=op0), r, w)
        else:
            self.em.op(eng, lambda e: e.tensor_scalar(out=out, in0=in0, scalar1=s1, scalar2=s2, op0=op0, op1=op1), r, w)

    def stt(self, out, in0, sc, in1, op0, op1, r=(), w=()):
        self.em.op("dve", lambda e: e.scalar_tensor_tensor(out=out, in0=in0, scalar=sc, in1=in1, op0=op0, op1=op1), r, w)

    def cp(self, eng, out, in_, r=(), w=()):
        self.em.op(eng, lambda e: e.tensor_copy(out=out, in_=in_), r, w)

    def memset(self, eng, ap, val, w=()):
        self.em.op(eng, lambda e: e.memset(ap, val), (), w)

    def scan(self, out, d0, d1, init, r=(), w=()):
        self.em.op("dve", lambda e: e.tensor_tensor_scan(out=out, data0=d0, data1=d1, initial=init, op0=ALU.mult, op1=ALU.add), r, w)

    def dma(self, q, out, in_, r=(), w=()):
        self.em.dma(q, lambda e: e.dma_start(out=out, in_=in_), r, w)

    def segs(self):
        return [(0, 2, LCTX), (LCTX, LCTX + 4, self.cfg.T)]

    def phase_lru(self, l):
        cfg = self.cfg
        NT, T = cfg.NT, cfg.T
        psb, b_ps = self.psb, self.b_ps
        dr = self.dram
        uT, yB = dr["uT"], dr["yB"]
        B0 = A_COLS
        with contextlib.ExitStack() as st:
            cw = self.sb(st, "l_cw", [128, 8, 4], F32)
            cb = self.sb(st, "l_cb", [128, 8], F32)
            gab = self.sb(st, "l_gab", [128, 2, 2, 8], F32)
            lam = self.sb(st, "l_lam", [128, 2, 8], F32)
            cl = self.sb(st, "l_cl", [128, 2, 8], F32)
            b_par, b_cl = Buf("lpar"), Buf("lcl")
            self.dma("sp", cw[:], dr["lru_conv_w"][l], w=[b_par])
            self.dma("sp", cb[:], dr["lru_conv_b"][l], w=[b_par])
            self.dma("sp", gab[:], dr["lru_gate_b"][l], w=[b_par])
            self.dma("sp", lam[:], dr["lru_lambda"][l], w=[b_par])
            self.act(cl[:], lam[:], AF.Exp, [b_par], [b_cl], scale=-1.0)
            self.act(cl[:], cl[:], AF.Ln, [b_cl], [b_cl], bias=1.0)
            self.ts("dve", cl[:], cl[:], -8.0, None, ALU.mult, r=[b_cl], w=[b_cl])
            xp = self.sb(st, "l_xp", [128, NT + 6], F32)
            xb = self.sb(st, "l_xb", [128, NT], F32)
            xbb = self.sb(st, "l_xbb", [128, NT], BF16)
            gt = self.sb(st, "l_gt", [128, NT], F32)
            gtb = self.sb(st, "l_gtb", [128, NT], BF16)
            A = self.sb(st, "l_A", [128, NT], F32)
            Bt = self.sb(st, "l_B", [128, NT], F32)
            Ct = self.sb(st, "l_C", [128, NT], F32)
            hf = self.sb(st, "l_hf", [128, NT], F32)
            ys = self.sb(st, "l_ys", [128, NT], BF16)
            wgf = [self.sb(st, f"l_wgf{i}", [128, 128], F32) for i in range(2)]
            wgb = [self.sb(st, f"l_wgb{i}", [128, 128], BF16) for i in range(2)]
            b_xp, b_xb, b_xbb, b_gt, b_gtb, b_A, b_B, b_C, b_hf, b_ys = [Buf(n) for n in
                "xp xb xbb gt gtb A B C hf ys".split()]
            b_wgf, b_wgb = mkbufs("wgf", 2), mkbufs("wgb", 2)
            self.memset("pool", xp[:], 0.0, w=[b_xp])
            for i in range(2):
                self.memset("pool", wgf[i][:], 0.0, w=[b_wgf[i]])
            gw = [dr["lru_gate_a_w"], dr["lru_gate_x_w"]]

            def rev(ap):
                aps = [list(p) for p in ap.ap]
                n, stp = aps[-1][1], aps[-1][0]
                aps[-1] = [-stp, n]
                return bass.AP(ap.tensor, ap.offset + stp * (n - 1), aps)
            k = 0
            for j in range(8):
                for (d0, s0, n) in self.segs():
                    self.dma("sp", xp[:, s0:s0 + n], uT[B0 + j * 128:B0 + (j + 1) * 128, d0:d0 + n], w=[b_xp])
                self.dma("sp", gt[:], uT[B0 + 1024 + j * 128:B0 + 1024 + (j + 1) * 128, :], w=[b_gt])
                for (d0, s0, n) in self.segs():
                    self.ts("dve", xb[:, d0:d0 + n], xp[:, s0 - 2:s0 - 2 + n], cw[:, j, 0:1], cb[:, j:j + 1], ALU.mult, ALU.add,
                            r=[b_xp, b_par], w=[b_xb])
                    for tap in range(1, 4):
                        self.stt(xb[:, d0:d0 + n], xp[:, s0 - 2 + tap:s0 - 2 + tap + n], cw[:, j, tap:tap + 1], xb[:, d0:d0 + n],
                                 ALU.mult, ALU.add, r=[b_xp, b_par, b_xb], w=[b_xb])
                self.cp("pool", xbb[:], xb[:], r=[b_xb], w=[b_xbb])
                self.act(gtb[:], gt[:], AF.Gelu_apprx_tanh, [b_gt], [b_gtb])
                for d in range(2):
                    for g in range(2):
                        for hb in range(2):
                            self.dma("sp", wgf[g][hb * 64:(hb + 1) * 64, hb * 64:(hb + 1) * 64], gw[g][l, d, 2 * j + hb],
                                     w=[b_wgf[g]])
                        self.cp("pool", wgb[g][:], wgf[g][:], r=[b_wgf[g]], w=[b_wgb[g]])
                    for g, (dst, b_dst) in enumerate([(A, b_A), (Bt, b_B)]):
                        for (t0, n, seg) in cfg.tiles:
                            pb = k % 8
                            k += 1
                            self.mm(psb[pb][:, 0:n], wgb[g][:], xbb[:, t0:t0 + n], r=[b_wgb[g], b_xbb], w=[b_ps[pb]])
                            self.act(dst[:, t0:t0 + n], psb[pb][:, 0:n], AF.Sigmoid, [b_ps[pb], b_par], [b_dst],
                                     bias=gab[:, g, d, j:j + 1])
                    self.act(A[:], A[:], AF.Exp, [b_A, b_cl], [b_A], scale=cl[:, d, j:j + 1])
                    self.tt("dve", Ct[:], A[:], A[:], ALU.mult, r=[b_A], w=[b_C])
                    self.act(Ct[:], Ct[:], AF.Sqrt, [b_C], [b_C], scale=-1.0, bias=1.0)
                    self.tt("pool", Bt[:], Bt[:], xb[:], ALU.mult, r=[b_B, b_xb], w=[b_B])
                    self.tt("dve", Ct[:], Ct[:], Bt[:], ALU.mult, r=[b_C, b_B], w=[b_C])
                    if d == 0:
                        self.scan(hf[:], A[:], Ct[:], 0.0, r=[b_A, b_C], w=[b_hf])
                    else:
                        for (d0, s0, n) in self.segs():
                            self.cp("pool", Bt[:, d0:d0 + n], rev(A[:, d0:d0 + n]), r=[b_A], w=[b_B])
                            self.cp("pool", gt[:, d0:d0 + n], rev(Ct[:, d0:d0 + n]), r=[b_C], w=[b_gt])
                        self.scan(A[:], Bt[:], gt[:], 0.0, r=[b_B, b_gt, b_A], w=[b_A])
                        for (d0, s0, n) in self.segs():
                            self.cp("pool", Ct[:, d0:d0 + n], rev(A[:, d0:d0 + n]), r=[b_A], w=[b_C])
                        self.tt("dve", hf[:], hf[:], Ct[:], ALU.add, r=[b_hf, b_C], w=[b_hf])
                self.tt("dve", ys[:], hf[:], gtb[:], ALU.mult, r=[b_hf, b_gtb], w=[b_ys])
                self.dma("pool", yB[j * 128:(j + 1) * 128, :], ys[:], r=[b_ys])
            self.em.flush()

    def phase_na(self, l):
        cfg = self.cfg
        NT, T = cfg.NT, cfg.T
        psb, b_ps = self.psb, self.b_ps
        dr = self.dram
        uT, yC = dr["uT"], dr["yC"]
        update_ctx = (l < cfg.depth - 1) or getattr(cfg, 'force_ctx', False)
        rows = T // 64
        blocks, ntype = na_blocks(rows)
        NTB = NT // 128
        CQ, CK, CV, CQP, CKP = 42, 50, 58, 90, 98
        rp = dr["rpb_pad"]
        with contextlib.ExitStack() as st:
            ident = self.sb(st, "n_ident", [128, 128], F32)
            onesp = self.sb(st, "n_onesp", [128, 2, 128], BF16)
            maskb = self.sb(st, "n_maskb", [128, ntype, 512], BF16)
            mtmp = [self.sb(st, f"n_mtmp{i}", [128, 512], F32) for i in range(2)]
            b_id, b_op, b_mk = Buf("ident"), Buf("onesp"), Buf("maskb")
            b_mt = mkbufs("mtmp", 2)
            self.dma("sp", ident[:], dr["ident"][:, :], w=[b_id])
            self.memset("pool", onesp[:], 0.0, w=[b_op])
            self.memset("pool", onesp[:, 0, 0:64], 1.0, w=[b_op])
            self.memset("pool", onesp[:, 1, 64:128], 1.0, w=[b_op])
            for t in range(ntype):
                self.dma("sp", mtmp[t % 2][:], dr["na_mask"][t], w=[b_mt[t % 2]])
                self.cp("pool", maskb[:, t, :], mtmp[t % 2][:], r=[b_mt[t % 2]], w=[b_mk])
            NIN = 7
            tl = [[self.sb(st, f"n_tl{a}_{i}", [128, 512], F32) for i in range(2)] for a in range(NIN)]
            b_tl = [mkbufs(f"tl{a}_", 2) for a in range(NIN)]
            qpl = self.sb(st, "n_qpl", [128, NT], BF16)
            kpl = self.sb(st, "n_kpl", [128, 2, LCTX], BF16)
            qrot = self.sb(st, "n_qrot", [128, T], BF16)
            krot = self.sb(st, "n_krot", [128, 2, T], BF16)
            Vp = self.sb(st, "n_Vp", [128, NTB, 2, 128], BF16)
            Tc2 = self.sb(st, "n_Tc2", [128, 22 * 64], F32)
            biasd = self.sb(st, "n_biasd", [128, 8, 512], F32)
            bm = [self.sb(st, f"n_bm{i}", [128, ntype, 512], BF16) for i in range(2)]
            sT = [self.sb(st, f"n_sT{i}", [128, 512], F32) for i in range(2)]
            pT = [self.sb(st, f"n_pT{i}", [128, 512], BF16) for i in range(3)]
            rc = [self.sb(st, f"n_rc{i}", [128, 512], F32) for i in range(2)]
            yst = self.sb(st, "n_yst", [128, NT], BF16)
            b_qpl, b_kpl, b_qrot, b_krot, b_Vp, b_Tc2, b_biasd, b_yst = [Buf(n) for n in
                "qpl kpl qrot krot Vp Tc2 biasd yst".split()]
            b_bm, b_sT, b_pT, b_rc = mkbufs("bm", 2), mkbufs("sT", 2), mkbufs("pT", 3), mkbufs("rc", 2)
            self.memset("pool", Vp[:], 0.0, w=[b_Vp])
            self.memset("pool", yst[:], 0.0, w=[b_yst])
            self.memset("pool", kpl[:], 0.0, w=[b_kpl])
            self.memset("pool", krot[:], 0.0, w=[b_krot])

            def rev(ap):
                aps = [list(p) for p in ap.ap]
                n, stp = aps[-1][1], aps[-1][0]
                aps[-1] = [-stp, n]
                return bass.AP(ap.tensor, ap.offset + stp * (n - 1), aps)
            kq = 0
            ks = 0
            kp_ = 0
            kacc = 0
            for j in range(8):
                for ti, (t0, n, seg) in enumerate(cfg.tiles):
                    s = ti % 2
                    rowsrc = [CQ + j, CQP + j, CK + j, CKP + j, CV + j]
                    need = [0, 2, 4] if seg == 1 else [0, 1, 2, 3, 4]
                    for a in need:
                        c = rowsrc[a]
                        self.dma("sp", tl[a][s][:, 0:n], uT[c * 128:(c + 1) * 128, t0:t0 + n], w=[b_tl[a][s]])
                    if seg == 1:
                        self.act(qpl[:, t0:t0 + n], tl[0][s][:, 0:n], AF.Copy, [b_tl[0][s]], [b_qpl])
                        self.act(kpl[0:64, 0, 0:n], tl[2][s][0:64, 0:n], AF.Copy, [b_tl[2][s]], [b_kpl])
                        self.act(kpl[64:128, 1, 0:n], tl[2][s][64:128, 0:n], AF.Copy, [b_tl[2][s]], [b_kpl])
                    else:
                        lt0 = t0 - LCTX
                        self.dma("sp", tl[5][s][:], dr["rope_cos"][:, lt0:lt0 + 512], w=[b_tl[5][s]])
                        self.dma("sp", tl[6][s][:], dr["rope_sin"][:, lt0:lt0 + 512], w=[b_tl[6][s]])
                        self.act(qpl[:, t0:t0 + n], tl[0][s][:], AF.Copy, [b_tl[0][s]], [b_qpl])
                        for (a, ap_, dst, b_dst, eng) in [(0, 1, qrot, b_qrot, "dve"), (2, 3, krot, b_krot, "dve")]:
                            self.tt(eng, tl[a][s][:], tl[a][s][:], tl[5][s][:], ALU.mult,
                                    r=[b_tl[a][s], b_tl[5][s]], w=[b_tl[a][s]])
                            self.tt(eng, tl[ap_][s][:], tl[ap_][s][:], tl[6][s][:], ALU.mult,
                                    r=[b_tl[ap_][s], b_tl[6][s]], w=[b_tl[ap_][s]])
                            if dst is krot:
                                for hh in range(2):
                                    hs = slice(hh * 64, (hh + 1) * 64)
                                    self.tt(eng, krot[hs, hh, lt0:lt0 + 512], tl[a][s][hs, :], tl[ap_][s][hs, :], ALU.add,
                                            r=[b_tl[a][s], b_tl[ap_][s]], w=[b_dst])
                            else:
                                self.tt(eng, dst[:, lt0:lt0 + 512], tl[a][s][:], tl[ap_][s][:], ALU.add,
                                        r=[b_tl[a][s], b_tl[ap_][s]], w=[b_dst])
                    nb = n // 128
                    pb = kq % 4
                    kq += 1
                    for q in range(nb):
                        self.em.op("pe", lambda e, pb=pb, q=q, s=s: e.transpose(
                            psb[pb][:, q * 128:(q + 1) * 128], tl[4][s][:, q * 128:(q + 1) * 128], ident[:]),
                            [b_tl[4][s], b_id], [b_ps[pb]], defer=(q != nb - 1))
                    tb0 = t0 // 128
                    pv = psb[pb][:, 0:nb * 128].rearrange("p (a b) -> p a b", b=128)
                    self.cp("dve", Vp[:, tb0:tb0 + nb, 0, 0:64], pv[:, :, 0:64], r=[b_ps[pb]], w=[b_Vp])
                    self.act(Vp[:, tb0:tb0 + nb, 1, 64:128], pv[:, :, 64:128], AF.Copy, [b_ps[pb]], [b_Vp])
                for hh in range(2):
                    h = 2 * j + hh
                    for krl in range(2):
                        base = ((l * 16 + h) * 24 + krl) * 128
                        src = bass.AP(rp.tensor, rp.offset + base, [[1, 64], [128, 22], [1, 64]])
                        self.dma("sp", Tc2[krl * 64:(krl + 1) * 64, :].rearrange("p (a b) -> p a b", b=64), src, w=[b_Tc2])
                    for di in range(8):
                        self.cp("pool", biasd[:, di, :], rev(Tc2[:, di * 128:di * 128 + 512]), r=[b_Tc2], w=[b_biasd])
                    for qb, items in enumerate(blocks):
                        for (kr0, ty, di) in items:
                            if ty is not None:
                                self.tt(("dve", "pool")[ty % 2], bm[hh][:, ty, :], biasd[:, di, :], maskb[:, ty, :], ALU.add,
                                        r=[b_biasd, b_mk], w=[b_bm[hh]])
                qk_list, pv_list = [], []
                for qb, items in enumerate(blocks):
                    a1, a2 = 4 + 2 * (kacc % 2), 5 + 2 * (kacc % 2)
                    kacc += 1
                    s3 = kacc % 2
                    q0t = qb * 512
                    work = []
                    for hh in range(2):
                        for (kr0, ty, di) in items:
                            work.append((hh, "loc", kr0, ty))
                        for cc in range(2):
                            work.append((hh, "ctx", cc, None))
                    for wi, (hh, kind, a, ty) in enumerate(work):
                        hb = hh * 64
                        pb = kq % 4
                        kq += 1
                        s2 = kp_ % 3
                        kp_ += 1
                        first, last = (wi == 0), (wi == len(work) - 1)
                        if kind == "loc":
                            s1 = ks % 2
                            ks += 1

                            def qk(hb=hb, pb=pb, s2=s2, s1=s1, a=a, ty=ty, hh=hh, q0t=q0t):
                                ktok = a * 64
                                self.mm(psb[pb][:, :], krot[:, hh, ktok:ktok + 128], qrot[:, q0t:q0t + 512],
                                        r=[b_krot, b_qrot], w=[b_ps[pb]])
                                self.stt(sT[s1][:], psb[pb][:, :], 0.125, bm[hh][:, ty, :], ALU.mult, ALU.add,
                                         r=[b_ps[pb], b_bm[hh]], w=[b_sT[s1]])
                                self.act(pT[s2][:], sT[s1][:], AF.Exp, [b_sT[s1]], [b_pT[s2]])
                            vch = 2 + a // 2
                        else:
                            def qk(hb=hb, pb=pb, s2=s2, a=a, q0t=q0t, hh=hh):
                                self.mm(psb[pb][:, :], kpl[:, hh, a * 128:(a + 1) * 128],
                                        qpl[:, LCTX + q0t:LCTX + q0t + 512], r=[b_kpl, b_qpl], w=[b_ps[pb]])
                                self.act(pT[s2][:], psb[pb][:, :], AF.Exp, [b_ps[pb]], [b_pT[s2]], scale=0.125)
                            vch = a

                        def pv(a1=a1, a2=a2, vch=vch, hh=hh, s2=s2, first=first, last=last, s3=s3, q0t=q0t):
                            self.mm(psb[a1][:, :], Vp[:, vch, hh, :], pT[s2][:], start=first, stop=last,
                                    r=[b_Vp, b_pT[s2]], w=[b_ps[a1]], defer=True)
                            self.mm(psb[a2][:, :], onesp[:, hh, :], pT[s2][:], start=first, stop=last,
                                    r=[b_op, b_pT[s2]], w=[b_ps[a2]], defer=(not last))
                            if last:
                                self.em.op("dve", lambda e: e.reciprocal(out=rc[s3][:], in_=psb[a2][:, :]),
                                           [b_ps[a2]], [b_rc[s3]])
                                self.tt("dve", yst[:, LCTX + q0t:LCTX + q0t + 512], psb[a1][:, :], rc[s3][:], ALU.mult,
                                        r=[b_ps[a1], b_rc[s3]], w=[b_yst])
                        qk_list.append(qk)
                        pv_list.append(pv)
                LA = 2
                for idx in range(len(qk_list) + LA):
                    if idx < len(qk_list):
                        qk_list[idx]()
                    if idx - LA >= 0:
                        pv_list[idx - LA]()
                if update_ctx:
                    a1, a2 = 4 + 2 * (kacc % 2), 5 + 2 * (kacc % 2)
                    kacc += 1
                    work = [(hh, cc) for hh in range(2) for cc in range(2)]
                    for wi, (hh, cc) in enumerate(work):
                        hb = hh * 64
                        pb = kq % 4
                        kq += 1
                        s2 = kp_ % 3
                        kp_ += 1
                        self.mm(psb[pb][:, 0:LCTX], kpl[:, hh, cc * 128:(cc + 1) * 128], qpl[:, 0:LCTX],
                                r=[b_kpl, b_qpl], w=[b_ps[pb]])
                        self.act(pT[s2][:, 0:LCTX], psb[pb][:, 0:LCTX], AF.Exp, [b_ps[pb]], [b_pT[s2]], scale=0.125)
                        first, last = (wi == 0), (wi == len(work) - 1)
                        self.mm(psb[a1][:, 0:LCTX], Vp[:, cc, hh, :], pT[s2][:, 0:LCTX], start=first, stop=last,
                                r=[b_Vp, b_pT[s2]], w=[b_ps[a1]])
                        self.mm(psb[a2][:, 0:LCTX], onesp[:, hh, :], pT[s2][:, 0:LCTX], start=first, stop=last,
                                r=[b_op, b_pT[s2]], w=[b_ps[a2]])
                    s3 = kacc % 2
                    self.em.op("dve", lambda e, s3=s3, a2=a2: e.reciprocal(out=rc[s3][:, 0:LCTX], in_=psb[a2][:, 0:LCTX]),
                               [b_ps[a2]], [b_rc[s3]])
                    self.tt("dve", yst[:, 0:LCTX], psb[a1][:, 0:LCTX], rc[s3][:, 0:LCTX], ALU.mult,
                            r=[b_ps[a1], b_rc[s3]], w=[b_yst])
                self.dma("pool", yC[j * 128:(j + 1) * 128, :], yst[:], r=[b_yst])
            self.em.flush()

    def phase_rwkv(self, l):
        cfg = self.cfg
        NT, T = cfg.NT, cfg.T
        psb, b_ps = self.psb, self.b_ps
        dr = self.dram
        uT, yA = dr["uT"], dr["yA"]
        NCK = NT // 64
        CW = 0.6065306597126334
        SEGC = 16
        SEGN = SEGC * 64
        segs = [(0, 4)] + [(4 + 16 * i, 16) for i in range((NCK - 4) // 16)]
        with contextlib.ExitStack() as st:
            sb = lambda name, shape, dt=F32: self.sb(st, "r_" + name, shape, dt)
            ident_bf = sb("ident_bf", [128, 128], BF16)
            bdones = sb("bdones", [128, 128], BF16)
            istack = sb("istack", [128, 64], BF16)
            rkm = sb("rkm", [128, 2, 1152], BF16)
            cmask = sb("cmask", [128, SEGN])
            mu = sb("mu", [128, 26, 2])
            c0 = sb("c0", [128, 26])
            w0a0 = sb("w0a0", [128, 2, 2, 8])
            vec = sb("vec", [128, 5, 8])
            omk = sb("omk", [128, 8])
            WA = sb("WA", [128, 2, 1024], BF16)
            GU = sb("GU", [128, 1024], BF16)
            b_cst, b_k, b_par, b_wst, b_W = Buf("cst"), Buf("rk_consts"), Buf("rpar"), Buf("wst"), Buf("WA")
            with contextlib.ExitStack() as st_tmp:
                cst_f = self.sb(st_tmp, "r_cst_f", [128, 1152], F32)
                wst = self.sb(st_tmp, "r_wst", [128, 1024], F32)
                for (dst, src, n) in [(ident_bf, dr["ident"], 128), (bdones, dr["bdones"], 128), (istack, dr["istack"], 64)]:
                    self.dma("sp", cst_f[:, 0:n], src[:, :], w=[b_cst])
                    self.cp("dve", dst[:], cst_f[:, 0:n], r=[b_cst], w=[b_k])
                for d in range(2):
                    self.dma("sp", cst_f[:], dr["rk_mask"][d], w=[b_cst])
                    self.cp("dve", rkm[:, d, :], cst_f[:], r=[b_cst], w=[b_k])
                self.memset("pool", cmask[:], 1.0, w=[b_k])
                self.memset("pool", cmask[:].rearrange("p (c s) -> p c s", s=64)[:, :, 0:1], 0.0, w=[b_k])
                self.dma("sp", mu[:], dr["rwkv_mu"][l], w=[b_par])
                self.dma("sp", w0a0[:], dr["rwkv_w0a0"][l], w=[b_par])
                self.dma("sp", vec[:], dr["rwkv_vec"][l], w=[b_par])
                self.ts("dve", c0[:], mu[:, :, 0], -1.0, 1.0, ALU.mult, ALU.add, r=[b_par], w=[b_par])
                self.tt("dve", c0[:], c0[:], mu[:, :, 1], ALU.subtract, r=[b_par], w=[b_par])
                self.ts("dve", omk[:], vec[:, 1, :], -1.0, 1.0, ALU.mult, ALU.add, r=[b_par], w=[b_par])
                for d in range(2):
                    self.dma("sp", wst[0:64, :], dr["rwkv_w_up"][l, d], w=[b_wst])
                    self.dma("sp", wst[64:128, :], dr["rwkv_a_up"][l, d], w=[b_wst])
                    self.cp("pool", WA[:, d, :], wst[:], r=[b_wst], w=[b_W])
                self.dma("sp", wst[:], dr["rwkv_g_up"][l], w=[b_wst])
                self.cp("pool", GU[:], wst[:], r=[b_wst], w=[b_W])
                self.em.flush()
            LW = sb("LW", [128, NT], BF16)
            GL = sb("GL", [128, NT], BF16)
            Yacc = sb("Yacc", [128, NCK, 64])
            ksum = sb("ksum", [128, NT])
            b_LW, b_GL, b_Yacc, b_ksum = Buf("LW"), Buf("GL"), Buf("Yacc"), Buf("ksum")
            xps = [sb(f"xp{i}", [128, SEGN + 2]) for i in range(3)]
            b_xps = mkbufs("xp", 3)
            kxp = [0]
            rTs = [sb(f"rT{i}", [128, SEGN]) for i in range(2)]
            kT, vT, kap = sb("kT", [128, SEGN]), sb("vT", [128, SEGN]), sb("kap", [128, SEGN])
            T1, T2, T3, T4 = [sb(f"T{i}", [128, SEGN]) for i in range(1, 5)]
            F1, F2, F3 = [sb(f"F{i}", [128, SEGN]) for i in range(1, 4)]
            ynT = sb("ynT", [128, SEGN])
            Vbs = [sb(f"Vb{i}", [128, SEGN], BF16) for i in range(2)]
            gTbs = [sb(f"gTb{i}", [128, SEGN], BF16) for i in range(2)]
            sqb, fsqb = sb("sqb", [128, SEGN], BF16), sb("fsqb", [128, SEGN], BF16)
            stks = [sb(f"stk{i}", [128, 4, SEGN], BF16) for i in range(2)]
            YBD = sb("YBD", [128, SEGC, 128], BF16)
            yst = sb("yst", [128, SEGN], BF16)
            gCs = [sb(f"gC{i}", [128, SEGC]) for i in range(2)]
            lnst = sb("lnst", [128, 6, SEGC])
            ptot = sb("ptot", [128, SEGC])
            b_kT, b_vT, b_kap, b_T1, b_T2, b_T3, b_T4, b_ynT, b_sqb, b_YBD, b_yst, b_ln, b_F1, b_F2, b_F3, b_fsqb, b_ptot = [
                Buf(n) for n in "kT vT kap T1 T2 T3 T4 ynT sqb YBD yst lnst F1 F2 F3 fsqb ptot".split()]
            b_rTs, b_Vbs, b_gTbs, b_stks, b_gCs = [mkbufs(n, 2) for n in "rT Vb gTb stk gC".split()]
            self.memset("pool", YBD[:], 0.0, w=[b_YBD])
            G = 4
            BDg = [sb(f"BDg{i}", [128, G, 5, 128], BF16) for i in range(2)]
            b_BDg = mkbufs("BDg", 2)
            for i in range(2):
                self.memset("pool", BDg[i][:], 0.0, w=[b_BDg[i]])
            SA = [sb(f"SA{i}", [128, 512], BF16) for i in range(G)]
            SB_ = [sb(f"SB{i}", [128, 512], BF16) for i in range(G)]
            MN = [[sb(f"MN{i}_{k}", [128, 256], BF16) for k in range(2)] for i in range(G)]
            Rb = [[sb(f"Rb{i}_{k}", [128, 256], BF16) for k in range(2)] for i in range(G)]
            Pb = [[sb(f"Pb{i}_{k}", [128, 128], BF16) for k in range(2)] for i in range(G)]
            MO = [sb(f"MO{i}", [128, 128], BF16) for i in range(G)]
            QP = [sb(f"QP{i}", [128, 256], BF16) for i in range(G)]
            AK = [sb(f"AK{i}", [128, 256], BF16) for i in range(G)]
            VM = [sb(f"VM{i}", [128, G, 64], BF16) for i in range(2)]
            b_VM = mkbufs("VM", 2)
            b_MO = mkbufs("MO", G)
            pending = []
            ST = sb("ST", [128, 64], BF16)
            S32 = sb("S32", [128, 64])
            S32g = sb("S32g", [128, 64])
            b_S32, b_S32g = Buf("S32"), Buf("S32g")
            b_SA, b_SB, b_QP, b_AK = [mkbufs(n, G) for n in "SA SB QP AK".split()]
            b_MN, b_Rb, b_Pb = [[mkbufs(f"{n}{i}_", 2) for i in range(G)] for n in "MN Rb Pb".split()]
            b_ST = Buf("ST")
            kps = [0]

            def shift(dst, b_dst, c, c0_, n):
                xi = kxp[0] % 3
                kxp[0] += 1
                xp, b_xp = xps[xi], b_xps[xi]
                p0 = c0_ * 64
                p1 = p0 + n
                hasL = p0 not in (0, LCTX)
                hasR = p1 not in (LCTX, NT)
                if not hasL:
                    self.memset("pool", xp[:, 0:1], 0.0, w=[b_xp])
                if not hasR:
                    self.memset("pool", xp[:, n + 1:n + 2], 0.0, w=[b_xp])
                lo, hi = p0 - int(hasL), p1 + int(hasR)
                self.dma("sp", xp[:, 1 - int(hasL):1 + n + int(hasR)], uT[c * 128:(c + 1) * 128, lo:hi], w=[b_xp])
                self.act(dst[:, 0:n], xp[:, 1:1 + n], AF.Copy, [b_xp, b_par], [b_dst], scale=c0[:, c:c + 1])
                self.stt(dst[:, 0:n], xp[:, 0:n], mu[:, c, 0:1], dst[:, 0:n], ALU.mult, ALU.add, r=[b_xp, b_par, b_dst], w=[b_dst])
                self.stt(dst[:, 0:n], xp[:, 2:2 + n], mu[:, c, 1:2], dst[:, 0:n], ALU.mult, ALU.add, r=[b_xp, b_par, b_dst], w=[b_dst])

            def tiles_of(n):
                return [(o, min(512, n - o)) for o in range(0, n, 512)]

            def nextps():
                kps[0] += 1
                return 7

            for (ck0, nck) in segs:
                n = nck * 64
                t0 = ck0 * 64
                shift(T1, b_T1, 24, ck0, n)
                self.act(LW[0:64, t0:t0 + n], T1[0:64, 0:n], AF.Tanh, [b_T1], [b_LW])
                self.act(LW[64:128, t0:t0 + n], T1[64:128, 0:n], AF.Copy, [b_T1], [b_LW])
                shift(T2, b_T2, 25, ck0, n)
                self.act(GL[:, t0:t0 + n], T2[:, 0:n], AF.Sigmoid, [b_T2], [b_GL])

            kbd = [0]

            def prep(j, d, ck0, nck, sp):
                n = nck * 64
                t0 = ck0 * 64
                rT, b_rT, Vb, b_Vb, gTb, b_gTb, stk, b_stk, gC, b_gC = (rTs[sp], b_rTs[sp], Vbs[sp], b_Vbs[sp], gTbs[sp], b_gTbs[sp],
                                                                       stks[sp], b_stks[sp], gCs[sp], b_gCs[sp])
                for (o, m) in tiles_of(n):
                    pb = nextps()
                    self.mm(psb[pb][:, 0:m], WA[0:64, d, j * 128:(j + 1) * 128], LW[0:64, t0 + o:t0 + o + m],
                            r=[b_W, b_LW], w=[b_ps[pb]])
                    self.act(T1[:, o:o + m], psb[pb][:, 0:m], AF.Sigmoid, [b_ps[pb], b_par], [b_T1],
                             bias=w0a0[:, 0, d, j:j + 1])
                    pb = nextps()
                    self.mm(psb[pb][:, 0:m], WA[64:128, d, j * 128:(j + 1) * 128], LW[64:128, t0 + o:t0 + o + m],
                            r=[b_W, b_LW], w=[b_ps[pb]])
                    self.act(T2[:, o:o + m], psb[pb][:, 0:m], AF.Sigmoid, [b_ps[pb], b_par], [b_T2],
                             bias=w0a0[:, 1, d, j:j + 1])
                shift(rT, b_rT, j, ck0, n)
                shift(kT, b_kT, 8 + j, ck0, n)
                shift(vT, b_vT, 16 + j, ck0, n)
                self.scan(T3[:, 0:n], cmask[:, 0:n], T1[:, 0:n], 0.0, r=[b_k, b_T1], w=[b_T3])
                T3v = T3[:, 0:n].rearrange("p (c s) -> p c s", s=64)
                self.act(gC[:, 0:nck], T3v[:, :, 63], AF.Exp, [b_T3], [b_gC], scale=-CW)
                if d == 1:
                    self.cp("pool", ptot[:, 0:nck], T3v[:, :, 63], r=[b_T3], w=[b_ptot])
                    self.tt("dve", T3[:, 0:n], T1[:, 0:n], T3[:, 0:n], ALU.subtract, r=[b_T1, b_T3], w=[b_T3])
                    self.tt("dve", T3v, T3v, ptot[:, 0:nck].unsqueeze(2).to_broadcast([128, nck, 64]), ALU.add,
                            r=[b_T3, b_ptot], w=[b_T3])
                self.tt("dve", T1[:, 0:n], T3[:, 0:n], T1[:, 0:n], ALU.subtract, r=[b_T1, b_T3], w=[b_T1])
                self.act(T1[:, 0:n], T1[:, 0:n], AF.Exp, [b_T1], [b_T1], scale=-CW)
                self.act(T4[:, 0:n], T3[:, 0:n], AF.Exp, [b_T3], [b_T4], scale=CW)
                self.act(T3[:, 0:n], T3[:, 0:n], AF.Exp, [b_T3], [b_T3], scale=-CW)
                self.act(Vb[:, 0:n], vT[:, 0:n], AF.Copy, [b_vT], [b_Vb])
                self.ts("pool", kap[:, 0:n], kT[:, 0:n], vec[:, 0, j:j + 1], None, ALU.mult, r=[b_kT, b_par], w=[b_kap])
                self.act(sqb[:, 0:n], kap[:, 0:n], AF.Square, [b_kap], [b_sqb])
                for (o, m) in tiles_of(n):
                    pb = nextps()
                    self.mm(psb[pb][:, 0:m], bdones[:], sqb[:, o:o + m], r=[b_k, b_sqb], w=[b_ps[pb]])
                    self.act(F3[:, o:o + m], psb[pb][:, 0:m], AF.Sqrt, [b_ps[pb]], [b_F3], bias=1e-24)
                self.em.op("dve", lambda e, n=n: e.reciprocal(out=F3[:, 0:n], in_=F3[:, 0:n]), [b_F3], [b_F3])
                self.tt("dve", kap[:, 0:n], kap[:, 0:n], F3[:, 0:n], ALU.mult, r=[b_kap, b_F3], w=[b_kap])
                if d == 1:
                    for (o, m) in tiles_of(n):
                        pb = nextps()
                        self.mm(psb[pb][:, 0:m], GU[:, j * 128:(j + 1) * 128], GL[:, t0 + o:t0 + o + m],
                                r=[b_W, b_GL], w=[b_ps[pb]])
                        self.act(gTb[:, o:o + m], psb[pb][:, 0:m], AF.Copy, [b_ps[pb]], [b_gTb])
                self.tt("dve", stk[:, 0, 0:n], kap[:, 0:n], T1[:, 0:n], ALU.mult, r=[b_kap, b_T1], w=[b_stk])
                self.tt("pool", stk[:, 1, 0:n], rT[:, 0:n], T3[:, 0:n], ALU.mult, r=[b_rT, b_T3], w=[b_stk])
                self.ts("dve", T1[:, 0:n], T2[:, 0:n], vec[:, 1, j:j + 1], omk[:, j:j + 1], ALU.mult, ALU.add,
                        r=[b_T2, b_par], w=[b_T1])
                self.tt("dve", T1[:, 0:n], T1[:, 0:n], kT[:, 0:n], ALU.mult, r=[b_T1, b_kT], w=[b_T1])
                if d == 0:
                    self.cp("pool", ksum[:, t0:t0 + n], T1[:, 0:n], r=[b_T1], w=[b_ksum])
                else:
                    self.tt("pool", ksum[:, t0:t0 + n], ksum[:, t0:t0 + n], T1[:, 0:n], ALU.add, r=[b_T1, b_ksum], w=[b_ksum])
                self.tt("dve", stk[:, 3, 0:n], T1[:, 0:n], T4[:, 0:n], ALU.mult, r=[b_T1, b_T4], w=[b_stk])
                self.tt("pool", T2[:, 0:n], T2[:, 0:n], kap[:, 0:n], ALU.mult, r=[b_T2, b_kap], w=[b_T2])
                self.tt("dve", stk[:, 2, 0:n], T2[:, 0:n], T4[:, 0:n], ALU.mult, r=[b_T2, b_T4], w=[b_stk])

            def finalize(j, ck0, nck, sp):
                n = nck * 64
                t0 = ck0 * 64
                rT, b_rT, Vb, b_Vb, gTb, b_gTb = rTs[sp], b_rTs[sp], Vbs[sp], b_Vbs[sp], gTbs[sp], b_gTbs[sp]
                Ys = Yacc[:, ck0:ck0 + nck, :]
                T3v = F3[:, 0:n].rearrange("p (c s) -> p c s", s=64)
                mean, ssq, m2, var = lnst[:, 1, 0:nck], lnst[:, 2, 0:nck], lnst[:, 3, 0:nck], lnst[:, 4, 0:nck]
                self.em.op("dve", lambda e, Ys=Ys, mean=mean: e.tensor_reduce(out=mean, in_=Ys, axis=AX.X, op=ALU.add),
                           [b_Yacc], [b_ln])
                self.act(T3v, Ys, AF.Square, [b_Yacc], [b_F3])
                self.em.op("dve", lambda e, T3v=T3v, ssq=ssq: e.tensor_reduce(out=ssq, in_=T3v, axis=AX.X, op=ALU.add),
                           [b_F3], [b_ln])
                self.ts("dve", mean, mean, 1.0 / 64, None, ALU.mult, r=[b_ln], w=[b_ln])
                self.tt("dve", m2, mean, mean, ALU.mult, r=[b_ln], w=[b_ln])
                self.stt(var, ssq, 1.0 / 64, m2, ALU.mult, ALU.subtract, r=[b_ln], w=[b_ln])
                self.act(var, var, AF.Sqrt, [b_ln], [b_ln], bias=64e-5)
                self.em.op("dve", lambda e, var=var: e.reciprocal(out=var, in_=var), [b_ln], [b_ln])
                self.tt("dve", T3v, Ys, mean.unsqueeze(2).to_broadcast([128, nck, 64]), ALU.subtract,
                        r=[b_Yacc, b_ln], w=[b_F3])
                for hh in range(2):
                    hs = slice(hh * 64, (hh + 1) * 64)
                    self.tt("dve", YBD[hs, 0:nck, hs], T3v[hs], var[hs].unsqueeze(2).to_broadcast([64, nck, 64]), ALU.mult,
                            r=[b_F3, b_ln], w=[b_YBD])
                for c8 in range(0, nck, 8):
                    m8 = min(8, nck - c8)
                    pb = 7
                    for ci in range(c8, c8 + m8):
                        self.mm(psb[pb][:, (ci - c8) * 64:(ci - c8 + 1) * 64], YBD[:, ci, :], istack[:],
                                r=[b_YBD, b_k], w=[b_ps[pb]], defer=(ci != c8 + m8 - 1))
                    self.ts("dve", ynT[:, c8 * 64:(c8 + m8) * 64], psb[pb][:, 0:m8 * 64], vec[:, 3, j:j + 1], vec[:, 4, j:j + 1],
                            ALU.mult, ALU.add, r=[b_ps[pb], b_par], w=[b_ynT])
                self.tt("pool", F1[:, 0:n], rT[:, 0:n], ksum[:, t0:t0 + n], ALU.mult, r=[b_rT, b_ksum], w=[b_F1])
                self.ts("dve", fsqb[:, 0:n], F1[:, 0:n], vec[:, 2, j:j + 1], None, ALU.mult, r=[b_F1, b_par], w=[b_fsqb])
                for (o, m) in tiles_of(n):
                    pb = 7
                    self.mm(psb[pb][:, 0:m], bdones[:], fsqb[:, o:o + m], r=[b_k, b_fsqb], w=[b_ps[pb]])
                    self.tt("dve", F2[:, o:o + m], psb[pb][:, 0:m], Vb[:, o:o + m], ALU.mult, r=[b_ps[pb], b_Vb], w=[b_F2])
                self.tt("dve", ynT[:, 0:n], ynT[:, 0:n], F2[:, 0:n], ALU.add, r=[b_ynT, b_F2], w=[b_ynT])
                self.tt("dve", yst[:, 0:n], ynT[:, 0:n], gTb[:, 0:n], ALU.mult, r=[b_ynT, b_gTb], w=[b_yst])
                self.dma("pool", yA[j * 128:(j + 1) * 128, t0:t0 + n], yst[:, 0:n], r=[b_yst])

            for j in range(8):
                for d in range(2):
                    self.memset("pool", ST[:], 0.0, w=[b_ST])
                    self.memset("pool", S32[:], 0.0, w=[b_S32])
                    order = segs if d == 0 else [segs[0]] + segs[1:][::-1]
                    bgq = []
                    prep(j, d, order[0][0], order[0][1], 0)
                    for si, (ck0, nck) in enumerate(order):
                        sp = si % 2
                        n = nck * 64
                        t0 = ck0 * 64
                        rT, b_rT, Vb, b_Vb, gTb, b_gTb, stk, b_stk, gC, b_gC = (rTs[sp], b_rTs[sp], Vbs[sp], b_Vbs[sp], gTbs[sp],
                                                                               b_gTbs[sp], stks[sp], b_stks[sp], gCs[sp], b_gCs[sp])
                        if si + 1 < len(order):
                            nxt = order[si + 1]
                            bgq += self.em.captured(lambda: prep(j, d, nxt[0], nxt[1], 1 - sp))
                        nstage = 30 * ((nck + G - 1) // G)
                        bgk = max(1, (len(bgq) + nstage - 1) // nstage)

                        def bgrun(k=None):
                            self.em.replay(bgq, bgk if k is None else k)
                        corder = list(range(nck)) if d == 0 else list(range(nck))[::-1]
                        for gi in range(0, nck, G):
                            grp = corder[gi:gi + G]
                            cl = min(grp)
                            bg = kbd[0] % 2
                            kbd[0] += 1
                            for qi in range(5):
                                for hh in range(2):
                                    hs = slice(hh * 64, (hh + 1) * 64)
                                    src = (Vb[hs, cl * 64:(cl + G) * 64] if qi == 4 else stk[hs, qi, cl * 64:(cl + G) * 64])
                                    src = src.rearrange("p (c s) -> p c s", s=64)
                                    eng = ("pool", "act", "pool")[(qi * 2 + hh) % 3]
                                    if eng == "act":
                                        self.act(BDg[bg][hs, :, qi, hs], src, AF.Copy, [b_stk, b_Vb], [b_BDg[bg]])
                                    else:
                                        self.cp(eng, BDg[bg][hs, :, qi, hs], src, r=[b_stk, b_Vb], w=[b_BDg[bg]])
                            rB = [b_BDg[bg], b_k]
                            R4 = range(len(grp))
                            gof = [ci - cl for ci in grp]
                            bd = lambda i, q: BDg[bg][:, gof[i], q, :]
                            slot = lambda i: psb[i]
                            bsl = lambda i: b_ps[i]
                            for i in R4:
                                self.mm(psb[4][:, i * 64:(i + 1) * 64], bd(i, 4), istack[:], r=rB, w=[b_ps[4]], defer=(i != len(grp) - 1))
                            self.act(VM[bg][:, 0:len(grp), :], psb[4][:, 0:64 * len(grp)].rearrange("p (c s) -> p c s", s=64), AF.Copy,
                                     [b_ps[4]], [b_VM[bg]])
                            pend = pending[:]
                            del pending[:]

                            def drain(k=1):
                                for _ in range(k):
                                    if pend:
                                        pend.pop(0)()
                            for i in R4:
                                self.mm(slot(i)[:, 0:256], bd(i, 2), BDg[bg][:, gof[i], 0:2, :], r=rB, w=[bsl(i)], defer=True)
                                self.mm(slot(i)[:, 256:384], bd(i, 2), ident_bf[:], r=rB, w=[bsl(i)], defer=True)
                                self.mm(slot(i)[:, 384:512], bd(i, 0), ident_bf[:], r=rB, w=[bsl(i)])
                            drain()
                            bgrun()
                            for i in R4:
                                self.tt("dve", SA[i][:, 0:256], slot(i)[:, 0:256], rkm[:, d, 0:256], ALU.mult, r=[bsl(i), b_k], w=[b_SA[i]])
                                self.act(SA[i][:, 256:512], slot(i)[:, 256:512], AF.Copy, [bsl(i)], [b_SA[i]])
                            bgrun()
                            for i in R4:
                                self.mm(slot(i)[:, 0:128], bd(i, 3), bd(i, 1), r=rB, w=[bsl(i)], defer=True)
                                self.mm(slot(i)[:, 128:256], bd(i, 3), ident_bf[:], r=rB, w=[bsl(i)], defer=True)
                                self.mm(slot(i)[:, 256:512], bd(i, 0), BDg[bg][:, gof[i], 2:4, :], r=rB, w=[bsl(i)])
                            drain()
                            bgrun()
                            for i in R4:
                                self.tt("dve", SB_[i][:], slot(i)[:, :], rkm[:, d, 512:1024], ALU.mult, r=[bsl(i), b_k], w=[b_SB[i]])
                                self.tt("dve", MO[i][:], slot(i)[:, 256:384], rkm[:, d, 1024:1152], ALU.mult, r=[bsl(i), b_k], w=[b_MO[i]])
                            bgrun()
                            for i in R4:
                                self.tt("pool", Pb[i][0][:], ident_bf[:], SB_[i][:, 256:384], ALU.subtract, r=[b_k, b_SB[i]], w=[b_Pb[i][0]])
                            cur = [(SB_[i][:, 256:384], SA[i][:, 0:128], b_SB[i], b_SA[i]) for i in R4]
                            for lev in range(4):
                                lastl = lev == 3
                                mi = lev % 2
                                for i in R4:
                                    Mc, Nc, bM, bN = cur[i]
                                    self.mm(slot(i)[:, 128:256], Mc, Nc, r=[bM, bN], w=[bsl(i)], defer=(not lastl))
                                    if not lastl:
                                        self.mm(slot(i)[:, 0:128], Nc, Mc, r=[bM, bN], w=[bsl(i)])
                                drain()
                                bgrun()
                                lo = 128 if lastl else 0
                                for i in R4:
                                    self.act(MN[i][mi][:, lo:256], slot(i)[:, lo:256], AF.Copy, [bsl(i)], [b_MN[i][mi]])
                                    cur[i] = (MN[i][mi][:, 0:128], MN[i][mi][:, 128:256], b_MN[i][mi], b_MN[i][mi])
                                bgrun()
                                for i in R4:
                                    self.mm(slot(i)[:, 256:384], cur[i][1], Pb[i][lev % 2][:], r=[cur[i][3], b_Pb[i][lev % 2]], w=[bsl(i)])
                                bgrun()
                                for i in R4:
                                    self.tt("dve", Pb[i][(lev + 1) % 2][:], Pb[i][lev % 2][:], slot(i)[:, 256:384], ALU.add,
                                            r=[b_Pb[i][lev % 2], bsl(i)], w=[b_Pb[i][(lev + 1) % 2]])
                            drain(4)
                            for i in R4:
                                self.mm(slot(i)[:, 0:256], Pb[i][0][:], SA[i][:, 128:384], r=[b_Pb[i][0], b_SA[i]], w=[bsl(i)])
                            bgrun()
                            for i in R4:
                                self.act(Rb[i][0][:], slot(i)[:, 0:256], AF.Copy, [bsl(i)], [b_Rb[i][0]])
                            for i in R4:
                                self.mm(slot(i)[:, 256:512], MO[i][:], Rb[i][0][:], r=[b_MO[i], b_Rb[i][0]], w=[bsl(i)])
                            bgrun()
                            for i in R4:
                                self.tt("dve", Rb[i][1][:], SA[i][:, 128:384], slot(i)[:, 256:512], ALU.subtract, r=[b_SA[i], bsl(i)], w=[b_Rb[i][1]])
                            bgrun()
                            for i in R4:
                                self.mm(slot(i)[:, 0:256], Pb[i][0][:], Rb[i][1][:], r=[b_Pb[i][0], b_Rb[i][1]], w=[bsl(i)])
                            for i in R4:
                                self.act(Rb[i][0][:], slot(i)[:, 0:256], AF.Copy, [bsl(i)], [b_Rb[i][0]])
                            bgrun()
                            for i in R4:
                                self.mm(slot(i)[:, 256:512], SA[i][:, 384:512], Rb[i][0][:], r=[b_SA[i], b_Rb[i][0]], w=[bsl(i)], defer=True)
                                self.mm(slot(i)[:, 0:256], SB_[i][:, 384:512], Rb[i][0][:], r=[b_SB[i], b_Rb[i][0]], w=[bsl(i)])
                            bgrun()
                            for i in R4:
                                self.tt("dve", QP[i][:, 0:128], bd(i, 1), slot(i)[:, 256:384], ALU.subtract, r=[b_BDg[bg], bsl(i)], w=[b_QP[i]])
                                self.ts("dve", QP[i][:, 128:256], slot(i)[:, 384:512], -1.0, None, ALU.mult, r=[bsl(i)], w=[b_QP[i]])
                                self.tt("dve", AK[i][:], SB_[i][:, 0:256], slot(i)[:, 0:256], ALU.subtract, r=[b_SB[i], bsl(i)], w=[b_AK[i]])
                            for i in R4:
                                def step(i=i, ci=grp[i], bg=bg, d=d, ck0=ck0):
                                    SQ = psb[6][:, i * 128:(i + 1) * 128]
                                    vm = VM[bg][:, i, :]
                                    self.ts("pool", S32g[:], S32[:], gC[:, ci:ci + 1], None, ALU.mult, r=[b_S32, b_gC], w=[b_S32g])
                                    self.mm(SQ[:, 0:64], QP[i][:, 0:128], ST[:], start=True, stop=False, r=[b_QP[i], b_ST], w=[b_ps[6]])
                                    self.mm(SQ[:, 0:64], AK[i][:, 0:128], vm, start=False, stop=True, r=[b_AK[i], b_VM[bg]], w=[b_ps[6]])
                                    self.mm(SQ[:, 64:128], QP[i][:, 128:256], ST[:], start=True, stop=False, r=[b_QP[i], b_ST], w=[b_ps[6]])
                                    self.mm(SQ[:, 64:128], AK[i][:, 128:256], vm, start=False, stop=True, r=[b_AK[i], b_VM[bg]], w=[b_ps[6]])
                                    self.stt(S32[:], SQ[:, 64:128], gC[:, ci:ci + 1], S32g[:], ALU.mult, ALU.add,
                                             r=[b_ps[6], b_gC, b_S32g], w=[b_S32])
                                    self.act(ST[:], S32[:], AF.Copy, [b_S32], [b_ST])
                                    ck = ck0 + ci
                                    if d == 0:
                                        self.cp("dve", Yacc[:, ck, :], SQ[:, 0:64], r=[b_ps[6]], w=[b_Yacc])
                                    else:
                                        self.tt("dve", Yacc[:, ck, :], Yacc[:, ck, :], SQ[:, 0:64], ALU.add, r=[b_ps[6], b_Yacc], w=[b_Yacc])
                                pending.append(step)
                            while pend:
                                pend.pop(0)()
                        while pending:
                            pending.pop(0)()
                        bgrun(10 ** 9)
                        if d == 1:
                            bgq += self.em.captured(lambda ck0=ck0, nck=nck, sp=sp: finalize(j, ck0, nck, sp))
                    bgrun(10 ** 9) if False else self.em.replay(bgq, 10 ** 9)
            self.em.flush()

    def rstd_tile(self, src, b_src, n, sq, b_sq, rs, b_rs, pb):
        psb, b_ps = self.psb, self.b_ps
        self.act(sq[:, :, 0:n], src[:, :, 0:n], AF.Square, [b_src], [b_sq])
        for kc in range(KC):
            self.mm(psb[pb][:, 0:n], self.ones_bf[:], sq[:, kc, 0:n], start=(kc == 0), stop=(kc == KC - 1),
                    r=[b_sq, self.b_ones], w=[b_ps[pb]])
        self.act(rs[:, 0:n], psb[pb][:, 0:n], AF.Sqrt, [b_ps[pb]], [b_rs], scale=1.0 / D, bias=1e-6)
        self.em.op("dve", lambda e: e.reciprocal(out=rs[:, 0:n], in_=rs[:, 0:n]), [b_rs], [b_rs])

    def phase_merge_ffn(self, l, xsrc):
        cfg = self.cfg
        NT, T = cfg.NT, cfg.T
        psb, b_ps = self.psb, self.b_ps
        dr = self.dram
        mod, b_mod = self.mod, self.b_mod
        uT, xmid, aT = dr["uT"], dr["xmid"], dr["aT"]
        last = l == cfg.depth - 1
        TW = 256
        tiles = [(t0, TW, 1 if t0 < LCTX else 0) for t0 in range(0, NT, TW)]
        with contextlib.ExitStack() as st0:
            h2T = self.sb(st0, "m_h2T", [128, KC, NT], BF16)
            b_h2 = mkbufs("h2T", len(tiles))
            ng = self.sb(st0, "m_ng", [128, 4, KC], F32)
            cols = self.sb(st0, "m_cols", [128, 4, KC, 2], F32)
            wst = self.sb(st0, "m_wst", [128, 1024], F32)
            b_ng, b_cols, b_wst = Buf("ng"), Buf("cols"), Buf("wst")
            self.dma("sp", ng[:], dr["norm_g"][l], w=[b_ng])
            for kc in range(KC):
                self.ts("dve", cols[:, 0, kc, :], mod[:, 16 + kc, :], ng[:, 1, kc:kc + 1], None, ALU.mult, r=[b_mod, b_ng], w=[b_cols])
                self.ts("dve", cols[:, 1, kc, :], mod[:, 32 + kc, :], 1.0, ng[:, 2, kc:kc + 1], ALU.add, ALU.mult, r=[b_mod, b_ng], w=[b_cols])
                self.ts("dve", cols[:, 2, kc, :], mod[:, 40 + kc, :], ng[:, 3, kc:kc + 1], None, ALU.mult, r=[b_mod, b_ng], w=[b_cols])
            with contextlib.ExitStack() as st:
                sb = lambda name, shape, dt=F32: self.sb(st, "m_" + name, shape, dt)
                Wbr = sb("Wbr", [128, 3, KC, 1024], BF16)
                Wo = sb("Wo", [128, KC, 1024], BF16)
                b_W = Buf("Wm")
                for br in range(3):
                    for kc in range(KC):
                        self.dma("sp", wst[:], dr["w_branch"][l, br, kc * 128:(kc + 1) * 128, :], w=[b_wst])
                        self.cp("pool", Wbr[:, br, kc, :], wst[:], r=[b_wst], w=[b_W])
                for kc in range(KC):
                    self.dma("sp", wst[:], dr["w_out"][l, kc * 128:(kc + 1) * 128, :], w=[b_wst])
                    self.cp("pool", Wo[:, kc, :], wst[:], r=[b_wst], w=[b_W])
                yt = [sb(f"yt{i}", [128, KC, TW], BF16) for i in range(3)]
                gt = sb("gt", [128, KC, TW])
                mt = sb("mt", [128, KC, TW])
                tmp = sb("tmp", [128, TW])
                mb = sb("mb", [128, KC, TW], BF16)
                xt = sb("xt", [128, KC, TW])
                mo = sb("mo", [128, KC, TW])
                sq = sb("sq", [128, KC, TW], BF16)
                rs = sb("rs", [128, TW])
                b_yt = mkbufs("yt", 3)
                b_gt, b_mt, b_tmp, b_mb, b_xt, b_mo, b_sq, b_rs = [Buf(n) for n in "gt mt tmp mb xt mo sq rs".split()]
                ysrc = [dr["yA"], dr["yB"], dr["yC"]]
                xview = xsrc.rearrange("(kc p) n -> p kc n", p=128)
                xmview = xmid.rearrange("(kc p) n -> p kc n", p=128)
                k = 0
                for ti, (t0, n, seg) in enumerate(tiles):
                    self.dma("sp", xt[:], xview[:, :, t0:t0 + n], w=[b_xt])
                    for br in range(3):
                        self.dma("sp", yt[br][:], ysrc[br].rearrange("(kc p) n -> p kc n", p=128)[:, :, t0:t0 + n], w=[b_yt[br]])
                        g0 = (66 + br * 8) * 128
                        self.dma("sp", gt[:], uT[g0:g0 + 1024, :].rearrange("(kc p) n -> p kc n", p=128)[:, :, t0:t0 + n], w=[b_gt])
                        self.act(gt[:], gt[:], AF.Sigmoid, [b_gt], [b_gt])
                        for oc in range(KC):
                            pb = k % 6
                            k += 1
                            for kc in range(KC):
                                self.mm(psb[pb][:, 0:n], Wbr[:, br, kc, oc * 128:(oc + 1) * 128], yt[br][:, kc, :],
                                        start=(kc == 0), stop=(kc == KC - 1), r=[b_W, b_yt[br]], w=[b_ps[pb]])
                            if br == 0:
                                self.tt("dve", mt[:, oc, :], psb[pb][:, 0:n], gt[:, oc, :], ALU.mult, r=[b_ps[pb], b_gt], w=[b_mt])
                            else:
                                self.tt("dve", tmp[:], psb[pb][:, 0:n], gt[:, oc, :], ALU.mult, r=[b_ps[pb], b_gt], w=[b_tmp])
                                self.tt("pool", mt[:, oc, :], mt[:, oc, :], tmp[:], ALU.add, r=[b_mt, b_tmp], w=[b_mt])
                    self.act(mb[:], mt[:], AF.Copy, [b_mt], [b_mb])
                    for oc in range(KC):
                        pb = k % 6
                        k += 1
                        for kc in range(KC):
                            self.mm(psb[pb][:, 0:n], Wo[:, kc, oc * 128:(oc + 1) * 128], mb[:, kc, :],
                                    start=(kc == 0), stop=(kc == KC - 1), r=[b_W, b_mb], w=[b_ps[pb]])
                        self.act(mo[:, oc, :], psb[pb][:, 0:n], AF.Copy, [b_ps[pb]], [b_mo])
                    self.rstd_tile(mo, b_mo, n, sq, b_sq, rs, b_rs, 6)
                    for oc in range(KC):
                        self.tt("dve", mo[:, oc, :], mo[:, oc, :], rs[:], ALU.mult, r=[b_mo, b_rs], w=[b_mo])
                        self.stt(xt[:, oc, :], mo[:, oc, :], cols[:, 0, oc, seg:seg + 1], xt[:, oc, :], ALU.mult, ALU.add,
                                 r=[b_mo, b_cols, b_xt], w=[b_xt])
                    self.dma("pool", xmview[:, :, t0:t0 + n], xt[:], r=[b_xt])
                    self.rstd_tile(xt, b_xt, n, sq, b_sq, rs, b_rs, 7)
                    for kc in range(KC):
                        self.tt("dve", mo[:, kc, :], xt[:, kc, :], rs[:], ALU.mult, r=[b_xt, b_rs, b_mo], w=[b_mo])
                        self.act(h2T[:, kc, t0:t0 + n], mo[:, kc, :], AF.Identity, [b_mo, b_cols, b_mod], [b_h2[ti]],
                                 scale=cols[:, 1, kc, seg:seg + 1], bias=mod[:, 24 + kc, seg:seg + 1])
                self.em.flush()
            with contextlib.ExitStack() as st:
                sb = lambda name, shape, dt=F32: self.sb(st, "f_" + name, shape, dt)
                cw = sb("cw", [128, 22, 3])
                cb = sb("cb", [128, 22])
                b_par = Buf("fpar")
                self.dma("sp", cw[:], dr["ffn_conv_w"][l], w=[b_par])
                self.dma("sp", cb[:], dr["ffn_conv_b"][l], w=[b_par])
                wf = [sb(f"wf{i}", [128, KC, 128]) for i in range(2)]
                wb = [sb(f"wb{i}", [128, KC, 128], BF16) for i in range(2)]
                b_wf, b_wb = mkbufs("fwf", 2), mkbufs("fwb", 2)
                gp = sb("gp", [128, NT + 6])
                gc = sb("gc", [128, NT])
                ast = sb("ast", [128, NT], BF16)
                b_gp, b_gc, b_ast = Buf("gp"), Buf("gc"), Buf("ast")
                self.memset("pool", gp[:], 0.0, w=[b_gp])
                wview = dr["ffn_w_in"][l].rearrange("(kc p) n -> p kc n", p=128)
                k = 0
                kw = 0
                alltiles = list(enumerate(tiles))
                h2tiles = self.cfg.tiles
                for jc in range(22):
                    for part in range(2):
                        s = kw % 2
                        kw += 1
                        c0 = part * DFF + jc * 128
                        self.dma("sp", wf[s][:], wview[:, :, c0:c0 + 128], w=[b_wf[s]])
                        self.cp("pool", wb[s][:], wf[s][:], r=[b_wf[s]], w=[b_wb[s]])
                        for (t0, n, seg) in h2tiles:
                            pb = k % 8
                            k += 1
                            hb = [b_h2[i] for i, (a, m, sg_) in alltiles if a < t0 + n and a + m > t0]
                            for kc in range(KC):
                                self.mm(psb[pb][:, 0:n], wb[s][:, kc, :], h2T[:, kc, t0:t0 + n], start=(kc == 0), stop=(kc == KC - 1),
                                        r=[b_wb[s]] + hb, w=[b_ps[pb]])
                            if part == 0:
                                p0 = t0 + 2 if t0 < LCTX else t0 + 4
                                self.act(gp[:, p0:p0 + n], psb[pb][:, 0:n], AF.Copy, [b_ps[pb]], [b_gp])
                            else:
                                self.tt("dve", ast[:, t0:t0 + n], psb[pb][:, 0:n], gc[:, t0:t0 + n], ALU.mult,
                                        r=[b_ps[pb], b_gc], w=[b_ast])
                        if part == 0:
                            for (d0, s0, n) in self.segs():
                                self.ts("dve", gc[:, d0:d0 + n], gp[:, s0 - 1:s0 - 1 + n], cw[:, jc, 0:1], cb[:, jc:jc + 1], ALU.mult, ALU.add,
                                        r=[b_gp, b_par], w=[b_gc])
                                for tap in (1, 2):
                                    self.stt(gc[:, d0:d0 + n], gp[:, s0 - 1 + tap:s0 - 1 + tap + n], cw[:, jc, tap:tap + 1], gc[:, d0:d0 + n],
                                             ALU.mult, ALU.add, r=[b_gp, b_par, b_gc], w=[b_gc])
                            self.act(gc[:], gc[:], AF.Silu, [b_gc], [b_gc])
                    self.dma("pool", aT[jc * 128:(jc + 1) * 128, :], ast[:], r=[b_ast])
                self.em.flush()
        with contextlib.ExitStack() as st:
            sb = lambda name, shape, dt=F32: self.sb(st, "g_" + name, shape, dt)
            ng = sb("ng", [128, 4, KC])
            cols = sb("cols", [128, KC, 2])
            wst = sb("wst", [128, 1024])
            Wf = sb("Wf", [128, 22, 1024], BF16)
            b_ng, b_cols, b_wst, b_W = Buf("ng"), Buf("cols"), Buf("wst"), Buf("Wf")
            self.dma("sp", ng[:], dr["norm_g"][l], w=[b_ng])
            for kc in range(KC):
                self.ts("dve", cols[:, kc, :], mod[:, 40 + kc, :], ng[:, 3, kc:kc + 1], None, ALU.mult, r=[b_mod, b_ng], w=[b_cols])
            for kc in range(22):
                self.dma("sp", wst[:], dr["ffn_w_out"][l, kc * 128:(kc + 1) * 128, :], w=[b_wst])
                self.cp("pool", Wf[:, kc, :], wst[:], r=[b_wst], w=[b_W])
            at = [sb(f"at{i}", [128, 22, 512], BF16) for i in range(2)]
            xt = [sb(f"xt{i}", [128, KC, 512]) for i in range(2)]
            fo = sb("fo", [128, KC, 512])
            sq = sb("sq", [128, KC, 512], BF16)
            rs = sb("rs", [128, 512])
            b_at, b_xt = mkbufs("at", 2), mkbufs("gxt", 2)
            b_fo, b_sq, b_rs = Buf("fo"), Buf("gsq"), Buf("grs")
            xmview = xmid.rearrange("(kc p) n -> p kc n", p=128)
            aview = aT.rearrange("(kc p) n -> p kc n", p=128)
            k = 0
            for ti, (t0, n, seg) in enumerate(cfg.tiles):
                if last and seg == 1:
                    continue
                s = ti % 2
                self.dma("sp", at[s][:, :, 0:n], aview[:, :, t0:t0 + n], w=[b_at[s]])
                self.dma("sp", xt[s][:, :, 0:n], xmview[:, :, t0:t0 + n], w=[b_xt[s]])
                for oc in range(KC):
                    pb = k % 6
                    k += 1
                    for kc in range(22):
                        self.mm(psb[pb][:, 0:n], Wf[:, kc, oc * 128:(oc + 1) * 128], at[s][:, kc, 0:n], start=(kc == 0), stop=(kc == 21),
                                r=[b_W, b_at[s]], w=[b_ps[pb]])
                    self.act(fo[:, oc, 0:n], psb[pb][:, 0:n], AF.Copy, [b_ps[pb]], [b_fo])
                self.rstd_tile(fo, b_fo, n, sq, b_sq, rs, b_rs, 6 + ti % 2)
                for oc in range(KC):
                    self.tt("dve", fo[:, oc, 0:n], fo[:, oc, 0:n], rs[:, 0:n], ALU.mult, r=[b_fo, b_rs], w=[b_fo])
                    self.stt(xt[s][:, oc, 0:n], fo[:, oc, 0:n], cols[:, oc, seg:seg + 1], xt[s][:, oc, 0:n], ALU.mult, ALU.add,
                             r=[b_fo, b_cols, b_xt[s]], w=[b_xt[s]])
                if last:
                    dst = dr["outT"].rearrange("(kc p) n -> p kc n", p=128)[:, :, t0 - LCTX:t0 - LCTX + n]
                else:
                    dst = dr["xcur"].rearrange("(kc p) n -> p kc n", p=128)[:, :, t0:t0 + n]
                self.dma("pool", dst, xt[s][:, :, 0:n], r=[b_xt[s]])
            self.em.flush()

    def phase_mod(self, l, cvec, ada_w, ada_b, mod, b_mod):
        em = self.em
        psb, b_ps = self.psb, self.b_ps
        with contextlib.ExitStack() as st:
            cv = self.sb(st, "cv", [128, KC, 2], F32)
            cs = self.sb(st, "cs", [128, KC, 2], BF16)
            sg = self.sb(st, "cv_sg", [128, KC, 2], F32)
            ab = self.sb(st, "ab", [128, 48], F32)
            wf = [self.sb(st, f"adaw_f{i}", [128, KC, 512], F32) for i in range(2)]
            wb = [self.sb(st, f"adaw_b{i}", [128, KC, 512], BF16) for i in range(2)]
            b_cv, b_cs, b_ab, b_sg = Buf("cv"), Buf("cs"), Buf("ab"), Buf("sg")
            b_wf, b_wb = mkbufs("wf", 2), mkbufs("wb", 2)
            em.dma("sp", lambda e: e.dma_start(out=cv[:], in_=cvec[:, :, :]), writes=[b_cv])
            em.dma("sp", lambda e: e.dma_start(out=ab[:], in_=ada_b[l]), writes=[b_ab])
            em.op("act", lambda e: e.activation(out=sg[:], in_=cv[:], func=AF.Sigmoid), reads=[b_cv], writes=[b_sg])
            em.op("dve", lambda e: e.tensor_tensor(out=cs[:], in0=cv[:], in1=sg[:], op=ALU.mult),
                  reads=[b_cv, b_sg], writes=[b_cs])
            wview = ada_w[l].rearrange("(kc p) n -> p kc n", p=128)
            for g in range(12):
                s = g % 2
                em.dma("sp", lambda e, s=s, g=g: e.dma_start(out=wf[s][:], in_=wview[:, :, g * 512:(g + 1) * 512]),
                       writes=[b_wf[s]])
                em.op("pool", lambda e, s=s: e.tensor_copy(out=wb[s][:], in_=wf[s][:]),
                      reads=[b_wf[s]], writes=[b_wb[s]])
                pb = g % 8
                for q in range(4):
                    i = g * 4 + q
                    for kc in range(KC):
                        em.op("pe", lambda e, s=s, q=q, kc=kc, pb=pb: e.matmul(
                            psb[pb][:, q * 2:q * 2 + 2], lhsT=wb[s][:, kc, q * 128:(q + 1) * 128],
                            rhs=cs[:, kc, :], start=(kc == 0), stop=(kc == KC - 1)),
                            reads=[b_wb[s], b_cs], writes=[b_ps[pb]], defer=(kc != KC - 1))
                for q in range(4):
                    i = g * 4 + q
                    em.op("dve", lambda e, q=q, i=i, pb=pb: e.tensor_scalar(
                        out=mod[:, i, :], in0=psb[pb][:, q * 2:q * 2 + 2], scalar1=ab[:, i:i + 1], scalar2=None,
                        op0=ALU.add), reads=[b_ps[pb], b_ab], writes=[b_mod])
            em.flush()

    def phase_norm_inproj(self, l, xc, norm_g, w_in, uT, mod, b_mod, ones_bf, b_ones):
        cfg = self.cfg
        em = self.em
        NT = cfg.NT
        psb, b_ps = self.psb, self.b_ps
        with contextlib.ExitStack() as st:
            hT = self.sb(st, "hT", [128, KC, NT], BF16)
            b_h = mkbufs("hT", len(cfg.tiles))
            ng = self.sb(st, "ng", [128, 4, KC], F32)
            A1 = self.sb(st, "A1", [128, KC, 2], F32)
            b_ng, b_A1 = Buf("ng"), Buf("A1")
            em.dma("sp", lambda e: e.dma_start(out=ng[:], in_=norm_g[l]), writes=[b_ng])
            for kc in range(KC):
                em.op("dve", lambda e, kc=kc: e.tensor_scalar(
                    out=A1[:, kc, :], in0=mod[:, 8 + kc, :], scalar1=1.0, scalar2=ng[:, 0, kc:kc + 1],
                    op0=ALU.add, op1=ALU.mult), reads=[b_mod, b_ng], writes=[b_A1])
            with contextlib.ExitStack() as st2:
                xt = [self.sb(st2, f"xt{i}", [128, KC, 512], F32) for i in range(2)]
                sq = [self.sb(st2, f"sq{i}", [128, KC, 512], BF16) for i in range(2)]
                rs = [self.sb(st2, f"rs{i}", [128, 512], F32) for i in range(2)]
                tmp = [self.sb(st2, f"tmp{i}", [128, KC, 512], F32) for i in range(2)]
                b_xt, b_sq, b_rs, b_tmp = mkbufs("xt", 2), mkbufs("sq", 2), mkbufs("rs", 2), mkbufs("tmp", 2)
                xview = xc.rearrange("(kc p) n -> p kc n", p=128)
                for ti, (t0, n, seg) in enumerate(cfg.tiles):
                    s = ti % 2
                    pb = ti % 8
                    em.dma("sp", lambda e, s=s, t0=t0, n=n: e.dma_start(out=xt[s][:, :, 0:n], in_=xview[:, :, t0:t0 + n]),
                           writes=[b_xt[s]])
                    em.op("act", lambda e, s=s, n=n: e.activation(out=sq[s][:, :, 0:n], in_=xt[s][:, :, 0:n], func=AF.Square),
                          reads=[b_xt[s]], writes=[b_sq[s]])
                    for kc in range(KC):
                        em.op("pe", lambda e, s=s, n=n, kc=kc, pb=pb: e.matmul(
                            psb[pb][:, 0:n], lhsT=ones_bf[:], rhs=sq[s][:, kc, 0:n], start=(kc == 0), stop=(kc == KC - 1)),
                            reads=[b_sq[s], b_ones], writes=[b_ps[pb]], defer=(kc != KC - 1))
                    em.op("act", lambda e, s=s, n=n, pb=pb: e.activation(
                        out=rs[s][:, 0:n], in_=psb[pb][:, 0:n], func=AF.Sqrt, scale=1.0 / D, bias=1e-6),
                        reads=[b_ps[pb]], writes=[b_rs[s]])
                    em.op("dve", lambda e, s=s, n=n: e.reciprocal(out=rs[s][:, 0:n], in_=rs[s][:, 0:n]),
                          reads=[b_rs[s]], writes=[b_rs[s]])
                    for kc in range(KC):
                        em.op("dve", lambda e, s=s, n=n, kc=kc: e.tensor_tensor(
                            out=tmp[s][:, kc, 0:n], in0=xt[s][:, kc, 0:n], in1=rs[s][:, 0:n], op=ALU.mult),
                            reads=[b_xt[s], b_rs[s]], writes=[b_tmp[s]])
                        em.op("act", lambda e, s=s, n=n, kc=kc, t0=t0, seg=seg: e.activation(
                            out=hT[:, kc, t0:t0 + n], in_=tmp[s][:, kc, 0:n], func=AF.Identity,
                            scale=A1[:, kc, seg:seg + 1], bias=mod[:, kc, seg:seg + 1]),
                            reads=[b_tmp[s], b_A1, b_mod], writes=[b_h[ti]])
                em.flush()
            with contextlib.ExitStack() as st2:
                wf = [self.sb(st2, f"wf{i}", [128, KC, 128], F32) for i in range(2)]
                wb = [self.sb(st2, f"wb{i}", [128, KC, 128], BF16) for i in range(2)]
                stg = [self.sb(st2, f"stg{i}", [128, NT], F32) for i in range(2)]
                b_wf, b_wb, b_stg = mkbufs("wf", 2), mkbufs("wb", 2), mkbufs("stg", 2)
                wview = w_in[l].rearrange("(kc p) n -> p kc n", p=128)
                k = 0
                for j in range(self.NCH):
                    s = j % 2
                    em.dma("sp", lambda e, s=s, j=j: e.dma_start(out=wf[s][:], in_=wview[:, :, j * 128:(j + 1) * 128]),
                           writes=[b_wf[s]])
                    em.op("pool", lambda e, s=s: e.tensor_copy(out=wb[s][:], in_=wf[s][:]),
                          reads=[b_wf[s]], writes=[b_wb[s]])
                    for ti, (t0, n, seg) in enumerate(cfg.tiles):
                        pb = k % 8
                        k += 1
                        for kc in range(KC):
                            em.op("pe", lambda e, s=s, n=n, kc=kc, pb=pb, t0=t0: e.matmul(
                                psb[pb][:, 0:n], lhsT=wb[s][:, kc, :], rhs=hT[:, kc, t0:t0 + n],
                                start=(kc == 0), stop=(kc == KC - 1)),
                                reads=[b_wb[s], b_h[ti]], writes=[b_ps[pb]], defer=(kc != KC - 1))
                        if k % 2 == 0:
                            em.op("act", lambda e, s=s, n=n, pb=pb, t0=t0: e.activation(
                                out=stg[s][:, t0:t0 + n], in_=psb[pb][:, 0:n], func=AF.Copy),
                                reads=[b_ps[pb]], writes=[b_stg[s]])
                        else:
                            em.op("dve", lambda e, s=s, n=n, pb=pb, t0=t0: e.tensor_copy(
                                out=stg[s][:, t0:t0 + n], in_=psb[pb][:, 0:n]),
                                reads=[b_ps[pb]], writes=[b_stg[s]])
                    em.dma("pool", lambda e, s=s, j=j: e.dma_start(out=uT[j * 128:(j + 1) * 128, :], in_=stg[s][:]),
                           reads=[b_stg[s]], writes=[])
                em.flush()


def na_blocks(rows):
    nqb = rows // 8
    types = {}
    blocks = []
    for qb in range(nqb):
        q0 = qb * 8
        rs = lambda r: min(max(r - 4, 0), rows - 8)
        lo, hi = rs(q0), rs(q0 + 7) + 8
        cls = "f" if qb == 0 else ("l" if qb == nqb - 1 else "i")
        items = []
        for kr0 in range(lo, hi, 2):
            delta = kr0 - q0
            key = (cls, delta)
            if key not in types:
                types[key] = (len(types), q0, kr0)
            items.append((kr0, types[key][0], (delta + 4) // 2))
        blocks.append(items)
    return blocks, len(types)


def blocks_type(blocks, qb, kr0):
    for (k, ty, di) in blocks[qb]:
        if k == kr0:
            return ty
    raise KeyError


def na_mask_np(rows):
    nqb = rows // 8
    out = {}
    for qb in range(nqb):
        q0 = qb * 8
        rsf = lambda r: min(max(r - 4, 0), rows - 8)
        lo, hi = rsf(q0), rsf(q0 + 7) + 8
        cls = "f" if qb == 0 else ("l" if qb == nqb - 1 else "i")
        for kr0 in range(lo, hi, 2):
            key = (cls, kr0 - q0)
            if key in out:
                continue
            krow = kr0 + np.arange(2)[:, None, None, None]
            kc = np.arange(64)[None, :, None, None]
            qrow = q0 + np.arange(8)[None, None, :, None]
            qc = np.arange(64)[None, None, None, :]
            rs = np.clip(qrow - 4, 0, rows - 8)
            cs = np.clip(qc - 8, 0, 48)
            ok = (krow >= rs) & (krow < rs + 8) & (kc >= cs) & (kc < cs + 16)
            out[key] = np.where(ok, 0.0, -30000.0).reshape(128, 512).astype(np.float32)
    return np.stack(list(out.values()), axis=0)


def rope_tables(T):
    half = 32
    freqs = 10000.0 ** (-np.arange(0, half, 2, dtype=np.float32) / half)
    pos = np.arange(T)
    prow, pcol = pos // 64, pos % 64
    cos = np.zeros((64, T), np.float32)
    sin = np.zeros((64, T), np.float32)
    for d in range(64):
        p = prow if d < 32 else pcol
        dd = d % 32
        ang = p.astype(np.float32) * freqs[dd % 16]
        cos[d] = np.cos(ang)
        sin[d] = -np.sin(ang) if dd < 16 else np.sin(ang)
    return np.concatenate([cos, cos], 0), np.concatenate([sin, sin], 0)


def input_shapes(cfg):
    Ld, NT, T = cfg.depth, cfg.NT, cfg.T
    return {
        "xc": [D, NT], "cvec": [128, KC, 2],
        "ada_w": [Ld, D, 6 * D], "ada_b": [Ld, 128, 48], "norm_g": [Ld, 128, 4, KC],
        "w_in": [Ld, D, NIN_X],
        "lru_conv_w": [Ld, 128, 8, 4], "lru_conv_b": [Ld, 128, 8],
        "lru_gate_a_w": [Ld, 2, 16, 64, 64], "lru_gate_x_w": [Ld, 2, 16, 64, 64],
        "lru_gate_b": [Ld, 128, 2, 2, 8], "lru_lambda": [Ld, 128, 2, 8],
        "ident": [128, 128], "rope_cos": [128, T], "rope_sin": [128, T],
        "na_mask": [na_blocks(T // 64)[1], 128, 512], "rpb_pad": [Ld, 16, 24, 128],
        "bdones": [128, 128], "istack": [128, 64], "rk_mask": [2, 128, 1152],
        "rwkv_mu": [Ld, 128, 26, 2], "rwkv_w0a0": [Ld, 128, 2, 2, 8], "rwkv_vec": [Ld, 128, 5, 8],
        "w_branch": [Ld, 3, D, D], "w_out": [Ld, D, D], "ffn_w_in": [Ld, D, 2 * DFF], "ffn_w_out": [Ld, DFF, D],
        "ffn_conv_w": [Ld, 128, 22, 3], "ffn_conv_b": [Ld, 128, 22],
        "rwkv_w_up": [Ld, 2, 64, 1024], "rwkv_a_up": [Ld, 2, 64, 1024], "rwkv_g_up": [Ld, 128, 1024],
    }


def colfmt(v, n):
    v = np.asarray(v)
    return np.moveaxis(v.reshape(v.shape[:-1] + (n, 128)), -1, -2)


def rope_perm():
    idx = np.arange(1024)
    d = idx % 64
    dd = d % 32
    partner = np.where(dd < 16, idx + 16, idx - 16)
    return partner


def prep_shared(inp, cfg):
    f = lambda a: np.ascontiguousarray(a, dtype=np.float32)
    Ld = cfg.depth
    m = {}
    m["ada_w"] = f(inp["ada_w"][:Ld])
    m["ada_b"] = f(colfmt(inp["ada_b"][:Ld], 48))
    m["norm_g"] = f(colfmt(inp["norm_g"][:Ld], KC).transpose(0, 2, 1, 3))
    w_in = inp["w_in"][:Ld]
    C0 = A_COLS + 2048
    perm = rope_perm()
    wq = w_in[:, :, C0:C0 + 1024][:, :, perm]
    wk = w_in[:, :, C0 + 1024:C0 + 2048][:, :, perm]
    m["w_in"] = f(np.concatenate([w_in, wq, wk], axis=2))
    m["lru_conv_w"] = f(colfmt(inp["lru_conv_w"][:Ld], 8).transpose(0, 2, 3, 1))
    m["lru_conv_b"] = f(colfmt(inp["lru_conv_b"][:Ld], 8))
    m["lru_gate_a_w"] = f(inp["lru_gate_a_w"][:Ld])
    m["lru_gate_x_w"] = f(inp["lru_gate_x_w"][:Ld])
    gb = np.stack([inp["lru_gate_a_b"][:Ld], inp["lru_gate_x_b"][:Ld]], axis=1)
    m["lru_gate_b"] = f(colfmt(gb, 8).transpose(0, 3, 1, 2, 4))
    m["lru_lambda"] = f(colfmt(inp["lru_lambda"][:Ld], 8).transpose(0, 2, 1, 3))
    m["ident"] = np.eye(128, dtype=np.float32)
    cos, sin = rope_tables(cfg.T)
    m["rope_cos"], m["rope_sin"] = f(cos), f(sin)
    m["na_mask"] = f(na_mask_np(cfg.T // 64))
    rp = np.zeros((Ld, 16, 24, 128), np.float32)
    rp[:, :, 4:19, 48:79] = inp["na_rpb"][:Ld]
    m["rpb_pad"] = rp
    blk = np.kron(np.eye(2, dtype=np.float32), np.ones((64, 64), np.float32))
    m["bdones"] = blk
    m["istack"] = np.concatenate([np.eye(64, dtype=np.float32)] * 2, axis=0)
    i64 = np.arange(64)
    U = np.kron(np.eye(2), (i64[:, None] < i64[None, :])).astype(np.float32)
    UI = np.kron(np.eye(2), (i64[:, None] <= i64[None, :])).astype(np.float32)
    Lw, LI = U.T.copy(), UI.T.copy()
    ONE = np.ones((128, 128), np.float32)
    m32 = np.kron(np.eye(4), np.ones((32, 32))).astype(np.float32)
    fwd = np.concatenate([U * m32, UI, ONE, ONE, UI, ONE, Lw * m32, Lw, Lw * (1 - m32)], axis=1)
    bwd = np.concatenate([Lw * m32, LI, ONE, ONE, LI, ONE, U * m32, U, U * (1 - m32)], axis=1)
    m["rk_mask"] = np.stack([fwd, bwd], axis=0)
    m["rwkv_mu"] = f(colfmt(inp["rwkv_mu"][:Ld], 26).transpose(0, 2, 3, 1))
    w0a0 = np.stack([inp["rwkv_w0"][:Ld], inp["rwkv_a0"][:Ld]], axis=1)
    m["rwkv_w0a0"] = f(colfmt(w0a0, 8).transpose(0, 3, 1, 2, 4))
    vec = np.stack([inp["rwkv_k_k"][:Ld], inp["rwkv_k_a"][:Ld], inp["rwkv_r_k"][:Ld].reshape(Ld, 1024),
                    inp["rwkv_lnx_w"][:Ld], inp["rwkv_lnx_b"][:Ld]], axis=1)
    m["rwkv_vec"] = f(colfmt(vec, 8).transpose(0, 2, 1, 3))
    m["rwkv_w_up"] = f(inp["rwkv_w_up"][:Ld])
    m["rwkv_a_up"] = f(inp["rwkv_a_up"][:Ld])
    m["rwkv_g_up"] = f(inp["rwkv_g_up"][:Ld])
    for k in ("w_branch", "w_out", "ffn_w_in", "ffn_w_out"):
        m[k] = f(inp[k][:Ld])
    m["ffn_conv_w"] = f(colfmt(inp["ffn_conv_w"][:Ld], 22).transpose(0, 2, 3, 1))
    m["ffn_conv_b"] = f(colfmt(inp["ffn_conv_b"][:Ld], 22))
    return m


def prep_inputs(inp, b, cfg, shared=None):
    T = cfg.T
    f = lambda a: np.ascontiguousarray(a, dtype=np.float32)
    m = dict(shared if shared is not None else prep_shared(inp, cfg))
    m["xc"] = f(np.concatenate([inp["ctx"][b].T, inp["x"][b, :T].T], axis=1))
    cv = np.stack([inp["c"][b], inp["c_ctx"]], axis=1)
    m["cvec"] = f(cv.reshape(KC, 128, 2).transpose(1, 0, 2))
    return m


def kernel(**inputs):
    cfg = Cfg()
    bld = Builder(cfg)
    nc = bld.build()
    inp = {k: np.asarray(v) for k, v in inputs.items()}
    shared = prep_shared(inp, cfg)
    in_maps = [prep_inputs(inp, b, cfg, shared) for b in range(8)]
    res = run_bass_kernel_spmd(nc, in_maps, core_ids=list(range(8)))
    out = np.stack([r["outT"].T for r in res.results], axis=0)
    return out.astype(np.float32)
```

```python
import contextlib
import numpy as np
import concourse.bass as bass
import concourse.mybir as mybir
from concourse.bass_utils import run_bass_kernel_spmd

F32 = mybir.dt.float32
BF16 = mybir.dt.bfloat16
AF = mybir.ActivationFunctionType
ALU = mybir.AluOpType
AX = mybir.AxisListType

D = 1024
KC = 8
LCTX = 256
DFF = 2816
NHEAD = 16
A_COLS = 3 * 1024 + 256
N_IN = 11520
NIN_X = N_IN + 2048
ENGS = ("pe", "act", "dve", "pool", "sp")
NDMA_SEMS = 24
NODEFER = False
CHECK_DEADLOCK = True


class Buf:
    __slots__ = ("name", "w", "readers")

    def __init__(self, name):
        self.name = name
        self.w = None
        self.readers = []


def mkbufs(name, n):
    return [Buf(f"{name}{i}") for i in range(n)]


class Emit:
    def __init__(self, nc, stack):
        self.nc = nc
        self.prog = {e: [] for e in ENGS}
        self.count = {e: 0 for e in ENGS}
        self.seen = {e: {} for e in ENGS}
        self.pend_inc = {}
        self.capture_list = None
        self.dma_total = [0] * NDMA_SEMS
        self.dma_rr = 0
        self.n_instr = 0
        self.sems = {}
        for e in ENGS:
            self.sems[e] = stack.enter_context(nc.semaphore(f"c_{e}"))
        for k in range(NDMA_SEMS):
            self.sems[("dma", k)] = stack.enter_context(nc.semaphore(f"d_{k}"))

    def _deps(self, eng, reads, writes):
        deps = {}

        def add(d):
            if d is None:
                return
            k, v = d
            if deps.get(k, 0) < v:
                deps[k] = v
        for b in reads:
            add(b.w)
        for b in writes:
            add(b.w)
            for r in b.readers:
                add(r)
        waits = []
        seen = self.seen[eng]
        for k, v in deps.items():
            if k == eng and v > self.count[eng]:
                continue
            if seen.get(k, 0) < v:
                seen[k] = v
                waits.append((k, v))
        return waits

    def _commit(self, me, reads, writes):
        for b in reads:
            b.readers.append(me)
            if len(b.readers) > 32:
                mx = {}
                for k, v in b.readers:
                    if mx.get(k, 0) < v:
                        mx[k] = v
                b.readers = list(mx.items())
        for b in writes:
            b.w = me
            b.readers = []

    def op(self, eng, fn, reads=(), writes=(), defer=False):
        if self.capture_list is not None:
            self.capture_list.append(("op", eng, fn, reads, writes, defer))
            return
        waits = self._deps(eng, reads, writes)
        if defer and not NODEFER:
            me = (eng, self.count[eng] + 1)
            self.pend_inc[eng] = 1
            self.prog[eng].append((waits, fn, None))
        else:
            self.count[eng] += 1
            me = (eng, self.count[eng])
            self.prog[eng].append((waits, fn, (eng, 1)))
            self.pend_inc[eng] = 0
        self._commit(me, reads, writes)
        self.n_instr += 1 + len(waits)

    def dma(self, q, fn, reads=(), writes=()):
        if self.capture_list is not None:
            self.capture_list.append(("dma", q, fn, reads, writes))
            return
        k = self.dma_rr
        self.dma_rr = (self.dma_rr + 1) % NDMA_SEMS
        key = ("dma", k)
        waits = self._deps(q, reads, writes)
        prev = self.dma_total[k]
        if prev > 0 and self.seen[q].get(key, 0) < prev:
            self.seen[q][key] = prev
            waits.append((key, prev))
        self.dma_total[k] += 16
        me = (key, self.dma_total[k])
        self.prog[q].append((waits, fn, (key, 16)))
        self._commit(me, reads, writes)
        self.n_instr += 1 + len(waits)

    def captured(self, f):
        lst = []
        self.capture_list = lst
        f()
        self.capture_list = None
        return lst

    def replay(self, lst, k):
        while k > 0 and lst:
            e = lst.pop(0)
            if e[0] == "op":
                self.op(*e[1:])
            else:
                self.dma(*e[1:])
            k -= 1

    def check_deadlock(self):
        val = dict(getattr(self, "_simval", {}))
        pos = {e: 0 for e in ENGS}
        progress = True
        while progress:
            progress = False
            for e in ENGS:
                q = self.prog[e]
                while pos[e] < len(q):
                    waits, fn, inc = q[pos[e]]
                    if all(val.get(k, 0) >= v for k, v in waits):
                        if inc is not None:
                            val[inc[0]] = val.get(inc[0], 0) + inc[1]
                        pos[e] += 1
                        progress = True
                    else:
                        break
        stuck = {e: pos[e] for e in ENGS if pos[e] < len(self.prog[e])}
        if stuck:
            for e, p in stuck.items():
                waits, fn, inc = self.prog[e][p]
                print("DEADLOCK: engine", e, "stuck at", p, "/", len(self.prog[e]), "waits",
                      [(k, v, val.get(k, 0)) for k, v in waits if val.get(k, 0) < v])
            raise RuntimeError("deadlock detected in emitted program")
        self._simval = val

    def flush(self):
        assert all(v == 0 for v in self.pend_inc.values()), "deferred semaphore increment left dangling"
        if CHECK_DEADLOCK:
            self.check_deadlock()
        nc = self.nc
        prog = self.prog
        sems = self.sems
        dma_fin = [(("dma", k), v) for k, v in enumerate(self.dma_total) if v > 0]

        def run(name, eng):
            for waits, fn, inc in prog[name]:
                for k, v in waits:
                    eng.wait_ge(sems[k], v)
                ins = fn(eng)
                if inc is not None:
                    ins.then_inc(sems[inc[0]], inc[1])
            if name in ("sp", "pool", "act"):
                for k, v in dma_fin:
                    eng.wait_ge(sems[k], v)

        with nc.Block() as block:
            @block.tensor
            def _(t):
                run("pe", t)

            @block.scalar
            def _(a):
                run("act", a)

            @block.vector
            def _(v):
                run("dve", v)

            @block.gpsimd
            def _(g):
                run("pool", g)

            @block.sync
            def _(s):
                run("sp", s)
        for k, v in dma_fin:
            for e in ENGS:
                self.seen[e][k] = v
        self.prog = {e: [] for e in ENGS}


class Cfg:
    def __init__(self, T=4096, depth=2, dbg=False):
        self.T = T
        self.NT = LCTX + T
        self.depth = depth
        self.dbg = dbg
        self.phases = ("lru", "na", "rwkv", "merge", "ffn")
        self.tiles = [(0, LCTX, 1)] + [(LCTX + 512 * i, 512, 0) for i in range(T // 512)]


class Builder:
    def __init__(self, cfg):
        self.cfg = cfg
        self.nc = bass.Bass("TRN2", target_bir_lowering=False)
        self.dram = {}

    def din(self, name, shape, dt=F32):
        t = self.nc.dram_tensor(name, list(shape), dt, kind="ExternalInput").ap()
        self.dram[name] = t
        return t

    def dscratch(self, name, shape, dt=F32, out=False):
        kind = "ExternalOutput" if (out or self.cfg.dbg) else "Internal"
        t = self.nc.dram_tensor(name, list(shape), dt, kind=kind).ap()
        self.dram[name] = t
        return t

    def sb(self, st, name, shape, dt):
        self._uid = getattr(self, "_uid", 0) + 1
        return st.enter_context(self.nc.sbuf_tensor(f"{name}_{self._uid}", list(shape), dt))

    def ps(self, st, name, shape, dt=F32):
        return st.enter_context(self.nc.psum_tensor(name, list(shape), dt))

    def build(self, upto=99):
        cfg = self.cfg
        nc = self.nc
        NT, T = cfg.NT, cfg.T
        Ld = cfg.depth
        self.NCH = NIN_X // 128
        for name, shape in input_shapes(cfg).items():
            self.din(name, shape)
        self.dscratch("uT", [NIN_X, NT])
        self.dscratch("yA", [D, NT], BF16)
        self.dscratch("yB", [D, NT], BF16)
        self.dscratch("yC", [D, NT], BF16)
        self.dscratch("aT", [DFF, NT], BF16)
        self.dscratch("xcur", [D, NT])
        self.dscratch("xmid", [D, NT])
        self.dscratch("outT", [D, T], out=True)
        dr = self.dram
        with contextlib.ExitStack() as outer:
            em = Emit(nc, outer)
            self.em = em
            mod = self.sb(outer, "mod", [128, 48, 2], F32)
            ones_bf = self.sb(outer, "ones_bf", [128, 128], BF16)
            b_mod = Buf("mod")
            b_ones = Buf("ones")
            self.mod, self.b_mod, self.ones_bf, self.b_ones = mod, b_mod, ones_bf, b_ones
            em.op("pool", lambda e: e.memset(ones_bf[:], 1.0), writes=[b_ones])
            psb = [self.ps(outer, f"psb{i}", [128, 512]) for i in range(8)]
            b_ps = mkbufs("ps", 8)
            self.psb, self.b_ps = psb, b_ps
            for l in range(Ld):
                xsrc = dr["xc"] if l == 0 else dr["xcur"]
                self.phase_mod(l, dr["cvec"], dr["ada_w"], dr["ada_b"], mod, b_mod)
                self.phase_norm_inproj(l, xsrc, dr["norm_g"], dr["w_in"], dr["uT"], mod, b_mod, ones_bf, b_ones)
                if upto <= 2:
                    break
                if "lru" in cfg.phases:
                    self.phase_lru(l)
                if "na" in cfg.phases:
                    self.phase_na(l)
                if "rwkv" in cfg.phases:
                    self.phase_rwkv(l)
                if "merge" in cfg.phases:
                    self.phase_merge_ffn(l, xsrc)
        return nc

    def mm(self, out, lhsT, rhs, start=True, stop=True, r=(), w=(), defer=None):
        if defer is None:
            defer = not stop
        self.em.op("pe", lambda e: e.matmul(out, lhsT=lhsT, rhs=rhs, start=start, stop=stop), r, w, defer=defer)

    def act(self, out, in_, func, r=(), w=(), scale=1.0, bias=0.0):
        self.em.op("act", lambda e: e.activation(out=out, in_=in_, func=func, scale=scale, bias=bias), r, w)

    def tt(self, eng, out, in0, in1, op, r=(), w=()):
        self.em.op(eng, lambda e: e.tensor_tensor(out=out, in0=in0, in1=in1, op=op), r, w)

    def ts(self, eng, out, in0, s1, s2, op0, op1=None, r=(), w=()):
        if op1 is None:
            self.em.op(eng, lambda e: e.tensor_scalar(out=out, in0=in0, scalar1=s1, scalar2=None, op0=op0), r, w)
        else:
            self.em.op(eng, lambda e: e.tensor_scalar(out=out, in0=in0, scalar1=s1, scalar2=s2, op0=op0, op1=op1), r, w)

    def stt(self, out, in0, sc, in1, op0, op1, r=(), w=()):
        self.em.op("dve", lambda e: e.scalar_tensor_tensor(out=out, in0=in0, scalar=sc, in1=in1, op0=op0, op1=op1), r, w)

    def cp(self, eng, out, in_, r=(), w=()):
        self.em.op(eng, lambda e: e.tensor_copy(out=out, in_=in_), r, w)

    def memset(self, eng, ap, val, w=()):
        self.em.op(eng, lambda e: e.memset(ap, val), (), w)

    def scan(self, out, d0, d1, init, r=(), w=()):
        self.em.op("dve", lambda e: e.tensor_tensor_scan(out=out, data0=d0, data1=d1, initial=init, op0=ALU.mult, op1=ALU.add), r, w)

    def dma(self, q, out, in_, r=(), w=()):
        self.em.dma(q, lambda e: e.dma_start(out=out, in_=in_), r, w)

    def segs(self):
        return [(0, 2, LCTX), (LCTX, LCTX + 4, self.cfg.T)]

    def phase_lru(self, l):
        cfg = self.cfg
        NT, T = cfg.NT, cfg.T
        psb, b_ps = self.psb, self.b_ps
        dr = self.dram
        uT, yB = dr["uT"], dr["yB"]
        B0 = A_COLS
        with contextlib.ExitStack() as st:
            cw = self.sb(st, "l_cw", [128, 8, 4], F32)
            cb = self.sb(st, "l_cb", [128, 8], F32)
            gab = self.sb(st, "l_gab", [128, 2, 2, 8], F32)
            lam = self.sb(st, "l_lam", [128, 2, 8], F32)
            cl = self.sb(st, "l_cl", [128, 2, 8], F32)
            b_par, b_cl = Buf("lpar"), Buf("lcl")
            self.dma("sp", cw[:], dr["lru_conv_w"][l], w=[b_par])
            self.dma("sp", cb[:], dr["lru_conv_b"][l], w=[b_par])
            self.dma("sp", gab[:], dr["lru_gate_b"][l], w=[b_par])
            self.dma("sp", lam[:], dr["lru_lambda"][l], w=[b_par])
            self.act(cl[:], lam[:], AF.Exp, [b_par], [b_cl], scale=-1.0)
            self.act(cl[:], cl[:], AF.Ln, [b_cl], [b_cl], bias=1.0)
            self.ts("dve", cl[:], cl[:], -8.0, None, ALU.mult, r=[b_cl], w=[b_cl])
            xp = self.sb(st, "l_xp", [128, NT + 6], F32)
            xb = self.sb(st, "l_xb", [128, NT], F32)
            xbb = self.sb(st, "l_xbb", [128, NT], BF16)
            gt = self.sb(st, "l_gt", [128, NT], F32)
            gtb = self.sb(st, "l_gtb", [128, NT], BF16)
            A = self.sb(st, "l_A", [128, NT], F32)
            Bt = self.sb(st, "l_B", [128, NT], F32)
            Ct = self.sb(st, "l_C", [128, NT], F32)
            hf = self.sb(st, "l_hf", [128, NT], F32)
            ys = self.sb(st, "l_ys", [128, NT], BF16)
            wgf = [self.sb(st, f"l_wgf{i}", [128, 128], F32) for i in range(2)]
            wgb = [self.sb(st, f"l_wgb{i}", [128, 128], BF16) for i in range(2)]
            b_xp, b_xb, b_xbb, b_gt, b_gtb, b_A, b_B, b_C, b_hf, b_ys = [Buf(n) for n in
                "xp xb xbb gt gtb A B C hf ys".split()]
            b_wgf, b_wgb = mkbufs("wgf", 2), mkbufs("wgb", 2)
            self.memset("pool", xp[:], 0.0, w=[b_xp])
            for i in range(2):
                self.memset("pool", wgf[i][:], 0.0, w=[b_wgf[i]])
            gw = [dr["lru_gate_a_w"], dr["lru_gate_x_w"]]

            def rev(ap):
                aps = [list(p) for p in ap.ap]
                n, stp = aps[-1][1], aps[-1][0]
                aps[-1] = [-stp, n]
                return bass.AP(ap.tensor, ap.offset + stp * (n - 1), aps)
            k = 0
            for j in range(8):
                for (d0, s0, n) in self.segs():
                    self.dma("sp", xp[:, s0:s0 + n], uT[B0 + j * 128:B0 + (j + 1) * 128, d0:d0 + n], w=[b_xp])
                self.dma("sp", gt[:], uT[B0 + 1024 + j * 128:B0 + 1024 + (j + 1) * 128, :], w=[b_gt])
                for (d0, s0, n) in self.segs():
                    self.ts("dve", xb[:, d0:d0 + n], xp[:, s0 - 2:s0 - 2 + n], cw[:, j, 0:1], cb[:, j:j + 1], ALU.mult, ALU.add,
                            r=[b_xp, b_par], w=[b_xb])
                    for tap in range(1, 4):
                        self.stt(xb[:, d0:d0 + n], xp[:, s0 - 2 + tap:s0 - 2 + tap + n], cw[:, j, tap:tap + 1], xb[:, d0:d0 + n],
                                 ALU.mult, ALU.add, r=[b_xp, b_par, b_xb], w=[b_xb])
                self.cp("pool", xbb[:], xb[:], r=[b_xb], w=[b_xbb])
                self.act(gtb[:], gt[:], AF.Gelu_apprx_tanh, [b_gt], [b_gtb])
                for d in range(2):
                    for g in range(2):
                        for hb in range(2):
                            self.dma("sp", wgf[g][hb * 64:(hb + 1) * 64, hb * 64:(hb + 1) * 64], gw[g][l, d, 2 * j + hb],
                                     w=[b_wgf[g]])
                        self.cp("pool", wgb[g][:], wgf[g][:], r=[b_wgf[g]], w=[b_wgb[g]])
                    for g, (dst, b_dst) in enumerate([(A, b_A), (Bt, b_B)]):
                        for (t0, n, seg) in cfg.tiles:
                            pb = k % 8
                            k += 1
                            self.mm(psb[pb][:, 0:n], wgb[g][:], xbb[:, t0:t0 + n], r=[b_wgb[g], b_xbb], w=[b_ps[pb]])
                            self.act(dst[:, t0:t0 + n], psb[pb][:, 0:n], AF.Sigmoid, [b_ps[pb], b_par], [b_dst],
                                     bias=gab[:, g, d, j:j + 1])
                    self.act(A[:], A[:], AF.Exp, [b_A, b_cl], [b_A], scale=cl[:, d, j:j + 1])
                    self.tt("dve", Ct[:], A[:], A[:], ALU.mult, r=[b_A], w=[b_C])
                    self.act(Ct[:], Ct[:], AF.Sqrt, [b_C], [b_C], scale=-1.0, bias=1.0)
                    self.tt("pool", Bt[:], Bt[:], xb[:], ALU.mult, r=[b_B, b_xb], w=[b_B])
                    self.tt("dve", Ct[:], Ct[:], Bt[:], ALU.mult, r=[b_C, b_B], w=[b_C])
                    if d == 0:
                        self.scan(hf[:], A[:], Ct[:], 0.0, r=[b_A, b_C], w=[b_hf])
                    else:
                        for (d0, s0, n) in self.segs():
                            self.cp("pool", Bt[:, d0:d0 + n], rev(A[:, d0:d0 + n]), r=[b_A], w=[b_B])
                            self.cp("pool", gt[:, d0:d0 + n], rev(Ct[:, d0:d0 + n]), r=[b_C], w=[b_gt])
                        self.scan(A[:], Bt[:], gt[:], 0.0, r=[b_B, b_gt, b_A], w=[b_A])
                        for (d0, s0, n) in self.segs():
                            self.cp("pool", Ct[:, d0:d0 + n], rev(A[:, d0:d0 + n]), r=[b_A], w=[b_C])
                        self.tt("dve", hf[:], hf[:], Ct[:], ALU.add, r=[b_hf, b_C], w=[b_hf])
                self.tt("dve", ys[:], hf[:], gtb[:], ALU.mult, r=[b_hf, b_gtb], w=[b_ys])
                self.dma("pool", yB[j * 128:(j + 1) * 128, :], ys[:], r=[b_ys])
            self.em.flush()

    def phase_na(self, l):
        cfg = self.cfg
        NT, T = cfg.NT, cfg.T
        psb, b_ps = self.psb, self.b_ps
        dr = self.dram
        uT, yC = dr["uT"], dr["yC"]
        update_ctx = (l < cfg.depth - 1) or getattr(cfg, 'force_ctx', False)
        rows = T // 64
        blocks, ntype = na_blocks(rows)
        NTB = NT // 128
        CQ, CK, CV, CQP, CKP = 42, 50, 58, 90, 98
        rp = dr["rpb_pad"]
        with contextlib.ExitStack() as st:
            ident = self.sb(st, "n_ident", [128, 128], F32)
            onesp = self.sb(st, "n_onesp", [128, 2, 128], BF16)
            maskb = self.sb(st, "n_maskb", [128, ntype, 512], BF16)
            mtmp = [self.sb(st, f"n_mtmp{i}", [128, 512], F32) for i in range(2)]
            b_id, b_op, b_mk = Buf("ident"), Buf("onesp"), Buf("maskb")
            b_mt = mkbufs("mtmp", 2)
            self.dma("sp", ident[:], dr["ident"][:, :], w=[b_id])
            self.memset("pool", onesp[:], 0.0, w=[b_op])
            self.memset("pool", onesp[:, 0, 0:64], 1.0, w=[b_op])
            self.memset("pool", onesp[:, 1, 64:128], 1.0, w=[b_op])
            for t in range(ntype):
                self.dma("sp", mtmp[t % 2][:], dr["na_mask"][t], w=[b_mt[t % 2]])
                self.cp("pool", maskb[:, t, :], mtmp[t % 2][:], r=[b_mt[t % 2]], w=[b_mk])
            NIN = 7
            tl = [[self.sb(st, f"n_tl{a}_{i}", [128, 512], F32) for i in range(2)] for a in range(NIN)]
            b_tl = [mkbufs(f"tl{a}_", 2) for a in range(NIN)]
            qpl = self.sb(st, "n_qpl", [128, NT], BF16)
            kpl = self.sb(st, "n_kpl", [128, 2, LCTX], BF16)
            qrot = self.sb(st, "n_qrot", [128, T], BF16)
            krot = self.sb(st, "n_krot", [128, 2, T], BF16)
            Vp = self.sb(st, "n_Vp", [128, NTB, 2, 128], BF16)
            Tc2 = self.sb(st, "n_Tc2", [128, 22 * 64], F32)
            biasd = self.sb(st, "n_biasd", [128, 8, 512], F32)
            bm = [self.sb(st, f"n_bm{i}", [128, ntype, 512], BF16) for i in range(2)]
            sT = [self.sb(st, f"n_sT{i}", [128, 512], F32) for i in range(2)]
            pT = [self.sb(st, f"n_pT{i}", [128, 512], BF16) for i in range(3)]
            rc = [self.sb(st, f"n_rc{i}", [128, 512], F32) for i in range(2)]
            yst = self.sb(st, "n_yst", [128, NT], BF16)
            b_qpl, b_kpl, b_qrot, b_krot, b_Vp, b_Tc2, b_biasd, b_yst = [Buf(n) for n in
                "qpl kpl qrot krot Vp Tc2 biasd yst".split()]
            b_bm, b_sT, b_pT, b_rc = mkbufs("bm", 2), mkbufs("sT", 2), mkbufs("pT", 3), mkbufs("rc", 2)
            self.memset("pool", Vp[:], 0.0, w=[b_Vp])
            self.memset("pool", yst[:], 0.0, w=[b_yst])
            self.memset("pool", kpl[:], 0.0, w=[b_kpl])
            self.memset("pool", krot[:], 0.0, w=[b_krot])

            def rev(ap):
                aps = [list(p) for p in ap.ap]
                n, stp = aps[-1][1], aps[-1][0]
                aps[-1] = [-stp, n]
                return bass.AP(ap.tensor, ap.offset + stp * (n - 1), aps)
            kq = 0
            ks = 0
            kp_ = 0
            kacc = 0
            for j in range(8):
                for ti, (t0, n, seg) in enumerate(cfg.tiles):
                    s = ti % 2
                    rowsrc = [CQ + j, CQP + j, CK + j, CKP + j, CV + j]
                    need = [0, 2, 4] if seg == 1 else [0, 1, 2, 3, 4]
                    for a in need:
                        c = rowsrc[a]
                        self.dma("sp", tl[a][s][:, 0:n], uT[c * 128:(c + 1) * 128, t0:t0 + n], w=[b_tl[a][s]])
                    if seg == 1:
                        self.act(qpl[:, t0:t0 + n], tl[0][s][:, 0:n], AF.Copy, [b_tl[0][s]], [b_qpl])
                        self.act(kpl[0:64, 0, 0:n], tl[2][s][0:64, 0:n], AF.Copy, [b_tl[2][s]], [b_kpl])
                        self.act(kpl[64:128, 1, 0:n], tl[2][s][64:128, 0:n], AF.Copy, [b_tl[2][s]], [b_kpl])
                    else:
                        lt0 = t0 - LCTX
                        self.dma("sp", tl[5][s][:], dr["rope_cos"][:, lt0:lt0 + 512], w=[b_tl[5][s]])
                        self.dma("sp", tl[6][s][:], dr["rope_sin"][:, lt0:lt0 + 512], w=[b_tl[6][s]])
                        self.act(qpl[:, t0:t0 + n], tl[0][s][:], AF.Copy, [b_tl[0][s]], [b_qpl])
                        for (a, ap_, dst, b_dst, eng) in [(0, 1, qrot, b_qrot, "dve"), (2, 3, krot, b_krot, "dve")]:
                            self.tt(eng, tl[a][s][:], tl[a][s][:], tl[5][s][:], ALU.mult,
                                    r=[b_tl[a][s], b_tl[5][s]], w=[b_tl[a][s]])
                            self.tt(eng, tl[ap_][s][:], tl[ap_][s][:], tl[6][s][:], ALU.mult,
                                    r=[b_tl[ap_][s], b_tl[6][s]], w=[b_tl[ap_][s]])
                            if dst is krot:
                                for hh in range(2):
                                    hs = slice(hh * 64, (hh + 1) * 64)
                                    self.tt(eng, krot[hs, hh, lt0:lt0 + 512], tl[a][s][hs, :], tl[ap_][s][hs, :], ALU.add,
                                            r=[b_tl[a][s], b_tl[ap_][s]], w=[b_dst])
                            else:
                                self.tt(eng, dst[:, lt0:lt0 + 512], tl[a][s][:], tl[ap_][s][:], ALU.add,
                                        r=[b_tl[a][s], b_tl[ap_][s]], w=[b_dst])
                    nb = n // 128
                    pb = kq % 4
                    kq += 1
                    for q in range(nb):
                        self.em.op("pe", lambda e, pb=pb, q=q, s=s: e.transpose(
                            psb[pb][:, q * 128:(q + 1) * 128], tl[4][s][:, q * 128:(q + 1) * 128], ident[:]),
                            [b_tl[4][s], b_id], [b_ps[pb]], defer=(q != nb - 1))
                    tb0 = t0 // 128
                    pv = psb[pb][:, 0:nb * 128].rearrange("p (a b) -> p a b", b=128)
                    self.cp("dve", Vp[:, tb0:tb0 + nb, 0, 0:64], pv[:, :, 0:64], r=[b_ps[pb]], w=[b_Vp])
                    self.act(Vp[:, tb0:tb0 + nb, 1, 64:128], pv[:, :, 64:128], AF.Copy, [b_ps[pb]], [b_Vp])
                for hh in range(2):
                    h = 2 * j + hh
                    for krl in range(2):
                        base = ((l * 16 + h) * 24 + krl) * 128
                        src = bass.AP(rp.tensor, rp.offset + base, [[1, 64], [128, 22], [1, 64]])
                        self.dma("sp", Tc2[krl * 64:(krl + 1) * 64, :].rearrange("p (a b) -> p a b", b=64), src, w=[b_Tc2])
                    for di in range(8):
                        self.cp("pool", biasd[:, di, :], rev(Tc2[:, di * 128:di * 128 + 512]), r=[b_Tc2], w=[b_biasd])
                    for qb, items in enumerate(blocks):
                        for (kr0, ty, di) in items:
                            if ty is not None:
                                self.tt(("dve", "pool")[ty % 2], bm[hh][:, ty, :], biasd[:, di, :], maskb[:, ty, :], ALU.add,
                                        r=[b_biasd, b_mk], w=[b_bm[hh]])
                qk_list, pv_list = [], []
                for qb, items in enumerate(blocks):
                    a1, a2 = 4 + 2 * (kacc % 2), 5 + 2 * (kacc % 2)
                    kacc += 1
                    s3 = kacc % 2
                    q0t = qb * 512
                    work = []
                    for hh in range(2):
                        for (kr0, ty, di) in items:
                            work.append((hh, "loc", kr0, ty))
                        for cc in range(2):
                            work.append((hh, "ctx", cc, None))
                    for wi, (hh, kind, a, ty) in enumerate(work):
                        hb = hh * 64
                        pb = kq % 4
                        kq += 1
                        s2 = kp_ % 3
                        kp_ += 1
                        first, last = (wi == 0), (wi == len(work) - 1)
                        if kind == "loc":
                            s1 = ks % 2
                            ks += 1

                            def qk(hb=hb, pb=pb, s2=s2, s1=s1, a=a, ty=ty, hh=hh, q0t=q0t):
                                ktok = a * 64
                                self.mm(psb[pb][:, :], krot[:, hh, ktok:ktok + 128], qrot[:, q0t:q0t + 512],
                                        r=[b_krot, b_qrot], w=[b_ps[pb]])
                                self.stt(sT[s1][:], psb[pb][:, :], 0.125, bm[hh][:, ty, :], ALU.mult, ALU.add,
                                         r=[b_ps[pb], b_bm[hh]], w=[b_sT[s1]])
                                self.act(pT[s2][:], sT[s1][:], AF.Exp, [b_sT[s1]], [b_pT[s2]])
                            vch = 2 + a // 2
                        else:
                            def qk(hb=hb, pb=pb, s2=s2, a=a, q0t=q0t, hh=hh):
                                self.mm(psb[pb][:, :], kpl[:, hh, a * 128:(a + 1) * 128],
                                        qpl[:, LCTX + q0t:LCTX + q0t + 512], r=[b_kpl, b_qpl], w=[b_ps[pb]])
                                self.act(pT[s2][:], psb[pb][:, :], AF.Exp, [b_ps[pb]], [b_pT[s2]], scale=0.125)
                            vch = a

                        def pv(a1=a1, a2=a2, vch=vch, hh=hh, s2=s2, first=first, last=last, s3=s3, q0t=q0t):
                            self.mm(psb[a1][:, :], Vp[:, vch, hh, :], pT[s2][:], start=first, stop=last,
                                    r=[b_Vp, b_pT[s2]], w=[b_ps[a1]], defer=True)
                            self.mm(psb[a2][:, :], onesp[:, hh, :], pT[s2][:], start=first, stop=last,
                                    r=[b_op, b_pT[s2]], w=[b_ps[a2]], defer=(not last))
                            if last:
                                self.em.op("dve", lambda e: e.reciprocal(out=rc[s3][:], in_=psb[a2][:, :]),
                                           [b_ps[a2]], [b_rc[s3]])
                                self.tt("dve", yst[:, LCTX + q0t:LCTX + q0t + 512], psb[a1][:, :], rc[s3][:], ALU.mult,
                                        r=[b_ps[a1], b_rc[s3]], w=[b_yst])
                        qk_list.append(qk)
                        pv_list.append(pv)
                LA = 2
                for idx in range(len(qk_list) + LA):
                    if idx < len(qk_list):
                        qk_list[idx]()
                    if idx - LA >= 0:
                        pv_list[idx - LA]()
                if update_ctx:
                    a1, a2 = 4 + 2 * (kacc % 2), 5 + 2 * (kacc % 2)
                    kacc += 1
                    work = [(hh, cc) for hh in range(2) for cc in range(2)]
                    for wi, (hh, cc) in enumerate(work):
                        hb = hh * 64
                        pb = kq % 4
                        kq += 1
                        s2 = kp_ % 3
                        kp_ += 1
                        self.mm(psb[pb][:, 0:LCTX], kpl[:, hh, cc * 128:(cc + 1) * 128], qpl[:, 0:LCTX],
                                r=[b_kpl, b_qpl], w=[b_ps[pb]])
                        self.act(pT[s2][:, 0:LCTX], psb[pb][:, 0:LCTX], AF.Exp, [b_ps[pb]], [b_pT[s2]], scale=0.125)
                        first, last = (wi == 0), (wi == len(work) - 1)
                        self.mm(psb[a1][:, 0:LCTX], Vp[:, cc, hh, :], pT[s2][:, 0:LCTX], start=first, stop=last,
                                r=[b_Vp, b_pT[s2]], w=[b_ps[a1]])
                        self.mm(psb[a2][:, 0:LCTX], onesp[:, hh, :], pT[s2][:, 0:LCTX], start=first, stop=last,
                                r=[b_op, b_pT[s2]], w=[b_ps[a2]])
                    s3 = kacc % 2
                    self.em.op("dve", lambda e, s3=s3, a2=a2: e.reciprocal(out=rc[s3][:, 0:LCTX], in_=psb[a2][:, 0:LCTX]),
                               [b_ps[a2]], [b_rc[s3]])
                    self.tt("dve", yst[:, 0:LCTX], psb[a1][:, 0:LCTX], rc[s3][:, 0:LCTX], ALU.mult,
                            r=[b_ps[a1], b_rc[s3]], w=[b_yst])
                self.dma("pool", yC[j * 128:(j + 1) * 128, :], yst[:], r=[b_yst])
            self.em.flush()

    def phase_rwkv(self, l):
        cfg = self.cfg
        NT, T = cfg.NT, cfg.T
        psb, b_ps = self.psb, self.b_ps
        dr = self.dram
        uT, yA = dr["uT"], dr["yA"]
        NCK = NT // 64
        CW = 0.6065306597126334
        SEGC = 16
        SEGN = SEGC * 64
        segs = [(0, 4)] + [(4 + 16 * i, 16) for i in range((NCK - 4) // 16)]
        with contextlib.ExitStack() as st:
            sb = lambda name, shape, dt=F32: self.sb(st, "r_" + name, shape, dt)
            ident_bf = sb("ident_bf", [128, 128], BF16)
            bdones = sb("bdones", [128, 128], BF16)
            istack = sb("istack", [128, 64], BF16)
            rkm = sb("rkm", [128, 2, 1152], BF16)
            cmask = sb("cmask", [128, SEGN])
            mu = sb("mu", [128, 26, 2])
            c0 = sb("c0", [128, 26])
            w0a0 = sb("w0a0", [128, 2, 2, 8])
            vec = sb("vec", [128, 5, 8])
            omk = sb("omk", [128, 8])
            WA = sb("WA", [128, 2, 1024], BF16)
            GU = sb("GU", [128, 1024], BF16)
            b_cst, b_k, b_par, b_wst, b_W = Buf("cst"), Buf("rk_consts"), Buf("rpar"), Buf("wst"), Buf("WA")
            with contextlib.ExitStack() as st_tmp:
                cst_f = self.sb(st_tmp, "r_cst_f", [128, 1152], F32)
                wst = self.sb(st_tmp, "r_wst", [128, 1024], F32)
                for (dst, src, n) in [(ident_bf, dr["ident"], 128), (bdones, dr["bdones"], 128), (istack, dr["istack"], 64)]:
                    self.dma("sp", cst_f[:, 0:n], src[:, :], w=[b_cst])
                    self.cp("dve", dst[:], cst_f[:, 0:n], r=[b_cst], w=[b_k])
                for d in range(2):
                    self.dma("sp", cst_f[:], dr["rk_mask"][d], w=[b_cst])
                    self.cp("dve", rkm[:, d, :], cst_f[:], r=[b_cst], w=[b_k])
                self.memset("pool", cmask[:], 1.0, w=[b_k])
                self.memset("pool", cmask[:].rearrange("p (c s) -> p c s", s=64)[:, :, 0:1], 0.0, w=[b_k])
                self.dma("sp", mu[:], dr["rwkv_mu"][l], w=[b_par])
                self.dma("sp", w0a0[:], dr["rwkv_w0a0"][l], w=[b_par])
                self.dma("sp", vec[:], dr["rwkv_vec"][l], w=[b_par])
                self.ts("dve", c0[:], mu[:, :, 0], -1.0, 1.0, ALU.mult, ALU.add, r=[b_par], w=[b_par])
                self.tt("dve", c0[:], c0[:], mu[:, :, 1], ALU.subtract, r=[b_par], w=[b_par])
                self.ts("dve", omk[:], vec[:, 1, :], -1.0, 1.0, ALU.mult, ALU.add, r=[b_par], w=[b_par])
                for d in range(2):
                    self.dma("sp", wst[0:64, :], dr["rwkv_w_up"][l, d], w=[b_wst])
                    self.dma("sp", wst[64:128, :], dr["rwkv_a_up"][l, d], w=[b_wst])
                    self.cp("pool", WA[:, d, :], wst[:], r=[b_wst], w=[b_W])
                self.dma("sp", wst[:], dr["rwkv_g_up"][l], w=[b_wst])
                self.cp("pool", GU[:], wst[:], r=[b_wst], w=[b_W])
                self.em.flush()
            LW = sb("LW", [128, NT], BF16)
            GL = sb("GL", [128, NT], BF16)
            Yacc = sb("Yacc", [128, NCK, 64])
            ksum = sb("ksum", [128, NT])
            b_LW, b_GL, b_Yacc, b_ksum = Buf("LW"), Buf("GL"), Buf("Yacc"), Buf("ksum")
            xps = [sb(f"xp{i}", [128, SEGN + 2]) for i in range(3)]
            b_xps = mkbufs("xp", 3)
            kxp = [0]
            rTs = [sb(f"rT{i}", [128, SEGN]) for i in range(2)]
            kT, vT, kap = sb("kT", [128, SEGN]), sb("vT", [128, SEGN]), sb("kap", [128, SEGN])
            T1, T2, T3, T4 = [sb(f"T{i}", [128, SEGN]) for i in range(1, 5)]
            F1, F2, F3 = [sb(f"F{i}", [128, SEGN]) for i in range(1, 4)]
            ynT = sb("ynT", [128, SEGN])
            Vbs = [sb(f"Vb{i}", [128, SEGN], BF16) for i in range(2)]
            gTbs = [sb(f"gTb{i}", [128, SEGN], BF16) for i in range(2)]
            sqb, fsqb = sb("sqb", [128, SEGN], BF16), sb("fsqb", [128, SEGN], BF16)
            stks = [sb(f"stk{i}", [128, 4, SEGN], BF16) for i in range(2)]
            YBD = sb("YBD", [128, SEGC, 128], BF16)
            yst = sb("yst", [128, SEGN], BF16)
            gCs = [sb(f"gC{i}", [128, SEGC]) for i in range(2)]
            lnst = sb("lnst", [128, 6, SEGC])
            ptot = sb("ptot", [128, SEGC])
            b_kT, b_vT, b_kap, b_T1, b_T2, b_T3, b_T4, b_ynT, b_sqb, b_YBD, b_yst, b_ln, b_F1, b_F2, b_F3, b_fsqb, b_ptot = [
                Buf(n) for n in "kT vT kap T1 T2 T3 T4 ynT sqb YBD yst lnst F1 F2 F3 fsqb ptot".split()]
            b_rTs, b_Vbs, b_gTbs, b_stks, b_gCs = [mkbufs(n, 2) for n in "rT Vb gTb stk gC".split()]
            self.memset("pool", YBD[:], 0.0, w=[b_YBD])
            G = 4
            BDg = [sb(f"BDg{i}", [128, G, 5, 128], BF16) for i in range(2)]
            b_BDg = mkbufs("BDg", 2)
            for i in range(2):
                self.memset("pool", BDg[i][:], 0.0, w=[b_BDg[i]])
            SA = [sb(f"SA{i}", [128, 512], BF16) for i in range(G)]
            SB_ = [sb(f"SB{i}", [128, 512], BF16) for i in range(G)]
            MN = [[sb(f"MN{i}_{k}", [128, 256], BF16) for k in range(2)] for i in range(G)]
            Rb = [[sb(f"Rb{i}_{k}", [128, 256], BF16) for k in range(2)] for i in range(G)]
            Pb = [[sb(f"Pb{i}_{k}", [128, 128], BF16) for k in range(2)] for i in range(G)]
            MO = [sb(f"MO{i}", [128, 128], BF16) for i in range(G)]
            QP = [sb(f"QP{i}", [128, 256], BF16) for i in range(G)]
            AK = [sb(f"AK{i}", [128, 256], BF16) for i in range(G)]
            VM = [sb(f"VM{i}", [128, G, 64], BF16) for i in range(2)]
            b_VM = mkbufs("VM", 2)
            b_MO = mkbufs("MO", G)
            pending = []
            ST = sb("ST", [128, 64], BF16)
            S32 = sb("S32", [128, 64])
            S32g = sb("S32g", [128, 64])
            b_S32, b_S32g = Buf("S32"), Buf("S32g")
            b_SA, b_SB, b_QP, b_AK = [mkbufs(n, G) for n in "SA SB QP AK".split()]
            b_MN, b_Rb, b_Pb = [[mkbufs(f"{n}{i}_", 2) for i in range(G)] for n in "MN Rb Pb".split()]
            b_ST = Buf("ST")
            kps = [0]

            def shift(dst, b_dst, c, c0_, n):
                xi = kxp[0] % 3
                kxp[0] += 1
                xp, b_xp = xps[xi], b_xps[xi]
                p0 = c0_ * 64
                p1 = p0 + n
                hasL = p0 not in (0, LCTX)
                hasR = p1 not in (LCTX, NT)
                if not hasL:
                    self.memset("pool", xp[:, 0:1], 0.0, w=[b_xp])
                if not hasR:
                    self.memset("pool", xp[:, n + 1:n + 2], 0.0, w=[b_xp])
                lo, hi = p0 - int(hasL), p1 + int(hasR)
                self.dma("sp", xp[:, 1 - int(hasL):1 + n + int(hasR)], uT[c * 128:(c + 1) * 128, lo:hi], w=[b_xp])
                self.act(dst[:, 0:n], xp[:, 1:1 + n], AF.Copy, [b_xp, b_par], [b_dst], scale=c0[:, c:c + 1])
                self.stt(dst[:, 0:n], xp[:, 0:n], mu[:, c, 0:1], dst[:, 0:n], ALU.mult, ALU.add, r=[b_xp, b_par, b_dst], w=[b_dst])
                self.stt(dst[:, 0:n], xp[:, 2:2 + n], mu[:, c, 1:2], dst[:, 0:n], ALU.mult, ALU.add, r=[b_xp, b_par, b_dst], w=[b_dst])

            def tiles_of(n):
                return [(o, min(512, n - o)) for o in range(0, n, 512)]

            def nextps():
                kps[0] += 1
                return 7

            for (ck0, nck) in segs:
                n = nck * 64
                t0 = ck0 * 64
                shift(T1, b_T1, 24, ck0, n)
                self.act(LW[0:64, t0:t0 + n], T1[0:64, 0:n], AF.Tanh, [b_T1], [b_LW])
                self.act(LW[64:128, t0:t0 + n], T1[64:128, 0:n], AF.Copy, [b_T1], [b_LW])
                shift(T2, b_T2, 25, ck0, n)
                self.act(GL[:, t0:t0 + n], T2[:, 0:n], AF.Sigmoid, [b_T2], [b_GL])

            kbd = [0]

            def prep(j, d, ck0, nck, sp):
                n = nck * 64
                t0 = ck0 * 64
                rT, b_rT, Vb, b_Vb, gTb, b_gTb, stk, b_stk, gC, b_gC = (rTs[sp], b_rTs[sp], Vbs[sp], b_Vbs[sp], gTbs[sp], b_gTbs[sp],
                                                                       stks[sp], b_stks[sp], gCs[sp], b_gCs[sp])
                for (o, m) in tiles_of(n):
                    pb = nextps()
                    self.mm(psb[pb][:, 0:m], WA[0:64, d, j * 128:(j + 1) * 128], LW[0:64, t0 + o:t0 + o + m],
                            r=[b_W, b_LW], w=[b_ps[pb]])
                    self.act(T1[:, o:o + m], psb[pb][:, 0:m], AF.Sigmoid, [b_ps[pb], b_par], [b_T1],
                             bias=w0a0[:, 0, d, j:j + 1])
                    pb = nextps()
                    self.mm(psb[pb][:, 0:m], WA[64:128, d, j * 128:(j + 1) * 128], LW[64:128, t0 + o:t0 + o + m],
                            r=[b_W, b_LW], w=[b_ps[pb]])
                    self.act(T2[:, o:o + m], psb[pb][:, 0:m], AF.Sigmoid, [b_ps[pb], b_par], [b_T2],
                             bias=w0a0[:, 1, d, j:j + 1])
                shift(rT, b_rT, j, ck0, n)
                shift(kT, b_kT, 8 + j, ck0, n)
                shift(vT, b_vT, 16 + j, ck0, n)
                self.scan(T3[:, 0:n], cmask[:, 0:n], T1[:, 0:n], 0.0, r=[b_k, b_T1], w=[b_T3])
                T3v = T3[:, 0:n].rearrange("p (c s) -> p c s", s=64)
                self.act(gC[:, 0:nck], T3v[:, :, 63], AF.Exp, [b_T3], [b_gC], scale=-CW)
                if d == 1:
                    self.cp("pool", ptot[:, 0:nck], T3v[:, :, 63], r=[b_T3], w=[b_ptot])
                    self.tt("dve", T3[:, 0:n], T1[:, 0:n], T3[:, 0:n], ALU.subtract, r=[b_T1, b_T3], w=[b_T3])
                    self.tt("dve", T3v, T3v, ptot[:, 0:nck].unsqueeze(2).to_broadcast([128, nck, 64]), ALU.add,
                            r=[b_T3, b_ptot], w=[b_T3])
                self.tt("dve", T1[:, 0:n], T3[:, 0:n], T1[:, 0:n], ALU.subtract, r=[b_T1, b_T3], w=[b_T1])
                self.act(T1[:, 0:n], T1[:, 0:n], AF.Exp, [b_T1], [b_T1], scale=-CW)
                self.act(T4[:, 0:n], T3[:, 0:n], AF.Exp, [b_T3], [b_T4], scale=CW)
                self.act(T3[:, 0:n], T3[:, 0:n], AF.Exp, [b_T3], [b_T3], scale=-CW)
                self.act(Vb[:, 0:n], vT[:, 0:n], AF.Copy, [b_vT], [b_Vb])
                self.ts("pool", kap[:, 0:n], kT[:, 0:n], vec[:, 0, j:j + 1], None, ALU.mult, r=[b_kT, b_par], w=[b_kap])
                self.act(sqb[:, 0:n], kap[:, 0:n], AF.Square, [b_kap], [b_sqb])
                for (o, m) in tiles_of(n):
                    pb = nextps()
                    self.mm(psb[pb][:, 0:m], bdones[:], sqb[:, o:o + m], r=[b_k, b_sqb], w=[b_ps[pb]])
                    self.act(F3[:, o:o + m], psb[pb][:, 0:m], AF.Sqrt, [b_ps[pb]], [b_F3], bias=1e-24)
                self.em.op("dve", lambda e, n=n: e.reciprocal(out=F3[:, 0:n], in_=F3[:, 0:n]), [b_F3], [b_F3])
                self.tt("dve", kap[:, 0:n], kap[:, 0:n], F3[:, 0:n], ALU.mult, r=[b_kap, b_F3], w=[b_kap])
                if d == 1:
                    for (o, m) in tiles_of(n):
                        pb = nextps()
                        self.mm(psb[pb][:, 0:m], GU[:, j * 128:(j + 1) * 128], GL[:, t0 + o:t0 + o + m],
                                r=[b_W, b_GL], w=[b_ps[pb]])
                        self.act(gTb[:, o:o + m], psb[pb][:, 0:m], AF.Copy, [b_ps[pb]], [b_gTb])
                self.tt("dve", stk[:, 0, 0:n], kap[:, 0:n], T1[:, 0:n], ALU.mult, r=[b_kap, b_T1], w=[b_stk])
                self.tt("pool", stk[:, 1, 0:n], rT[:, 0:n], T3[:, 0:n], ALU.mult, r=[b_rT, b_T3], w=[b_stk])
                self.ts("dve", T1[:, 0:n], T2[:, 0:n], vec[:, 1, j:j + 1], omk[:, j:j + 1], ALU.mult, ALU.add,
                        r=[b_T2, b_par], w=[b_T1])
                self.tt("dve", T1[:, 0:n], T1[:, 0:n], kT[:, 0:n], ALU.mult, r=[b_T1, b_kT], w=[b_T1])
                if d == 0:
                    self.cp("pool", ksum[:, t0:t0 + n], T1[:, 0:n], r=[b_T1], w=[b_ksum])
                else:
                    self.tt("pool", ksum[:, t0:t0 + n], ksum[:, t0:t0 + n], T1[:, 0:n], ALU.add, r=[b_T1, b_ksum], w=[b_ksum])
                self.tt("dve", stk[:, 3, 0:n], T1[:, 0:n], T4[:, 0:n], ALU.mult, r=[b_T1, b_T4], w=[b_stk])
                self.tt("pool", T2[:, 0:n], T2[:, 0:n], kap[:, 0:n], ALU.mult, r=[b_T2, b_kap], w=[b_T2])
                self.tt("dve", stk[:, 2, 0:n], T2[:, 0:n], T4[:, 0:n], ALU.mult, r=[b_T2, b_T4], w=[b_stk])

            def finalize(j, ck0, nck, sp):
                n = nck * 64
                t0 = ck0 * 64
                rT, b_rT, Vb, b_Vb, gTb, b_gTb = rTs[sp], b_rTs[sp], Vbs[sp], b_Vbs[sp], gTbs[sp], b_gTbs[sp]
                Ys = Yacc[:, ck0:ck0 + nck, :]
                T3v = F3[:, 0:n].rearrange("p (c s) -> p c s", s=64)
                mean, ssq, m2, var = lnst[:, 1, 0:nck], lnst[:, 2, 0:nck], lnst[:, 3, 0:nck], lnst[:, 4, 0:nck]
                self.em.op("dve", lambda e, Ys=Ys, mean=mean: e.tensor_reduce(out=mean, in_=Ys, axis=AX.X, op=ALU.add),
                           [b_Yacc], [b_ln])
                self.act(T3v, Ys, AF.Square, [b_Yacc], [b_F3])
                self.em.op("dve", lambda e, T3v=T3v, ssq=ssq: e.tensor_reduce(out=ssq, in_=T3v, axis=AX.X, op=ALU.add),
                           [b_F3], [b_ln])
                self.ts("dve", mean, mean, 1.0 / 64, None, ALU.mult, r=[b_ln], w=[b_ln])
                self.tt("dve", m2, mean, mean, ALU.mult, r=[b_ln], w=[b_ln])
                self.stt(var, ssq, 1.0 / 64, m2, ALU.mult, ALU.subtract, r=[b_ln], w=[b_ln])
                self.act(var, var, AF.Sqrt, [b_ln], [b_ln], bias=64e-5)
                self.em.op("dve", lambda e, var=var: e.reciprocal(out=var, in_=var), [b_ln], [b_ln])
                self.tt("dve", T3v, Ys, mean.unsqueeze(2).to_broadcast([128, nck, 64]), ALU.subtract,
                        r=[b_Yacc, b_ln], w=[b_F3])
                for hh in range(2):
                    hs = slice(hh * 64, (hh + 1) * 64)
                    self.tt("dve", YBD[hs, 0:nck, hs], T3v[hs], var[hs].unsqueeze(2).to_broadcast([64, nck, 64]), ALU.mult,
                            r=[b_F3, b_ln], w=[b_YBD])
                for c8 in range(0, nck, 8):
                    m8 = min(8, nck - c8)
                    pb = 7
                    for ci in range(c8, c8 + m8):
                        self.mm(psb[pb][:, (ci - c8) * 64:(ci - c8 + 1) * 64], YBD[:, ci, :], istack[:],
                                r=[b_YBD, b_k], w=[b_ps[pb]], defer=(ci != c8 + m8 - 1))
                    self.ts("dve", ynT[:, c8 * 64:(c8 + m8) * 64], psb[pb][:, 0:m8 * 64], vec[:, 3, j:j + 1], vec[:, 4, j:j + 1],
                            ALU.mult, ALU.add, r=[b_ps[pb], b_par], w=[b_ynT])
                self.tt("pool", F1[:, 0:n], rT[:, 0:n], ksum[:, t0:t0 + n], ALU.mult, r=[b_rT, b_ksum], w=[b_F1])
                self.ts("dve", fsqb[:, 0:n], F1[:, 0:n], vec[:, 2, j:j + 1], None, ALU.mult, r=[b_F1, b_par], w=[b_fsqb])
                for (o, m) in tiles_of(n):
                    pb = 7
                    self.mm(psb[pb][:, 0:m], bdones[:], fsqb[:, o:o + m], r=[b_k, b_fsqb], w=[b_ps[pb]])
                    self.tt("dve", F2[:, o:o + m], psb[pb][:, 0:m], Vb[:, o:o + m], ALU.mult, r=[b_ps[pb], b_Vb], w=[b_F2])
                self.tt("dve", ynT[:, 0:n], ynT[:, 0:n], F2[:, 0:n], ALU.add, r=[b_ynT, b_F2], w=[b_ynT])
                self.tt("dve", yst[:, 0:n], ynT[:, 0:n], gTb[:, 0:n], ALU.mult, r=[b_ynT, b_gTb], w=[b_yst])
                self.dma("pool", yA[j * 128:(j + 1) * 128, t0:t0 + n], yst[:, 0:n], r=[b_yst])

            for j in range(8):
                for d in range(2):
                    self.memset("pool", ST[:], 0.0, w=[b_ST])
                    self.memset("pool", S32[:], 0.0, w=[b_S32])
                    order = segs if d == 0 else [segs[0]] + segs[1:][::-1]
                    bgq = []
                    prep(j, d, order[0][0], order[0][1], 0)
                    for si, (ck0, nck) in enumerate(order):
                        sp = si % 2
                        n = nck * 64
                        t0 = ck0 * 64
                        rT, b_rT, Vb, b_Vb, gTb, b_gTb, stk, b_stk, gC, b_gC = (rTs[sp], b_rTs[sp], Vbs[sp], b_Vbs[sp], gTbs[sp],
                                                                               b_gTbs[sp], stks[sp], b_stks[sp], gCs[sp], b_gCs[sp])
                        if si + 1 < len(order):
                            nxt = order[si + 1]
                            bgq += self.em.captured(lambda: prep(j, d, nxt[0], nxt[1], 1 - sp))
                        nstage = 30 * ((nck + G - 1) // G)
                        bgk = max(1, (len(bgq) + nstage - 1) // nstage)

                        def bgrun(k=None):
                            self.em.replay(bgq, bgk if k is None else k)
                        corder = list(range(nck)) if d == 0 else list(range(nck))[::-1]
                        for gi in range(0, nck, G):
                            grp = corder[gi:gi + G]
                            cl = min(grp)
                            bg = kbd[0] % 2
                            kbd[0] += 1
                            for qi in range(5):
                                for hh in range(2):
                                    hs = slice(hh * 64, (hh + 1) * 64)
                                    src = (Vb[hs, cl * 64:(cl + G) * 64] if qi == 4 else stk[hs, qi, cl * 64:(cl + G) * 64])
                                    src = src.rearrange("p (c s) -> p c s", s=64)
                                    eng = ("pool", "act", "pool")[(qi * 2 + hh) % 3]
                                    if eng == "act":
                                        self.act(BDg[bg][hs, :, qi, hs], src, AF.Copy, [b_stk, b_Vb], [b_BDg[bg]])
                                    else:
                                        self.cp(eng, BDg[bg][hs, :, qi, hs], src, r=[b_stk, b_Vb], w=[b_BDg[bg]])
                            rB = [b_BDg[bg], b_k]
                            R4 = range(len(grp))
                            gof = [ci - cl for ci in grp]
                            bd = lambda i, q: BDg[bg][:, gof[i], q, :]
                            slot = lambda i: psb[i]
                            bsl = lambda i: b_ps[i]
                            for i in R4:
                                self.mm(psb[4][:, i * 64:(i + 1) * 64], bd(i, 4), istack[:], r=rB, w=[b_ps[4]], defer=(i != len(grp) - 1))
                            self.act(VM[bg][:, 0:len(grp), :], psb[4][:, 0:64 * len(grp)].rearrange("p (c s) -> p c s", s=64), AF.Copy,
                                     [b_ps[4]], [b_VM[bg]])
                            pend = pending[:]
                            del pending[:]

                            def drain(k=1):
                                for _ in range(k):
                                    if pend:
                                        pend.pop(0)()
                            for i in R4:
                                self.mm(slot(i)[:, 0:256], bd(i, 2), BDg[bg][:, gof[i], 0:2, :], r=rB, w=[bsl(i)], defer=True)
                                self.mm(slot(i)[:, 256:384], bd(i, 2), ident_bf[:], r=rB, w=[bsl(i)], defer=True)
                                self.mm(slot(i)[:, 384:512], bd(i, 0), ident_bf[:], r=rB, w=[bsl(i)])
                            drain()
                            bgrun()
                            for i in R4:
                                self.tt("dve", SA[i][:, 0:256], slot(i)[:, 0:256], rkm[:, d, 0:256], ALU.mult, r=[bsl(i), b_k], w=[b_SA[i]])
                                self.act(SA[i][:, 256:512], slot(i)[:, 256:512], AF.Copy, [bsl(i)], [b_SA[i]])
                            bgrun()
                            for i in R4:
                                self.mm(slot(i)[:, 0:128], bd(i, 3), bd(i, 1), r=rB, w=[bsl(i)], defer=True)
                                self.mm(slot(i)[:, 128:256], bd(i, 3), ident_bf[:], r=rB, w=[bsl(i)], defer=True)
                                self.mm(slot(i)[:, 256:512], bd(i, 0), BDg[bg][:, gof[i], 2:4, :], r=rB, w=[bsl(i)])
                            drain()
                            bgrun()
                            for i in R4:
                                self.tt("dve", SB_[i][:], slot(i)[:, :], rkm[:, d, 512:1024], ALU.mult, r=[bsl(i), b_k], w=[b_SB[i]])
                                self.tt("dve", MO[i][:], slot(i)[:, 256:384], rkm[:, d, 1024:1152], ALU.mult, r=[bsl(i), b_k], w=[b_MO[i]])
                            bgrun()
                            for i in R4:
                                self.tt("pool", Pb[i][0][:], ident_bf[:], SB_[i][:, 256:384], ALU.subtract, r=[b_k, b_SB[i]], w=[b_Pb[i][0]])
                            cur = [(SB_[i][:, 256:384], SA[i][:, 0:128], b_SB[i], b_SA[i]) for i in R4]
                            for lev in range(4):
                                lastl = lev == 3
                                mi = lev % 2
                                for i in R4:
                                    Mc, Nc, bM, bN = cur[i]
                                    self.mm(slot(i)[:, 128:256], Mc, Nc, r=[bM, bN], w=[bsl(i)], defer=(not lastl))
                                    if not lastl:
                                        self.mm(slot(i)[:, 0:128], Nc, Mc, r=[bM, bN], w=[bsl(i)])
                                drain()
                                bgrun()
                                lo = 128 if lastl else 0
                                for i in R4:
                                    self.act(MN[i][mi][:, lo:256], slot(i)[:, lo:256], AF.Copy, [bsl(i)], [b_MN[i][mi]])
                                    cur[i] = (MN[i][mi][:, 0:128], MN[i][mi][:, 128:256], b_MN[i][mi], b_MN[i][mi])
                                bgrun()
                                for i in R4:
                                    self.mm(slot(i)[:, 256:384], cur[i][1], Pb[i][lev % 2][:], r=[cur[i][3], b_Pb[i][lev % 2]], w=[bsl(i)])
                                bgrun()
                                for i in R4:
                                    self.tt("dve", Pb[i][(lev + 1) % 2][:], Pb[i][lev % 2][:], slot(i)[:, 256:384], ALU.add,
                                            r=[b_Pb[i][lev % 2], bsl(i)], w=[b_Pb[i][(lev + 1) % 2]])
                            drain(4)
                            for i in R4:
                                self.mm(slot(i)[:, 0:256], Pb[i][0][:], SA[i][:, 128:384], r=[b_Pb[i][0], b_SA[i]], w=[bsl(i)])
                            bgrun()
                            for i in R4:
                                self.act(Rb[i][0][:], slot(i)[:, 0:256], AF.Copy, [bsl(i)], [b_Rb[i][0]])
                            for i in R4:
                                self.mm(slot(i)[:, 256:512], MO[i][:], Rb[i][0][:], r=[b_MO[i], b_Rb[i][0]], w=[bsl(i)])
                            bgrun()
                            for i in R4:
                                self.tt("dve", Rb[i][1][:], SA[i][:, 128:384], slot(i)[:, 256:512], ALU.subtract, r=[b_SA[i], bsl(i)], w=[b_Rb[i][1]])
                            bgrun()
                            for i in R4:
                                self.mm(slot(i)[:, 0:256], Pb[i][0][:], Rb[i][1][:], r=[b_Pb[i][0], b_Rb[i][1]], w=[bsl(i)])
                            for i in R4:
                                self.act(Rb[i][0][:], slot(i)[:, 0:256], AF.Copy, [bsl(i)], [b_Rb[i][0]])
                            bgrun()
                            for i in R4:
                                self.mm(slot(i)[:, 256:512], SA[i][:, 384:512], Rb[i][0][:], r=[b_SA[i], b_Rb[i][0]], w=[bsl(i)], defer=True)
                                self.mm(slot(i)[:, 0:256], SB_[i][:, 384:512], Rb[i][0][:], r=[b_SB[i], b_Rb[i][0]], w=[bsl(i)])
                            bgrun()
                            for i in R4:
                                self.tt("dve", QP[i][:, 0:128], bd(i, 1), slot(i)[:, 256:384], ALU.subtract, r=[b_BDg[bg], bsl(i)], w=[b_QP[i]])
                                self.ts("dve", QP[i][:, 128:256], slot(i)[:, 384:512], -1.0, None, ALU.mult, r=[bsl(i)], w=[b_QP[i]])
                                self.tt("dve", AK[i][:], SB_[i][:, 0:256], slot(i)[:, 0:256], ALU.subtract, r=[b_SB[i], bsl(i)], w=[b_AK[i]])
                            for i in R4:
                                def step(i=i, ci=grp[i], bg=bg, d=d, ck0=ck0):
                                    SQ = psb[6][:, i * 128:(i + 1) * 128]
                                    vm = VM[bg][:, i, :]
                                    self.ts("pool", S32g[:], S32[:], gC[:, ci:ci + 1], None, ALU.mult, r=[b_S32, b_gC], w=[b_S32g])
                                    self.mm(SQ[:, 0:64], QP[i][:, 0:128], ST[:], start=True, stop=False, r=[b_QP[i], b_ST], w=[b_ps[6]])
                                    self.mm(SQ[:, 0:64], AK[i][:, 0:128], vm, start=False, stop=True, r=[b_AK[i], b_VM[bg]], w=[b_ps[6]])
                                    self.mm(SQ[:, 64:128], QP[i][:, 128:256], ST[:], start=True, stop=False, r=[b_QP[i], b_ST], w=[b_ps[6]])
                                    self.mm(SQ[:, 64:128], AK[i][:, 128:256], vm, start=False, stop=True, r=[b_AK[i], b_VM[bg]], w=[b_ps[6]])
                                    self.stt(S32[:], SQ[:, 64:128], gC[:, ci:ci + 1], S32g[:], ALU.mult, ALU.add,
                                             r=[b_ps[6], b_gC, b_S32g], w=[b_S32])
                                    self.act(ST[:], S32[:], AF.Copy, [b_S32], [b_ST])
                                    ck = ck0 + ci
                                    if d == 0:
                                        self.cp("dve", Yacc[:, ck, :], SQ[:, 0:64], r=[b_ps[6]], w=[b_Yacc])
                                    else:
                                        self.tt("dve", Yacc[:, ck, :], Yacc[:, ck, :], SQ[:, 0:64], ALU.add, r=[b_ps[6], b_Yacc], w=[b_Yacc])
                                pending.append(step)
                            while pend:
                                pend.pop(0)()
                        while pending:
                            pending.pop(0)()
                        bgrun(10 ** 9)
                        if d == 1:
                            bgq += self.em.captured(lambda ck0=ck0, nck=nck, sp=sp: finalize(j, ck0, nck, sp))
                    bgrun(10 ** 9) if False else self.em.replay(bgq, 10 ** 9)
            self.em.flush()

    def rstd_tile(self, src, b_src, n, sq, b_sq, rs, b_rs, pb):
        psb, b_ps = self.psb, self.b_ps
        self.act(sq[:, :, 0:n], src[:, :, 0:n], AF.Square, [b_src], [b_sq])
        for kc in range(KC):
            self.mm(psb[pb][:, 0:n], self.ones_bf[:], sq[:, kc, 0:n], start=(kc == 0), stop=(kc == KC - 1),
                    r=[b_sq, self.b_ones], w=[b_ps[pb]])
        self.act(rs[:, 0:n], psb[pb][:, 0:n], AF.Sqrt, [b_ps[pb]], [b_rs], scale=1.0 / D, bias=1e-6)
        self.em.op("dve", lambda e: e.reciprocal(out=rs[:, 0:n], in_=rs[:, 0:n]), [b_rs], [b_rs])

    def phase_merge_ffn(self, l, xsrc):
        cfg = self.cfg
        NT, T = cfg.NT, cfg.T
        psb, b_ps = self.psb, self.b_ps
        dr = self.dram
        mod, b_mod = self.mod, self.b_mod
        uT, xmid, aT = dr["uT"], dr["xmid"], dr["aT"]
        last = l == cfg.depth - 1
        TW = 256
        tiles = [(t0, TW, 1 if t0 < LCTX else 0) for t0 in range(0, NT, TW)]
        with contextlib.ExitStack() as st0:
            h2T = self.sb(st0, "m_h2T", [128, KC, NT], BF16)
            b_h2 = mkbufs("h2T", len(tiles))
            ng = self.sb(st0, "m_ng", [128, 4, KC], F32)
            cols = self.sb(st0, "m_cols", [128, 4, KC, 2], F32)
            wst = self.sb(st0, "m_wst", [128, 1024], F32)
            b_ng, b_cols, b_wst = Buf("ng"), Buf("cols"), Buf("wst")
            self.dma("sp", ng[:], dr["norm_g"][l], w=[b_ng])
            for kc in range(KC):
                self.ts("dve", cols[:, 0, kc, :], mod[:, 16 + kc, :], ng[:, 1, kc:kc + 1], None, ALU.mult, r=[b_mod, b_ng], w=[b_cols])
                self.ts("dve", cols[:, 1, kc, :], mod[:, 32 + kc, :], 1.0, ng[:, 2, kc:kc + 1], ALU.add, ALU.mult, r=[b_mod, b_ng], w=[b_cols])
                self.ts("dve", cols[:, 2, kc, :], mod[:, 40 + kc, :], ng[:, 3, kc:kc + 1], None, ALU.mult, r=[b_mod, b_ng], w=[b_cols])
            with contextlib.ExitStack() as st:
                sb = lambda name, shape, dt=F32: self.sb(st, "m_" + name, shape, dt)
                Wbr = sb("Wbr", [128, 3, KC, 1024], BF16)
                Wo = sb("Wo", [128, KC, 1024], BF16)
                b_W = Buf("Wm")
                for br in range(3):
                    for kc in range(KC):
                        self.dma("sp", wst[:], dr["w_branch"][l, br, kc * 128:(kc + 1) * 128, :], w=[b_wst])
                        self.cp("pool", Wbr[:, br, kc, :], wst[:], r=[b_wst], w=[b_W])
                for kc in range(KC):
                    self.dma("sp", wst[:], dr["w_out"][l, kc * 128:(kc + 1) * 128, :], w=[b_wst])
                    self.cp("pool", Wo[:, kc, :], wst[:], r=[b_wst], w=[b_W])
                yt = [sb(f"yt{i}", [128, KC, TW], BF16) for i in range(3)]
                gt = sb("gt", [128, KC, TW])
                mt = sb("mt", [128, KC, TW])
                tmp = sb("tmp", [128, TW])
                mb = sb("mb", [128, KC, TW], BF16)
                xts = [sb(f"xt{i}", [128, KC, TW]) for i in range(2)]
                b_xts = mkbufs("mxt", 2)
                mo = sb("mo", [128, KC, TW])
                sq = sb("sq", [128, KC, TW], BF16)
                rs = sb("rs", [128, TW])
                b_yt = mkbufs("yt", 3)
                b_gt, b_mt, b_tmp, b_mb, b_mo, b_sq, b_rs = [Buf(n) for n in "gt mt tmp mb mo sq rs".split()]
                ysrc = [dr["yA"], dr["yB"], dr["yC"]]
                xview = xsrc.rearrange("(kc p) n -> p kc n", p=128)
                xmview = xmid.rearrange("(kc p) n -> p kc n", p=128)
                k = 0
                bgq = []
                for ti, (t0, n, seg) in enumerate(tiles):
                    xs = ti % 2
                    xt, b_xt = xts[xs], b_xts[xs]
                    self.dma("sp", xt[:], xview[:, :, t0:t0 + n], w=[b_xt])
                    for br in range(3):
                        self.dma("sp", yt[br][:], ysrc[br].rearrange("(kc p) n -> p kc n", p=128)[:, :, t0:t0 + n], w=[b_yt[br]])
                        g0 = (66 + br * 8) * 128
                        self.dma("sp", gt[:], uT[g0:g0 + 1024, :].rearrange("(kc p) n -> p kc n", p=128)[:, :, t0:t0 + n], w=[b_gt])
                        self.act(gt[:], gt[:], AF.Sigmoid, [b_gt], [b_gt])
                        for oc in range(KC):
                            pb = k % 5
                            k += 1
                            for kc in range(KC):
                                self.mm(psb[pb][:, 0:n], Wbr[:, br, kc, oc * 128:(oc + 1) * 128], yt[br][:, kc, :],
                                        start=(kc == 0), stop=(kc == KC - 1), r=[b_W, b_yt[br]], w=[b_ps[pb]])
                            if br == 0:
                                self.tt("dve", mt[:, oc, :], psb[pb][:, 0:n], gt[:, oc, :], ALU.mult, r=[b_ps[pb], b_gt], w=[b_mt])
                            else:
                                self.tt("dve", tmp[:], psb[pb][:, 0:n], gt[:, oc, :], ALU.mult, r=[b_ps[pb], b_gt], w=[b_tmp])
                                self.tt("pool", mt[:, oc, :], mt[:, oc, :], tmp[:], ALU.add, r=[b_mt, b_tmp], w=[b_mt])
                            self.em.replay(bgq, (len(bgq) + (3 - br) * KC - oc - 1) // ((3 - br) * KC - oc) if bgq else 0)
                    self.em.replay(bgq, 10 ** 9)
                    self.act(mb[:], mt[:], AF.Copy, [b_mt], [b_mb])

                    def epilogue(ti=ti, t0=t0, n=n, seg=seg, xt=xt, b_xt=b_xt):
                        kk_ = 0
                        for oc in range(KC):
                            pb = 5
                            for kc in range(KC):
                                self.mm(psb[pb][:, 0:n], Wo[:, kc, oc * 128:(oc + 1) * 128], mb[:, kc, :],
                                        start=(kc == 0), stop=(kc == KC - 1), r=[b_W, b_mb], w=[b_ps[pb]])
                            self.act(mo[:, oc, :], psb[pb][:, 0:n], AF.Copy, [b_ps[pb]], [b_mo])
                        self.rstd_tile(mo, b_mo, n, sq, b_sq, rs, b_rs, 6)
                        for oc in range(KC):
                            self.tt("dve", mo[:, oc, :], mo[:, oc, :], rs[:], ALU.mult, r=[b_mo, b_rs], w=[b_mo])
                            self.stt(xt[:, oc, :], mo[:, oc, :], cols[:, 0, oc, seg:seg + 1], xt[:, oc, :], ALU.mult, ALU.add,
                                     r=[b_mo, b_cols, b_xt], w=[b_xt])
                        self.dma("pool", xmview[:, :, t0:t0 + n], xt[:], r=[b_xt])
                        self.rstd_tile(xt, b_xt, n, sq, b_sq, rs, b_rs, 7)
                        for kc in range(KC):
                            self.tt("dve", mo[:, kc, :], xt[:, kc, :], rs[:], ALU.mult, r=[b_xt, b_rs, b_mo], w=[b_mo])
                            self.act(h2T[:, kc, t0:t0 + n], mo[:, kc, :], AF.Identity, [b_mo, b_cols, b_mod], [b_h2[ti]],
                                     scale=cols[:, 1, kc, seg:seg + 1], bias=mod[:, 24 + kc, seg:seg + 1])
                    bgq += self.em.captured(epilogue)
                self.em.replay(bgq, 10 ** 9)
                self.em.flush()
            with contextlib.ExitStack() as st:
                sb = lambda name, shape, dt=F32: self.sb(st, "f_" + name, shape, dt)
                cw = sb("cw", [128, 22, 3])
                cb = sb("cb", [128, 22])
                b_par = Buf("fpar")
                self.dma("sp", cw[:], dr["ffn_conv_w"][l], w=[b_par])
                self.dma("sp", cb[:], dr["ffn_conv_b"][l], w=[b_par])
                wf = [sb(f"wf{i}", [128, KC, 128]) for i in range(2)]
                wb = [sb(f"wb{i}", [128, KC, 128], BF16) for i in range(2)]
                b_wf, b_wb = mkbufs("fwf", 2), mkbufs("fwb", 2)
                gp = sb("gp", [128, NT + 6])
                gc = sb("gc", [128, NT])
                ast = sb("ast", [128, NT], BF16)
                b_gp, b_gc, b_ast = Buf("gp"), Buf("gc"), Buf("ast")
                self.memset("pool", gp[:], 0.0, w=[b_gp])
                wview = dr["ffn_w_in"][l].rearrange("(kc p) n -> p kc n", p=128)
                k = 0
                kw = 0
                alltiles = list(enumerate(tiles))
                h2tiles = self.cfg.tiles
                for jc in range(22):
                    for part in range(2):
                        s = kw % 2
                        kw += 1
                        c0 = part * DFF + jc * 128
                        self.dma("sp", wf[s][:], wview[:, :, c0:c0 + 128], w=[b_wf[s]])
                        self.cp("pool", wb[s][:], wf[s][:], r=[b_wf[s]], w=[b_wb[s]])
                        for (t0, n, seg) in h2tiles:
                            pb = k % 8
                            k += 1
                            hb = [b_h2[i] for i, (a, m, sg_) in alltiles if a < t0 + n and a + m > t0]
                            for kc in range(KC):
                                self.mm(psb[pb][:, 0:n], wb[s][:, kc, :], h2T[:, kc, t0:t0 + n], start=(kc == 0), stop=(kc == KC - 1),
                                        r=[b_wb[s]] + hb, w=[b_ps[pb]])
                            if part == 0:
                                p0 = t0 + 2 if t0 < LCTX else t0 + 4
                                self.act(gp[:, p0:p0 + n], psb[pb][:, 0:n], AF.Copy, [b_ps[pb]], [b_gp])
                            else:
                                self.tt("dve", ast[:, t0:t0 + n], psb[pb][:, 0:n], gc[:, t0:t0 + n], ALU.mult,
                                        r=[b_ps[pb], b_gc], w=[b_ast])
                        if part == 0:
                            for (d0, s0, n) in self.segs():
                                self.ts("dve", gc[:, d0:d0 + n], gp[:, s0 - 1:s0 - 1 + n], cw[:, jc, 0:1], cb[:, jc:jc + 1], ALU.mult, ALU.add,
                                        r=[b_gp, b_par], w=[b_gc])
                                for tap in (1, 2):
                                    self.stt(gc[:, d0:d0 + n], gp[:, s0 - 1 + tap:s0 - 1 + tap + n], cw[:, jc, tap:tap + 1], gc[:, d0:d0 + n],
                                             ALU.mult, ALU.add, r=[b_gp, b_par, b_gc], w=[b_gc])
                            self.act(gc[:], gc[:], AF.Silu, [b_gc], [b_gc])
                    self.dma("pool", aT[jc * 128:(jc + 1) * 128, :], ast[:], r=[b_ast])
                self.em.flush()
        with contextlib.ExitStack() as st:
            sb = lambda name, shape, dt=F32: self.sb(st, "g_" + name, shape, dt)
            ng = sb("ng", [128, 4, KC])
            cols = sb("cols", [128, KC, 2])
            wst = sb("wst", [128, 1024])
            Wf = sb("Wf", [128, 22, 1024], BF16)
            b_ng, b_cols, b_wst, b_W = Buf("ng"), Buf("cols"), Buf("wst"), Buf("Wf")
            self.dma("sp", ng[:], dr["norm_g"][l], w=[b_ng])
            for kc in range(KC):
                self.ts("dve", cols[:, kc, :], mod[:, 40 + kc, :], ng[:, 3, kc:kc + 1], None, ALU.mult, r=[b_mod, b_ng], w=[b_cols])
            for kc in range(22):
                self.dma("sp", wst[:], dr["ffn_w_out"][l, kc * 128:(kc + 1) * 128, :], w=[b_wst])
                self.cp("pool", Wf[:, kc, :], wst[:], r=[b_wst], w=[b_W])
            at = [sb(f"at{i}", [128, 22, 512], BF16) for i in range(2)]
            xt = [sb(f"xt{i}", [128, KC, 512]) for i in range(2)]
            fo = sb("fo", [128, KC, 512])
            sq = sb("sq", [128, KC, 512], BF16)
            rs = sb("rs", [128, 512])
            b_at, b_xt = mkbufs("at", 2), mkbufs("gxt", 2)
            b_fo, b_sq, b_rs = Buf("fo"), Buf("gsq"), Buf("grs")
            xmview = xmid.rearrange("(kc p) n -> p kc n", p=128)
            aview = aT.rearrange("(kc p) n -> p kc n", p=128)
            k = 0
            for ti, (t0, n, seg) in enumerate(cfg.tiles):
                if last and seg == 1:
                    continue
                s = ti % 2
                self.dma("sp", at[s][:, :, 0:n], aview[:, :, t0:t0 + n], w=[b_at[s]])
                self.dma("sp", xt[s][:, :, 0:n], xmview[:, :, t0:t0 + n], w=[b_xt[s]])
                for oc in range(KC):
                    pb = k % 6
                    k += 1
                    for kc in range(22):
                        self.mm(psb[pb][:, 0:n], Wf[:, kc, oc * 128:(oc + 1) * 128], at[s][:, kc, 0:n], start=(kc == 0), stop=(kc == 21),
                                r=[b_W, b_at[s]], w=[b_ps[pb]])
                    self.act(fo[:, oc, 0:n], psb[pb][:, 0:n], AF.Copy, [b_ps[pb]], [b_fo])
                self.rstd_tile(fo, b_fo, n, sq, b_sq, rs, b_rs, 6 + ti % 2)
                for oc in range(KC):
                    self.tt("dve", fo[:, oc, 0:n], fo[:, oc, 0:n], rs[:, 0:n], ALU.mult, r=[b_fo, b_rs], w=[b_fo])
                    self.stt(xt[s][:, oc, 0:n], fo[:, oc, 0:n], cols[:, oc, seg:seg + 1], xt[s][:, oc, 0:n], ALU.mult, ALU.add,
                             r=[b_fo, b_cols, b_xt[s]], w=[b_xt[s]])
                if last:
                    dst = dr["outT"].rearrange("(kc p) n -> p kc n", p=128)[:, :, t0 - LCTX:t0 - LCTX + n]
                else:
                    dst = dr["xcur"].rearrange("(kc p) n -> p kc n", p=128)[:, :, t0:t0 + n]
                self.dma("pool", dst, xt[s][:, :, 0:n], r=[b_xt[s]])
            self.em.flush()

    def phase_mod(self, l, cvec, ada_w, ada_b, mod, b_mod):
        em = self.em
        psb, b_ps = self.psb, self.b_ps
        with contextlib.ExitStack() as st:
            cv = self.sb(st, "cv", [128, KC, 2], F32)
            cs = self.sb(st, "cs", [128, KC, 2], BF16)
            sg = self.sb(st, "cv_sg", [128, KC, 2], F32)
            ab = self.sb(st, "ab", [128, 48], F32)
            wf = [self.sb(st, f"adaw_f{i}", [128, KC, 512], F32) for i in range(2)]
            wb = [self.sb(st, f"adaw_b{i}", [128, KC, 512], BF16) for i in range(2)]
            b_cv, b_cs, b_ab, b_sg = Buf("cv"), Buf("cs"), Buf("ab"), Buf("sg")
            b_wf, b_wb = mkbufs("wf", 2), mkbufs("wb", 2)
            em.dma("sp", lambda e: e.dma_start(out=cv[:], in_=cvec[:, :, :]), writes=[b_cv])
            em.dma("sp", lambda e: e.dma_start(out=ab[:], in_=ada_b[l]), writes=[b_ab])
            em.op("act", lambda e: e.activation(out=sg[:], in_=cv[:], func=AF.Sigmoid), reads=[b_cv], writes=[b_sg])
            em.op("dve", lambda e: e.tensor_tensor(out=cs[:], in0=cv[:], in1=sg[:], op=ALU.mult),
                  reads=[b_cv, b_sg], writes=[b_cs])
            wview = ada_w[l].rearrange("(kc p) n -> p kc n", p=128)
            for g in range(12):
                s = g % 2
                em.dma("sp", lambda e, s=s, g=g: e.dma_start(out=wf[s][:], in_=wview[:, :, g * 512:(g + 1) * 512]),
                       writes=[b_wf[s]])
                em.op("pool", lambda e, s=s: e.tensor_copy(out=wb[s][:], in_=wf[s][:]),
                      reads=[b_wf[s]], writes=[b_wb[s]])
                pb = g % 8
                for q in range(4):
                    i = g * 4 + q
                    for kc in range(KC):
                        em.op("pe", lambda e, s=s, q=q, kc=kc, pb=pb: e.matmul(
                            psb[pb][:, q * 2:q * 2 + 2], lhsT=wb[s][:, kc, q * 128:(q + 1) * 128],
                            rhs=cs[:, kc, :], start=(kc == 0), stop=(kc == KC - 1)),
                            reads=[b_wb[s], b_cs], writes=[b_ps[pb]], defer=(kc != KC - 1))
                for q in range(4):
                    i = g * 4 + q
                    em.op("dve", lambda e, q=q, i=i, pb=pb: e.tensor_scalar(
                        out=mod[:, i, :], in0=psb[pb][:, q * 2:q * 2 + 2], scalar1=ab[:, i:i + 1], scalar2=None,
                        op0=ALU.add), reads=[b_ps[pb], b_ab], writes=[b_mod])
            em.flush()

    def phase_norm_inproj(self, l, xc, norm_g, w_in, uT, mod, b_mod, ones_bf, b_ones):
        cfg = self.cfg
        em = self.em
        NT = cfg.NT
        psb, b_ps = self.psb, self.b_ps
        with contextlib.ExitStack() as st:
            hT = self.sb(st, "hT", [128, KC, NT], BF16)
            b_h = mkbufs("hT", len(cfg.tiles))
            ng = self.sb(st, "ng", [128, 4, KC], F32)
            A1 = self.sb(st, "A1", [128, KC, 2], F32)
            b_ng, b_A1 = Buf("ng"), Buf("A1")
            em.dma("sp", lambda e: e.dma_start(out=ng[:], in_=norm_g[l]), writes=[b_ng])
            for kc in range(KC):
                em.op("dve", lambda e, kc=kc: e.tensor_scalar(
                    out=A1[:, kc, :], in0=mod[:, 8 + kc, :], scalar1=1.0, scalar2=ng[:, 0, kc:kc + 1],
                    op0=ALU.add, op1=ALU.mult), reads=[b_mod, b_ng], writes=[b_A1])
            with contextlib.ExitStack() as st2:
                xt = [self.sb(st2, f"xt{i}", [128, KC, 512], F32) for i in range(2)]
                sq = [self.sb(st2, f"sq{i}", [128, KC, 512], BF16) for i in range(2)]
                rs = [self.sb(st2, f"rs{i}", [128, 512], F32) for i in range(2)]
                tmp = [self.sb(st2, f"tmp{i}", [128, KC, 512], F32) for i in range(2)]
                b_xt, b_sq, b_rs, b_tmp = mkbufs("xt", 2), mkbufs("sq", 2), mkbufs("rs", 2), mkbufs("tmp", 2)
                xview = xc.rearrange("(kc p) n -> p kc n", p=128)
                for ti, (t0, n, seg) in enumerate(cfg.tiles):
                    s = ti % 2
                    pb = ti % 8
                    em.dma("sp", lambda e, s=s, t0=t0, n=n: e.dma_start(out=xt[s][:, :, 0:n], in_=xview[:, :, t0:t0 + n]),
                           writes=[b_xt[s]])
                    em.op("act", lambda e, s=s, n=n: e.activation(out=sq[s][:, :, 0:n], in_=xt[s][:, :, 0:n], func=AF.Square),
                          reads=[b_xt[s]], writes=[b_sq[s]])
                    for kc in range(KC):
                        em.op("pe", lambda e, s=s, n=n, kc=kc, pb=pb: e.matmul(
                            psb[pb][:, 0:n], lhsT=ones_bf[:], rhs=sq[s][:, kc, 0:n], start=(kc == 0), stop=(kc == KC - 1)),
                            reads=[b_sq[s], b_ones], writes=[b_ps[pb]], defer=(kc != KC - 1))
                    em.op("act", lambda e, s=s, n=n, pb=pb: e.activation(
                        out=rs[s][:, 0:n], in_=psb[pb][:, 0:n], func=AF.Sqrt, scale=1.0 / D, bias=1e-6),
                        reads=[b_ps[pb]], writes=[b_rs[s]])
                    em.op("dve", lambda e, s=s, n=n: e.reciprocal(out=rs[s][:, 0:n], in_=rs[s][:, 0:n]),
                          reads=[b_rs[s]], writes=[b_rs[s]])
                    for kc in range(KC):
                        em.op("dve", lambda e, s=s, n=n, kc=kc: e.tensor_tensor(
                            out=tmp[s][:, kc, 0:n], in0=xt[s][:, kc, 0:n], in1=rs[s][:, 0:n], op=ALU.mult),
                            reads=[b_xt[s], b_rs[s]], writes=[b_tmp[s]])
                        em.op("act", lambda e, s=s, n=n, kc=kc, t0=t0, seg=seg: e.activation(
                            out=hT[:, kc, t0:t0 + n], in_=tmp[s][:, kc, 0:n], func=AF.Identity,
                            scale=A1[:, kc, seg:seg + 1], bias=mod[:, kc, seg:seg + 1]),
                            reads=[b_tmp[s], b_A1, b_mod], writes=[b_h[ti]])
                em.flush()
            with contextlib.ExitStack() as st2:
                wf = [self.sb(st2, f"wf{i}", [128, KC, 128], F32) for i in range(2)]
                wb = [self.sb(st2, f"wb{i}", [128, KC, 128], BF16) for i in range(2)]
                stg = [self.sb(st2, f"stg{i}", [128, NT], F32) for i in range(2)]
                b_wf, b_wb, b_stg = mkbufs("wf", 2), mkbufs("wb", 2), mkbufs("stg", 2)
                wview = w_in[l].rearrange("(kc p) n -> p kc n", p=128)
                k = 0
                for j in range(self.NCH):
                    s = j % 2
                    em.dma("sp", lambda e, s=s, j=j: e.dma_start(out=wf[s][:], in_=wview[:, :, j * 128:(j + 1) * 128]),
                           writes=[b_wf[s]])
                    em.op("pool", lambda e, s=s: e.tensor_copy(out=wb[s][:], in_=wf[s][:]),
                          reads=[b_wf[s]], writes=[b_wb[s]])
                    for ti, (t0, n, seg) in enumerate(cfg.tiles):
                        pb = k % 8
                        k += 1
                        for kc in range(KC):
                            em.op("pe", lambda e, s=s, n=n, kc=kc, pb=pb, t0=t0: e.matmul(
                                psb[pb][:, 0:n], lhsT=wb[s][:, kc, :], rhs=hT[:, kc, t0:t0 + n],
                                start=(kc == 0), stop=(kc == KC - 1)),
                                reads=[b_wb[s], b_h[ti]], writes=[b_ps[pb]], defer=(kc != KC - 1))
                        if k % 2 == 0:
                            em.op("act", lambda e, s=s, n=n, pb=pb, t0=t0: e.activation(
                                out=stg[s][:, t0:t0 + n], in_=psb[pb][:, 0:n], func=AF.Copy),
                                reads=[b_ps[pb]], writes=[b_stg[s]])
                        else:
                            em.op("dve", lambda e, s=s, n=n, pb=pb, t0=t0: e.tensor_copy(
                                out=stg[s][:, t0:t0 + n], in_=psb[pb][:, 0:n]),
                                reads=[b_ps[pb]], writes=[b_stg[s]])
                    em.dma("pool", lambda e, s=s, j=j: e.dma_start(out=uT[j * 128:(j + 1) * 128, :], in_=stg[s][:]),
                           reads=[b_stg[s]], writes=[])
                em.flush()


def na_blocks(rows):
    nqb = rows // 8
    types = {}
    blocks = []
    for qb in range(nqb):
        q0 = qb * 8
        rs = lambda r: min(max(r - 4, 0), rows - 8)
        lo, hi = rs(q0), rs(q0 + 7) + 8
        cls = "f" if qb == 0 else ("l" if qb == nqb - 1 else "i")
        items = []
        for kr0 in range(lo, hi, 2):
            delta = kr0 - q0
            key = (cls, delta)
            if key not in types:
                types[key] = (len(types), q0, kr0)
            items.append((kr0, types[key][0], (delta + 4) // 2))
        blocks.append(items)
    return blocks, len(types)


def blocks_type(blocks, qb, kr0):
    for (k, ty, di) in blocks[qb]:
        if k == kr0:
            return ty
    raise KeyError


def na_mask_np(rows):
    nqb = rows // 8
    out = {}
    for qb in range(nqb):
        q0 = qb * 8
        rsf = lambda r: min(max(r - 4, 0), rows - 8)
        lo, hi = rsf(q0), rsf(q0 + 7) + 8
        cls = "f" if qb == 0 else ("l" if qb == nqb - 1 else "i")
        for kr0 in range(lo, hi, 2):
            key = (cls, kr0 - q0)
            if key in out:
                continue
            krow = kr0 + np.arange(2)[:, None, None, None]
            kc = np.arange(64)[None, :, None, None]
            qrow = q0 + np.arange(8)[None, None, :, None]
            qc = np.arange(64)[None, None, None, :]
            rs = np.clip(qrow - 4, 0, rows - 8)
            cs = np.clip(qc - 8, 0, 48)
            ok = (krow >= rs) & (krow < rs + 8) & (kc >= cs) & (kc < cs + 16)
            out[key] = np.where(ok, 0.0, -30000.0).reshape(128, 512).astype(np.float32)
    return np.stack(list(out.values()), axis=0)


def rope_tables(T):
    half = 32
    freqs = 10000.0 ** (-np.arange(0, half, 2, dtype=np.float32) / half)
    pos = np.arange(T)
    prow, pcol = pos // 64, pos % 64
    cos = np.zeros((64, T), np.float32)
    sin = np.zeros((64, T), np.float32)
    for d in range(64):
        p = prow if d < 32 else pcol
        dd = d % 32
        ang = p.astype(np.float32) * freqs[dd % 16]
        cos[d] = np.cos(ang)
        sin[d] = -np.sin(ang) if dd < 16 else np.sin(ang)
    return np.concatenate([cos, cos], 0), np.concatenate([sin, sin], 0)


def input_shapes(cfg):
    Ld, NT, T = cfg.depth, cfg.NT, cfg.T
    return {
        "xc": [D, NT], "cvec": [128, KC, 2],
        "ada_w": [Ld, D, 6 * D], "ada_b": [Ld, 128, 48], "norm_g": [Ld, 128, 4, KC],
        "w_in": [Ld, D, NIN_X],
        "lru_conv_w": [Ld, 128, 8, 4], "lru_conv_b": [Ld, 128, 8],
        "lru_gate_a_w": [Ld, 2, 16, 64, 64], "lru_gate_x_w": [Ld, 2, 16, 64, 64],
        "lru_gate_b": [Ld, 128, 2, 2, 8], "lru_lambda": [Ld, 128, 2, 8],
        "ident": [128, 128], "rope_cos": [128, T], "rope_sin": [128, T],
        "na_mask": [na_blocks(T // 64)[1], 128, 512], "rpb_pad": [Ld, 16, 24, 128],
        "bdones": [128, 128], "istack": [128, 64], "rk_mask": [2, 128, 1152],
        "rwkv_mu": [Ld, 128, 26, 2], "rwkv_w0a0": [Ld, 128, 2, 2, 8], "rwkv_vec": [Ld, 128, 5, 8],
        "w_branch": [Ld, 3, D, D], "w_out": [Ld, D, D], "ffn_w_in": [Ld, D, 2 * DFF], "ffn_w_out": [Ld, DFF, D],
        "ffn_conv_w": [Ld, 128, 22, 3], "ffn_conv_b": [Ld, 128, 22],
        "rwkv_w_up": [Ld, 2, 64, 1024], "rwkv_a_up": [Ld, 2, 64, 1024], "rwkv_g_up": [Ld, 128, 1024],
    }


def colfmt(v, n):
    v = np.asarray(v)
    return np.moveaxis(v.reshape(v.shape[:-1] + (n, 128)), -1, -2)


def rope_perm():
    idx = np.arange(1024)
    d = idx % 64
    dd = d % 32
    partner = np.where(dd < 16, idx + 16, idx - 16)
    return partner


def prep_shared(inp, cfg):
    f = lambda a: np.ascontiguousarray(a, dtype=np.float32)
    Ld = cfg.depth
    m = {}
    m["ada_w"] = f(inp["ada_w"][:Ld])
    m["ada_b"] = f(colfmt(inp["ada_b"][:Ld], 48))
    m["norm_g"] = f(colfmt(inp["norm_g"][:Ld], KC).transpose(0, 2, 1, 3))
    w_in = inp["w_in"][:Ld]
    C0 = A_COLS + 2048
    perm = rope_perm()
    wq = w_in[:, :, C0:C0 + 1024][:, :, perm]
    wk = w_in[:, :, C0 + 1024:C0 + 2048][:, :, perm]
    m["w_in"] = f(np.concatenate([w_in, wq, wk], axis=2))
    m["lru_conv_w"] = f(colfmt(inp["lru_conv_w"][:Ld], 8).transpose(0, 2, 3, 1))
    m["lru_conv_b"] = f(colfmt(inp["lru_conv_b"][:Ld], 8))
    m["lru_gate_a_w"] = f(inp["lru_gate_a_w"][:Ld])
    m["lru_gate_x_w"] = f(inp["lru_gate_x_w"][:Ld])
    gb = np.stack([inp["lru_gate_a_b"][:Ld], inp["lru_gate_x_b"][:Ld]], axis=1)
    m["lru_gate_b"] = f(colfmt(gb, 8).transpose(0, 3, 1, 2, 4))
    m["lru_lambda"] = f(colfmt(inp["lru_lambda"][:Ld], 8).transpose(0, 2, 1, 3))
    m["ident"] = np.eye(128, dtype=np.float32)
    cos, sin = rope_tables(cfg.T)
    m["rope_cos"], m["rope_sin"] = f(cos), f(sin)
    m["na_mask"] = f(na_mask_np(cfg.T // 64))
    rp = np.zeros((Ld, 16, 24, 128), np.float32)
    rp[:, :, 4:19, 48:79] = inp["na_rpb"][:Ld]
    m["rpb_pad"] = rp
    blk = np.kron(np.eye(2, dtype=np.float32), np.ones((64, 64), np.float32))
    m["bdones"] = blk
    m["istack"] = np.concatenate([np.eye(64, dtype=np.float32)] * 2, axis=0)
    i64 = np.arange(64)
    U = np.kron(np.eye(2), (i64[:, None] < i64[None, :])).astype(np.float32)
    UI = np.kron(np.eye(2), (i64[:, None] <= i64[None, :])).astype(np.float32)
    Lw, LI = U.T.copy(), UI.T.copy()
    ONE = np.ones((128, 128), np.float32)
    m32 = np.kron(np.eye(4), np.ones((32, 32))).astype(np.float32)
    fwd = np.concatenate([U * m32, UI, ONE, ONE, UI, ONE, Lw * m32, Lw, Lw * (1 - m32)], axis=1)
    bwd = np.concatenate([Lw * m32, LI, ONE, ONE, LI, ONE, U * m32, U, U * (1 - m32)], axis=1)
    m["rk_mask"] = np.stack([fwd, bwd], axis=0)
    m["rwkv_mu"] = f(colfmt(inp["rwkv_mu"][:Ld], 26).transpose(0, 2, 3, 1))
    w0a0 = np.stack([inp["rwkv_w0"][:Ld], inp["rwkv_a0"][:Ld]], axis=1)
    m["rwkv_w0a0"] = f(colfmt(w0a0, 8).transpose(0, 3, 1, 2, 4))
    vec = np.stack([inp["rwkv_k_k"][:Ld], inp["rwkv_k_a"][:Ld], inp["rwkv_r_k"][:Ld].reshape(Ld, 1024),
                    inp["rwkv_lnx_w"][:Ld], inp["rwkv_lnx_b"][:Ld]], axis=1)
    m["rwkv_vec"] = f(colfmt(vec, 8).transpose(0, 2, 1, 3))
    m["rwkv_w_up"] = f(inp["rwkv_w_up"][:Ld])
    m["rwkv_a_up"] = f(inp["rwkv_a_up"][:Ld])
    m["rwkv_g_up"] = f(inp["rwkv_g_up"][:Ld])
    for k in ("w_branch", "w_out", "ffn_w_in", "ffn_w_out"):
        m[k] = f(inp[k][:Ld])
    m["ffn_conv_w"] = f(colfmt(inp["ffn_conv_w"][:Ld], 22).transpose(0, 2, 3, 1))
    m["ffn_conv_b"] = f(colfmt(inp["ffn_conv_b"][:Ld], 22))
    return m


def prep_inputs(inp, b, cfg, shared=None):
    T = cfg.T
    f = lambda a: np.ascontiguousarray(a, dtype=np.float32)
    m = dict(shared if shared is not None else prep_shared(inp, cfg))
    m["xc"] = f(np.concatenate([inp["ctx"][b].T, inp["x"][b, :T].T], axis=1))
    cv = np.stack([inp["c"][b], inp["c_ctx"]], axis=1)
    m["cvec"] = f(cv.reshape(KC, 128, 2).transpose(1, 0, 2))
    return m


def kernel(**inputs):
    cfg = Cfg()
    bld = Builder(cfg)
    nc = bld.build()
    inp = {k: np.asarray(v) for k, v in inputs.items()}
    shared = prep_shared(inp, cfg)
    in_maps = [prep_inputs(inp, b, cfg, shared) for b in range(8)]
    res = run_bass_kernel_spmd(nc, in_maps, core_ids=list(range(8)))
    out = np.stack([r["outT"].T for r in res.results], axis=0)
    return out.astype(np.float32)
```

```python
import contextlib
import numpy as np
import concourse.bass as bass
import concourse.mybir as mybir
from concourse.bass_utils import run_bass_kernel_spmd

F32 = mybir.dt.float32
BF16 = mybir.dt.bfloat16
AF = mybir.ActivationFunctionType
ALU = mybir.AluOpType
AX = mybir.AxisListType

D = 1024
KC = 8
LCTX = 256
DFF = 2816
NHEAD = 16
A_COLS = 3 * 1024 + 256
N_IN = 11520
NIN_X = N_IN + 2048
ENGS = ("pe", "act", "dve", "pool", "sp")
NDMA_SEMS = 24
NODEFER = False
CHECK_DEADLOCK = True


class Buf:
    __slots__ = ("name", "w", "readers")

    def __init__(self, name):
        self.name = name
        self.w = None
        self.readers = []


def mkbufs(name, n):
    return [Buf(f"{name}{i}") for i in range(n)]


class Emit:
    def __init__(self, nc, stack):
        self.nc = nc
        self.prog = {e: [] for e in ENGS}
        self.count = {e: 0 for e in ENGS}
        self.seen = {e: {} for e in ENGS}
        self.pend_inc = {}
        self.capture_list = None
        self.dma_total = [0] * NDMA_SEMS
        self.dma_rr = 0
        self.n_instr = 0
        self.sems = {}
        for e in ENGS:
            self.sems[e] = stack.enter_context(nc.semaphore(f"c_{e}"))
        for k in range(NDMA_SEMS):
            self.sems[("dma", k)] = stack.enter_context(nc.semaphore(f"d_{k}"))

    def _deps(self, eng, reads, writes):
        deps = {}

        def add(d):
            if d is None:
                return
            k, v = d
            if deps.get(k, 0) < v:
                deps[k] = v
        for b in reads:
            add(b.w)
        for b in writes:
            add(b.w)
            for r in b.readers:
                add(r)
        waits = []
        seen = self.seen[eng]
        for k, v in deps.items():
            if k == eng and v > self.count[eng]:
                continue
            if seen.get(k, 0) < v:
                seen[k] = v
                waits.append((k, v))
        return waits

    def _commit(self, me, reads, writes):
        for b in reads:
            b.readers.append(me)
            if len(b.readers) > 32:
                mx = {}
                for k, v in b.readers:
                    if mx.get(k, 0) < v:
                        mx[k] = v
                b.readers = list(mx.items())
        for b in writes:
            b.w = me
            b.readers = []

    def op(self, eng, fn, reads=(), writes=(), defer=False):
        if self.capture_list is not None:
            self.capture_list.append(("op", eng, fn, reads, writes, defer))
            return
        waits = self._deps(eng, reads, writes)
        if defer and not NODEFER:
            me = (eng, self.count[eng] + 1)
            self.pend_inc[eng] = 1
            self.prog[eng].append((waits, fn, None))
        else:
            self.count[eng] += 1
            me = (eng, self.count[eng])
            self.prog[eng].append((waits, fn, (eng, 1)))
            self.pend_inc[eng] = 0
        self._commit(me, reads, writes)
        self.n_instr += 1 + len(waits)

    def dma(self, q, fn, reads=(), writes=()):
        if self.capture_list is not None:
            self.capture_list.append(("dma", q, fn, reads, writes))
            return
        k = self.dma_rr
        self.dma_rr = (self.dma_rr + 1) % NDMA_SEMS
        key = ("dma", k)
        waits = self._deps(q, reads, writes)
        prev = self.dma_total[k]
        if prev > 0 and self.seen[q].get(key, 0) < prev:
            self.seen[q][key] = prev
            waits.append((key, prev))
        self.dma_total[k] += 16
        me = (key, self.dma_total[k])
        self.prog[q].append((waits, fn, (key, 16)))
        self._commit(me, reads, writes)
        self.n_instr += 1 + len(waits)

    def captured(self, f):
        lst = []
        self.capture_list = lst
        f()
        self.capture_list = None
        return lst

    def replay(self, lst, k):
        while k > 0 and lst:
            e = lst.pop(0)
            if e[0] == "op":
                self.op(*e[1:])
            else:
                self.dma(*e[1:])
            k -= 1

    def check_deadlock(self):
        val = dict(getattr(self, "_simval", {}))
        pos = {e: 0 for e in ENGS}
        progress = True
        while progress:
            progress = False
            for e in ENGS:
                q = self.prog[e]
                while pos[e] < len(q):
                    waits, fn, inc = q[pos[e]]
                    if all(val.get(k, 0) >= v for k, v in waits):
                        if inc is not None:
                            val[inc[0]] = val.get(inc[0], 0) + inc[1]
                        pos[e] += 1
                        progress = True
                    else:
                        break
        stuck = {e: pos[e] for e in ENGS if pos[e] < len(self.prog[e])}
        if stuck:
            for e, p in stuck.items():
                waits, fn, inc = self.prog[e][p]
                print("DEADLOCK: engine", e, "stuck at", p, "/", len(self.prog[e]), "waits",
                      [(k, v, val.get(k, 0)) for k, v in waits if val.get(k, 0) < v])
            raise RuntimeError("deadlock detected in emitted program")
        self._simval = val

    def flush(self):
        assert all(v == 0 for v in self.pend_inc.values()), "deferred semaphore increment left dangling"
        if CHECK_DEADLOCK:
            self.check_deadlock()
        nc = self.nc
        prog = self.prog
        sems = self.sems
        dma_fin = [(("dma", k), v) for k, v in enumerate(self.dma_total) if v > 0]

        def run(name, eng):
            for waits, fn, inc in prog[name]:
                for k, v in waits:
                    eng.wait_ge(sems[k], v)
                ins = fn(eng)
                if inc is not None:
                    ins.then_inc(sems[inc[0]], inc[1])
            if name in ("sp", "pool", "act"):
                for k, v in dma_fin:
                    eng.wait_ge(sems[k], v)

        with nc.Block() as block:
            @block.tensor
            def _(t):
                run("pe", t)

            @block.scalar
            def _(a):
                run("act", a)

            @block.vector
            def _(v):
                run("dve", v)

            @block.gpsimd
            def _(g):
                run("pool", g)

            @block.sync
            def _(s):
                run("sp", s)
        for k, v in dma_fin:
            for e in ENGS:
                self.seen[e][k] = v
        self.prog = {e: [] for e in ENGS}


class Cfg:
    def __init__(self, T=4096, depth=2, dbg=False):
        self.T = T
        self.NT = LCTX + T
        self.depth = depth
        self.dbg = dbg
        self.phases = ("lru", "na", "rwkv", "merge", "ffn")
        self.tiles = [(0, LCTX, 1)] + [(LCTX + 512 * i, 512, 0) for i in range(T // 512)]


class Builder:
    def __init__(self, cfg):
        self.cfg = cfg
        self.nc = bass.Bass("TRN2", target_bir_lowering=False)
        self.dram = {}

    def din(self, name, shape, dt=F32):
        t = self.nc.dram_tensor(name, list(shape), dt, kind="ExternalInput").ap()
        self.dram[name] = t
        return t

    def dscratch(self, name, shape, dt=F32, out=False):
        kind = "ExternalOutput" if (out or self.cfg.dbg) else "Internal"
        t = self.nc.dram_tensor(name, list(shape), dt, kind=kind).ap()
        self.dram[name] = t
        return t

    def sb(self, st, name, shape, dt):
        self._uid = getattr(self, "_uid", 0) + 1
        return st.enter_context(self.nc.sbuf_tensor(f"{name}_{self._uid}", list(shape), dt))

    def ps(self, st, name, shape, dt=F32):
        return st.enter_context(self.nc.psum_tensor(name, list(shape), dt))

    def build(self, upto=99):
        cfg = self.cfg
        nc = self.nc
        NT, T = cfg.NT, cfg.T
        Ld = cfg.depth
        self.NCH = NIN_X // 128
        for name, shape in input_shapes(cfg).items():
            self.din(name, shape)
        self.dscratch("uT", [NIN_X, NT])
        self.dscratch("yA", [D, NT], BF16)
        self.dscratch("yB", [D, NT], BF16)
        self.dscratch("yC", [D, NT], BF16)
        self.dscratch("aT", [DFF, NT], BF16)
        self.dscratch("xcur", [D, NT])
        self.dscratch("xmid", [D, NT])
        self.dscratch("outT", [D, T], out=True)
        dr = self.dram
        with contextlib.ExitStack() as outer:
            em = Emit(nc, outer)
            self.em = em
            mod = self.sb(outer, "mod", [128, 48, 2], F32)
            ones_bf = self.sb(outer, "ones_bf", [128, 128], BF16)
            b_mod = Buf("mod")
            b_ones = Buf("ones")
            self.mod, self.b_mod, self.ones_bf, self.b_ones = mod, b_mod, ones_bf, b_ones
            em.op("pool", lambda e: e.memset(ones_bf[:], 1.0), writes=[b_ones])
            psb = [self.ps(outer, f"psb{i}", [128, 512]) for i in range(8)]
            b_ps = mkbufs("ps", 8)
            self.psb, self.b_ps = psb, b_ps
            for l in range(Ld):
                xsrc = dr["xc"] if l == 0 else dr["xcur"]
                self.phase_mod(l, dr["cvec"], dr["ada_w"], dr["ada_b"], mod, b_mod)
                self.phase_norm_inproj(l, xsrc, dr["norm_g"], dr["w_in"], dr["uT"], mod, b_mod, ones_bf, b_ones)
                if upto <= 2:
                    break
                if "lru" in cfg.phases:
                    self.phase_lru(l)
                if "na" in cfg.phases:
                    self.phase_na(l)
                if "rwkv" in cfg.phases:
                    self.phase_rwkv(l)
                if "merge" in cfg.phases:
                    self.phase_merge_ffn(l, xsrc)
        return nc

    def mm(self, out, lhsT, rhs, start=True, stop=True, r=(), w=(), defer=None):
        if defer is None:
            defer = not stop
        self.em.op("pe", lambda e: e.matmul(out, lhsT=lhsT, rhs=rhs, start=start, stop=stop), r, w, defer=defer)

    def act(self, out, in_, func, r=(), w=(), scale=1.0, bias=0.0):
        self.em.op("act", lambda e: e.activation(out=out, in_=in_, func=func, scale=scale, bias=bias), r, w)

    def tt(self, eng, out, in0, in1, op, r=(), w=()):
        self.em.op(eng, lambda e: e.tensor_tensor(out=out, in0=in0, in1=in1, op=op), r, w)

    def ts(self, eng, out, in0, s1, s2, op0, op1=None, r=(), w=()):
        if op1 is None:
            self.em.op(eng, lambda e: e.tensor_scalar(out=out, in0=in0, scalar1=s1, scalar2=None, op0=op0), r, w)
        else:
            self.em.op(eng, lambda e: e.tensor_scalar(out=out, in0=in0, scalar1=s1, scalar2=s2, op0=op0, op1=op1), r, w)

    def stt(self, out, in0, sc, in1, op0, op1, r=(), w=()):
        self.em.op("dve", lambda e: e.scalar_tensor_tensor(out=out, in0=in0, scalar=sc, in1=in1, op0=op0, op1=op1), r, w)

    def cp(self, eng, out, in_, r=(), w=()):
        self.em.op(eng, lambda e: e.tensor_copy(out=out, in_=in_), r, w)

    def memset(self, eng, ap, val, w=()):
        self.em.op(eng, lambda e: e.memset(ap, val), (), w)

    def scan(self, out, d0, d1, init, r=(), w=()):
        self.em.op("dve", lambda e: e.tensor_tensor_scan(out=out, data0=d0, data1=d1, initial=init, op0=ALU.mult, op1=ALU.add), r, w)

    def dma(self, q, out, in_, r=(), w=()):
        self.em.dma(q, lambda e: e.dma_start(out=out, in_=in_), r, w)

    def segs(self):
        return [(0, 2, LCTX), (LCTX, LCTX + 4, self.cfg.T)]

    def phase_lru(self, l):
        cfg = self.cfg
        NT, T = cfg.NT, cfg.T
        psb, b_ps = self.psb, self.b_ps
        dr = self.dram
        uT, yB = dr["uT"], dr["yB"]
        B0 = A_COLS
        with contextlib.ExitStack() as st:
            cw = self.sb(st, "l_cw", [128, 8, 4], F32)
            cb = self.sb(st, "l_cb", [128, 8], F32)
            gab = self.sb(st, "l_gab", [128, 2, 2, 8], F32)
            lam = self.sb(st, "l_lam", [128, 2, 8], F32)
            cl = self.sb(st, "l_cl", [128, 2, 8], F32)
            b_par, b_cl = Buf("lpar"), Buf("lcl")
            self.dma("sp", cw[:], dr["lru_conv_w"][l], w=[b_par])
            self.dma("sp", cb[:], dr["lru_conv_b"][l], w=[b_par])
            self.dma("sp", gab[:], dr["lru_gate_b"][l], w=[b_par])
            self.dma("sp", lam[:], dr["lru_lambda"][l], w=[b_par])
            self.act(cl[:], lam[:], AF.Exp, [b_par], [b_cl], scale=-1.0)
            self.act(cl[:], cl[:], AF.Ln, [b_cl], [b_cl], bias=1.0)
            self.ts("dve", cl[:], cl[:], -8.0, None, ALU.mult, r=[b_cl], w=[b_cl])
            xp = self.sb(st, "l_xp", [128, NT + 6], F32)
            xb = self.sb(st, "l_xb", [128, NT], F32)
            xbb = self.sb(st, "l_xbb", [128, NT], BF16)
            gt = self.sb(st, "l_gt", [128, NT], F32)
            gtb = self.sb(st, "l_gtb", [128, NT], BF16)
            A = self.sb(st, "l_A", [128, NT], F32)
            Bt = self.sb(st, "l_B", [128, NT], F32)
            Ct = self.sb(st, "l_C", [128, NT], F32)
            hf = self.sb(st, "l_hf", [128, NT], F32)
            ys = self.sb(st, "l_ys", [128, NT], BF16)
            wgf = [self.sb(st, f"l_wgf{i}", [128, 128], F32) for i in range(2)]
            wgb = [self.sb(st, f"l_wgb{i}", [128, 128], BF16) for i in range(2)]
            b_xp, b_xb, b_xbb, b_gt, b_gtb, b_A, b_B, b_C, b_hf, b_ys = [Buf(n) for n in
                "xp xb xbb gt gtb A B C hf ys".split()]
            b_wgf, b_wgb = mkbufs("wgf", 2), mkbufs("wgb", 2)
            self.memset("pool", xp[:], 0.0, w=[b_xp])
            for i in range(2):
                self.memset("pool", wgf[i][:], 0.0, w=[b_wgf[i]])
            gw = [dr["lru_gate_a_w"], dr["lru_gate_x_w"]]

            def rev(ap):
                aps = [list(p) for p in ap.ap]
                n, stp = aps[-1][1], aps[-1][0]
                aps[-1] = [-stp, n]
                return bass.AP(ap.tensor, ap.offset + stp * (n - 1), aps)
            k = 0
            for j in range(8):
                for (d0, s0, n) in self.segs():
                    self.dma("sp", xp[:, s0:s0 + n], uT[B0 + j * 128:B0 + (j + 1) * 128, d0:d0 + n], w=[b_xp])
                self.dma("sp", gt[:], uT[B0 + 1024 + j * 128:B0 + 1024 + (j + 1) * 128, :], w=[b_gt])
                for (d0, s0, n) in self.segs():
                    self.ts("dve", xb[:, d0:d0 + n], xp[:, s0 - 2:s0 - 2 + n], cw[:, j, 0:1], cb[:, j:j + 1], ALU.mult, ALU.add,
                            r=[b_xp, b_par], w=[b_xb])
                    for tap in range(1, 4):
                        self.stt(xb[:, d0:d0 + n], xp[:, s0 - 2 + tap:s0 - 2 + tap + n], cw[:, j, tap:tap + 1], xb[:, d0:d0 + n],
                                 ALU.mult, ALU.add, r=[b_xp, b_par, b_xb], w=[b_xb])
                self.cp("pool", xbb[:], xb[:], r=[b_xb], w=[b_xbb])
                self.act(gtb[:], gt[:], AF.Gelu_apprx_tanh, [b_gt], [b_gtb])
                for d in range(2):
                    for g in range(2):
                        for hb in range(2):
                            self.dma("sp", wgf[g][hb * 64:(hb + 1) * 64, hb * 64:(hb + 1) * 64], gw[g][l, d, 2 * j + hb],
                                     w=[b_wgf[g]])
                        self.cp("pool", wgb[g][:], wgf[g][:], r=[b_wgf[g]], w=[b_wgb[g]])
                    for g, (dst, b_dst) in enumerate([(A, b_A), (Bt, b_B)]):
                        for (t0, n, seg) in cfg.tiles:
                            pb = k % 8
                            k += 1
                            self.mm(psb[pb][:, 0:n], wgb[g][:], xbb[:, t0:t0 + n], r=[b_wgb[g], b_xbb], w=[b_ps[pb]])
                            self.act(dst[:, t0:t0 + n], psb[pb][:, 0:n], AF.Sigmoid, [b_ps[pb], b_par], [b_dst],
                                     bias=gab[:, g, d, j:j + 1])
                    self.act(A[:], A[:], AF.Exp, [b_A, b_cl], [b_A], scale=cl[:, d, j:j + 1])
                    self.tt("dve", Ct[:], A[:], A[:], ALU.mult, r=[b_A], w=[b_C])
                    self.act(Ct[:], Ct[:], AF.Sqrt, [b_C], [b_C], scale=-1.0, bias=1.0)
                    self.tt("pool", Bt[:], Bt[:], xb[:], ALU.mult, r=[b_B, b_xb], w=[b_B])
                    self.tt("dve", Ct[:], Ct[:], Bt[:], ALU.mult, r=[b_C, b_B], w=[b_C])
                    if d == 0:
                        self.scan(hf[:], A[:], Ct[:], 0.0, r=[b_A, b_C], w=[b_hf])
                    else:
                        for (d0, s0, n) in self.segs():
                            self.cp("pool", Bt[:, d0:d0 + n], rev(A[:, d0:d0 + n]), r=[b_A], w=[b_B])
                            self.cp("pool", gt[:, d0:d0 + n], rev(Ct[:, d0:d0 + n]), r=[b_C], w=[b_gt])
                        self.scan(A[:], Bt[:], gt[:], 0.0, r=[b_B, b_gt, b_A], w=[b_A])
                        for (d0, s0, n) in self.segs():
                            self.cp("pool", Ct[:, d0:d0 + n], rev(A[:, d0:d0 + n]), r=[b_A], w=[b_C])
                        self.tt("dve", hf[:], hf[:], Ct[:], ALU.add, r=[b_hf, b_C], w=[b_hf])
                self.tt("dve", ys[:], hf[:], gtb[:], ALU.mult, r=[b_hf, b_gtb], w=[b_ys])
                self.dma("pool", yB[j * 128:(j + 1) * 128, :], ys[:], r=[b_ys])
            self.em.flush()

    def phase_na(self, l):
        cfg = self.cfg
        NT, T = cfg.NT, cfg.T
        psb, b_ps = self.psb, self.b_ps
        dr = self.dram
        uT, yC = dr["uT"], dr["yC"]
        update_ctx = (l < cfg.depth - 1) or getattr(cfg, 'force_ctx', False)
        rows = T // 64
        blocks, ntype = na_blocks(rows)
        NTB = NT // 128
        CQ, CK, CV, CQP, CKP = 42, 50, 58, 90, 98
        rp = dr["rpb_pad"]
        with contextlib.ExitStack() as st:
            ident = self.sb(st, "n_ident", [128, 128], F32)
            onesp = self.sb(st, "n_onesp", [128, 2, 128], BF16)
            maskb = self.sb(st, "n_maskb", [128, ntype, 512], BF16)
            mtmp = [self.sb(st, f"n_mtmp{i}", [128, 512], F32) for i in range(2)]
            b_id, b_op, b_mk = Buf("ident"), Buf("onesp"), Buf("maskb")
            b_mt = mkbufs("mtmp", 2)
            self.dma("sp", ident[:], dr["ident"][:, :], w=[b_id])
            self.memset("pool", onesp[:], 0.0, w=[b_op])
            self.memset("pool", onesp[:, 0, 0:64], 1.0, w=[b_op])
            self.memset("pool", onesp[:, 1, 64:128], 1.0, w=[b_op])
            for t in range(ntype):
                self.dma("sp", mtmp[t % 2][:], dr["na_mask"][t], w=[b_mt[t % 2]])
                self.cp("pool", maskb[:, t, :], mtmp[t % 2][:], r=[b_mt[t % 2]], w=[b_mk])
            NIN = 7
            tl = [[self.sb(st, f"n_tl{a}_{i}", [128, 512], F32) for i in range(2)] for a in range(NIN)]
            b_tl = [mkbufs(f"tl{a}_", 2) for a in range(NIN)]
            qpl = self.sb(st, "n_qpl", [128, NT], BF16)
            kpl = self.sb(st, "n_kpl", [128, 2, LCTX], BF16)
            qrot = self.sb(st, "n_qrot", [128, T], BF16)
            krot = self.sb(st, "n_krot", [128, 2, T], BF16)
            Vp = self.sb(st, "n_Vp", [128, NTB, 2, 128], BF16)
            Tc2 = self.sb(st, "n_Tc2", [128, 22 * 64], F32)
            biasd = self.sb(st, "n_biasd", [128, 8, 512], F32)
            bm = [self.sb(st, f"n_bm{i}", [128, ntype, 512], BF16) for i in range(2)]
            sT = [self.sb(st, f"n_sT{i}", [128, 512], F32) for i in range(2)]
            pT = [self.sb(st, f"n_pT{i}", [128, 512], BF16) for i in range(3)]
            rc = [self.sb(st, f"n_rc{i}", [128, 512], F32) for i in range(2)]
            yst = self.sb(st, "n_yst", [128, NT], BF16)
            b_qpl, b_kpl, b_qrot, b_krot, b_Vp, b_Tc2, b_biasd, b_yst = [Buf(n) for n in
                "qpl kpl qrot krot Vp Tc2 biasd yst".split()]
            b_bm, b_sT, b_pT, b_rc = mkbufs("bm", 2), mkbufs("sT", 2), mkbufs("pT", 3), mkbufs("rc", 2)
            self.memset("pool", Vp[:], 0.0, w=[b_Vp])
            self.memset("pool", yst[:], 0.0, w=[b_yst])
            self.memset("pool", kpl[:], 0.0, w=[b_kpl])
            self.memset("pool", krot[:], 0.0, w=[b_krot])

            def rev(ap):
                aps = [list(p) for p in ap.ap]
                n, stp = aps[-1][1], aps[-1][0]
                aps[-1] = [-stp, n]
                return bass.AP(ap.tensor, ap.offset + stp * (n - 1), aps)
            kq = 0
            ks = 0
            kp_ = 0
            kacc = 0
            for j in range(8):
                for ti, (t0, n, seg) in enumerate(cfg.tiles):
                    s = ti % 2
                    rowsrc = [CQ + j, CQP + j, CK + j, CKP + j, CV + j]
                    need = [0, 2, 4] if seg == 1 else [0, 1, 2, 3, 4]
                    for a in need:
                        c = rowsrc[a]
                        self.dma("sp", tl[a][s][:, 0:n], uT[c * 128:(c + 1) * 128, t0:t0 + n], w=[b_tl[a][s]])
                    if seg == 1:
                        self.act(qpl[:, t0:t0 + n], tl[0][s][:, 0:n], AF.Copy, [b_tl[0][s]], [b_qpl])
                        self.act(kpl[0:64, 0, 0:n], tl[2][s][0:64, 0:n], AF.Copy, [b_tl[2][s]], [b_kpl])
                        self.act(kpl[64:128, 1, 0:n], tl[2][s][64:128, 0:n], AF.Copy, [b_tl[2][s]], [b_kpl])
                    else:
                        lt0 = t0 - LCTX
                        self.dma("sp", tl[5][s][:], dr["rope_cos"][:, lt0:lt0 + 512], w=[b_tl[5][s]])
                        self.dma("sp", tl[6][s][:], dr["rope_sin"][:, lt0:lt0 + 512], w=[b_tl[6][s]])
                        self.act(qpl[:, t0:t0 + n], tl[0][s][:], AF.Copy, [b_tl[0][s]], [b_qpl])
                        for (a, ap_, dst, b_dst, eng) in [(0, 1, qrot, b_qrot, "dve"), (2, 3, krot, b_krot, "dve")]:
                            self.tt(eng, tl[a][s][:], tl[a][s][:], tl[5][s][:], ALU.mult,
                                    r=[b_tl[a][s], b_tl[5][s]], w=[b_tl[a][s]])
                            self.tt(eng, tl[ap_][s][:], tl[ap_][s][:], tl[6][s][:], ALU.mult,
                                    r=[b_tl[ap_][s], b_tl[6][s]], w=[b_tl[ap_][s]])
                            if dst is krot:
                                for hh in range(2):
                                    hs = slice(hh * 64, (hh + 1) * 64)
                                    self.tt(eng, krot[hs, hh, lt0:lt0 + 512], tl[a][s][hs, :], tl[ap_][s][hs, :], ALU.add,
                                            r=[b_tl[a][s], b_tl[ap_][s]], w=[b_dst])
                            else:
                                self.tt(eng, dst[:, lt0:lt0 + 512], tl[a][s][:], tl[ap_][s][:], ALU.add,
                                        r=[b_tl[a][s], b_tl[ap_][s]], w=[b_dst])
                    nb = n // 128
                    pb = kq % 4
                    kq += 1
                    for q in range(nb):
                        self.em.op("pe", lambda e, pb=pb, q=q, s=s: e.transpose(
                            psb[pb][:, q * 128:(q + 1) * 128], tl[4][s][:, q * 128:(q + 1) * 128], ident[:]),
                            [b_tl[4][s], b_id], [b_ps[pb]], defer=(q != nb - 1))
                    tb0 = t0 // 128
                    pv = psb[pb][:, 0:nb * 128].rearrange("p (a b) -> p a b", b=128)
                    self.cp("dve", Vp[:, tb0:tb0 + nb, 0, 0:64], pv[:, :, 0:64], r=[b_ps[pb]], w=[b_Vp])
                    self.act(Vp[:, tb0:tb0 + nb, 1, 64:128], pv[:, :, 64:128], AF.Copy, [b_ps[pb]], [b_Vp])
                for hh in range(2):
                    h = 2 * j + hh
                    for krl in range(2):
                        base = ((l * 16 + h) * 24 + krl) * 128
                        src = bass.AP(rp.tensor, rp.offset + base, [[1, 64], [128, 22], [1, 64]])
                        self.dma("sp", Tc2[krl * 64:(krl + 1) * 64, :].rearrange("p (a b) -> p a b", b=64), src, w=[b_Tc2])
                    for di in range(8):
                        self.cp("pool", biasd[:, di, :], rev(Tc2[:, di * 128:di * 128 + 512]), r=[b_Tc2], w=[b_biasd])
                    for qb, items in enumerate(blocks):
                        for (kr0, ty, di) in items:
                            if ty is not None:
                                self.tt(("dve", "pool")[ty % 2], bm[hh][:, ty, :], biasd[:, di, :], maskb[:, ty, :], ALU.add,
                                        r=[b_biasd, b_mk], w=[b_bm[hh]])
                qk_list, pv_list = [], []
                for qb, items in enumerate(blocks):
                    a1, a2 = 4 + 2 * (kacc % 2), 5 + 2 * (kacc % 2)
                    kacc += 1
                    s3 = kacc % 2
                    q0t = qb * 512
                    work = []
                    for hh in range(2):
                        for (kr0, ty, di) in items:
                            work.append((hh, "loc", kr0, ty))
                        for cc in range(2):
                            work.append((hh, "ctx", cc, None))
                    for wi, (hh, kind, a, ty) in enumerate(work):
                        hb = hh * 64
                        pb = kq % 4
                        kq += 1
                        s2 = kp_ % 3
                        kp_ += 1
                        first, last = (wi == 0), (wi == len(work) - 1)
                        if kind == "loc":
                            s1 = ks % 2
                            ks += 1

                            def qk(hb=hb, pb=pb, s2=s2, s1=s1, a=a, ty=ty, hh=hh, q0t=q0t):
                                ktok = a * 64
                                self.mm(psb[pb][:, :], krot[:, hh, ktok:ktok + 128], qrot[:, q0t:q0t + 512],
                                        r=[b_krot, b_qrot], w=[b_ps[pb]])
                                self.stt(sT[s1][:], psb[pb][:, :], 0.125, bm[hh][:, ty, :], ALU.mult, ALU.add,
                                         r=[b_ps[pb], b_bm[hh]], w=[b_sT[s1]])
                                self.act(pT[s2][:], sT[s1][:], AF.Exp, [b_sT[s1]], [b_pT[s2]])
                            vch = 2 + a // 2
                        else:
                            def qk(hb=hb, pb=pb, s2=s2, a=a, q0t=q0t, hh=hh):
                                self.mm(psb[pb][:, :], kpl[:, hh, a * 128:(a + 1) * 128],
                                        qpl[:, LCTX + q0t:LCTX + q0t + 512], r=[b_kpl, b_qpl], w=[b_ps[pb]])
                                self.act(pT[s2][:], psb[pb][:, :], AF.Exp, [b_ps[pb]], [b_pT[s2]], scale=0.125)
                            vch = a

                        def pv(a1=a1, a2=a2, vch=vch, hh=hh, s2=s2, first=first, last=last, s3=s3, q0t=q0t):
                            self.mm(psb[a1][:, :], Vp[:, vch, hh, :], pT[s2][:], start=first, stop=last,
                                    r=[b_Vp, b_pT[s2]], w=[b_ps[a1]], defer=True)
                            self.mm(psb[a2][:, :], onesp[:, hh, :], pT[s2][:], start=first, stop=last,
                                    r=[b_op, b_pT[s2]], w=[b_ps[a2]], defer=(not last))
                            if last:
                                self.em.op("dve", lambda e: e.reciprocal(out=rc[s3][:], in_=psb[a2][:, :]),
                                           [b_ps[a2]], [b_rc[s3]])
                                self.tt("dve", yst[:, LCTX + q0t:LCTX + q0t + 512], psb[a1][:, :], rc[s3][:], ALU.mult,
                                        r=[b_ps[a1], b_rc[s3]], w=[b_yst])
                        qk_list.append(qk)
                        pv_list.append(pv)
                LA = 2
                for idx in range(len(qk_list) + LA):
                    if idx < len(qk_list):
                        qk_list[idx]()
                    if idx - LA >= 0:
                        pv_list[idx - LA]()
                if update_ctx:
                    a1, a2 = 4 + 2 * (kacc % 2), 5 + 2 * (kacc % 2)
                    kacc += 1
                    work = [(hh, cc) for hh in range(2) for cc in range(2)]
                    for wi, (hh, cc) in enumerate(work):
                        hb = hh * 64
                        pb = kq % 4
                        kq += 1
                        s2 = kp_ % 3
                        kp_ += 1
                        self.mm(psb[pb][:, 0:LCTX], kpl[:, hh, cc * 128:(cc + 1) * 128], qpl[:, 0:LCTX],
                                r=[b_kpl, b_qpl], w=[b_ps[pb]])
                        self.act(pT[s2][:, 0:LCTX], psb[pb][:, 0:LCTX], AF.Exp, [b_ps[pb]], [b_pT[s2]], scale=0.125)
                        first, last = (wi == 0), (wi == len(work) - 1)
                        self.mm(psb[a1][:, 0:LCTX], Vp[:, cc, hh, :], pT[s2][:, 0:LCTX], start=first, stop=last,
                                r=[b_Vp, b_pT[s2]], w=[b_ps[a1]])
                        self.mm(psb[a2][:, 0:LCTX], onesp[:, hh, :], pT[s2][:, 0:LCTX], start=first, stop=last,
                                r=[b_op, b_pT[s2]], w=[b_ps[a2]])
                    s3 = kacc % 2
                    self.em.op("dve", lambda e, s3=s3, a2=a2: e.reciprocal(out=rc[s3][:, 0:LCTX], in_=psb[a2][:, 0:LCTX]),
                               [b_ps[a2]], [b_rc[s3]])
                    self.tt("dve", yst[:, 0:LCTX], psb[a1][:, 0:LCTX], rc[s3][:, 0:LCTX], ALU.mult,
                            r=[b_ps[a1], b_rc[s3]], w=[b_yst])
                self.dma("pool", yC[j * 128:(j + 1) * 128, :], yst[:], r=[b_yst])
            self.em.flush()

    def phase_rwkv(self, l):
        cfg = self.cfg
        NT, T = cfg.NT, cfg.T
        psb, b_ps = self.psb, self.b_ps
        dr = self.dram
        uT, yA = dr["uT"], dr["yA"]
        NCK = NT // 64
        CW = 0.6065306597126334
        SEGC = 16
        SEGN = SEGC * 64
        segs = [(0, 4)] + [(4 + 16 * i, 16) for i in range((NCK - 4) // 16)]
        with contextlib.ExitStack() as st:
            sb = lambda name, shape, dt=F32: self.sb(st, "r_" + name, shape, dt)
            ident_bf = sb("ident_bf", [128, 128], BF16)
            bdones = sb("bdones", [128, 128], BF16)
            istack = sb("istack", [128, 64], BF16)
            rkm = sb("rkm", [128, 2, 1152], BF16)
            cmask = sb("cmask", [128, SEGN])
            mu = sb("mu", [128, 26, 2])
            c0 = sb("c0", [128, 26])
            w0a0 = sb("w0a0", [128, 2, 2, 8])
            vec = sb("vec", [128, 5, 8])
            omk = sb("omk", [128, 8])
            WA = sb("WA", [128, 2, 1024], BF16)
            GU = sb("GU", [128, 1024], BF16)
            b_cst, b_k, b_par, b_wst, b_W = Buf("cst"), Buf("rk_consts"), Buf("rpar"), Buf("wst"), Buf("WA")
            with contextlib.ExitStack() as st_tmp:
                cst_f = self.sb(st_tmp, "r_cst_f", [128, 1152], F32)
                wst = self.sb(st_tmp, "r_wst", [128, 1024], F32)
                for (dst, src, n) in [(ident_bf, dr["ident"], 128), (bdones, dr["bdones"], 128), (istack, dr["istack"], 64)]:
                    self.dma("sp", cst_f[:, 0:n], src[:, :], w=[b_cst])
                    self.cp("dve", dst[:], cst_f[:, 0:n], r=[b_cst], w=[b_k])
                for d in range(2):
                    self.dma("sp", cst_f[:], dr["rk_mask"][d], w=[b_cst])
                    self.cp("dve", rkm[:, d, :], cst_f[:], r=[b_cst], w=[b_k])
                self.memset("pool", cmask[:], 1.0, w=[b_k])
                self.memset("pool", cmask[:].rearrange("p (c s) -> p c s", s=64)[:, :, 0:1], 0.0, w=[b_k])
                self.dma("sp", mu[:], dr["rwkv_mu"][l], w=[b_par])
                self.dma("sp", w0a0[:], dr["rwkv_w0a0"][l], w=[b_par])
                self.dma("sp", vec[:], dr["rwkv_vec"][l], w=[b_par])
                self.ts("dve", c0[:], mu[:, :, 0], -1.0, 1.0, ALU.mult, ALU.add, r=[b_par], w=[b_par])
                self.tt("dve", c0[:], c0[:], mu[:, :, 1], ALU.subtract, r=[b_par], w=[b_par])
                self.ts("dve", omk[:], vec[:, 1, :], -1.0, 1.0, ALU.mult, ALU.add, r=[b_par], w=[b_par])
                for d in range(2):
                    self.dma("sp", wst[0:64, :], dr["rwkv_w_up"][l, d], w=[b_wst])
                    self.dma("sp", wst[64:128, :], dr["rwkv_a_up"][l, d], w=[b_wst])
                    self.cp("pool", WA[:, d, :], wst[:], r=[b_wst], w=[b_W])
                self.dma("sp", wst[:], dr["rwkv_g_up"][l], w=[b_wst])
                self.cp("pool", GU[:], wst[:], r=[b_wst], w=[b_W])
                self.em.flush()
            LW = sb("LW", [128, NT], BF16)
            GL = sb("GL", [128, NT], BF16)
            Yacc = sb("Yacc", [128, NCK, 64])
            ksum = sb("ksum", [128, NT])
            b_LW, b_GL, b_Yacc, b_ksum = Buf("LW"), Buf("GL"), Buf("Yacc"), Buf("ksum")
            xps = [sb(f"xp{i}", [128, SEGN + 2]) for i in range(3)]
            b_xps = mkbufs("xp", 3)
            kxp = [0]
            rTs = [sb(f"rT{i}", [128, SEGN]) for i in range(2)]
            kT, vT, kap = sb("kT", [128, SEGN]), sb("vT", [128, SEGN]), sb("kap", [128, SEGN])
            T1, T2, T3, T4 = [sb(f"T{i}", [128, SEGN]) for i in range(1, 5)]
            F1, F2, F3 = [sb(f"F{i}", [128, SEGN]) for i in range(1, 4)]
            ynT = sb("ynT", [128, SEGN])
            Vbs = [sb(f"Vb{i}", [128, SEGN], BF16) for i in range(2)]
            gTbs = [sb(f"gTb{i}", [128, SEGN], BF16) for i in range(2)]
            sqb, fsqb = sb("sqb", [128, SEGN], BF16), sb("fsqb", [128, SEGN], BF16)
            stks = [sb(f"stk{i}", [128, 4, SEGN], BF16) for i in range(2)]
            YBD = sb("YBD", [128, SEGC, 128], BF16)
            yst = sb("yst", [128, SEGN], BF16)
            gCs = [sb(f"gC{i}", [128, SEGC]) for i in range(2)]
            lnst = sb("lnst", [128, 6, SEGC])
            ptot = sb("ptot", [128, SEGC])
            b_kT, b_vT, b_kap, b_T1, b_T2, b_T3, b_T4, b_ynT, b_sqb, b_YBD, b_yst, b_ln, b_F1, b_F2, b_F3, b_fsqb, b_ptot = [
                Buf(n) for n in "kT vT kap T1 T2 T3 T4 ynT sqb YBD yst lnst F1 F2 F3 fsqb ptot".split()]
            b_rTs, b_Vbs, b_gTbs, b_stks, b_gCs = [mkbufs(n, 2) for n in "rT Vb gTb stk gC".split()]
            self.memset("pool", YBD[:], 0.0, w=[b_YBD])
            G = 4
            BDg = [sb(f"BDg{i}", [128, G, 5, 128], BF16) for i in range(2)]
            b_BDg = mkbufs("BDg", 2)
            for i in range(2):
                self.memset("pool", BDg[i][:], 0.0, w=[b_BDg[i]])
            SA = [sb(f"SA{i}", [128, 512], BF16) for i in range(G)]
            SB_ = [sb(f"SB{i}", [128, 512], BF16) for i in range(G)]
            MN = [[sb(f"MN{i}_{k}", [128, 256], BF16) for k in range(2)] for i in range(G)]
            Rb = [[sb(f"Rb{i}_{k}", [128, 256], BF16) for k in range(2)] for i in range(G)]
            Pb = [[sb(f"Pb{i}_{k}", [128, 128], BF16) for k in range(2)] for i in range(G)]
            MO = [sb(f"MO{i}", [128, 128], BF16) for i in range(G)]
            QP = [sb(f"QP{i}", [128, 256], BF16) for i in range(G)]
            AK = [sb(f"AK{i}", [128, 256], BF16) for i in range(G)]
            VM = [sb(f"VM{i}", [128, G, 64], BF16) for i in range(2)]
            b_VM = mkbufs("VM", 2)
            b_MO = mkbufs("MO", G)
            pending = []
            ST = sb("ST", [128, 64], BF16)
            S32 = sb("S32", [128, 64])
            S32g = sb("S32g", [128, 64])
            b_S32, b_S32g = Buf("S32"), Buf("S32g")
            b_SA, b_SB, b_QP, b_AK = [mkbufs(n, G) for n in "SA SB QP AK".split()]
            b_MN, b_Rb, b_Pb = [[mkbufs(f"{n}{i}_", 2) for i in range(G)] for n in "MN Rb Pb".split()]
            b_ST = Buf("ST")
            kps = [0]

            def shift(dst, b_dst, c, c0_, n):
                xi = kxp[0] % 3
                kxp[0] += 1
                xp, b_xp = xps[xi], b_xps[xi]
                p0 = c0_ * 64
                p1 = p0 + n
                hasL = p0 not in (0, LCTX)
                hasR = p1 not in (LCTX, NT)
                if not hasL:
                    self.memset("pool", xp[:, 0:1], 0.0, w=[b_xp])
                if not hasR:
                    self.memset("pool", xp[:, n + 1:n + 2], 0.0, w=[b_xp])
                lo, hi = p0 - int(hasL), p1 + int(hasR)
                self.dma("sp", xp[:, 1 - int(hasL):1 + n + int(hasR)], uT[c * 128:(c + 1) * 128, lo:hi], w=[b_xp])
                self.act(dst[:, 0:n], xp[:, 1:1 + n], AF.Copy, [b_xp, b_par], [b_dst], scale=c0[:, c:c + 1])
                self.stt(dst[:, 0:n], xp[:, 0:n], mu[:, c, 0:1], dst[:, 0:n], ALU.mult, ALU.add, r=[b_xp, b_par, b_dst], w=[b_dst])
                self.stt(dst[:, 0:n], xp[:, 2:2 + n], mu[:, c, 1:2], dst[:, 0:n], ALU.mult, ALU.add, r=[b_xp, b_par, b_dst], w=[b_dst])

            def tiles_of(n):
                return [(o, min(512, n - o)) for o in range(0, n, 512)]

            def nextps():
                kps[0] += 1
                return 7

            for (ck0, nck) in segs:
                n = nck * 64
                t0 = ck0 * 64
                shift(T1, b_T1, 24, ck0, n)
                self.act(LW[0:64, t0:t0 + n], T1[0:64, 0:n], AF.Tanh, [b_T1], [b_LW])
                self.act(LW[64:128, t0:t0 + n], T1[64:128, 0:n], AF.Copy, [b_T1], [b_LW])
                shift(T2, b_T2, 25, ck0, n)
                self.act(GL[:, t0:t0 + n], T2[:, 0:n], AF.Sigmoid, [b_T2], [b_GL])

            kbd = [0]

            def prep(j, d, ck0, nck, sp):
                n = nck * 64
                t0 = ck0 * 64
                rT, b_rT, Vb, b_Vb, gTb, b_gTb, stk, b_stk, gC, b_gC = (rTs[sp], b_rTs[sp], Vbs[sp], b_Vbs[sp], gTbs[sp], b_gTbs[sp],
                                                                       stks[sp], b_stks[sp], gCs[sp], b_gCs[sp])
                for (o, m) in tiles_of(n):
                    pb = nextps()
                    self.mm(psb[pb][:, 0:m], WA[0:64, d, j * 128:(j + 1) * 128], LW[0:64, t0 + o:t0 + o + m],
                            r=[b_W, b_LW], w=[b_ps[pb]])
                    self.act(T1[:, o:o + m], psb[pb][:, 0:m], AF.Sigmoid, [b_ps[pb], b_par], [b_T1],
                             bias=w0a0[:, 0, d, j:j + 1])
                    pb = nextps()
                    self.mm(psb[pb][:, 0:m], WA[64:128, d, j * 128:(j + 1) * 128], LW[64:128, t0 + o:t0 + o + m],
                            r=[b_W, b_LW], w=[b_ps[pb]])
                    self.act(T2[:, o:o + m], psb[pb][:, 0:m], AF.Sigmoid, [b_ps[pb], b_par], [b_T2],
                             bias=w0a0[:, 1, d, j:j + 1])
                shift(rT, b_rT, j, ck0, n)
                shift(kT, b_kT, 8 + j, ck0, n)
                shift(vT, b_vT, 16 + j, ck0, n)
                self.scan(T3[:, 0:n], cmask[:, 0:n], T1[:, 0:n], 0.0, r=[b_k, b_T1], w=[b_T3])
                T3v = T3[:, 0:n].rearrange("p (c s) -> p c s", s=64)
                self.act(gC[:, 0:nck], T3v[:, :, 63], AF.Exp, [b_T3], [b_gC], scale=-CW)
                if d == 1:
                    self.cp("pool", ptot[:, 0:nck], T3v[:, :, 63], r=[b_T3], w=[b_ptot])
                    self.tt("dve", T3[:, 0:n], T1[:, 0:n], T3[:, 0:n], ALU.subtract, r=[b_T1, b_T3], w=[b_T3])
                    self.tt("dve", T3v, T3v, ptot[:, 0:nck].unsqueeze(2).to_broadcast([128, nck, 64]), ALU.add,
                            r=[b_T3, b_ptot], w=[b_T3])
                self.tt("dve", T1[:, 0:n], T3[:, 0:n], T1[:, 0:n], ALU.subtract, r=[b_T1, b_T3], w=[b_T1])
                self.act(T1[:, 0:n], T1[:, 0:n], AF.Exp, [b_T1], [b_T1], scale=-CW)
                self.act(T4[:, 0:n], T3[:, 0:n], AF.Exp, [b_T3], [b_T4], scale=CW)
                self.act(T3[:, 0:n], T3[:, 0:n], AF.Exp, [b_T3], [b_T3], scale=-CW)
                self.act(Vb[:, 0:n], vT[:, 0:n], AF.Copy, [b_vT], [b_Vb])
                self.act(kap[:, 0:n], kT[:, 0:n], AF.Copy, [b_kT, b_par], [b_kap], scale=vec[:, 0, j:j + 1])
                self.act(sqb[:, 0:n], kap[:, 0:n], AF.Square, [b_kap], [b_sqb])
                for (o, m) in tiles_of(n):
                    pb = nextps()
                    self.mm(psb[pb][:, 0:m], bdones[:], sqb[:, o:o + m], r=[b_k, b_sqb], w=[b_ps[pb]])
                    self.act(F3[:, o:o + m], psb[pb][:, 0:m], AF.Sqrt, [b_ps[pb]], [b_F3], bias=1e-24)
                self.em.op("dve", lambda e, n=n: e.reciprocal(out=F3[:, 0:n], in_=F3[:, 0:n]), [b_F3], [b_F3])
                self.tt("dve", kap[:, 0:n], kap[:, 0:n], F3[:, 0:n], ALU.mult, r=[b_kap, b_F3], w=[b_kap])
                if d == 1:
                    for (o, m) in tiles_of(n):
                        pb = nextps()
                        self.mm(psb[pb][:, 0:m], GU[:, j * 128:(j + 1) * 128], GL[:, t0 + o:t0 + o + m],
                                r=[b_W, b_GL], w=[b_ps[pb]])
                        self.act(gTb[:, o:o + m], psb[pb][:, 0:m], AF.Copy, [b_ps[pb]], [b_gTb])
                self.tt("dve", stk[:, 0, 0:n], kap[:, 0:n], T1[:, 0:n], ALU.mult, r=[b_kap, b_T1], w=[b_stk])
                for o_ in range(0, n, 256):
                    self.tt("pool", stk[:, 1, o_:o_ + 256], rT[:, o_:o_ + 256], T3[:, o_:o_ + 256], ALU.mult, r=[b_rT, b_T3], w=[b_stk])
                self.ts("dve", T1[:, 0:n], T2[:, 0:n], vec[:, 1, j:j + 1], omk[:, j:j + 1], ALU.mult, ALU.add,
                        r=[b_T2, b_par], w=[b_T1])
                self.tt("dve", T1[:, 0:n], T1[:, 0:n], kT[:, 0:n], ALU.mult, r=[b_T1, b_kT], w=[b_T1])
                if d == 0:
                    self.act(ksum[:, t0:t0 + n], T1[:, 0:n], AF.Copy, [b_T1], [b_ksum])
                else:
                    for o_ in range(0, n, 256):
                        self.tt("pool", ksum[:, t0 + o_:t0 + o_ + 256], ksum[:, t0 + o_:t0 + o_ + 256], T1[:, o_:o_ + 256], ALU.add,
                                r=[b_T1, b_ksum], w=[b_ksum])
                self.tt("dve", stk[:, 3, 0:n], T1[:, 0:n], T4[:, 0:n], ALU.mult, r=[b_T1, b_T4], w=[b_stk])
                for o_ in range(0, n, 256):
                    self.tt("pool", T2[:, o_:o_ + 256], T2[:, o_:o_ + 256], kap[:, o_:o_ + 256], ALU.mult, r=[b_T2, b_kap], w=[b_T2])
                self.tt("dve", stk[:, 2, 0:n], T2[:, 0:n], T4[:, 0:n], ALU.mult, r=[b_T2, b_T4], w=[b_stk])

            def finalize(j, ck0, nck, sp):
                n = nck * 64
                t0 = ck0 * 64
                rT, b_rT, Vb, b_Vb, gTb, b_gTb = rTs[sp], b_rTs[sp], Vbs[sp], b_Vbs[sp], gTbs[sp], b_gTbs[sp]
                Ys = Yacc[:, ck0:ck0 + nck, :]
                T3v = F3[:, 0:n].rearrange("p (c s) -> p c s", s=64)
                mean, ssq, m2, var = lnst[:, 1, 0:nck], lnst[:, 2, 0:nck], lnst[:, 3, 0:nck], lnst[:, 4, 0:nck]
                self.em.op("dve", lambda e, Ys=Ys, mean=mean: e.tensor_reduce(out=mean, in_=Ys, axis=AX.X, op=ALU.add),
                           [b_Yacc], [b_ln])
                self.act(T3v, Ys, AF.Square, [b_Yacc], [b_F3])
                self.em.op("dve", lambda e, T3v=T3v, ssq=ssq: e.tensor_reduce(out=ssq, in_=T3v, axis=AX.X, op=ALU.add),
                           [b_F3], [b_ln])
                self.ts("dve", mean, mean, 1.0 / 64, None, ALU.mult, r=[b_ln], w=[b_ln])
                self.tt("dve", m2, mean, mean, ALU.mult, r=[b_ln], w=[b_ln])
                self.stt(var, ssq, 1.0 / 64, m2, ALU.mult, ALU.subtract, r=[b_ln], w=[b_ln])
                self.act(var, var, AF.Sqrt, [b_ln], [b_ln], bias=64e-5)
                self.em.op("dve", lambda e, var=var: e.reciprocal(out=var, in_=var), [b_ln], [b_ln])
                self.tt("dve", T3v, Ys, mean.unsqueeze(2).to_broadcast([128, nck, 64]), ALU.subtract,
                        r=[b_Yacc, b_ln], w=[b_F3])
                for hh in range(2):
                    hs = slice(hh * 64, (hh + 1) * 64)
                    self.tt("dve", YBD[hs, 0:nck, hs], T3v[hs], var[hs].unsqueeze(2).to_broadcast([64, nck, 64]), ALU.mult,
                            r=[b_F3, b_ln], w=[b_YBD])
                for c8 in range(0, nck, 8):
                    m8 = min(8, nck - c8)
                    pb = 7
                    for ci in range(c8, c8 + m8):
                        self.mm(psb[pb][:, (ci - c8) * 64:(ci - c8 + 1) * 64], YBD[:, ci, :], istack[:],
                                r=[b_YBD, b_k], w=[b_ps[pb]], defer=(ci != c8 + m8 - 1))
                    self.ts("dve", ynT[:, c8 * 64:(c8 + m8) * 64], psb[pb][:, 0:m8 * 64], vec[:, 3, j:j + 1], vec[:, 4, j:j + 1],
                            ALU.mult, ALU.add, r=[b_ps[pb], b_par], w=[b_ynT])
                for o_ in range(0, n, 256):
                    self.tt("pool", F1[:, o_:o_ + 256], rT[:, o_:o_ + 256], ksum[:, t0 + o_:t0 + o_ + 256], ALU.mult, r=[b_rT, b_ksum], w=[b_F1])
                self.ts("dve", fsqb[:, 0:n], F1[:, 0:n], vec[:, 2, j:j + 1], None, ALU.mult, r=[b_F1, b_par], w=[b_fsqb])
                for (o, m) in tiles_of(n):
                    pb = 7
                    self.mm(psb[pb][:, 0:m], bdones[:], fsqb[:, o:o + m], r=[b_k, b_fsqb], w=[b_ps[pb]])
                    self.tt("dve", F2[:, o:o + m], psb[pb][:, 0:m], Vb[:, o:o + m], ALU.mult, r=[b_ps[pb], b_Vb], w=[b_F2])
                self.tt("dve", ynT[:, 0:n], ynT[:, 0:n], F2[:, 0:n], ALU.add, r=[b_ynT, b_F2], w=[b_ynT])
                self.tt("dve", yst[:, 0:n], ynT[:, 0:n], gTb[:, 0:n], ALU.mult, r=[b_ynT, b_gTb], w=[b_yst])
                self.dma("pool", yA[j * 128:(j + 1) * 128, t0:t0 + n], yst[:, 0:n], r=[b_yst])

            for j in range(8):
                for d in range(2):
                    self.memset("pool", ST[:], 0.0, w=[b_ST])
                    self.memset("pool", S32[:], 0.0, w=[b_S32])
                    order = segs if d == 0 else [segs[0]] + segs[1:][::-1]
                    bgq = []
                    prep(j, d, order[0][0], order[0][1], 0)
                    for si, (ck0, nck) in enumerate(order):
                        sp = si % 2
                        n = nck * 64
                        t0 = ck0 * 64
                        rT, b_rT, Vb, b_Vb, gTb, b_gTb, stk, b_stk, gC, b_gC = (rTs[sp], b_rTs[sp], Vbs[sp], b_Vbs[sp], gTbs[sp],
                                                                               b_gTbs[sp], stks[sp], b_stks[sp], gCs[sp], b_gCs[sp])
                        if si + 1 < len(order):
                            nxt = order[si + 1]
                            bgq += self.em.captured(lambda: prep(j, d, nxt[0], nxt[1], 1 - sp))
                        nstage = 30 * ((nck + G - 1) // G)
                        bgk = max(1, (len(bgq) + nstage - 1) // nstage)

                        def bgrun(k=None):
                            self.em.replay(bgq, bgk if k is None else k)
                        corder = list(range(nck)) if d == 0 else list(range(nck))[::-1]
                        for gi in range(0, nck, G):
                            grp = corder[gi:gi + G]
                            cl = min(grp)
                            bg = kbd[0] % 2
                            kbd[0] += 1
                            for qi in range(5):
                                for hh in range(2):
                                    hs = slice(hh * 64, (hh + 1) * 64)
                                    src = (Vb[hs, cl * 64:(cl + G) * 64] if qi == 4 else stk[hs, qi, cl * 64:(cl + G) * 64])
                                    src = src.rearrange("p (c s) -> p c s", s=64)
                                    eng = ("act", "pool", "act", "dve", "act", "pool", "act", "pool", "act", "dve")[qi * 2 + hh]
                                    if eng == "act":
                                        self.act(BDg[bg][hs, :, qi, hs], src, AF.Copy, [b_stk, b_Vb], [b_BDg[bg]])
                                    else:
                                        self.cp(eng, BDg[bg][hs, :, qi, hs], src, r=[b_stk, b_Vb], w=[b_BDg[bg]])
                            rB = [b_BDg[bg], b_k]
                            R4 = range(len(grp))
                            gof = [ci - cl for ci in grp]
                            bd = lambda i, q: BDg[bg][:, gof[i], q, :]
                            slot = lambda i: psb[i]
                            bsl = lambda i: b_ps[i]
                            for i in R4:
                                self.mm(psb[4][:, i * 64:(i + 1) * 64], bd(i, 4), istack[:], r=rB, w=[b_ps[4]], defer=(i != len(grp) - 1))
                            self.act(VM[bg][:, 0:len(grp), :], psb[4][:, 0:64 * len(grp)].rearrange("p (c s) -> p c s", s=64), AF.Copy,
                                     [b_ps[4]], [b_VM[bg]])
                            pend = pending[:]
                            del pending[:]

                            def drain(k=1):
                                for _ in range(k):
                                    if pend:
                                        pend.pop(0)()
                            for i in R4:
                                self.mm(slot(i)[:, 0:256], bd(i, 2), BDg[bg][:, gof[i], 0:2, :], r=rB, w=[bsl(i)], defer=True)
                                self.mm(slot(i)[:, 256:384], bd(i, 2), ident_bf[:], r=rB, w=[bsl(i)], defer=True)
                                self.mm(slot(i)[:, 384:512], bd(i, 0), ident_bf[:], r=rB, w=[bsl(i)])
                            drain()
                            bgrun()
                            for i in R4:
                                self.tt("dve", SA[i][:, 0:256], slot(i)[:, 0:256], rkm[:, d, 0:256], ALU.mult, r=[bsl(i), b_k], w=[b_SA[i]])
                                self.act(SA[i][:, 256:512], slot(i)[:, 256:512], AF.Copy, [bsl(i)], [b_SA[i]])
                            bgrun()
                            for i in R4:
                                self.mm(slot(i)[:, 0:128], bd(i, 3), bd(i, 1), r=rB, w=[bsl(i)], defer=True)
                                self.mm(slot(i)[:, 128:256], bd(i, 3), ident_bf[:], r=rB, w=[bsl(i)], defer=True)
                                self.mm(slot(i)[:, 256:512], bd(i, 0), BDg[bg][:, gof[i], 2:4, :], r=rB, w=[bsl(i)])
                            drain()
                            bgrun()
                            for i in R4:
                                self.tt("dve", SB_[i][:], slot(i)[:, :], rkm[:, d, 512:1024], ALU.mult, r=[bsl(i), b_k], w=[b_SB[i]])
                                self.tt("dve", MO[i][:], slot(i)[:, 256:384], rkm[:, d, 1024:1152], ALU.mult, r=[bsl(i), b_k], w=[b_MO[i]])
                            bgrun()
                            for i in R4:
                                self.tt("pool", Pb[i][0][:], ident_bf[:], SB_[i][:, 256:384], ALU.subtract, r=[b_k, b_SB[i]], w=[b_Pb[i][0]])
                            cur = [(SB_[i][:, 256:384], SA[i][:, 0:128], b_SB[i], b_SA[i]) for i in R4]
                            for lev in range(4):
                                lastl = lev == 3
                                mi = lev % 2
                                for i in R4:
                                    Mc, Nc, bM, bN = cur[i]
                                    self.mm(slot(i)[:, 128:256], Mc, Nc, r=[bM, bN], w=[bsl(i)], defer=(not lastl))
                                    if not lastl:
                                        self.mm(slot(i)[:, 0:128], Nc, Mc, r=[bM, bN], w=[bsl(i)])
                                drain()
                                bgrun()
                                lo = 128 if lastl else 0
                                for i in R4:
                                    self.act(MN[i][mi][:, lo:256], slot(i)[:, lo:256], AF.Copy, [bsl(i)], [b_MN[i][mi]])
                                    cur[i] = (MN[i][mi][:, 0:128], MN[i][mi][:, 128:256], b_MN[i][mi], b_MN[i][mi])
                                bgrun()
                                for i in R4:
                                    self.mm(slot(i)[:, 256:384], cur[i][1], Pb[i][lev % 2][:], r=[cur[i][3], b_Pb[i][lev % 2]], w=[bsl(i)])
                                bgrun()
                                for i in R4:
                                    self.tt("dve", Pb[i][(lev + 1) % 2][:], Pb[i][lev % 2][:], slot(i)[:, 256:384], ALU.add,
                                            r=[b_Pb[i][lev % 2], bsl(i)], w=[b_Pb[i][(lev + 1) % 2]])
                            drain(4)
                            for i in R4:
                                self.mm(slot(i)[:, 0:256], Pb[i][0][:], SA[i][:, 128:384], r=[b_Pb[i][0], b_SA[i]], w=[bsl(i)])
                            bgrun()
                            for i in R4:
                                self.act(Rb[i][0][:], slot(i)[:, 0:256], AF.Copy, [bsl(i)], [b_Rb[i][0]])
                            for i in R4:
                                self.mm(slot(i)[:, 256:512], MO[i][:], Rb[i][0][:], r=[b_MO[i], b_Rb[i][0]], w=[bsl(i)])
                            bgrun()
                            for i in R4:
                                self.tt("dve", Rb[i][1][:], SA[i][:, 128:384], slot(i)[:, 256:512], ALU.subtract, r=[b_SA[i], bsl(i)], w=[b_Rb[i][1]])
                            bgrun()
                            for i in R4:
                                self.mm(slot(i)[:, 0:256], Pb[i][0][:], Rb[i][1][:], r=[b_Pb[i][0], b_Rb[i][1]], w=[bsl(i)])
                            for i in R4:
                                self.act(Rb[i][0][:], slot(i)[:, 0:256], AF.Copy, [bsl(i)], [b_Rb[i][0]])
                            bgrun()
                            for i in R4:
                                self.mm(slot(i)[:, 256:512], SA[i][:, 384:512], Rb[i][0][:], r=[b_SA[i], b_Rb[i][0]], w=[bsl(i)], defer=True)
                                self.mm(slot(i)[:, 0:256], SB_[i][:, 384:512], Rb[i][0][:], r=[b_SB[i], b_Rb[i][0]], w=[bsl(i)])
                            bgrun()
                            for i in R4:
                                self.tt("dve", QP[i][:, 0:128], bd(i, 1), slot(i)[:, 256:384], ALU.subtract, r=[b_BDg[bg], bsl(i)], w=[b_QP[i]])
                                self.ts("dve", QP[i][:, 128:256], slot(i)[:, 384:512], -1.0, None, ALU.mult, r=[bsl(i)], w=[b_QP[i]])
                                self.tt("dve", AK[i][:], SB_[i][:, 0:256], slot(i)[:, 0:256], ALU.subtract, r=[b_SB[i], bsl(i)], w=[b_AK[i]])
                            for i in R4:
                                def step(i=i, ci=grp[i], bg=bg, d=d, ck0=ck0):
                                    SQ = psb[6][:, i * 128:(i + 1) * 128]
                                    vm = VM[bg][:, i, :]
                                    self.act(S32g[:], S32[:], AF.Copy, [b_S32, b_gC], [b_S32g], scale=gC[:, ci:ci + 1])
                                    self.mm(SQ[:, 0:64], QP[i][:, 0:128], ST[:], start=True, stop=False, r=[b_QP[i], b_ST], w=[b_ps[6]])
                                    self.mm(SQ[:, 0:64], AK[i][:, 0:128], vm, start=False, stop=True, r=[b_AK[i], b_VM[bg]], w=[b_ps[6]])
                                    self.mm(SQ[:, 64:128], QP[i][:, 128:256], ST[:], start=True, stop=False, r=[b_QP[i], b_ST], w=[b_ps[6]])
                                    self.mm(SQ[:, 64:128], AK[i][:, 128:256], vm, start=False, stop=True, r=[b_AK[i], b_VM[bg]], w=[b_ps[6]])
                                    self.stt(S32[:], SQ[:, 64:128], gC[:, ci:ci + 1], S32g[:], ALU.mult, ALU.add,
                                             r=[b_ps[6], b_gC, b_S32g], w=[b_S32])
                                    self.act(ST[:], S32[:], AF.Copy, [b_S32], [b_ST])
                                    ck = ck0 + ci
                                    if d == 0:
                                        self.cp("dve", Yacc[:, ck, :], SQ[:, 0:64], r=[b_ps[6]], w=[b_Yacc])
                                    else:
                                        self.tt("dve", Yacc[:, ck, :], Yacc[:, ck, :], SQ[:, 0:64], ALU.add, r=[b_ps[6], b_Yacc], w=[b_Yacc])
                                pending.append(step)
                            while pend:
                                pend.pop(0)()
                        while pending:
                            pending.pop(0)()
                        bgrun(10 ** 9)
                        if d == 1:
                            bgq += self.em.captured(lambda ck0=ck0, nck=nck, sp=sp: finalize(j, ck0, nck, sp))
                    bgrun(10 ** 9) if False else self.em.replay(bgq, 10 ** 9)
            self.em.flush()

    def rstd_tile(self, src, b_src, n, sq, b_sq, rs, b_rs, pb):
        psb, b_ps = self.psb, self.b_ps
        self.act(sq[:, :, 0:n], src[:, :, 0:n], AF.Square, [b_src], [b_sq])
        for kc in range(KC):
            self.mm(psb[pb][:, 0:n], self.ones_bf[:], sq[:, kc, 0:n], start=(kc == 0), stop=(kc == KC - 1),
                    r=[b_sq, self.b_ones], w=[b_ps[pb]])
        self.act(rs[:, 0:n], psb[pb][:, 0:n], AF.Sqrt, [b_ps[pb]], [b_rs], scale=1.0 / D, bias=1e-6)
        self.em.op("dve", lambda e: e.reciprocal(out=rs[:, 0:n], in_=rs[:, 0:n]), [b_rs], [b_rs])

    def phase_merge_ffn(self, l, xsrc):
        cfg = self.cfg
        NT, T = cfg.NT, cfg.T
        psb, b_ps = self.psb, self.b_ps
        dr = self.dram
        mod, b_mod = self.mod, self.b_mod
        uT, xmid, aT = dr["uT"], dr["xmid"], dr["aT"]
        last = l == cfg.depth - 1
        TW = 256
        tiles = [(t0, TW, 1 if t0 < LCTX else 0) for t0 in range(0, NT, TW)]
        with contextlib.ExitStack() as st0:
            h2T = self.sb(st0, "m_h2T", [128, KC, NT], BF16)
            b_h2 = mkbufs("h2T", len(tiles))
            ng = self.sb(st0, "m_ng", [128, 4, KC], F32)
            cols = self.sb(st0, "m_cols", [128, 4, KC, 2], F32)
            wst = self.sb(st0, "m_wst", [128, 1024], F32)
            b_ng, b_cols, b_wst = Buf("ng"), Buf("cols"), Buf("wst")
            self.dma("sp", ng[:], dr["norm_g"][l], w=[b_ng])
            for kc in range(KC):
                self.ts("dve", cols[:, 0, kc, :], mod[:, 16 + kc, :], ng[:, 1, kc:kc + 1], None, ALU.mult, r=[b_mod, b_ng], w=[b_cols])
                self.ts("dve", cols[:, 1, kc, :], mod[:, 32 + kc, :], 1.0, ng[:, 2, kc:kc + 1], ALU.add, ALU.mult, r=[b_mod, b_ng], w=[b_cols])
                self.ts("dve", cols[:, 2, kc, :], mod[:, 40 + kc, :], ng[:, 3, kc:kc + 1], None, ALU.mult, r=[b_mod, b_ng], w=[b_cols])
            with contextlib.ExitStack() as st:
                sb = lambda name, shape, dt=F32: self.sb(st, "m_" + name, shape, dt)
                Wbr = sb("Wbr", [128, 3, KC, 1024], BF16)
                Wo = sb("Wo", [128, KC, 1024], BF16)
                b_W = Buf("Wm")
                for br in range(3):
                    for kc in range(KC):
                        self.dma("sp", wst[:], dr["w_branch"][l, br, kc * 128:(kc + 1) * 128, :], w=[b_wst])
                        self.cp("pool", Wbr[:, br, kc, :], wst[:], r=[b_wst], w=[b_W])
                for kc in range(KC):
                    self.dma("sp", wst[:], dr["w_out"][l, kc * 128:(kc + 1) * 128, :], w=[b_wst])
                    self.cp("pool", Wo[:, kc, :], wst[:], r=[b_wst], w=[b_W])
                yt = [sb(f"yt{i}", [128, KC, TW], BF16) for i in range(3)]
                gt = sb("gt", [128, KC, TW])
                mt = sb("mt", [128, KC, TW])
                tmp = sb("tmp", [128, TW])
                mb = sb("mb", [128, KC, TW], BF16)
                xts = [sb(f"xt{i}", [128, KC, TW]) for i in range(2)]
                b_xts = mkbufs("mxt", 2)
                mo = sb("mo", [128, KC, TW])
                sq = sb("sq", [128, KC, TW], BF16)
                rs = sb("rs", [128, TW])
                b_yt = mkbufs("yt", 3)
                b_gt, b_mt, b_tmp, b_mb, b_mo, b_sq, b_rs = [Buf(n) for n in "gt mt tmp mb mo sq rs".split()]
                ysrc = [dr["yA"], dr["yB"], dr["yC"]]
                xview = xsrc.rearrange("(kc p) n -> p kc n", p=128)
                xmview = xmid.rearrange("(kc p) n -> p kc n", p=128)
                k = 0
                bgq = []
                for ti, (t0, n, seg) in enumerate(tiles):
                    xs = ti % 2
                    xt, b_xt = xts[xs], b_xts[xs]
                    self.dma("sp", xt[:], xview[:, :, t0:t0 + n], w=[b_xt])
                    for br in range(3):
                        self.dma("sp", yt[br][:], ysrc[br].rearrange("(kc p) n -> p kc n", p=128)[:, :, t0:t0 + n], w=[b_yt[br]])
                        g0 = (66 + br * 8) * 128
                        self.dma("sp", gt[:], uT[g0:g0 + 1024, :].rearrange("(kc p) n -> p kc n", p=128)[:, :, t0:t0 + n], w=[b_gt])
                        self.act(gt[:], gt[:], AF.Sigmoid, [b_gt], [b_gt])
                        for oc in range(KC):
                            pb = k % 5
                            k += 1
                            for kc in range(KC):
                                self.mm(psb[pb][:, 0:n], Wbr[:, br, kc, oc * 128:(oc + 1) * 128], yt[br][:, kc, :],
                                        start=(kc == 0), stop=(kc == KC - 1), r=[b_W, b_yt[br]], w=[b_ps[pb]])
                            if br == 0:
                                self.tt("dve", mt[:, oc, :], psb[pb][:, 0:n], gt[:, oc, :], ALU.mult, r=[b_ps[pb], b_gt], w=[b_mt])
                            else:
                                self.tt("dve", tmp[:], psb[pb][:, 0:n], gt[:, oc, :], ALU.mult, r=[b_ps[pb], b_gt], w=[b_tmp])
                                self.tt("pool", mt[:, oc, :], mt[:, oc, :], tmp[:], ALU.add, r=[b_mt, b_tmp], w=[b_mt])
                            self.em.replay(bgq, (len(bgq) + (3 - br) * KC - oc - 1) // ((3 - br) * KC - oc) if bgq else 0)
                    self.em.replay(bgq, 10 ** 9)
                    self.act(mb[:], mt[:], AF.Copy, [b_mt], [b_mb])

                    def epilogue(ti=ti, t0=t0, n=n, seg=seg, xt=xt, b_xt=b_xt):
                        kk_ = 0
                        for oc in range(KC):
                            pb = 5
                            for kc in range(KC):
                                self.mm(psb[pb][:, 0:n], Wo[:, kc, oc * 128:(oc + 1) * 128], mb[:, kc, :],
                                        start=(kc == 0), stop=(kc == KC - 1), r=[b_W, b_mb], w=[b_ps[pb]])
                            self.act(mo[:, oc, :], psb[pb][:, 0:n], AF.Copy, [b_ps[pb]], [b_mo])
                        self.rstd_tile(mo, b_mo, n, sq, b_sq, rs, b_rs, 6)
                        for oc in range(KC):
                            self.tt("dve", mo[:, oc, :], mo[:, oc, :], rs[:], ALU.mult, r=[b_mo, b_rs], w=[b_mo])
                            self.stt(xt[:, oc, :], mo[:, oc, :], cols[:, 0, oc, seg:seg + 1], xt[:, oc, :], ALU.mult, ALU.add,
                                     r=[b_mo, b_cols, b_xt], w=[b_xt])
                        self.dma("pool", xmview[:, :, t0:t0 + n], xt[:], r=[b_xt])
                        self.rstd_tile(xt, b_xt, n, sq, b_sq, rs, b_rs, 7)
                        for kc in range(KC):
                            self.tt("dve", mo[:, kc, :], xt[:, kc, :], rs[:], ALU.mult, r=[b_xt, b_rs, b_mo], w=[b_mo])
                            self.act(h2T[:, kc, t0:t0 + n], mo[:, kc, :], AF.Identity, [b_mo, b_cols, b_mod], [b_h2[ti]],
                                     scale=cols[:, 1, kc, seg:seg + 1], bias=mod[:, 24 + kc, seg:seg + 1])
                    bgq += self.em.captured(epilogue)
                self.em.replay(bgq, 10 ** 9)
                self.em.flush()
            with contextlib.ExitStack() as st:
                sb = lambda name, shape, dt=F32: self.sb(st, "f_" + name, shape, dt)
                cw = sb("cw", [128, 22, 3])
                cb = sb("cb", [128, 22])
                b_par = Buf("fpar")
                self.dma("sp", cw[:], dr["ffn_conv_w"][l], w=[b_par])
                self.dma("sp", cb[:], dr["ffn_conv_b"][l], w=[b_par])
                wf = [sb(f"wf{i}", [128, KC, 128]) for i in range(2)]
                wb = [sb(f"wb{i}", [128, KC, 128], BF16) for i in range(2)]
                b_wf, b_wb = mkbufs("fwf", 2), mkbufs("fwb", 2)
                gp = sb("gp", [128, NT + 6])
                gc = sb("gc", [128, NT])
                ast = sb("ast", [128, NT], BF16)
                b_gp, b_gc, b_ast = Buf("gp"), Buf("gc"), Buf("ast")
                self.memset("pool", gp[:], 0.0, w=[b_gp])
                wview = dr["ffn_w_in"][l].rearrange("(kc p) n -> p kc n", p=128)
                k = 0
                kw = 0
                alltiles = list(enumerate(tiles))
                h2tiles = self.cfg.tiles
                for jc in range(22):
                    for part in range(2):
                        s = kw % 2
                        kw += 1
                        c0 = part * DFF + jc * 128
                        self.dma("sp", wf[s][:], wview[:, :, c0:c0 + 128], w=[b_wf[s]])
                        self.cp("pool", wb[s][:], wf[s][:], r=[b_wf[s]], w=[b_wb[s]])
                        for (t0, n, seg) in h2tiles:
                            pb = k % 8
                            k += 1
                            hb = [b_h2[i] for i, (a, m, sg_) in alltiles if a < t0 + n and a + m > t0]
                            for kc in range(KC):
                                self.mm(psb[pb][:, 0:n], wb[s][:, kc, :], h2T[:, kc, t0:t0 + n], start=(kc == 0), stop=(kc == KC - 1),
                                        r=[b_wb[s]] + hb, w=[b_ps[pb]])
                            if part == 0:
                                p0 = t0 + 2 if t0 < LCTX else t0 + 4
                                self.act(gp[:, p0:p0 + n], psb[pb][:, 0:n], AF.Copy, [b_ps[pb]], [b_gp])
                            else:
                                self.tt("dve", ast[:, t0:t0 + n], psb[pb][:, 0:n], gc[:, t0:t0 + n], ALU.mult,
                                        r=[b_ps[pb], b_gc], w=[b_ast])
                        if part == 0:
                            for (d0, s0, n) in self.segs():
                                self.ts("dve", gc[:, d0:d0 + n], gp[:, s0 - 1:s0 - 1 + n], cw[:, jc, 0:1], cb[:, jc:jc + 1], ALU.mult, ALU.add,
                                        r=[b_gp, b_par], w=[b_gc])
                                for tap in (1, 2):
                                    self.stt(gc[:, d0:d0 + n], gp[:, s0 - 1 + tap:s0 - 1 + tap + n], cw[:, jc, tap:tap + 1], gc[:, d0:d0 + n],
                                             ALU.mult, ALU.add, r=[b_gp, b_par, b_gc], w=[b_gc])
                            self.act(gc[:], gc[:], AF.Silu, [b_gc], [b_gc])
                    self.dma("pool", aT[jc * 128:(jc + 1) * 128, :], ast[:], r=[b_ast])
                self.em.flush()
        with contextlib.ExitStack() as st:
            sb = lambda name, shape, dt=F32: self.sb(st, "g_" + name, shape, dt)
            ng = sb("ng", [128, 4, KC])
            cols = sb("cols", [128, KC, 2])
            wst = sb("wst", [128, 1024])
            Wf = sb("Wf", [128, 22, 1024], BF16)
            b_ng, b_cols, b_wst, b_W = Buf("ng"), Buf("cols"), Buf("wst"), Buf("Wf")
            self.dma("sp", ng[:], dr["norm_g"][l], w=[b_ng])
            for kc in range(KC):
                self.ts("dve", cols[:, kc, :], mod[:, 40 + kc, :], ng[:, 3, kc:kc + 1], None, ALU.mult, r=[b_mod, b_ng], w=[b_cols])
            for kc in range(22):
                self.dma("sp", wst[:], dr["ffn_w_out"][l, kc * 128:(kc + 1) * 128, :], w=[b_wst])
                self.cp("pool", Wf[:, kc, :], wst[:], r=[b_wst], w=[b_W])
            at = [sb(f"at{i}", [128, 22, 512], BF16) for i in range(2)]
            xt = [sb(f"xt{i}", [128, KC, 512]) for i in range(2)]
            fo = sb("fo", [128, KC, 512])
            sq = sb("sq", [128, KC, 512], BF16)
            rs = sb("rs", [128, 512])
            b_at, b_xt = mkbufs("at", 2), mkbufs("gxt", 2)
            b_fo, b_sq, b_rs = Buf("fo"), Buf("gsq"), Buf("grs")
            xmview = xmid.rearrange("(kc p) n -> p kc n", p=128)
            aview = aT.rearrange("(kc p) n -> p kc n", p=128)
            k = 0
            for ti, (t0, n, seg) in enumerate(cfg.tiles):
                if last and seg == 1:
                    continue
                s = ti % 2
                self.dma("sp", at[s][:, :, 0:n], aview[:, :, t0:t0 + n], w=[b_at[s]])
                self.dma("sp", xt[s][:, :, 0:n], xmview[:, :, t0:t0 + n], w=[b_xt[s]])
                for oc in range(KC):
                    pb = k % 6
                    k += 1
                    for kc in range(22):
                        self.mm(psb[pb][:, 0:n], Wf[:, kc, oc * 128:(oc + 1) * 128], at[s][:, kc, 0:n], start=(kc == 0), stop=(kc == 21),
                                r=[b_W, b_at[s]], w=[b_ps[pb]])
                    self.act(fo[:, oc, 0:n], psb[pb][:, 0:n], AF.Copy, [b_ps[pb]], [b_fo])
                self.rstd_tile(fo, b_fo, n, sq, b_sq, rs, b_rs, 6 + ti % 2)
                for oc in range(KC):
                    self.tt("dve", fo[:, oc, 0:n], fo[:, oc, 0:n], rs[:, 0:n], ALU.mult, r=[b_fo, b_rs], w=[b_fo])
                    self.stt(xt[s][:, oc, 0:n], fo[:, oc, 0:n], cols[:, oc, seg:seg + 1], xt[s][:, oc, 0:n], ALU.mult, ALU.add,
                             r=[b_fo, b_cols, b_xt[s]], w=[b_xt[s]])
                if last:
                    dst = dr["outT"].rearrange("(kc p) n -> p kc n", p=128)[:, :, t0 - LCTX:t0 - LCTX + n]
                else:
                    dst = dr["xcur"].rearrange("(kc p) n -> p kc n", p=128)[:, :, t0:t0 + n]
                self.dma("pool", dst, xt[s][:, :, 0:n], r=[b_xt[s]])
            self.em.flush()

    def phase_mod(self, l, cvec, ada_w, ada_b, mod, b_mod):
        em = self.em
        psb, b_ps = self.psb, self.b_ps
        with contextlib.ExitStack() as st:
            cv = self.sb(st, "cv", [128, KC, 2], F32)
            cs = self.sb(st, "cs", [128, KC, 2], BF16)
            sg = self.sb(st, "cv_sg", [128, KC, 2], F32)
            ab = self.sb(st, "ab", [128, 48], F32)
            wf = [self.sb(st, f"adaw_f{i}", [128, KC, 512], F32) for i in range(2)]
            wb = [self.sb(st, f"adaw_b{i}", [128, KC, 512], BF16) for i in range(2)]
            b_cv, b_cs, b_ab, b_sg = Buf("cv"), Buf("cs"), Buf("ab"), Buf("sg")
            b_wf, b_wb = mkbufs("wf", 2), mkbufs("wb", 2)
            em.dma("sp", lambda e: e.dma_start(out=cv[:], in_=cvec[:, :, :]), writes=[b_cv])
            em.dma("sp", lambda e: e.dma_start(out=ab[:], in_=ada_b[l]), writes=[b_ab])
            em.op("act", lambda e: e.activation(out=sg[:], in_=cv[:], func=AF.Sigmoid), reads=[b_cv], writes=[b_sg])
            em.op("dve", lambda e: e.tensor_tensor(out=cs[:], in0=cv[:], in1=sg[:], op=ALU.mult),
                  reads=[b_cv, b_sg], writes=[b_cs])
            wview = ada_w[l].rearrange("(kc p) n -> p kc n", p=128)
            for g in range(12):
                s = g % 2
                em.dma("sp", lambda e, s=s, g=g: e.dma_start(out=wf[s][:], in_=wview[:, :, g * 512:(g + 1) * 512]),
                       writes=[b_wf[s]])
                em.op("pool", lambda e, s=s: e.tensor_copy(out=wb[s][:], in_=wf[s][:]),
                      reads=[b_wf[s]], writes=[b_wb[s]])
                pb = g % 8
                for q in range(4):
                    i = g * 4 + q
                    for kc in range(KC):
                        em.op("pe", lambda e, s=s, q=q, kc=kc, pb=pb: e.matmul(
                            psb[pb][:, q * 2:q * 2 + 2], lhsT=wb[s][:, kc, q * 128:(q + 1) * 128],
                            rhs=cs[:, kc, :], start=(kc == 0), stop=(kc == KC - 1)),
                            reads=[b_wb[s], b_cs], writes=[b_ps[pb]], defer=(kc != KC - 1))
                for q in range(4):
                    i = g * 4 + q
                    em.op("dve", lambda e, q=q, i=i, pb=pb: e.tensor_scalar(
                        out=mod[:, i, :], in0=psb[pb][:, q * 2:q * 2 + 2], scalar1=ab[:, i:i + 1], scalar2=None,
                        op0=ALU.add), reads=[b_ps[pb], b_ab], writes=[b_mod])
            em.flush()

    def phase_norm_inproj(self, l, xc, norm_g, w_in, uT, mod, b_mod, ones_bf, b_ones):
        cfg = self.cfg
        em = self.em
        NT = cfg.NT
        psb, b_ps = self.psb, self.b_ps
        with contextlib.ExitStack() as st:
            hT = self.sb(st, "hT", [128, KC, NT], BF16)
            b_h = mkbufs("hT", len(cfg.tiles))
            ng = self.sb(st, "ng", [128, 4, KC], F32)
            A1 = self.sb(st, "A1", [128, KC, 2], F32)
            b_ng, b_A1 = Buf("ng"), Buf("A1")
            em.dma("sp", lambda e: e.dma_start(out=ng[:], in_=norm_g[l]), writes=[b_ng])
            for kc in range(KC):
                em.op("dve", lambda e, kc=kc: e.tensor_scalar(
                    out=A1[:, kc, :], in0=mod[:, 8 + kc, :], scalar1=1.0, scalar2=ng[:, 0, kc:kc + 1],
                    op0=ALU.add, op1=ALU.mult), reads=[b_mod, b_ng], writes=[b_A1])
            with contextlib.ExitStack() as st2:
                xt = [self.sb(st2, f"xt{i}", [128, KC, 512], F32) for i in range(2)]
                sq = [self.sb(st2, f"sq{i}", [128, KC, 512], BF16) for i in range(2)]
                rs = [self.sb(st2, f"rs{i}", [128, 512], F32) for i in range(2)]
                tmp = [self.sb(st2, f"tmp{i}", [128, KC, 512], F32) for i in range(2)]
                b_xt, b_sq, b_rs, b_tmp = mkbufs("xt", 2), mkbufs("sq", 2), mkbufs("rs", 2), mkbufs("tmp", 2)
                xview = xc.rearrange("(kc p) n -> p kc n", p=128)
                for ti, (t0, n, seg) in enumerate(cfg.tiles):
                    s = ti % 2
                    pb = ti % 8
                    em.dma("sp", lambda e, s=s, t0=t0, n=n: e.dma_start(out=xt[s][:, :, 0:n], in_=xview[:, :, t0:t0 + n]),
                           writes=[b_xt[s]])
                    em.op("act", lambda e, s=s, n=n: e.activation(out=sq[s][:, :, 0:n], in_=xt[s][:, :, 0:n], func=AF.Square),
                          reads=[b_xt[s]], writes=[b_sq[s]])
                    for kc in range(KC):
                        em.op("pe", lambda e, s=s, n=n, kc=kc, pb=pb: e.matmul(
                            psb[pb][:, 0:n], lhsT=ones_bf[:], rhs=sq[s][:, kc, 0:n], start=(kc == 0), stop=(kc == KC - 1)),
                            reads=[b_sq[s], b_ones], writes=[b_ps[pb]], defer=(kc != KC - 1))
                    em.op("act", lambda e, s=s, n=n, pb=pb: e.activation(
                        out=rs[s][:, 0:n], in_=psb[pb][:, 0:n], func=AF.Sqrt, scale=1.0 / D, bias=1e-6),
                        reads=[b_ps[pb]], writes=[b_rs[s]])
                    em.op("dve", lambda e, s=s, n=n: e.reciprocal(out=rs[s][:, 0:n], in_=rs[s][:, 0:n]),
                          reads=[b_rs[s]], writes=[b_rs[s]])
                    for kc in range(KC):
                        em.op("dve", lambda e, s=s, n=n, kc=kc: e.tensor_tensor(
                            out=tmp[s][:, kc, 0:n], in0=xt[s][:, kc, 0:n], in1=rs[s][:, 0:n], op=ALU.mult),
                            reads=[b_xt[s], b_rs[s]], writes=[b_tmp[s]])
                        em.op("act", lambda e, s=s, n=n, kc=kc, t0=t0, seg=seg: e.activation(
                            out=hT[:, kc, t0:t0 + n], in_=tmp[s][:, kc, 0:n], func=AF.Identity,
                            scale=A1[:, kc, seg:seg + 1], bias=mod[:, kc, seg:seg + 1]),
                            reads=[b_tmp[s], b_A1, b_mod], writes=[b_h[ti]])
                em.flush()
            with contextlib.ExitStack() as st2:
                wf = [self.sb(st2, f"wf{i}", [128, KC, 128], F32) for i in range(2)]
                wb = [self.sb(st2, f"wb{i}", [128, KC, 128], BF16) for i in range(2)]
                stg = [self.sb(st2, f"stg{i}", [128, NT], F32) for i in range(2)]
                b_wf, b_wb, b_stg = mkbufs("wf", 2), mkbufs("wb", 2), mkbufs("stg", 2)
                wview = w_in[l].rearrange("(kc p) n -> p kc n", p=128)
                k = 0
                for j in range(self.NCH):
                    s = j % 2
                    em.dma("sp", lambda e, s=s, j=j: e.dma_start(out=wf[s][:], in_=wview[:, :, j * 128:(j + 1) * 128]),
                           writes=[b_wf[s]])
                    em.op("pool", lambda e, s=s: e.tensor_copy(out=wb[s][:], in_=wf[s][:]),
                          reads=[b_wf[s]], writes=[b_wb[s]])
                    for ti, (t0, n, seg) in enumerate(cfg.tiles):
                        pb = k % 8
                        k += 1
                        for kc in range(KC):
                            em.op("pe", lambda e, s=s, n=n, kc=kc, pb=pb, t0=t0: e.matmul(
                                psb[pb][:, 0:n], lhsT=wb[s][:, kc, :], rhs=hT[:, kc, t0:t0 + n],
                                start=(kc == 0), stop=(kc == KC - 1)),
                                reads=[b_wb[s], b_h[ti]], writes=[b_ps[pb]], defer=(kc != KC - 1))
                        if k % 2 == 0:
                            em.op("act", lambda e, s=s, n=n, pb=pb, t0=t0: e.activation(
                                out=stg[s][:, t0:t0 + n], in_=psb[pb][:, 0:n], func=AF.Copy),
                                reads=[b_ps[pb]], writes=[b_stg[s]])
                        else:
                            em.op("dve", lambda e, s=s, n=n, pb=pb, t0=t0: e.tensor_copy(
                                out=stg[s][:, t0:t0 + n], in_=psb[pb][:, 0:n]),
                                reads=[b_ps[pb]], writes=[b_stg[s]])
                    em.dma("pool", lambda e, s=s, j=j: e.dma_start(out=uT[j * 128:(j + 1) * 128, :], in_=stg[s][:]),
                           reads=[b_stg[s]], writes=[])
                em.flush()


def na_blocks(rows):
    nqb = rows // 8
    types = {}
    blocks = []
    for qb in range(nqb):
        q0 = qb * 8
        rs = lambda r: min(max(r - 4, 0), rows - 8)
        lo, hi = rs(q0), rs(q0 + 7) + 8
        cls = "f" if qb == 0 else ("l" if qb == nqb - 1 else "i")
        items = []
        for kr0 in range(lo, hi, 2):
            delta = kr0 - q0
            key = (cls, delta)
            if key not in types:
                types[key] = (len(types), q0, kr0)
            items.append((kr0, types[key][0], (delta + 4) // 2))
        blocks.append(items)
    return blocks, len(types)


def blocks_type(blocks, qb, kr0):
    for (k, ty, di) in blocks[qb]:
        if k == kr0:
            return ty
    raise KeyError


def na_mask_np(rows):
    nqb = rows // 8
    out = {}
    for qb in range(nqb):
        q0 = qb * 8
        rsf = lambda r: min(max(r - 4, 0), rows - 8)
        lo, hi = rsf(q0), rsf(q0 + 7) + 8
        cls = "f" if qb == 0 else ("l" if qb == nqb - 1 else "i")
        for kr0 in range(lo, hi, 2):
            key = (cls, kr0 - q0)
            if key in out:
                continue
            krow = kr0 + np.arange(2)[:, None, None, None]
            kc = np.arange(64)[None, :, None, None]
            qrow = q0 + np.arange(8)[None, None, :, None]
            qc = np.arange(64)[None, None, None, :]
            rs = np.clip(qrow - 4, 0, rows - 8)
            cs = np.clip(qc - 8, 0, 48)
            ok = (krow >= rs) & (krow < rs + 8) & (kc >= cs) & (kc < cs + 16)
            out[key] = np.where(ok, 0.0, -30000.0).reshape(128, 512).astype(np.float32)
    return np.stack(list(out.values()), axis=0)


def rope_tables(T):
    half = 32
    freqs = 10000.0 ** (-np.arange(0, half, 2, dtype=np.float32) / half)
    pos = np.arange(T)
    prow, pcol = pos // 64, pos % 64
    cos = np.zeros((64, T), np.float32)
    sin = np.zeros((64, T), np.float32)
    for d in range(64):
        p = prow if d < 32 else pcol
        dd = d % 32
        ang = p.astype(np.float32) * freqs[dd % 16]
        cos[d] = np.cos(ang)
        sin[d] = -np.sin(ang) if dd < 16 else np.sin(ang)
    return np.concatenate([cos, cos], 0), np.concatenate([sin, sin], 0)


def input_shapes(cfg):
    Ld, NT, T = cfg.depth, cfg.NT, cfg.T
    return {
        "xc": [D, NT], "cvec": [128, KC, 2],
        "ada_w": [Ld, D, 6 * D], "ada_b": [Ld, 128, 48], "norm_g": [Ld, 128, 4, KC],
        "w_in": [Ld, D, NIN_X],
        "lru_conv_w": [Ld, 128, 8, 4], "lru_conv_b": [Ld, 128, 8],
        "lru_gate_a_w": [Ld, 2, 16, 64, 64], "lru_gate_x_w": [Ld, 2, 16, 64, 64],
        "lru_gate_b": [Ld, 128, 2, 2, 8], "lru_lambda": [Ld, 128, 2, 8],
        "ident": [128, 128], "rope_cos": [128, T], "rope_sin": [128, T],
        "na_mask": [na_blocks(T // 64)[1], 128, 512], "rpb_pad": [Ld, 16, 24, 128],
        "bdones": [128, 128], "istack": [128, 64], "rk_mask": [2, 128, 1152],
        "rwkv_mu": [Ld, 128, 26, 2], "rwkv_w0a0": [Ld, 128, 2, 2, 8], "rwkv_vec": [Ld, 128, 5, 8],
        "w_branch": [Ld, 3, D, D], "w_out": [Ld, D, D], "ffn_w_in": [Ld, D, 2 * DFF], "ffn_w_out": [Ld, DFF, D],
        "ffn_conv_w": [Ld, 128, 22, 3], "ffn_conv_b": [Ld, 128, 22],
        "rwkv_w_up": [Ld, 2, 64, 1024], "rwkv_a_up": [Ld, 2, 64, 1024], "rwkv_g_up": [Ld, 128, 1024],
    }


def colfmt(v, n):
    v = np.asarray(v)
    return np.moveaxis(v.reshape(v.shape[:-1] + (n, 128)), -1, -2)


def rope_perm():
    idx = np.arange(1024)
    d = idx % 64
    dd = d % 32
    partner = np.where(dd < 16, idx + 16, idx - 16)
    return partner


def prep_shared(inp, cfg):
    f = lambda a: np.ascontiguousarray(a, dtype=np.float32)
    Ld = cfg.depth
    m = {}
    m["ada_w"] = f(inp["ada_w"][:Ld])
    m["ada_b"] = f(colfmt(inp["ada_b"][:Ld], 48))
    m["norm_g"] = f(colfmt(inp["norm_g"][:Ld], KC).transpose(0, 2, 1, 3))
    w_in = inp["w_in"][:Ld]
    C0 = A_COLS + 2048
    perm = rope_perm()
    wq = w_in[:, :, C0:C0 + 1024][:, :, perm]
    wk = w_in[:, :, C0 + 1024:C0 + 2048][:, :, perm]
    m["w_in"] = f(np.concatenate([w_in, wq, wk], axis=2))
    m["lru_conv_w"] = f(colfmt(inp["lru_conv_w"][:Ld], 8).transpose(0, 2, 3, 1))
    m["lru_conv_b"] = f(colfmt(inp["lru_conv_b"][:Ld], 8))
    m["lru_gate_a_w"] = f(inp["lru_gate_a_w"][:Ld])
    m["lru_gate_x_w"] = f(inp["lru_gate_x_w"][:Ld])
    gb = np.stack([inp["lru_gate_a_b"][:Ld], inp["lru_gate_x_b"][:Ld]], axis=1)
    m["lru_gate_b"] = f(colfmt(gb, 8).transpose(0, 3, 1, 2, 4))
    m["lru_lambda"] = f(colfmt(inp["lru_lambda"][:Ld], 8).transpose(0, 2, 1, 3))
    m["ident"] = np.eye(128, dtype=np.float32)
    cos, sin = rope_tables(cfg.T)
    m["rope_cos"], m["rope_sin"] = f(cos), f(sin)
    m["na_mask"] = f(na_mask_np(cfg.T // 64))
    rp = np.zeros((Ld, 16, 24, 128), np.float32)
    rp[:, :, 4:19, 48:79] = inp["na_rpb"][:Ld]
    m["rpb_pad"] = rp
    blk = np.kron(np.eye(2, dtype=np.float32), np.ones((64, 64), np.float32))
    m["bdones"] = blk
    m["istack"] = np.concatenate([np.eye(64, dtype=np.float32)] * 2, axis=0)
    i64 = np.arange(64)
    U = np.kron(np.eye(2), (i64[:, None] < i64[None, :])).astype(np.float32)
    UI = np.kron(np.eye(2), (i64[:, None] <= i64[None, :])).astype(np.float32)
    Lw, LI = U.T.copy(), UI.T.copy()
    ONE = np.ones((128, 128), np.float32)
    m32 = np.kron(np.eye(4), np.ones((32, 32))).astype(np.float32)
    fwd = np.concatenate([U * m32, UI, ONE, ONE, UI, ONE, Lw * m32, Lw, Lw * (1 - m32)], axis=1)
    bwd = np.concatenate([Lw * m32, LI, ONE, ONE, LI, ONE, U * m32, U, U * (1 - m32)], axis=1)
    m["rk_mask"] = np.stack([fwd, bwd], axis=0)
    m["rwkv_mu"] = f(colfmt(inp["rwkv_mu"][:Ld], 26).transpose(0, 2, 3, 1))
    w0a0 = np.stack([inp["rwkv_w0"][:Ld], inp["rwkv_a0"][:Ld]], axis=1)
    m["rwkv_w0a0"] = f(colfmt(w0a0, 8).transpose(0, 3, 1, 2, 4))
    vec = np.stack([inp["rwkv_k_k"][:Ld], inp["rwkv_k_a"][:Ld], inp["rwkv_r_k"][:Ld].reshape(Ld, 1024),
                    inp["rwkv_lnx_w"][:Ld], inp["rwkv_lnx_b"][:Ld]], axis=1)
    m["rwkv_vec"] = f(colfmt(vec, 8).transpose(0, 2, 1, 3))
    m["rwkv_w_up"] = f(inp["rwkv_w_up"][:Ld])
    m["rwkv_a_up"] = f(inp["rwkv_a_up"][:Ld])
    m["rwkv_g_up"] = f(inp["rwkv_g_up"][:Ld])
    for k in ("w_branch", "w_out", "ffn_w_in", "ffn_w_out"):
        m[k] = f(inp[k][:Ld])
    m["ffn_conv_w"] = f(colfmt(inp["ffn_conv_w"][:Ld], 22).transpose(0, 2, 3, 1))
    m["ffn_conv_b"] = f(colfmt(inp["ffn_conv_b"][:Ld], 22))
    return m


def prep_inputs(inp, b, cfg, shared=None):
    T = cfg.T
    f = lambda a: np.ascontiguousarray(a, dtype=np.float32)
    m = dict(shared if shared is not None else prep_shared(inp, cfg))
    m["xc"] = f(np.concatenate([inp["ctx"][b].T, inp["x"][b, :T].T], axis=1))
    cv = np.stack([inp["c"][b], inp["c_ctx"]], axis=1)
    m["cvec"] = f(cv.reshape(KC, 128, 2).transpose(1, 0, 2))
    return m


def kernel(**inputs):
    cfg = Cfg()
    bld = Builder(cfg)
    nc = bld.build()
    inp = {k: np.asarray(v) for k, v in inputs.items()}
    shared = prep_shared(inp, cfg)
    in_maps = [prep_inputs(inp, b, cfg, shared) for b in range(8)]
    res = run_bass_kernel_spmd(nc, in_maps, core_ids=list(range(8)))
    out = np.stack([r["outT"].T for r in res.results], axis=0)
    return out.astype(np.float32)
```

```python
import contextlib
import numpy as np
import concourse.bass as bass
import concourse.mybir as mybir
from concourse.bass_utils import run_bass_kernel_spmd

F32 = mybir.dt.float32
BF16 = mybir.dt.bfloat16
AF = mybir.ActivationFunctionType
ALU = mybir.AluOpType
AX = mybir.AxisListType

D = 1024
KC = 8
LCTX = 256
DFF = 2816
NHEAD = 16
A_COLS = 3 * 1024 + 256
N_IN = 11520
NIN_X = N_IN + 2048
ENGS = ("pe", "act", "dve", "pool", "sp")
NDMA_SEMS = 24
NODEFER = False
CHECK_DEADLOCK = True


class Buf:
    __slots__ = ("name", "w", "readers")

    def __init__(self, name):
        self.name = name
        self.w = None
        self.readers = []


def mkbufs(name, n):
    return [Buf(f"{name}{i}") for i in range(n)]


class Emit:
    def __init__(self, nc, stack):
        self.nc = nc
        self.prog = {e: [] for e in ENGS}
        self.count = {e: 0 for e in ENGS}
        self.seen = {e: {} for e in ENGS}
        self.pend_inc = {}
        self.capture_list = None
        self.dma_total = [0] * NDMA_SEMS
        self.dma_rr = 0
        self.n_instr = 0
        self.sems = {}
        for e in ENGS:
            self.sems[e] = stack.enter_context(nc.semaphore(f"c_{e}"))
        for k in range(NDMA_SEMS):
            self.sems[("dma", k)] = stack.enter_context(nc.semaphore(f"d_{k}"))

    def _deps(self, eng, reads, writes):
        deps = {}

        def add(d):
            if d is None:
                return
            k, v = d
            if deps.get(k, 0) < v:
                deps[k] = v
        for b in reads:
            add(b.w)
        for b in writes:
            add(b.w)
            for r in b.readers:
                add(r)
        waits = []
        seen = self.seen[eng]
        for k, v in deps.items():
            if k == eng and v > self.count[eng]:
                continue
            if seen.get(k, 0) < v:
                seen[k] = v
                waits.append((k, v))
        return waits

    def _commit(self, me, reads, writes):
        for b in reads:
            b.readers.append(me)
            if len(b.readers) > 32:
                mx = {}
                for k, v in b.readers:
                    if mx.get(k, 0) < v:
                        mx[k] = v
                b.readers = list(mx.items())
        for b in writes:
            b.w = me
            b.readers = []

    def op(self, eng, fn, reads=(), writes=(), defer=False):
        if self.capture_list is not None:
            self.capture_list.append(("op", eng, fn, reads, writes, defer))
            return
        waits = self._deps(eng, reads, writes)
        if defer and not NODEFER:
            me = (eng, self.count[eng] + 1)
            self.pend_inc[eng] = 1
            self.prog[eng].append((waits, fn, None))
        else:
            self.count[eng] += 1
            me = (eng, self.count[eng])
            self.prog[eng].append((waits, fn, (eng, 1)))
            self.pend_inc[eng] = 0
        self._commit(me, reads, writes)
        self.n_instr += 1 + len(waits)

    def dma(self, q, fn, reads=(), writes=()):
        if self.capture_list is not None:
            self.capture_list.append(("dma", q, fn, reads, writes))
            return
        k = self.dma_rr
        self.dma_rr = (self.dma_rr + 1) % NDMA_SEMS
        key = ("dma", k)
        waits = self._deps(q, reads, writes)
        prev = self.dma_total[k]
        if prev > 0 and self.seen[q].get(key, 0) < prev:
            self.seen[q][key] = prev
            waits.append((key, prev))
        self.dma_total[k] += 16
        me = (key, self.dma_total[k])
        self.prog[q].append((waits, fn, (key, 16)))
        self._commit(me, reads, writes)
        self.n_instr += 1 + len(waits)

    def captured(self, f):
        lst = []
        self.capture_list = lst
        f()
        self.capture_list = None
        return lst

    def replay(self, lst, k):
        while k > 0 and lst:
            e = lst.pop(0)
            if e[0] == "op":
                self.op(*e[1:])
            else:
                self.dma(*e[1:])
            k -= 1

    def check_deadlock(self):
        val = dict(getattr(self, "_simval", {}))
        pos = {e: 0 for e in ENGS}
        progress = True
        while progress:
            progress = False
            for e in ENGS:
                q = self.prog[e]
                while pos[e] < len(q):
                    waits, fn, inc = q[pos[e]]
                    if all(val.get(k, 0) >= v for k, v in waits):
                        if inc is not None:
                            val[inc[0]] = val.get(inc[0], 0) + inc[1]
                        pos[e] += 1
                        progress = True
                    else:
                        break
        stuck = {e: pos[e] for e in ENGS if pos[e] < len(self.prog[e])}
        if stuck:
            for e, p in stuck.items():
                waits, fn, inc = self.prog[e][p]
                print("DEADLOCK: engine", e, "stuck at", p, "/", len(self.prog[e]), "waits",
                      [(k, v, val.get(k, 0)) for k, v in waits if val.get(k, 0) < v])
            raise RuntimeError("deadlock detected in emitted program")
        self._simval = val

    def flush(self):
        assert all(v == 0 for v in self.pend_inc.values()), "deferred semaphore increment left dangling"
        if CHECK_DEADLOCK:
            self.check_deadlock()
        nc = self.nc
        prog = self.prog
        sems = self.sems
        dma_fin = [(("dma", k), v) for k, v in enumerate(self.dma_total) if v > 0]

        def run(name, eng):
            for waits, fn, inc in prog[name]:
                for k, v in waits:
                    eng.wait_ge(sems[k], v)
                ins = fn(eng)
                if inc is not None:
                    ins.then_inc(sems[inc[0]], inc[1])
            if name in ("sp", "pool", "act"):
                for k, v in dma_fin:
                    eng.wait_ge(sems[k], v)

        with nc.Block() as block:
            @block.tensor
            def _(t):
                run("pe", t)

            @block.scalar
            def _(a):
                run("act", a)

            @block.vector
            def _(v):
                run("dve", v)

            @block.gpsimd
            def _(g):
                run("pool", g)

            @block.sync
            def _(s):
                run("sp", s)
        for k, v in dma_fin:
            for e in ENGS:
                self.seen[e][k] = v
        self.prog = {e: [] for e in ENGS}


class Cfg:
    def __init__(self, T=4096, depth=2, dbg=False):
        self.T = T
        self.NT = LCTX + T
        self.depth = depth
        self.dbg = dbg
        self.phases = ("lru", "na", "rwkv", "merge", "ffn")
        self.tiles = [(0, LCTX, 1)] + [(LCTX + 512 * i, 512, 0) for i in range(T // 512)]


class Builder:
    def __init__(self, cfg):
        self.cfg = cfg
        self.nc = bass.Bass("TRN2", target_bir_lowering=False)
        self.dram = {}

    def din(self, name, shape, dt=F32):
        t = self.nc.dram_tensor(name, list(shape), dt, kind="ExternalInput").ap()
        self.dram[name] = t
        return t

    def dscratch(self, name, shape, dt=F32, out=False):
        kind = "ExternalOutput" if (out or self.cfg.dbg) else "Internal"
        t = self.nc.dram_tensor(name, list(shape), dt, kind=kind).ap()
        self.dram[name] = t
        return t

    def sb(self, st, name, shape, dt):
        self._uid = getattr(self, "_uid", 0) + 1
        return st.enter_context(self.nc.sbuf_tensor(f"{name}_{self._uid}", list(shape), dt))

    def ps(self, st, name, shape, dt=F32):
        return st.enter_context(self.nc.psum_tensor(name, list(shape), dt))

    def build(self, upto=99):
        cfg = self.cfg
        nc = self.nc
        NT, T = cfg.NT, cfg.T
        Ld = cfg.depth
        self.NCH = NIN_X // 128
        for name, shape in input_shapes(cfg).items():
            self.din(name, shape)
        self.dscratch("uT", [NIN_X, NT])
        self.dscratch("yA", [D, NT], BF16)
        self.dscratch("yB", [D, NT], BF16)
        self.dscratch("yC", [D, NT], BF16)
        self.dscratch("aT", [DFF, NT], BF16)
        self.dscratch("xcur", [D, NT])
        self.dscratch("xmid", [D, NT])
        self.dscratch("outT", [D, T], out=True)
        dr = self.dram
        with contextlib.ExitStack() as outer:
            em = Emit(nc, outer)
            self.em = em
            mod = self.sb(outer, "mod", [128, 48, 2], F32)
            ones_bf = self.sb(outer, "ones_bf", [128, 128], BF16)
            b_mod = Buf("mod")
            b_ones = Buf("ones")
            self.mod, self.b_mod, self.ones_bf, self.b_ones = mod, b_mod, ones_bf, b_ones
            em.op("pool", lambda e: e.memset(ones_bf[:], 1.0), writes=[b_ones])
            psb = [self.ps(outer, f"psb{i}", [128, 512]) for i in range(8)]
            b_ps = mkbufs("ps", 8)
            self.psb, self.b_ps = psb, b_ps
            for l in range(Ld):
                xsrc = dr["xc"] if l == 0 else dr["xcur"]
                self.phase_mod(l, dr["cvec"], dr["ada_w"], dr["ada_b"], mod, b_mod)
                self.phase_norm_inproj(l, xsrc, dr["norm_g"], dr["w_in"], dr["uT"], mod, b_mod, ones_bf, b_ones)
                if upto <= 2:
                    break
                if "lru" in cfg.phases:
                    self.phase_lru(l)
                if "na" in cfg.phases:
                    self.phase_na(l)
                if "rwkv" in cfg.phases:
                    self.phase_rwkv(l)
                if "merge" in cfg.phases:
                    self.phase_merge_ffn(l, xsrc)
        return nc

    def mm(self, out, lhsT, rhs, start=True, stop=True, r=(), w=(), defer=None):
        if defer is None:
            defer = not stop
        self.em.op("pe", lambda e: e.matmul(out, lhsT=lhsT, rhs=rhs, start=start, stop=stop), r, w, defer=defer)

    def act(self, out, in_, func, r=(), w=(), scale=1.0, bias=0.0):
        self.em.op("act", lambda e: e.activation(out=out, in_=in_, func=func, scale=scale, bias=bias), r, w)

    def tt(self, eng, out, in0, in1, op, r=(), w=()):
        self.em.op(eng, lambda e: e.tensor_tensor(out=out, in0=in0, in1=in1, op=op), r, w)

    def ts(self, eng, out, in0, s1, s2, op0, op1=None, r=(), w=()):
        if op1 is None:
            self.em.op(eng, lambda e: e.tensor_scalar(out=out, in0=in0, scalar1=s1, scalar2=None, op0=op0), r, w)
        else:
            self.em.op(eng, lambda e: e.tensor_scalar(out=out, in0=in0, scalar1=s1, scalar2=s2, op0=op0, op1=op1), r, w)

    def stt(self, out, in0, sc, in1, op0, op1, r=(), w=()):
        self.em.op("dve", lambda e: e.scalar_tensor_tensor(out=out, in0=in0, scalar=sc, in1=in1, op0=op0, op1=op1), r, w)

    def cp(self, eng, out, in_, r=(), w=()):
        self.em.op(eng, lambda e: e.tensor_copy(out=out, in_=in_), r, w)

    def memset(self, eng, ap, val, w=()):
        self.em.op(eng, lambda e: e.memset(ap, val), (), w)

    def scan(self, out, d0, d1, init, r=(), w=()):
        self.em.op("dve", lambda e: e.tensor_tensor_scan(out=out, data0=d0, data1=d1, initial=init, op0=ALU.mult, op1=ALU.add), r, w)

    def dma(self, q, out, in_, r=(), w=()):
        self.em.dma(q, lambda e: e.dma_start(out=out, in_=in_), r, w)

    def segs(self):
        return [(0, 2, LCTX), (LCTX, LCTX + 4, self.cfg.T)]

    def phase_lru(self, l):
        cfg = self.cfg
        NT, T = cfg.NT, cfg.T
        psb, b_ps = self.psb, self.b_ps
        dr = self.dram
        uT, yB = dr["uT"], dr["yB"]
        B0 = A_COLS
        with contextlib.ExitStack() as st:
            cw = self.sb(st, "l_cw", [128, 8, 4], F32)
            cb = self.sb(st, "l_cb", [128, 8], F32)
            gab = self.sb(st, "l_gab", [128, 2, 2, 8], F32)
            lam = self.sb(st, "l_lam", [128, 2, 8], F32)
            cl = self.sb(st, "l_cl", [128, 2, 8], F32)
            b_par, b_cl = Buf("lpar"), Buf("lcl")
            self.dma("sp", cw[:], dr["lru_conv_w"][l], w=[b_par])
            self.dma("sp", cb[:], dr["lru_conv_b"][l], w=[b_par])
            self.dma("sp", gab[:], dr["lru_gate_b"][l], w=[b_par])
            self.dma("sp", lam[:], dr["lru_lambda"][l], w=[b_par])
            self.act(cl[:], lam[:], AF.Exp, [b_par], [b_cl], scale=-1.0)
            self.act(cl[:], cl[:], AF.Ln, [b_cl], [b_cl], bias=1.0)
            self.ts("dve", cl[:], cl[:], -8.0, None, ALU.mult, r=[b_cl], w=[b_cl])
            xp = self.sb(st, "l_xp", [128, NT + 6], F32)
            xb = self.sb(st, "l_xb", [128, NT], F32)
            xbb = self.sb(st, "l_xbb", [128, NT], BF16)
            gt = self.sb(st, "l_gt", [128, NT], F32)
            gtb = self.sb(st, "l_gtb", [128, NT], BF16)
            A = self.sb(st, "l_A", [128, NT], F32)
            Bt = self.sb(st, "l_B", [128, NT], F32)
            Ct = self.sb(st, "l_C", [128, NT], F32)
            hf = self.sb(st, "l_hf", [128, NT], F32)
            ys = self.sb(st, "l_ys", [128, NT], BF16)
            wgf = [self.sb(st, f"l_wgf{i}", [128, 128], F32) for i in range(2)]
            wgb = [self.sb(st, f"l_wgb{i}", [128, 128], BF16) for i in range(2)]
            b_xp, b_xb, b_xbb, b_gt, b_gtb, b_A, b_B, b_C, b_hf, b_ys = [Buf(n) for n in
                "xp xb xbb gt gtb A B C hf ys".split()]
            b_wgf, b_wgb = mkbufs("wgf", 2), mkbufs("wgb", 2)
            self.memset("pool", xp[:], 0.0, w=[b_xp])
            for i in range(2):
                self.memset("pool", wgf[i][:], 0.0, w=[b_wgf[i]])
            gw = [dr["lru_gate_a_w"], dr["lru_gate_x_w"]]

            def rev(ap):
                aps = [list(p) for p in ap.ap]
                n, stp = aps[-1][1], aps[-1][0]
                aps[-1] = [-stp, n]
                return bass.AP(ap.tensor, ap.offset + stp * (n - 1), aps)
            k = 0
            for j in range(8):
                for (d0, s0, n) in self.segs():
                    self.dma("sp", xp[:, s0:s0 + n], uT[B0 + j * 128:B0 + (j + 1) * 128, d0:d0 + n], w=[b_xp])
                self.dma("sp", gt[:], uT[B0 + 1024 + j * 128:B0 + 1024 + (j + 1) * 128, :], w=[b_gt])
                for (d0, s0, n) in self.segs():
                    self.act(xb[:, d0:d0 + n], xp[:, s0 - 2:s0 - 2 + n], AF.Identity, [b_xp, b_par], [b_xb],
                             scale=cw[:, j, 0:1], bias=cb[:, j:j + 1])
                    for tap in range(1, 4):
                        self.stt(xb[:, d0:d0 + n], xp[:, s0 - 2 + tap:s0 - 2 + tap + n], cw[:, j, tap:tap + 1], xb[:, d0:d0 + n],
                                 ALU.mult, ALU.add, r=[b_xp, b_par, b_xb], w=[b_xb])
                self.cp("pool", xbb[:], xb[:], r=[b_xb], w=[b_xbb])
                self.act(gtb[:], gt[:], AF.Gelu_apprx_tanh, [b_gt], [b_gtb])
                for d in range(2):
                    for g in range(2):
                        for hb in range(2):
                            self.dma("sp", wgf[g][hb * 64:(hb + 1) * 64, hb * 64:(hb + 1) * 64], gw[g][l, d, 2 * j + hb],
                                     w=[b_wgf[g]])
                        self.cp("pool", wgb[g][:], wgf[g][:], r=[b_wgf[g]], w=[b_wgb[g]])
                    for g, (dst, b_dst) in enumerate([(A, b_A), (Bt, b_B)]):
                        for (t0, n, seg) in cfg.tiles:
                            pb = k % 8
                            k += 1
                            self.mm(psb[pb][:, 0:n], wgb[g][:], xbb[:, t0:t0 + n], r=[b_wgb[g], b_xbb], w=[b_ps[pb]])
                            self.act(dst[:, t0:t0 + n], psb[pb][:, 0:n], AF.Sigmoid, [b_ps[pb], b_par], [b_dst],
                                     bias=gab[:, g, d, j:j + 1])
                    self.act(A[:], A[:], AF.Exp, [b_A, b_cl], [b_A], scale=cl[:, d, j:j + 1])
                    self.act(Ct[:], A[:], AF.Square, [b_A], [b_C])
                    self.act(Ct[:], Ct[:], AF.Sqrt, [b_C], [b_C], scale=-1.0, bias=1.0)
                    self.tt("pool", Bt[:], Bt[:], xb[:], ALU.mult, r=[b_B, b_xb], w=[b_B])
                    self.tt("dve", Ct[:], Ct[:], Bt[:], ALU.mult, r=[b_C, b_B], w=[b_C])
                    if d == 0:
                        self.scan(hf[:], A[:], Ct[:], 0.0, r=[b_A, b_C], w=[b_hf])
                    else:
                        for (d0, s0, n) in self.segs():
                            self.cp("pool", Bt[:, d0:d0 + n], rev(A[:, d0:d0 + n]), r=[b_A], w=[b_B])
                            self.cp("pool", gt[:, d0:d0 + n], rev(Ct[:, d0:d0 + n]), r=[b_C], w=[b_gt])
                        self.scan(A[:], Bt[:], gt[:], 0.0, r=[b_B, b_gt, b_A], w=[b_A])
                        for (d0, s0, n) in self.segs():
                            self.cp("pool", Ct[:, d0:d0 + n], rev(A[:, d0:d0 + n]), r=[b_A], w=[b_C])
                        self.tt("dve", hf[:], hf[:], Ct[:], ALU.add, r=[b_hf, b_C], w=[b_hf])
                self.tt("dve", ys[:], hf[:], gtb[:], ALU.mult, r=[b_hf, b_gtb], w=[b_ys])
                self.dma("pool", yB[j * 128:(j + 1) * 128, :], ys[:], r=[b_ys])
            self.em.flush()

    def phase_na(self, l):
        cfg = self.cfg
        NT, T = cfg.NT, cfg.T
        psb, b_ps = self.psb, self.b_ps
        dr = self.dram
        uT, yC = dr["uT"], dr["yC"]
        update_ctx = (l < cfg.depth - 1) or getattr(cfg, 'force_ctx', False)
        rows = T // 64
        blocks, ntype = na_blocks(rows)
        NTB = NT // 128
        CQ, CK, CV, CQP, CKP = 42, 50, 58, 90, 98
        rp = dr["rpb_pad"]
        with contextlib.ExitStack() as st:
            ident = self.sb(st, "n_ident", [128, 128], F32)
            onesp = self.sb(st, "n_onesp", [128, 2, 128], BF16)
            maskb = self.sb(st, "n_maskb", [128, ntype, 512], BF16)
            mtmp = [self.sb(st, f"n_mtmp{i}", [128, 512], F32) for i in range(2)]
            b_id, b_op, b_mk = Buf("ident"), Buf("onesp"), Buf("maskb")
            b_mt = mkbufs("mtmp", 2)
            self.dma("sp", ident[:], dr["ident"][:, :], w=[b_id])
            self.memset("pool", onesp[:], 0.0, w=[b_op])
            self.memset("pool", onesp[:, 0, 0:64], 1.0, w=[b_op])
            self.memset("pool", onesp[:, 1, 64:128], 1.0, w=[b_op])
            for t in range(ntype):
                self.dma("sp", mtmp[t % 2][:], dr["na_mask"][t], w=[b_mt[t % 2]])
                self.cp("pool", maskb[:, t, :], mtmp[t % 2][:], r=[b_mt[t % 2]], w=[b_mk])
            NIN = 7
            tl = [[self.sb(st, f"n_tl{a}_{i}", [128, 512], F32) for i in range(2)] for a in range(NIN)]
            b_tl = [mkbufs(f"tl{a}_", 2) for a in range(NIN)]
            qpl = self.sb(st, "n_qpl", [128, NT], BF16)
            kpl = self.sb(st, "n_kpl", [128, 2, LCTX], BF16)
            qrot = self.sb(st, "n_qrot", [128, T], BF16)
            krot = self.sb(st, "n_krot", [128, 2, T], BF16)
            Vp = self.sb(st, "n_Vp", [128, NTB, 2, 128], BF16)
            Tc2 = self.sb(st, "n_Tc2", [128, 22 * 64], F32)
            biasd = self.sb(st, "n_biasd", [128, 8, 512], F32)
            bm = [self.sb(st, f"n_bm{i}", [128, ntype, 512], BF16) for i in range(2)]
            sT = [self.sb(st, f"n_sT{i}", [128, 512], F32) for i in range(2)]
            pT = [self.sb(st, f"n_pT{i}", [128, 512], BF16) for i in range(3)]
            rc = [self.sb(st, f"n_rc{i}", [128, 512], F32) for i in range(2)]
            yst = self.sb(st, "n_yst", [128, NT], BF16)
            b_qpl, b_kpl, b_qrot, b_krot, b_Vp, b_Tc2, b_biasd, b_yst = [Buf(n) for n in
                "qpl kpl qrot krot Vp Tc2 biasd yst".split()]
            b_bm, b_sT, b_pT, b_rc = mkbufs("bm", 2), mkbufs("sT", 2), mkbufs("pT", 3), mkbufs("rc", 2)
            self.memset("pool", Vp[:], 0.0, w=[b_Vp])
            self.memset("pool", yst[:], 0.0, w=[b_yst])
            self.memset("pool", kpl[:], 0.0, w=[b_kpl])
            self.memset("pool", krot[:], 0.0, w=[b_krot])

            def rev(ap):
                aps = [list(p) for p in ap.ap]
                n, stp = aps[-1][1], aps[-1][0]
                aps[-1] = [-stp, n]
                return bass.AP(ap.tensor, ap.offset + stp * (n - 1), aps)
            kq = 0
            ks = 0
            kp_ = 0
            kacc = 0
            for j in range(8):
                for ti, (t0, n, seg) in enumerate(cfg.tiles):
                    s = ti % 2
                    rowsrc = [CQ + j, CQP + j, CK + j, CKP + j, CV + j]
                    need = [0, 2, 4] if seg == 1 else [0, 1, 2, 3, 4]
                    for a in need:
                        c = rowsrc[a]
                        self.dma("sp", tl[a][s][:, 0:n], uT[c * 128:(c + 1) * 128, t0:t0 + n], w=[b_tl[a][s]])
                    if seg == 1:
                        self.act(qpl[:, t0:t0 + n], tl[0][s][:, 0:n], AF.Copy, [b_tl[0][s]], [b_qpl])
                        self.act(kpl[0:64, 0, 0:n], tl[2][s][0:64, 0:n], AF.Copy, [b_tl[2][s]], [b_kpl])
                        self.act(kpl[64:128, 1, 0:n], tl[2][s][64:128, 0:n], AF.Copy, [b_tl[2][s]], [b_kpl])
                    else:
                        lt0 = t0 - LCTX
                        self.dma("sp", tl[5][s][:], dr["rope_cos"][:, lt0:lt0 + 512], w=[b_tl[5][s]])
                        self.dma("sp", tl[6][s][:], dr["rope_sin"][:, lt0:lt0 + 512], w=[b_tl[6][s]])
                        self.act(qpl[:, t0:t0 + n], tl[0][s][:], AF.Copy, [b_tl[0][s]], [b_qpl])
                        for (a, ap_, dst, b_dst, eng) in [(0, 1, qrot, b_qrot, "dve"), (2, 3, krot, b_krot, "dve")]:
                            self.tt(eng, tl[a][s][:], tl[a][s][:], tl[5][s][:], ALU.mult,
                                    r=[b_tl[a][s], b_tl[5][s]], w=[b_tl[a][s]])
                            self.tt(eng, tl[ap_][s][:], tl[ap_][s][:], tl[6][s][:], ALU.mult,
                                    r=[b_tl[ap_][s], b_tl[6][s]], w=[b_tl[ap_][s]])
                            if dst is krot:
                                for hh in range(2):
                                    hs = slice(hh * 64, (hh + 1) * 64)
                                    self.tt(eng, krot[hs, hh, lt0:lt0 + 512], tl[a][s][hs, :], tl[ap_][s][hs, :], ALU.add,
                                            r=[b_tl[a][s], b_tl[ap_][s]], w=[b_dst])
                            else:
                                self.tt(eng, dst[:, lt0:lt0 + 512], tl[a][s][:], tl[ap_][s][:], ALU.add,
                                        r=[b_tl[a][s], b_tl[ap_][s]], w=[b_dst])
                    nb = n // 128
                    pb = kq % 4
                    kq += 1
                    for q in range(nb):
                        self.em.op("pe", lambda e, pb=pb, q=q, s=s: e.transpose(
                            psb[pb][:, q * 128:(q + 1) * 128], tl[4][s][:, q * 128:(q + 1) * 128], ident[:]),
                            [b_tl[4][s], b_id], [b_ps[pb]], defer=(q != nb - 1))
                    tb0 = t0 // 128
                    pv = psb[pb][:, 0:nb * 128].rearrange("p (a b) -> p a b", b=128)
                    self.cp("dve", Vp[:, tb0:tb0 + nb, 0, 0:64], pv[:, :, 0:64], r=[b_ps[pb]], w=[b_Vp])
                    self.act(Vp[:, tb0:tb0 + nb, 1, 64:128], pv[:, :, 64:128], AF.Copy, [b_ps[pb]], [b_Vp])
                for hh in range(2):
                    h = 2 * j + hh
                    for krl in range(2):
                        base = ((l * 16 + h) * 24 + krl) * 128
                        src = bass.AP(rp.tensor, rp.offset + base, [[1, 64], [128, 22], [1, 64]])
                        self.dma("sp", Tc2[krl * 64:(krl + 1) * 64, :].rearrange("p (a b) -> p a b", b=64), src, w=[b_Tc2])
                    for di in range(8):
                        self.cp("pool", biasd[:, di, :], rev(Tc2[:, di * 128:di * 128 + 512]), r=[b_Tc2], w=[b_biasd])
                    for qb, items in enumerate(blocks):
                        for (kr0, ty, di) in items:
                            if ty is not None:
                                self.tt(("dve", "pool")[ty % 2], bm[hh][:, ty, :], biasd[:, di, :], maskb[:, ty, :], ALU.add,
                                        r=[b_biasd, b_mk], w=[b_bm[hh]])
                qk_list, pv_list = [], []
                for qb, items in enumerate(blocks):
                    a1, a2 = 4 + 2 * (kacc % 2), 5 + 2 * (kacc % 2)
                    kacc += 1
                    s3 = kacc % 2
                    q0t = qb * 512
                    work = []
                    for hh in range(2):
                        for (kr0, ty, di) in items:
                            work.append((hh, "loc", kr0, ty))
                        for cc in range(2):
                            work.append((hh, "ctx", cc, None))
                    for wi, (hh, kind, a, ty) in enumerate(work):
                        hb = hh * 64
                        pb = kq % 4
                        kq += 1
                        s2 = kp_ % 3
                        kp_ += 1
                        first, last = (wi == 0), (wi == len(work) - 1)
                        if kind == "loc":
                            s1 = ks % 2
                            ks += 1

                            def qk(hb=hb, pb=pb, s2=s2, s1=s1, a=a, ty=ty, hh=hh, q0t=q0t):
                                ktok = a * 64
                                self.mm(psb[pb][:, :], krot[:, hh, ktok:ktok + 128], qrot[:, q0t:q0t + 512],
                                        r=[b_krot, b_qrot], w=[b_ps[pb]])
                                self.stt(sT[s1][:], psb[pb][:, :], 0.125, bm[hh][:, ty, :], ALU.mult, ALU.add,
                                         r=[b_ps[pb], b_bm[hh]], w=[b_sT[s1]])
                                self.act(pT[s2][:], sT[s1][:], AF.Exp, [b_sT[s1]], [b_pT[s2]])
                            vch = 2 + a // 2
                        else:
                            def qk(hb=hb, pb=pb, s2=s2, a=a, q0t=q0t, hh=hh):
                                self.mm(psb[pb][:, :], kpl[:, hh, a * 128:(a + 1) * 128],
                                        qpl[:, LCTX + q0t:LCTX + q0t + 512], r=[b_kpl, b_qpl], w=[b_ps[pb]])
                                self.act(pT[s2][:], psb[pb][:, :], AF.Exp, [b_ps[pb]], [b_pT[s2]], scale=0.125)
                            vch = a

                        def pv(a1=a1, a2=a2, vch=vch, hh=hh, s2=s2, first=first, last=last, s3=s3, q0t=q0t):
                            self.mm(psb[a1][:, :], Vp[:, vch, hh, :], pT[s2][:], start=first, stop=last,
                                    r=[b_Vp, b_pT[s2]], w=[b_ps[a1]], defer=True)
                            self.mm(psb[a2][:, :], onesp[:, hh, :], pT[s2][:], start=first, stop=last,
                                    r=[b_op, b_pT[s2]], w=[b_ps[a2]], defer=(not last))
                            if last:
                                self.em.op("dve", lambda e: e.reciprocal(out=rc[s3][:], in_=psb[a2][:, :]),
                                           [b_ps[a2]], [b_rc[s3]])
                                self.tt("dve", yst[:, LCTX + q0t:LCTX + q0t + 512], psb[a1][:, :], rc[s3][:], ALU.mult,
                                        r=[b_ps[a1], b_rc[s3]], w=[b_yst])
                        qk_list.append(qk)
                        pv_list.append(pv)
                LA = 2
                for idx in range(len(qk_list) + LA):
                    if idx < len(qk_list):
                        qk_list[idx]()
                    if idx - LA >= 0:
                        pv_list[idx - LA]()
                if update_ctx:
                    a1, a2 = 4 + 2 * (kacc % 2), 5 + 2 * (kacc % 2)
                    kacc += 1
                    work = [(hh, cc) for hh in range(2) for cc in range(2)]
                    for wi, (hh, cc) in enumerate(work):
                        hb = hh * 64
                        pb = kq % 4
                        kq += 1
                        s2 = kp_ % 3
                        kp_ += 1
                        self.mm(psb[pb][:, 0:LCTX], kpl[:, hh, cc * 128:(cc + 1) * 128], qpl[:, 0:LCTX],
                                r=[b_kpl, b_qpl], w=[b_ps[pb]])
                        self.act(pT[s2][:, 0:LCTX], psb[pb][:, 0:LCTX], AF.Exp, [b_ps[pb]], [b_pT[s2]], scale=0.125)
                        first, last = (wi == 0), (wi == len(work) - 1)
                        self.mm(psb[a1][:, 0:LCTX], Vp[:, cc, hh, :], pT[s2][:, 0:LCTX], start=first, stop=last,
                                r=[b_Vp, b_pT[s2]], w=[b_ps[a1]])
                        self.mm(psb[a2][:, 0:LCTX], onesp[:, hh, :], pT[s2][:, 0:LCTX], start=first, stop=last,
                                r=[b_op, b_pT[s2]], w=[b_ps[a2]])
                    s3 = kacc % 2
                    self.em.op("dve", lambda e, s3=s3, a2=a2: e.reciprocal(out=rc[s3][:, 0:LCTX], in_=psb[a2][:, 0:LCTX]),
                               [b_ps[a2]], [b_rc[s3]])
                    self.tt("dve", yst[:, 0:LCTX], psb[a1][:, 0:LCTX], rc[s3][:, 0:LCTX], ALU.mult,
                            r=[b_ps[a1], b_rc[s3]], w=[b_yst])
                self.dma("pool", yC[j * 128:(j + 1) * 128, :], yst[:], r=[b_yst])
            self.em.flush()

    def phase_rwkv(self, l):
        cfg = self.cfg
        NT, T = cfg.NT, cfg.T
        psb, b_ps = self.psb, self.b_ps
        dr = self.dram
        uT, yA = dr["uT"], dr["yA"]
        NCK = NT // 64
        CW = 0.6065306597126334
        SEGC = 16
        SEGN = SEGC * 64
        segs = [(0, 4)] + [(4 + 16 * i, 16) for i in range((NCK - 4) // 16)]
        with contextlib.ExitStack() as st:
            sb = lambda name, shape, dt=F32: self.sb(st, "r_" + name, shape, dt)
            ident_bf = sb("ident_bf", [128, 128], BF16)
            bdones = sb("bdones", [128, 128], BF16)
            istack = sb("istack", [128, 64], BF16)
            rkm = sb("rkm", [128, 2, 1152], BF16)
            cmask = sb("cmask", [128, SEGN])
            mu = sb("mu", [128, 26, 2])
            c0 = sb("c0", [128, 26])
            w0a0 = sb("w0a0", [128, 2, 2, 8])
            vec = sb("vec", [128, 5, 8])
            omk = sb("omk", [128, 8])
            WA = sb("WA", [128, 2, 1024], BF16)
            GU = sb("GU", [128, 1024], BF16)
            b_cst, b_k, b_par, b_wst, b_W = Buf("cst"), Buf("rk_consts"), Buf("rpar"), Buf("wst"), Buf("WA")
            with contextlib.ExitStack() as st_tmp:
                cst_f = self.sb(st_tmp, "r_cst_f", [128, 1152], F32)
                wst = self.sb(st_tmp, "r_wst", [128, 1024], F32)
                for (dst, src, n) in [(ident_bf, dr["ident"], 128), (bdones, dr["bdones"], 128), (istack, dr["istack"], 64)]:
                    self.dma("sp", cst_f[:, 0:n], src[:, :], w=[b_cst])
                    self.cp("dve", dst[:], cst_f[:, 0:n], r=[b_cst], w=[b_k])
                for d in range(2):
                    self.dma("sp", cst_f[:], dr["rk_mask"][d], w=[b_cst])
                    self.cp("dve", rkm[:, d, :], cst_f[:], r=[b_cst], w=[b_k])
                self.memset("pool", cmask[:], 1.0, w=[b_k])
                self.memset("pool", cmask[:].rearrange("p (c s) -> p c s", s=64)[:, :, 0:1], 0.0, w=[b_k])
                self.dma("sp", mu[:], dr["rwkv_mu"][l], w=[b_par])
                self.dma("sp", w0a0[:], dr["rwkv_w0a0"][l], w=[b_par])
                self.dma("sp", vec[:], dr["rwkv_vec"][l], w=[b_par])
                self.ts("dve", c0[:], mu[:, :, 0], -1.0, 1.0, ALU.mult, ALU.add, r=[b_par], w=[b_par])
                self.tt("dve", c0[:], c0[:], mu[:, :, 1], ALU.subtract, r=[b_par], w=[b_par])
                self.ts("dve", omk[:], vec[:, 1, :], -1.0, 1.0, ALU.mult, ALU.add, r=[b_par], w=[b_par])
                for d in range(2):
                    self.dma("sp", wst[0:64, :], dr["rwkv_w_up"][l, d], w=[b_wst])
                    self.dma("sp", wst[64:128, :], dr["rwkv_a_up"][l, d], w=[b_wst])
                    self.cp("pool", WA[:, d, :], wst[:], r=[b_wst], w=[b_W])
                self.dma("sp", wst[:], dr["rwkv_g_up"][l], w=[b_wst])
                self.cp("pool", GU[:], wst[:], r=[b_wst], w=[b_W])
                self.em.flush()
            LW = sb("LW", [128, NT], BF16)
            GL = sb("GL", [128, NT], BF16)
            Yacc = sb("Yacc", [128, NCK, 64])
            ksum = sb("ksum", [128, NT])
            b_LW, b_GL, b_Yacc, b_ksum = Buf("LW"), Buf("GL"), Buf("Yacc"), Buf("ksum")
            xps = [sb(f"xp{i}", [128, SEGN + 2]) for i in range(3)]
            b_xps = mkbufs("xp", 3)
            kxp = [0]
            rTs = [sb(f"rT{i}", [128, SEGN]) for i in range(2)]
            kT, vT, kap = sb("kT", [128, SEGN]), sb("vT", [128, SEGN]), sb("kap", [128, SEGN])
            T1, T2, T3, T4 = [sb(f"T{i}", [128, SEGN]) for i in range(1, 5)]
            F1, F2, F3 = [sb(f"F{i}", [128, SEGN]) for i in range(1, 4)]
            ynT = sb("ynT", [128, SEGN])
            Vbs = [sb(f"Vb{i}", [128, SEGN], BF16) for i in range(2)]
            gTbs = [sb(f"gTb{i}", [128, SEGN], BF16) for i in range(2)]
            sqb, fsqb = sb("sqb", [128, SEGN], BF16), sb("fsqb", [128, SEGN], BF16)
            stks = [sb(f"stk{i}", [128, 4, SEGN], BF16) for i in range(2)]
            YBD = sb("YBD", [128, SEGC, 128], BF16)
            yst = sb("yst", [128, SEGN], BF16)
            gCs = [sb(f"gC{i}", [128, SEGC]) for i in range(2)]
            lnst = sb("lnst", [128, 6, SEGC])
            ptot = sb("ptot", [128, SEGC])
            b_kT, b_vT, b_kap, b_T1, b_T2, b_T3, b_T4, b_ynT, b_sqb, b_YBD, b_yst, b_ln, b_F1, b_F2, b_F3, b_fsqb, b_ptot = [
                Buf(n) for n in "kT vT kap T1 T2 T3 T4 ynT sqb YBD yst lnst F1 F2 F3 fsqb ptot".split()]
            b_rTs, b_Vbs, b_gTbs, b_stks, b_gCs = [mkbufs(n, 2) for n in "rT Vb gTb stk gC".split()]
            self.memset("pool", YBD[:], 0.0, w=[b_YBD])
            G = 4
            BDg = [sb(f"BDg{i}", [128, G, 5, 128], BF16) for i in range(2)]
            b_BDg = mkbufs("BDg", 2)
            for i in range(2):
                self.memset("pool", BDg[i][:], 0.0, w=[b_BDg[i]])
            SA = [sb(f"SA{i}", [128, 512], BF16) for i in range(G)]
            SB_ = [sb(f"SB{i}", [128, 512], BF16) for i in range(G)]
            MN = [[sb(f"MN{i}_{k}", [128, 256], BF16) for k in range(2)] for i in range(G)]
            Rb = [[sb(f"Rb{i}_{k}", [128, 256], BF16) for k in range(2)] for i in range(G)]
            Pb = [[sb(f"Pb{i}_{k}", [128, 128], BF16) for k in range(2)] for i in range(G)]
            MO = [sb(f"MO{i}", [128, 128], BF16) for i in range(G)]
            QP = [sb(f"QP{i}", [128, 256], BF16) for i in range(G)]
            AK = [sb(f"AK{i}", [128, 256], BF16) for i in range(G)]
            VM = [sb(f"VM{i}", [128, G, 64], BF16) for i in range(2)]
            b_VM = mkbufs("VM", 2)
            b_MO = mkbufs("MO", G)
            pending = []
            ST = sb("ST", [128, 64], BF16)
            S32 = sb("S32", [128, 64])
            S32g = sb("S32g", [128, 64])
            b_S32, b_S32g = Buf("S32"), Buf("S32g")
            b_SA, b_SB, b_QP, b_AK = [mkbufs(n, G) for n in "SA SB QP AK".split()]
            b_MN, b_Rb, b_Pb = [[mkbufs(f"{n}{i}_", 2) for i in range(G)] for n in "MN Rb Pb".split()]
            b_ST = Buf("ST")
            kps = [0]

            def shift(dst, b_dst, c, c0_, n):
                xi = kxp[0] % 3
                kxp[0] += 1
                xp, b_xp = xps[xi], b_xps[xi]
                p0 = c0_ * 64
                p1 = p0 + n
                hasL = p0 not in (0, LCTX)
                hasR = p1 not in (LCTX, NT)
                if not hasL:
                    self.memset("pool", xp[:, 0:1], 0.0, w=[b_xp])
                if not hasR:
                    self.memset("pool", xp[:, n + 1:n + 2], 0.0, w=[b_xp])
                lo, hi = p0 - int(hasL), p1 + int(hasR)
                self.dma("sp", xp[:, 1 - int(hasL):1 + n + int(hasR)], uT[c * 128:(c + 1) * 128, lo:hi], w=[b_xp])
                self.act(dst[:, 0:n], xp[:, 1:1 + n], AF.Copy, [b_xp, b_par], [b_dst], scale=c0[:, c:c + 1])
                self.stt(dst[:, 0:n], xp[:, 0:n], mu[:, c, 0:1], dst[:, 0:n], ALU.mult, ALU.add, r=[b_xp, b_par, b_dst], w=[b_dst])
                self.stt(dst[:, 0:n], xp[:, 2:2 + n], mu[:, c, 1:2], dst[:, 0:n], ALU.mult, ALU.add, r=[b_xp, b_par, b_dst], w=[b_dst])

            def tiles_of(n):
                return [(o, min(512, n - o)) for o in range(0, n, 512)]

            def nextps():
                kps[0] += 1
                return 7

            for (ck0, nck) in segs:
                n = nck * 64
                t0 = ck0 * 64
                shift(T1, b_T1, 24, ck0, n)
                self.act(LW[0:64, t0:t0 + n], T1[0:64, 0:n], AF.Tanh, [b_T1], [b_LW])
                self.act(LW[64:128, t0:t0 + n], T1[64:128, 0:n], AF.Copy, [b_T1], [b_LW])
                shift(T2, b_T2, 25, ck0, n)
                self.act(GL[:, t0:t0 + n], T2[:, 0:n], AF.Sigmoid, [b_T2], [b_GL])

            kbd = [0]

            def prep(j, d, ck0, nck, sp):
                n = nck * 64
                t0 = ck0 * 64
                rT, b_rT, Vb, b_Vb, gTb, b_gTb, stk, b_stk, gC, b_gC = (rTs[sp], b_rTs[sp], Vbs[sp], b_Vbs[sp], gTbs[sp], b_gTbs[sp],
                                                                       stks[sp], b_stks[sp], gCs[sp], b_gCs[sp])
                for (o, m) in tiles_of(n):
                    pb = nextps()
                    self.mm(psb[pb][:, 0:m], WA[0:64, d, j * 128:(j + 1) * 128], LW[0:64, t0 + o:t0 + o + m],
                            r=[b_W, b_LW], w=[b_ps[pb]])
                    self.act(T1[:, o:o + m], psb[pb][:, 0:m], AF.Sigmoid, [b_ps[pb], b_par], [b_T1],
                             bias=w0a0[:, 0, d, j:j + 1])
                    pb = nextps()
                    self.mm(psb[pb][:, 0:m], WA[64:128, d, j * 128:(j + 1) * 128], LW[64:128, t0 + o:t0 + o + m],
                            r=[b_W, b_LW], w=[b_ps[pb]])
                    self.act(T2[:, o:o + m], psb[pb][:, 0:m], AF.Sigmoid, [b_ps[pb], b_par], [b_T2],
                             bias=w0a0[:, 1, d, j:j + 1])
                shift(rT, b_rT, j, ck0, n)
                shift(kT, b_kT, 8 + j, ck0, n)
                shift(vT, b_vT, 16 + j, ck0, n)
                self.scan(T3[:, 0:n], cmask[:, 0:n], T1[:, 0:n], 0.0, r=[b_k, b_T1], w=[b_T3])
                T3v = T3[:, 0:n].rearrange("p (c s) -> p c s", s=64)
                self.act(gC[:, 0:nck], T3v[:, :, 63], AF.Exp, [b_T3], [b_gC], scale=-CW)
                if d == 1:
                    self.cp("pool", ptot[:, 0:nck], T3v[:, :, 63], r=[b_T3], w=[b_ptot])
                    self.tt("dve", T3[:, 0:n], T1[:, 0:n], T3[:, 0:n], ALU.subtract, r=[b_T1, b_T3], w=[b_T3])
                    self.tt("dve", T3v, T3v, ptot[:, 0:nck].unsqueeze(2).to_broadcast([128, nck, 64]), ALU.add,
                            r=[b_T3, b_ptot], w=[b_T3])
                self.tt("dve", T1[:, 0:n], T3[:, 0:n], T1[:, 0:n], ALU.subtract, r=[b_T1, b_T3], w=[b_T1])
                self.act(T1[:, 0:n], T1[:, 0:n], AF.Exp, [b_T1], [b_T1], scale=-CW)
                self.act(T4[:, 0:n], T3[:, 0:n], AF.Exp, [b_T3], [b_T4], scale=CW)
                self.act(T3[:, 0:n], T3[:, 0:n], AF.Exp, [b_T3], [b_T3], scale=-CW)
                self.act(Vb[:, 0:n], vT[:, 0:n], AF.Copy, [b_vT], [b_Vb])
                self.act(kap[:, 0:n], kT[:, 0:n], AF.Copy, [b_kT, b_par], [b_kap], scale=vec[:, 0, j:j + 1])
                self.act(sqb[:, 0:n], kap[:, 0:n], AF.Square, [b_kap], [b_sqb])
                for (o, m) in tiles_of(n):
                    pb = nextps()
                    self.mm(psb[pb][:, 0:m], bdones[:], sqb[:, o:o + m], r=[b_k, b_sqb], w=[b_ps[pb]])
                    self.act(F3[:, o:o + m], psb[pb][:, 0:m], AF.Sqrt, [b_ps[pb]], [b_F3], bias=1e-24)
                self.em.op("dve", lambda e, n=n: e.reciprocal(out=F3[:, 0:n], in_=F3[:, 0:n]), [b_F3], [b_F3])
                self.tt("dve", kap[:, 0:n], kap[:, 0:n], F3[:, 0:n], ALU.mult, r=[b_kap, b_F3], w=[b_kap])
                if d == 1:
                    for (o, m) in tiles_of(n):
                        pb = nextps()
                        self.mm(psb[pb][:, 0:m], GU[:, j * 128:(j + 1) * 128], GL[:, t0 + o:t0 + o + m],
                                r=[b_W, b_GL], w=[b_ps[pb]])
                        self.act(gTb[:, o:o + m], psb[pb][:, 0:m], AF.Copy, [b_ps[pb]], [b_gTb])
                self.tt("dve", stk[:, 0, 0:n], kap[:, 0:n], T1[:, 0:n], ALU.mult, r=[b_kap, b_T1], w=[b_stk])
                for o_ in range(0, n, 256):
                    self.tt("pool", stk[:, 1, o_:o_ + 256], rT[:, o_:o_ + 256], T3[:, o_:o_ + 256], ALU.mult, r=[b_rT, b_T3], w=[b_stk])
                self.ts("dve", T1[:, 0:n], T2[:, 0:n], vec[:, 1, j:j + 1], omk[:, j:j + 1], ALU.mult, ALU.add,
                        r=[b_T2, b_par], w=[b_T1])
                self.tt("dve", T1[:, 0:n], T1[:, 0:n], kT[:, 0:n], ALU.mult, r=[b_T1, b_kT], w=[b_T1])
                if d == 0:
                    self.act(ksum[:, t0:t0 + n], T1[:, 0:n], AF.Copy, [b_T1], [b_ksum])
                else:
                    for o_ in range(0, n, 256):
                        self.tt("pool", ksum[:, t0 + o_:t0 + o_ + 256], ksum[:, t0 + o_:t0 + o_ + 256], T1[:, o_:o_ + 256], ALU.add,
                                r=[b_T1, b_ksum], w=[b_ksum])
                self.tt("dve", stk[:, 3, 0:n], T1[:, 0:n], T4[:, 0:n], ALU.mult, r=[b_T1, b_T4], w=[b_stk])
                for o_ in range(0, n, 256):
                    self.tt("pool", T2[:, o_:o_ + 256], T2[:, o_:o_ + 256], kap[:, o_:o_ + 256], ALU.mult, r=[b_T2, b_kap], w=[b_T2])
                self.tt("dve", stk[:, 2, 0:n], T2[:, 0:n], T4[:, 0:n], ALU.mult, r=[b_T2, b_T4], w=[b_stk])

            def finalize(j, ck0, nck, sp):
                n = nck * 64
                t0 = ck0 * 64
                rT, b_rT, Vb, b_Vb, gTb, b_gTb = rTs[sp], b_rTs[sp], Vbs[sp], b_Vbs[sp], gTbs[sp], b_gTbs[sp]
                Ys = Yacc[:, ck0:ck0 + nck, :]
                T3v = F3[:, 0:n].rearrange("p (c s) -> p c s", s=64)
                mean, ssq, m2, var = lnst[:, 1, 0:nck], lnst[:, 2, 0:nck], lnst[:, 3, 0:nck], lnst[:, 4, 0:nck]
                self.em.op("dve", lambda e, Ys=Ys, mean=mean: e.tensor_reduce(out=mean, in_=Ys, axis=AX.X, op=ALU.add),
                           [b_Yacc], [b_ln])
                self.act(T3v, Ys, AF.Square, [b_Yacc], [b_F3])
                self.em.op("dve", lambda e, T3v=T3v, ssq=ssq: e.tensor_reduce(out=ssq, in_=T3v, axis=AX.X, op=ALU.add),
                           [b_F3], [b_ln])
                self.ts("dve", mean, mean, 1.0 / 64, None, ALU.mult, r=[b_ln], w=[b_ln])
                self.tt("dve", m2, mean, mean, ALU.mult, r=[b_ln], w=[b_ln])
                self.stt(var, ssq, 1.0 / 64, m2, ALU.mult, ALU.subtract, r=[b_ln], w=[b_ln])
                self.act(var, var, AF.Sqrt, [b_ln], [b_ln], bias=64e-5)
                self.em.op("dve", lambda e, var=var: e.reciprocal(out=var, in_=var), [b_ln], [b_ln])
                self.tt("dve", T3v, Ys, mean.unsqueeze(2).to_broadcast([128, nck, 64]), ALU.subtract,
                        r=[b_Yacc, b_ln], w=[b_F3])
                for hh in range(2):
                    hs = slice(hh * 64, (hh + 1) * 64)
                    self.tt("dve", YBD[hs, 0:nck, hs], T3v[hs], var[hs].unsqueeze(2).to_broadcast([64, nck, 64]), ALU.mult,
                            r=[b_F3, b_ln], w=[b_YBD])
                for c8 in range(0, nck, 8):
                    m8 = min(8, nck - c8)
                    pb = 7
                    for ci in range(c8, c8 + m8):
                        self.mm(psb[pb][:, (ci - c8) * 64:(ci - c8 + 1) * 64], YBD[:, ci, :], istack[:],
                                r=[b_YBD, b_k], w=[b_ps[pb]], defer=(ci != c8 + m8 - 1))
                    self.ts("dve", ynT[:, c8 * 64:(c8 + m8) * 64], psb[pb][:, 0:m8 * 64], vec[:, 3, j:j + 1], vec[:, 4, j:j + 1],
                            ALU.mult, ALU.add, r=[b_ps[pb], b_par], w=[b_ynT])
                for o_ in range(0, n, 256):
                    self.tt("pool", F1[:, o_:o_ + 256], rT[:, o_:o_ + 256], ksum[:, t0 + o_:t0 + o_ + 256], ALU.mult, r=[b_rT, b_ksum], w=[b_F1])
                self.ts("dve", fsqb[:, 0:n], F1[:, 0:n], vec[:, 2, j:j + 1], None, ALU.mult, r=[b_F1, b_par], w=[b_fsqb])
                for (o, m) in tiles_of(n):
                    pb = 7
                    self.mm(psb[pb][:, 0:m], bdones[:], fsqb[:, o:o + m], r=[b_k, b_fsqb], w=[b_ps[pb]])
                    self.tt("dve", F2[:, o:o + m], psb[pb][:, 0:m], Vb[:, o:o + m], ALU.mult, r=[b_ps[pb], b_Vb], w=[b_F2])
                self.tt("dve", ynT[:, 0:n], ynT[:, 0:n], F2[:, 0:n], ALU.add, r=[b_ynT, b_F2], w=[b_ynT])
                self.tt("dve", yst[:, 0:n], ynT[:, 0:n], gTb[:, 0:n], ALU.mult, r=[b_ynT, b_gTb], w=[b_yst])
                self.dma("pool", yA[j * 128:(j + 1) * 128, t0:t0 + n], yst[:, 0:n], r=[b_yst])

            for j in range(8):
                for d in range(2):
                    self.memset("pool", ST[:], 0.0, w=[b_ST])
                    self.memset("pool", S32[:], 0.0, w=[b_S32])
                    order = segs if d == 0 else [segs[0]] + segs[1:][::-1]
                    bgq = []
                    prep(j, d, order[0][0], order[0][1], 0)
                    for si, (ck0, nck) in enumerate(order):
                        sp = si % 2
                        n = nck * 64
                        t0 = ck0 * 64
                        rT, b_rT, Vb, b_Vb, gTb, b_gTb, stk, b_stk, gC, b_gC = (rTs[sp], b_rTs[sp], Vbs[sp], b_Vbs[sp], gTbs[sp],
                                                                               b_gTbs[sp], stks[sp], b_stks[sp], gCs[sp], b_gCs[sp])
                        if si + 1 < len(order):
                            nxt = order[si + 1]
                            bgq += self.em.captured(lambda: prep(j, d, nxt[0], nxt[1], 1 - sp))
                        nstage = 30 * ((nck + G - 1) // G)
                        bgk = max(1, (len(bgq) + nstage - 1) // nstage)

                        def bgrun(k=None):
                            self.em.replay(bgq, bgk if k is None else k)
                        corder = list(range(nck)) if d == 0 else list(range(nck))[::-1]
                        for gi in range(0, nck, G):
                            grp = corder[gi:gi + G]
                            cl = min(grp)
                            bg = kbd[0] % 2
                            kbd[0] += 1
                            for qi in range(5):
                                for hh in range(2):
                                    hs = slice(hh * 64, (hh + 1) * 64)
                                    src = (Vb[hs, cl * 64:(cl + G) * 64] if qi == 4 else stk[hs, qi, cl * 64:(cl + G) * 64])
                                    src = src.rearrange("p (c s) -> p c s", s=64)
                                    eng = ("act", "pool", "act", "dve", "act", "pool", "act", "pool", "act", "dve")[qi * 2 + hh]
                                    if eng == "act":
                                        self.act(BDg[bg][hs, :, qi, hs], src, AF.Copy, [b_stk, b_Vb], [b_BDg[bg]])
                                    else:
                                        self.cp(eng, BDg[bg][hs, :, qi, hs], src, r=[b_stk, b_Vb], w=[b_BDg[bg]])
                            rB = [b_BDg[bg], b_k]
                            R4 = range(len(grp))
                            gof = [ci - cl for ci in grp]
                            bd = lambda i, q: BDg[bg][:, gof[i], q, :]
                            slot = lambda i: psb[i]
                            bsl = lambda i: b_ps[i]
                            for i in R4:
                                self.mm(psb[4][:, i * 64:(i + 1) * 64], bd(i, 4), istack[:], r=rB, w=[b_ps[4]], defer=(i != len(grp) - 1))
                            self.act(VM[bg][:, 0:len(grp), :], psb[4][:, 0:64 * len(grp)].rearrange("p (c s) -> p c s", s=64), AF.Copy,
                                     [b_ps[4]], [b_VM[bg]])
                            pend = pending[:]
                            del pending[:]

                            def drain(k=1):
                                for _ in range(k):
                                    if pend:
                                        pend.pop(0)()
                            for i in R4:
                                self.mm(slot(i)[:, 0:256], bd(i, 2), BDg[bg][:, gof[i], 0:2, :], r=rB, w=[bsl(i)], defer=True)
                                self.mm(slot(i)[:, 256:384], bd(i, 2), ident_bf[:], r=rB, w=[bsl(i)], defer=True)
                                self.mm(slot(i)[:, 384:512], bd(i, 0), ident_bf[:], r=rB, w=[bsl(i)])
                            drain()
                            bgrun()
                            for i in R4:
                                self.tt("dve", SA[i][:, 0:256], slot(i)[:, 0:256], rkm[:, d, 0:256], ALU.mult, r=[bsl(i), b_k], w=[b_SA[i]])
                                self.act(SA[i][:, 256:512], slot(i)[:, 256:512], AF.Copy, [bsl(i)], [b_SA[i]])
                            bgrun()
                            for i in R4:
                                self.mm(slot(i)[:, 0:128], bd(i, 3), bd(i, 1), r=rB, w=[bsl(i)], defer=True)
                                self.mm(slot(i)[:, 128:256], bd(i, 3), ident_bf[:], r=rB, w=[bsl(i)], defer=True)
                                self.mm(slot(i)[:, 256:512], bd(i, 0), BDg[bg][:, gof[i], 2:4, :], r=rB, w=[bsl(i)])
                            drain()
                            bgrun()
                            for i in R4:
                                self.tt("dve", SB_[i][:], slot(i)[:, :], rkm[:, d, 512:1024], ALU.mult, r=[bsl(i), b_k], w=[b_SB[i]])
                                self.tt("dve", MO[i][:], slot(i)[:, 256:384], rkm[:, d, 1024:1152], ALU.mult, r=[bsl(i), b_k], w=[b_MO[i]])
                            bgrun()
                            for i in R4:
                                self.tt("pool", Pb[i][0][:], ident_bf[:], SB_[i][:, 256:384], ALU.subtract, r=[b_k, b_SB[i]], w=[b_Pb[i][0]])
                            cur = [(SB_[i][:, 256:384], SA[i][:, 0:128], b_SB[i], b_SA[i]) for i in R4]
                            for lev in range(4):
                                lastl = lev == 3
                                mi = lev % 2
                                for i in R4:
                                    Mc, Nc, bM, bN = cur[i]
                                    self.mm(slot(i)[:, 128:256], Mc, Nc, r=[bM, bN], w=[bsl(i)], defer=(not lastl))
                                    if not lastl:
                                        self.mm(slot(i)[:, 0:128], Nc, Mc, r=[bM, bN], w=[bsl(i)])
                                drain()
                                bgrun()
                                lo = 128 if lastl else 0
                                for i in R4:
                                    self.act(MN[i][mi][:, lo:256], slot(i)[:, lo:256], AF.Copy, [bsl(i)], [b_MN[i][mi]])
                                    cur[i] = (MN[i][mi][:, 0:128], MN[i][mi][:, 128:256], b_MN[i][mi], b_MN[i][mi])
                                bgrun()
                                for i in R4:
                                    self.mm(slot(i)[:, 256:384], cur[i][1], Pb[i][lev % 2][:], r=[cur[i][3], b_Pb[i][lev % 2]], w=[bsl(i)])
                                bgrun()
                                for i in R4:
                                    self.tt("dve", Pb[i][(lev + 1) % 2][:], Pb[i][lev % 2][:], slot(i)[:, 256:384], ALU.add,
                                            r=[b_Pb[i][lev % 2], bsl(i)], w=[b_Pb[i][(lev + 1) % 2]])
                            drain(4)
                            for i in R4:
                                self.mm(slot(i)[:, 0:256], Pb[i][0][:], SA[i][:, 128:384], r=[b_Pb[i][0], b_SA[i]], w=[bsl(i)])
                            bgrun()
                            for i in R4:
                                self.act(Rb[i][0][:], slot(i)[:, 0:256], AF.Copy, [bsl(i)], [b_Rb[i][0]])
                            for i in R4:
                                self.mm(slot(i)[:, 256:512], MO[i][:], Rb[i][0][:], r=[b_MO[i], b_Rb[i][0]], w=[bsl(i)])
                            bgrun()
                            for i in R4:
                                self.tt("dve", Rb[i][1][:], SA[i][:, 128:384], slot(i)[:, 256:512], ALU.subtract, r=[b_SA[i], bsl(i)], w=[b_Rb[i][1]])
                            bgrun()
                            for i in R4:
                                self.mm(slot(i)[:, 0:256], Pb[i][0][:], Rb[i][1][:], r=[b_Pb[i][0], b_Rb[i][1]], w=[bsl(i)])
                            for i in R4:
                                self.act(Rb[i][0][:], slot(i)[:, 0:256], AF.Copy, [bsl(i)], [b_Rb[i][0]])
                            bgrun()
                            for i in R4:
                                self.mm(slot(i)[:, 256:512], SA[i][:, 384:512], Rb[i][0][:], r=[b_SA[i], b_Rb[i][0]], w=[bsl(i)], defer=True)
                                self.mm(slot(i)[:, 0:256], SB_[i][:, 384:512], Rb[i][0][:], r=[b_SB[i], b_Rb[i][0]], w=[bsl(i)])
                            bgrun()
                            for i in R4:
                                self.tt("dve", QP[i][:, 0:128], bd(i, 1), slot(i)[:, 256:384], ALU.subtract, r=[b_BDg[bg], bsl(i)], w=[b_QP[i]])
                                self.ts("dve", QP[i][:, 128:256], slot(i)[:, 384:512], -1.0, None, ALU.mult, r=[bsl(i)], w=[b_QP[i]])
                                self.tt("dve", AK[i][:], SB_[i][:, 0:256], slot(i)[:, 0:256], ALU.subtract, r=[b_SB[i], bsl(i)], w=[b_AK[i]])
                            for i in R4:
                                def step(i=i, ci=grp[i], bg=bg, d=d, ck0=ck0):
                                    SQ = psb[6][:, i * 128:(i + 1) * 128]
                                    vm = VM[bg][:, i, :]
                                    self.act(S32g[:], S32[:], AF.Copy, [b_S32, b_gC], [b_S32g], scale=gC[:, ci:ci + 1])
                                    self.mm(SQ[:, 0:64], QP[i][:, 0:128], ST[:], start=True, stop=False, r=[b_QP[i], b_ST], w=[b_ps[6]])
                                    self.mm(SQ[:, 0:64], AK[i][:, 0:128], vm, start=False, stop=True, r=[b_AK[i], b_VM[bg]], w=[b_ps[6]])
                                    self.mm(SQ[:, 64:128], QP[i][:, 128:256], ST[:], start=True, stop=False, r=[b_QP[i], b_ST], w=[b_ps[6]])
                                    self.mm(SQ[:, 64:128], AK[i][:, 128:256], vm, start=False, stop=True, r=[b_AK[i], b_VM[bg]], w=[b_ps[6]])
                                    self.stt(S32[:], SQ[:, 64:128], gC[:, ci:ci + 1], S32g[:], ALU.mult, ALU.add,
                                             r=[b_ps[6], b_gC, b_S32g], w=[b_S32])
                                    self.act(ST[:], S32[:], AF.Copy, [b_S32], [b_ST])
                                    ck = ck0 + ci
                                    if d == 0:
                                        self.cp("dve", Yacc[:, ck, :], SQ[:, 0:64], r=[b_ps[6]], w=[b_Yacc])
                                    else:
                                        self.tt("dve", Yacc[:, ck, :], Yacc[:, ck, :], SQ[:, 0:64], ALU.add, r=[b_ps[6], b_Yacc], w=[b_Yacc])
                                pending.append(step)
                            while pend:
                                pend.pop(0)()
                        while pending:
                            pending.pop(0)()
                        bgrun(10 ** 9)
                        if d == 1:
                            bgq += self.em.captured(lambda ck0=ck0, nck=nck, sp=sp: finalize(j, ck0, nck, sp))
                    bgrun(10 ** 9) if False else self.em.replay(bgq, 10 ** 9)
            self.em.flush()

    def rstd_tile(self, src, b_src, n, sq, b_sq, rs, b_rs, pb):
        psb, b_ps = self.psb, self.b_ps
        self.act(sq[:, :, 0:n], src[:, :, 0:n], AF.Square, [b_src], [b_sq])
        for kc in range(KC):
            self.mm(psb[pb][:, 0:n], self.ones_bf[:], sq[:, kc, 0:n], start=(kc == 0), stop=(kc == KC - 1),
                    r=[b_sq, self.b_ones], w=[b_ps[pb]])
        self.act(rs[:, 0:n], psb[pb][:, 0:n], AF.Sqrt, [b_ps[pb]], [b_rs], scale=1.0 / D, bias=1e-6)
        self.em.op("dve", lambda e: e.reciprocal(out=rs[:, 0:n], in_=rs[:, 0:n]), [b_rs], [b_rs])

    def phase_merge_ffn(self, l, xsrc):
        cfg = self.cfg
        NT, T = cfg.NT, cfg.T
        psb, b_ps = self.psb, self.b_ps
        dr = self.dram
        mod, b_mod = self.mod, self.b_mod
        uT, xmid, aT = dr["uT"], dr["xmid"], dr["aT"]
        last = l == cfg.depth - 1
        TW = 256
        tiles = [(t0, TW, 1 if t0 < LCTX else 0) for t0 in range(0, NT, TW)]
        with contextlib.ExitStack() as st0:
            h2T = self.sb(st0, "m_h2T", [128, KC, NT], BF16)
            b_h2 = mkbufs("h2T", len(tiles))
            ng = self.sb(st0, "m_ng", [128, 4, KC], F32)
            cols = self.sb(st0, "m_cols", [128, 4, KC, 2], F32)
            wst = self.sb(st0, "m_wst", [128, 1024], F32)
            b_ng, b_cols, b_wst = Buf("ng"), Buf("cols"), Buf("wst")
            self.dma("sp", ng[:], dr["norm_g"][l], w=[b_ng])
            for kc in range(KC):
                self.ts("dve", cols[:, 0, kc, :], mod[:, 16 + kc, :], ng[:, 1, kc:kc + 1], None, ALU.mult, r=[b_mod, b_ng], w=[b_cols])
                self.ts("dve", cols[:, 1, kc, :], mod[:, 32 + kc, :], 1.0, ng[:, 2, kc:kc + 1], ALU.add, ALU.mult, r=[b_mod, b_ng], w=[b_cols])
                self.ts("dve", cols[:, 2, kc, :], mod[:, 40 + kc, :], ng[:, 3, kc:kc + 1], None, ALU.mult, r=[b_mod, b_ng], w=[b_cols])
            with contextlib.ExitStack() as st:
                sb = lambda name, shape, dt=F32: self.sb(st, "m_" + name, shape, dt)
                Wbr = sb("Wbr", [128, 3, KC, 1024], BF16)
                Wo = sb("Wo", [128, KC, 1024], BF16)
                b_W = Buf("Wm")
                for br in range(3):
                    for kc in range(KC):
                        self.dma("sp", wst[:], dr["w_branch"][l, br, kc * 128:(kc + 1) * 128, :], w=[b_wst])
                        self.cp("pool", Wbr[:, br, kc, :], wst[:], r=[b_wst], w=[b_W])
                for kc in range(KC):
                    self.dma("sp", wst[:], dr["w_out"][l, kc * 128:(kc + 1) * 128, :], w=[b_wst])
                    self.cp("pool", Wo[:, kc, :], wst[:], r=[b_wst], w=[b_W])
                yt = [sb(f"yt{i}", [128, KC, TW], BF16) for i in range(3)]
                gt = sb("gt", [128, KC, TW])
                mt = sb("mt", [128, KC, TW])
                tmp = sb("tmp", [128, TW])
                mb = sb("mb", [128, KC, TW], BF16)
                xts = [sb(f"xt{i}", [128, KC, TW]) for i in range(2)]
                b_xts = mkbufs("mxt", 2)
                mo = sb("mo", [128, KC, TW])
                sq = sb("sq", [128, KC, TW], BF16)
                rs = sb("rs", [128, TW])
                b_yt = mkbufs("yt", 3)
                b_gt, b_mt, b_tmp, b_mb, b_mo, b_sq, b_rs = [Buf(n) for n in "gt mt tmp mb mo sq rs".split()]
                ysrc = [dr["yA"], dr["yB"], dr["yC"]]
                xview = xsrc.rearrange("(kc p) n -> p kc n", p=128)
                xmview = xmid.rearrange("(kc p) n -> p kc n", p=128)
                k = 0
                bgq = []
                for ti, (t0, n, seg) in enumerate(tiles):
                    xs = ti % 2
                    xt, b_xt = xts[xs], b_xts[xs]
                    self.dma("sp", xt[:], xview[:, :, t0:t0 + n], w=[b_xt])
                    for br in range(3):
                        self.dma("sp", yt[br][:], ysrc[br].rearrange("(kc p) n -> p kc n", p=128)[:, :, t0:t0 + n], w=[b_yt[br]])
                        g0 = (66 + br * 8) * 128
                        self.dma("sp", gt[:], uT[g0:g0 + 1024, :].rearrange("(kc p) n -> p kc n", p=128)[:, :, t0:t0 + n], w=[b_gt])
                        self.act(gt[:], gt[:], AF.Sigmoid, [b_gt], [b_gt])
                        for oc in range(KC):
                            pb = k % 5
                            k += 1
                            for kc in range(KC):
                                self.mm(psb[pb][:, 0:n], Wbr[:, br, kc, oc * 128:(oc + 1) * 128], yt[br][:, kc, :],
                                        start=(kc == 0), stop=(kc == KC - 1), r=[b_W, b_yt[br]], w=[b_ps[pb]])
                            if br == 0:
                                self.tt("dve", mt[:, oc, :], psb[pb][:, 0:n], gt[:, oc, :], ALU.mult, r=[b_ps[pb], b_gt], w=[b_mt])
                            else:
                                self.tt("dve", tmp[:], psb[pb][:, 0:n], gt[:, oc, :], ALU.mult, r=[b_ps[pb], b_gt], w=[b_tmp])
                                self.tt("pool", mt[:, oc, :], mt[:, oc, :], tmp[:], ALU.add, r=[b_mt, b_tmp], w=[b_mt])
                            self.em.replay(bgq, (len(bgq) + (3 - br) * KC - oc - 1) // ((3 - br) * KC - oc) if bgq else 0)
                    self.em.replay(bgq, 10 ** 9)
                    self.act(mb[:], mt[:], AF.Copy, [b_mt], [b_mb])

                    def epilogue(ti=ti, t0=t0, n=n, seg=seg, xt=xt, b_xt=b_xt):
                        kk_ = 0
                        for oc in range(KC):
                            pb = 5
                            for kc in range(KC):
                                self.mm(psb[pb][:, 0:n], Wo[:, kc, oc * 128:(oc + 1) * 128], mb[:, kc, :],
                                        start=(kc == 0), stop=(kc == KC - 1), r=[b_W, b_mb], w=[b_ps[pb]])
                            self.act(mo[:, oc, :], psb[pb][:, 0:n], AF.Copy, [b_ps[pb]], [b_mo])
                        self.rstd_tile(mo, b_mo, n, sq, b_sq, rs, b_rs, 6)
                        for oc in range(KC):
                            self.tt("dve", mo[:, oc, :], mo[:, oc, :], rs[:], ALU.mult, r=[b_mo, b_rs], w=[b_mo])
                            self.stt(xt[:, oc, :], mo[:, oc, :], cols[:, 0, oc, seg:seg + 1], xt[:, oc, :], ALU.mult, ALU.add,
                                     r=[b_mo, b_cols, b_xt], w=[b_xt])
                        self.dma("pool", xmview[:, :, t0:t0 + n], xt[:], r=[b_xt])
                        self.rstd_tile(xt, b_xt, n, sq, b_sq, rs, b_rs, 7)
                        for kc in range(KC):
                            self.tt("dve", mo[:, kc, :], xt[:, kc, :], rs[:], ALU.mult, r=[b_xt, b_rs, b_mo], w=[b_mo])
                            self.act(h2T[:, kc, t0:t0 + n], mo[:, kc, :], AF.Identity, [b_mo, b_cols, b_mod], [b_h2[ti]],
                                     scale=cols[:, 1, kc, seg:seg + 1], bias=mod[:, 24 + kc, seg:seg + 1])
                    bgq += self.em.captured(epilogue)
                self.em.replay(bgq, 10 ** 9)
                self.em.flush()
            with contextlib.ExitStack() as st:
                sb = lambda name, shape, dt=F32: self.sb(st, "f_" + name, shape, dt)
                cw = sb("cw", [128, 22, 3])
                cb = sb("cb", [128, 22])
                b_par = Buf("fpar")
                self.dma("sp", cw[:], dr["ffn_conv_w"][l], w=[b_par])
                self.dma("sp", cb[:], dr["ffn_conv_b"][l], w=[b_par])
                wf = [sb(f"wf{i}", [128, KC, 128]) for i in range(2)]
                wb = [sb(f"wb{i}", [128, KC, 128], BF16) for i in range(2)]
                b_wf, b_wb = mkbufs("fwf", 2), mkbufs("fwb", 2)
                gp = sb("gp", [128, NT + 6])
                gc = sb("gc", [128, NT])
                ast = sb("ast", [128, NT], BF16)
                b_gp, b_gc, b_ast = Buf("gp"), Buf("gc"), Buf("ast")
                self.memset("pool", gp[:], 0.0, w=[b_gp])
                wview = dr["ffn_w_in"][l].rearrange("(kc p) n -> p kc n", p=128)
                k = 0
                kw = 0
                alltiles = list(enumerate(tiles))
                h2tiles = self.cfg.tiles
                for jc in range(22):
                    for part in range(2):
                        s = kw % 2
                        kw += 1
                        c0 = part * DFF + jc * 128
                        self.dma("sp", wf[s][:], wview[:, :, c0:c0 + 128], w=[b_wf[s]])
                        self.cp("pool", wb[s][:], wf[s][:], r=[b_wf[s]], w=[b_wb[s]])
                        for (t0, n, seg) in h2tiles:
                            pb = k % 8
                            k += 1
                            hb = [b_h2[i] for i, (a, m, sg_) in alltiles if a < t0 + n and a + m > t0]
                            for kc in range(KC):
                                self.mm(psb[pb][:, 0:n], wb[s][:, kc, :], h2T[:, kc, t0:t0 + n], start=(kc == 0), stop=(kc == KC - 1),
                                        r=[b_wb[s]] + hb, w=[b_ps[pb]])
                            if part == 0:
                                p0 = t0 + 2 if t0 < LCTX else t0 + 4
                                self.act(gp[:, p0:p0 + n], psb[pb][:, 0:n], AF.Copy, [b_ps[pb]], [b_gp])
                            else:
                                self.tt("dve", ast[:, t0:t0 + n], psb[pb][:, 0:n], gc[:, t0:t0 + n], ALU.mult,
                                        r=[b_ps[pb], b_gc], w=[b_ast])
                        if part == 0:
                            for (d0, s0, n) in self.segs():
                                self.ts("dve", gc[:, d0:d0 + n], gp[:, s0 - 1:s0 - 1 + n], cw[:, jc, 0:1], cb[:, jc:jc + 1], ALU.mult, ALU.add,
                                        r=[b_gp, b_par], w=[b_gc])
                                for tap in (1, 2):
                                    self.stt(gc[:, d0:d0 + n], gp[:, s0 - 1 + tap:s0 - 1 + tap + n], cw[:, jc, tap:tap + 1], gc[:, d0:d0 + n],
                                             ALU.mult, ALU.add, r=[b_gp, b_par, b_gc], w=[b_gc])
                            self.act(gc[:], gc[:], AF.Silu, [b_gc], [b_gc])
                    self.dma("pool", aT[jc * 128:(jc + 1) * 128, :], ast[:], r=[b_ast])
                self.em.flush()
        with contextlib.ExitStack() as st:
            sb = lambda name, shape, dt=F32: self.sb(st, "g_" + name, shape, dt)
            ng = sb("ng", [128, 4, KC])
            cols = sb("cols", [128, KC, 2])
            wst = sb("wst", [128, 1024])
            Wf = sb("Wf", [128, 22, 1024], BF16)
            b_ng, b_cols, b_wst, b_W = Buf("ng"), Buf("cols"), Buf("wst"), Buf("Wf")
            self.dma("sp", ng[:], dr["norm_g"][l], w=[b_ng])
            for kc in range(KC):
                self.ts("dve", cols[:, kc, :], mod[:, 40 + kc, :], ng[:, 3, kc:kc + 1], None, ALU.mult, r=[b_mod, b_ng], w=[b_cols])
            for kc in range(22):
                self.dma("sp", wst[:], dr["ffn_w_out"][l, kc * 128:(kc + 1) * 128, :], w=[b_wst])
                self.cp("pool", Wf[:, kc, :], wst[:], r=[b_wst], w=[b_W])
            at = [sb(f"at{i}", [128, 22, 512], BF16) for i in range(2)]
            xt = [sb(f"xt{i}", [128, KC, 512]) for i in range(2)]
            fo = sb("fo", [128, KC, 512])
            sq = sb("sq", [128, KC, 512], BF16)
            rs = sb("rs", [128, 512])
            b_at, b_xt = mkbufs("at", 2), mkbufs("gxt", 2)
            b_fo, b_sq, b_rs = Buf("fo"), Buf("gsq"), Buf("grs")
            xmview = xmid.rearrange("(kc p) n -> p kc n", p=128)
            aview = aT.rearrange("(kc p) n -> p kc n", p=128)
            k = 0
            for ti, (t0, n, seg) in enumerate(cfg.tiles):
                if last and seg == 1:
                    continue
                s = ti % 2
                self.dma("sp", at[s][:, :, 0:n], aview[:, :, t0:t0 + n], w=[b_at[s]])
                self.dma("sp", xt[s][:, :, 0:n], xmview[:, :, t0:t0 + n], w=[b_xt[s]])
                for oc in range(KC):
                    pb = k % 6
                    k += 1
                    for kc in range(22):
                        self.mm(psb[pb][:, 0:n], Wf[:, kc, oc * 128:(oc + 1) * 128], at[s][:, kc, 0:n], start=(kc == 0), stop=(kc == 21),
                                r=[b_W, b_at[s]], w=[b_ps[pb]])
                    self.act(fo[:, oc, 0:n], psb[pb][:, 0:n], AF.Copy, [b_ps[pb]], [b_fo])
                self.rstd_tile(fo, b_fo, n, sq, b_sq, rs, b_rs, 6 + ti % 2)
                for oc in range(KC):
                    self.tt("dve", fo[:, oc, 0:n], fo[:, oc, 0:n], rs[:, 0:n], ALU.mult, r=[b_fo, b_rs], w=[b_fo])
                    self.stt(xt[s][:, oc, 0:n], fo[:, oc, 0:n], cols[:, oc, seg:seg + 1], xt[s][:, oc, 0:n], ALU.mult, ALU.add,
                             r=[b_fo, b_cols, b_xt[s]], w=[b_xt[s]])
                if last:
                    dst = dr["outT"].rearrange("(kc p) n -> p kc n", p=128)[:, :, t0 - LCTX:t0 - LCTX + n]
                else:
                    dst = dr["xcur"].rearrange("(kc p) n -> p kc n", p=128)[:, :, t0:t0 + n]
                self.dma("pool", dst, xt[s][:, :, 0:n], r=[b_xt[s]])
            self.em.flush()

    def phase_mod(self, l, cvec, ada_w, ada_b, mod, b_mod):
        em = self.em
        psb, b_ps = self.psb, self.b_ps
        with contextlib.ExitStack() as st:
            cv = self.sb(st, "cv", [128, KC, 2], F32)
            cs = self.sb(st, "cs", [128, KC, 2], BF16)
            sg = self.sb(st, "cv_sg", [128, KC, 2], F32)
            ab = self.sb(st, "ab", [128, 48], F32)
            wf = [self.sb(st, f"adaw_f{i}", [128, KC, 512], F32) for i in range(2)]
            wb = [self.sb(st, f"adaw_b{i}", [128, KC, 512], BF16) for i in range(2)]
            b_cv, b_cs, b_ab, b_sg = Buf("cv"), Buf("cs"), Buf("ab"), Buf("sg")
            b_wf, b_wb = mkbufs("wf", 2), mkbufs("wb", 2)
            em.dma("sp", lambda e: e.dma_start(out=cv[:], in_=cvec[:, :, :]), writes=[b_cv])
            em.dma("sp", lambda e: e.dma_start(out=ab[:], in_=ada_b[l]), writes=[b_ab])
            em.op("act", lambda e: e.activation(out=sg[:], in_=cv[:], func=AF.Sigmoid), reads=[b_cv], writes=[b_sg])
            em.op("dve", lambda e: e.tensor_tensor(out=cs[:], in0=cv[:], in1=sg[:], op=ALU.mult),
                  reads=[b_cv, b_sg], writes=[b_cs])
            wview = ada_w[l].rearrange("(kc p) n -> p kc n", p=128)
            for g in range(12):
                s = g % 2
                em.dma("sp", lambda e, s=s, g=g: e.dma_start(out=wf[s][:], in_=wview[:, :, g * 512:(g + 1) * 512]),
                       writes=[b_wf[s]])
                em.op("pool", lambda e, s=s: e.tensor_copy(out=wb[s][:], in_=wf[s][:]),
                      reads=[b_wf[s]], writes=[b_wb[s]])
                pb = g % 8
                for q in range(4):
                    i = g * 4 + q
                    for kc in range(KC):
                        em.op("pe", lambda e, s=s, q=q, kc=kc, pb=pb: e.matmul(
                            psb[pb][:, q * 2:q * 2 + 2], lhsT=wb[s][:, kc, q * 128:(q + 1) * 128],
                            rhs=cs[:, kc, :], start=(kc == 0), stop=(kc == KC - 1)),
                            reads=[b_wb[s], b_cs], writes=[b_ps[pb]], defer=(kc != KC - 1))
                for q in range(4):
                    i = g * 4 + q
                    em.op("dve", lambda e, q=q, i=i, pb=pb: e.tensor_scalar(
                        out=mod[:, i, :], in0=psb[pb][:, q * 2:q * 2 + 2], scalar1=ab[:, i:i + 1], scalar2=None,
                        op0=ALU.add), reads=[b_ps[pb], b_ab], writes=[b_mod])
            em.flush()

    def phase_norm_inproj(self, l, xc, norm_g, w_in, uT, mod, b_mod, ones_bf, b_ones):
        cfg = self.cfg
        em = self.em
        NT = cfg.NT
        psb, b_ps = self.psb, self.b_ps
        with contextlib.ExitStack() as st:
            hT = self.sb(st, "hT", [128, KC, NT], BF16)
            b_h = mkbufs("hT", len(cfg.tiles))
            ng = self.sb(st, "ng", [128, 4, KC], F32)
            A1 = self.sb(st, "A1", [128, KC, 2], F32)
            b_ng, b_A1 = Buf("ng"), Buf("A1")
            em.dma("sp", lambda e: e.dma_start(out=ng[:], in_=norm_g[l]), writes=[b_ng])
            for kc in range(KC):
                em.op("dve", lambda e, kc=kc: e.tensor_scalar(
                    out=A1[:, kc, :], in0=mod[:, 8 + kc, :], scalar1=1.0, scalar2=ng[:, 0, kc:kc + 1],
                    op0=ALU.add, op1=ALU.mult), reads=[b_mod, b_ng], writes=[b_A1])
            with contextlib.ExitStack() as st2:
                xt = [self.sb(st2, f"xt{i}", [128, KC, 512], F32) for i in range(2)]
                sq = [self.sb(st2, f"sq{i}", [128, KC, 512], BF16) for i in range(2)]
                rs = [self.sb(st2, f"rs{i}", [128, 512], F32) for i in range(2)]
                tmp = [self.sb(st2, f"tmp{i}", [128, KC, 512], F32) for i in range(2)]
                b_xt, b_sq, b_rs, b_tmp = mkbufs("xt", 2), mkbufs("sq", 2), mkbufs("rs", 2), mkbufs("tmp", 2)
                xview = xc.rearrange("(kc p) n -> p kc n", p=128)
                for ti, (t0, n, seg) in enumerate(cfg.tiles):
                    s = ti % 2
                    pb = ti % 8
                    em.dma("sp", lambda e, s=s, t0=t0, n=n: e.dma_start(out=xt[s][:, :, 0:n], in_=xview[:, :, t0:t0 + n]),
                           writes=[b_xt[s]])
                    em.op("act", lambda e, s=s, n=n: e.activation(out=sq[s][:, :, 0:n], in_=xt[s][:, :, 0:n], func=AF.Square),
                          reads=[b_xt[s]], writes=[b_sq[s]])
                    for kc in range(KC):
                        em.op("pe", lambda e, s=s, n=n, kc=kc, pb=pb: e.matmul(
                            psb[pb][:, 0:n], lhsT=ones_bf[:], rhs=sq[s][:, kc, 0:n], start=(kc == 0), stop=(kc == KC - 1)),
                            reads=[b_sq[s], b_ones], writes=[b_ps[pb]], defer=(kc != KC - 1))
                    em.op("act", lambda e, s=s, n=n, pb=pb: e.activation(
                        out=rs[s][:, 0:n], in_=psb[pb][:, 0:n], func=AF.Sqrt, scale=1.0 / D, bias=1e-6),
                        reads=[b_ps[pb]], writes=[b_rs[s]])
                    em.op("dve", lambda e, s=s, n=n: e.reciprocal(out=rs[s][:, 0:n], in_=rs[s][:, 0:n]),
                          reads=[b_rs[s]], writes=[b_rs[s]])
                    for kc in range(KC):
                        em.op("dve", lambda e, s=s, n=n, kc=kc: e.tensor_tensor(
                            out=tmp[s][:, kc, 0:n], in0=xt[s][:, kc, 0:n], in1=rs[s][:, 0:n], op=ALU.mult),
                            reads=[b_xt[s], b_rs[s]], writes=[b_tmp[s]])
                        em.op("act", lambda e, s=s, n=n, kc=kc, t0=t0, seg=seg: e.activation(
                            out=hT[:, kc, t0:t0 + n], in_=tmp[s][:, kc, 0:n], func=AF.Identity,
                            scale=A1[:, kc, seg:seg + 1], bias=mod[:, kc, seg:seg + 1]),
                            reads=[b_tmp[s], b_A1, b_mod], writes=[b_h[ti]])
                em.flush()
            with contextlib.ExitStack() as st2:
                wf = [self.sb(st2, f"wf{i}", [128, KC, 128], F32) for i in range(2)]
                wb = [self.sb(st2, f"wb{i}", [128, KC, 128], BF16) for i in range(2)]
                stg = [self.sb(st2, f"stg{i}", [128, NT], F32) for i in range(2)]
                b_wf, b_wb, b_stg = mkbufs("wf", 2), mkbufs("wb", 2), mkbufs("stg", 2)
                wview = w_in[l].rearrange("(kc p) n -> p kc n", p=128)
                k = 0
                for j in range(self.NCH):
                    s = j % 2
                    em.dma("sp", lambda e, s=s, j=j: e.dma_start(out=wf[s][:], in_=wview[:, :, j * 128:(j + 1) * 128]),
                           writes=[b_wf[s]])
                    em.op("pool", lambda e, s=s: e.tensor_copy(out=wb[s][:], in_=wf[s][:]),
                          reads=[b_wf[s]], writes=[b_wb[s]])
                    for ti, (t0, n, seg) in enumerate(cfg.tiles):
                        pb = k % 8
                        k += 1
                        for kc in range(KC):
                            em.op("pe", lambda e, s=s, n=n, kc=kc, pb=pb, t0=t0: e.matmul(
                                psb[pb][:, 0:n], lhsT=wb[s][:, kc, :], rhs=hT[:, kc, t0:t0 + n],
                                start=(kc == 0), stop=(kc == KC - 1)),
                                reads=[b_wb[s], b_h[ti]], writes=[b_ps[pb]], defer=(kc != KC - 1))
                        if k % 2 == 0:
                            em.op("act", lambda e, s=s, n=n, pb=pb, t0=t0: e.activation(
                                out=stg[s][:, t0:t0 + n], in_=psb[pb][:, 0:n], func=AF.Copy),
                                reads=[b_ps[pb]], writes=[b_stg[s]])
                        else:
                            em.op("dve", lambda e, s=s, n=n, pb=pb, t0=t0: e.tensor_copy(
                                out=stg[s][:, t0:t0 + n], in_=psb[pb][:, 0:n]),
                                reads=[b_ps[pb]], writes=[b_stg[s]])
                    em.dma("pool", lambda e, s=s, j=j: e.dma_start(out=uT[j * 128:(j + 1) * 128, :], in_=stg[s][:]),
                           reads=[b_stg[s]], writes=[])
                em.flush()


def na_blocks(rows):
    nqb = rows // 8
    types = {}
    blocks = []
    for qb in range(nqb):
        q0 = qb * 8
        rs = lambda r: min(max(r - 4, 0), rows - 8)
        lo, hi = rs(q0), rs(q0 + 7) + 8
        cls = "f" if qb == 0 else ("l" if qb == nqb - 1 else "i")
        items = []
        for kr0 in range(lo, hi, 2):
            delta = kr0 - q0
            key = (cls, delta)
            if key not in types:
                types[key] = (len(types), q0, kr0)
            items.append((kr0, types[key][0], (delta + 4) // 2))
        blocks.append(items)
    return blocks, len(types)


def blocks_type(blocks, qb, kr0):
    for (k, ty, di) in blocks[qb]:
        if k == kr0:
            return ty
    raise KeyError


def na_mask_np(rows):
    nqb = rows // 8
    out = {}
    for qb in range(nqb):
        q0 = qb * 8
        rsf = lambda r: min(max(r - 4, 0), rows - 8)
        lo, hi = rsf(q0), rsf(q0 + 7) + 8
        cls = "f" if qb == 0 else ("l" if qb == nqb - 1 else "i")
        for kr0 in range(lo, hi, 2):
            key = (cls, kr0 - q0)
            if key in out:
                continue
            krow = kr0 + np.arange(2)[:, None, None, None]
            kc = np.arange(64)[None, :, None, None]
            qrow = q0 + np.arange(8)[None, None, :, None]
            qc = np.arange(64)[None, None, None, :]
            rs = np.clip(qrow - 4, 0, rows - 8)
            cs = np.clip(qc - 8, 0, 48)
            ok = (krow >= rs) & (krow < rs + 8) & (kc >= cs) & (kc < cs + 16)
            out[key] = np.where(ok, 0.0, -30000.0).reshape(128, 512).astype(np.float32)
    return np.stack(list(out.values()), axis=0)


def rope_tables(T):
    half = 32
    freqs = 10000.0 ** (-np.arange(0, half, 2, dtype=np.float32) / half)
    pos = np.arange(T)
    prow, pcol = pos // 64, pos % 64
    cos = np.zeros((64, T), np.float32)
    sin = np.zeros((64, T), np.float32)
    for d in range(64):
        p = prow if d < 32 else pcol
        dd = d % 32
        ang = p.astype(np.float32) * freqs[dd % 16]
        cos[d] = np.cos(ang)
        sin[d] = -np.sin(ang) if dd < 16 else np.sin(ang)
    return np.concatenate([cos, cos], 0), np.concatenate([sin, sin], 0)


def input_shapes(cfg):
    Ld, NT, T = cfg.depth, cfg.NT, cfg.T
    return {
        "xc": [D, NT], "cvec": [128, KC, 2],
        "ada_w": [Ld, D, 6 * D], "ada_b": [Ld, 128, 48], "norm_g": [Ld, 128, 4, KC],
        "w_in": [Ld, D, NIN_X],
        "lru_conv_w": [Ld, 128, 8, 4], "lru_conv_b": [Ld, 128, 8],
        "lru_gate_a_w": [Ld, 2, 16, 64, 64], "lru_gate_x_w": [Ld, 2, 16, 64, 64],
        "lru_gate_b": [Ld, 128, 2, 2, 8], "lru_lambda": [Ld, 128, 2, 8],
        "ident": [128, 128], "rope_cos": [128, T], "rope_sin": [128, T],
        "na_mask": [na_blocks(T // 64)[1], 128, 512], "rpb_pad": [Ld, 16, 24, 128],
        "bdones": [128, 128], "istack": [128, 64], "rk_mask": [2, 128, 1152],
        "rwkv_mu": [Ld, 128, 26, 2], "rwkv_w0a0": [Ld, 128, 2, 2, 8], "rwkv_vec": [Ld, 128, 5, 8],
        "w_branch": [Ld, 3, D, D], "w_out": [Ld, D, D], "ffn_w_in": [Ld, D, 2 * DFF], "ffn_w_out": [Ld, DFF, D],
        "ffn_conv_w": [Ld, 128, 22, 3], "ffn_conv_b": [Ld, 128, 22],
        "rwkv_w_up": [Ld, 2, 64, 1024], "rwkv_a_up": [Ld, 2, 64, 1024], "rwkv_g_up": [Ld, 128, 1024],
    }


def colfmt(v, n):
    v = np.asarray(v)
    return np.moveaxis(v.reshape(v.shape[:-1] + (n, 128)), -1, -2)


def rope_perm():
    idx = np.arange(1024)
    d = idx % 64
    dd = d % 32
    partner = np.where(dd < 16, idx + 16, idx - 16)
    return partner


def prep_shared(inp, cfg):
    f = lambda a: np.ascontiguousarray(a, dtype=np.float32)
    Ld = cfg.depth
    m = {}
    m["ada_w"] = f(inp["ada_w"][:Ld])
    m["ada_b"] = f(colfmt(inp["ada_b"][:Ld], 48))
    m["norm_g"] = f(colfmt(inp["norm_g"][:Ld], KC).transpose(0, 2, 1, 3))
    w_in = inp["w_in"][:Ld]
    C0 = A_COLS + 2048
    perm = rope_perm()
    wq = w_in[:, :, C0:C0 + 1024][:, :, perm]
    wk = w_in[:, :, C0 + 1024:C0 + 2048][:, :, perm]
    m["w_in"] = f(np.concatenate([w_in, wq, wk], axis=2))
    m["lru_conv_w"] = f(colfmt(inp["lru_conv_w"][:Ld], 8).transpose(0, 2, 3, 1))
    m["lru_conv_b"] = f(colfmt(inp["lru_conv_b"][:Ld], 8))
    m["lru_gate_a_w"] = f(inp["lru_gate_a_w"][:Ld])
    m["lru_gate_x_w"] = f(inp["lru_gate_x_w"][:Ld])
    gb = np.stack([inp["lru_gate_a_b"][:Ld], inp["lru_gate_x_b"][:Ld]], axis=1)
    m["lru_gate_b"] = f(colfmt(gb, 8).transpose(0, 3, 1, 2, 4))
    m["lru_lambda"] = f(colfmt(inp["lru_lambda"][:Ld], 8).transpose(0, 2, 1, 3))
    m["ident"] = np.eye(128, dtype=np.float32)
    cos, sin = rope_tables(cfg.T)
    m["rope_cos"], m["rope_sin"] = f(cos), f(sin)
    m["na_mask"] = f(na_mask_np(cfg.T // 64))
    rp = np.zeros((Ld, 16, 24, 128), np.float32)
    rp[:, :, 4:19, 48:79] = inp["na_rpb"][:Ld]
    m["rpb_pad"] = rp
    blk = np.kron(np.eye(2, dtype=np.float32), np.ones((64, 64), np.float32))
    m["bdones"] = blk
    m["istack"] = np.concatenate([np.eye(64, dtype=np.float32)] * 2, axis=0)
    i64 = np.arange(64)
    U = np.kron(np.eye(2), (i64[:, None] < i64[None, :])).astype(np.float32)
    UI = np.kron(np.eye(2), (i64[:, None] <= i64[None, :])).astype(np.float32)
    Lw, LI = U.T.copy(), UI.T.copy()
    ONE = np.ones((128, 128), np.float32)
    m32 = np.kron(np.eye(4), np.ones((32, 32))).astype(np.float32)
    fwd = np.concatenate([U * m32, UI, ONE, ONE, UI, ONE, Lw * m32, Lw, Lw * (1 - m32)], axis=1)
    bwd = np.concatenate([Lw * m32, LI, ONE, ONE, LI, ONE, U * m32, U, U * (1 - m32)], axis=1)
    m["rk_mask"] = np.stack([fwd, bwd], axis=0)
    m["rwkv_mu"] = f(colfmt(inp["rwkv_mu"][:Ld], 26).transpose(0, 2, 3, 1))
    w0a0 = np.stack([inp["rwkv_w0"][:Ld], inp["rwkv_a0"][:Ld]], axis=1)
    m["rwkv_w0a0"] = f(colfmt(w0a0, 8).transpose(0, 3, 1, 2, 4))
    vec = np.stack([inp["rwkv_k_k"][:Ld], inp["rwkv_k_a"][:Ld], inp["rwkv_r_k"][:Ld].reshape(Ld, 1024),
                    inp["rwkv_lnx_w"][:Ld], inp["rwkv_lnx_b"][:Ld]], axis=1)
    m["rwkv_vec"] = f(colfmt(vec, 8).transpose(0, 2, 1, 3))
    m["rwkv_w_up"] = f(inp["rwkv_w_up"][:Ld])
    m["rwkv_a_up"] = f(inp["rwkv_a_up"][:Ld])
    m["rwkv_g_up"] = f(inp["rwkv_g_up"][:Ld])
    for k in ("w_branch", "w_out", "ffn_w_in", "ffn_w_out"):
        m[k] = f(inp[k][:Ld])
    m["ffn_conv_w"] = f(colfmt(inp["ffn_conv_w"][:Ld], 22).transpose(0, 2, 3, 1))
    m["ffn_conv_b"] = f(colfmt(inp["ffn_conv_b"][:Ld], 22))
    return m


def prep_inputs(inp, b, cfg, shared=None):
    T = cfg.T
    f = lambda a: np.ascontiguousarray(a, dtype=np.float32)
    m = dict(shared if shared is not None else prep_shared(inp, cfg))
    m["xc"] = f(np.concatenate([inp["ctx"][b].T, inp["x"][b, :T].T], axis=1))
    cv = np.stack([inp["c"][b], inp["c_ctx"]], axis=1)
    m["cvec"] = f(cv.reshape(KC, 128, 2).transpose(1, 0, 2))
    return m


def kernel(**inputs):
    cfg = Cfg()
    bld = Builder(cfg)
    nc = bld.build()
    inp = {k: np.asarray(v) for k, v in inputs.items()}
    shared = prep_shared(inp, cfg)
    in_maps = [prep_inputs(inp, b, cfg, shared) for b in range(8)]
    res = run_bass_kernel_spmd(nc, in_maps, core_ids=list(range(8)))
    out = np.stack([r["outT"].T for r in res.results], axis=0)
    return out.astype(np.float32)
```
